# Optimizing a Trainium2 kernel written in Bass

```python
import math
import jax, jax.numpy as jnp
from jax import lax
import numpy as np

D_MODEL = 1024
BATCH = 8
SEQ = 4096
DEPTH = 2

CTX_LEN = 256
GRID_W = 64
HEAD_DIM = 64
D_MIX = D_MODEL
N_GROUPS = 4
GROUP_W = D_MIX // N_GROUPS
EPS = 1e-6
ROPE_BASE = 10000.0

SWA_HEADS = GROUP_W // HEAD_DIM
SWA_KV_HEADS = SWA_HEADS // 2
SWA_WINDOW = 128
SWA_BLOCK = 128
SWA_Q = SWA_HEADS * HEAD_DIM
SWA_KV = SWA_KV_HEADS * HEAD_DIM
SWA_COLS = SWA_Q + 2 * SWA_KV

RET_HEADS = GROUP_W // HEAD_DIM
RET_W = RET_HEADS * HEAD_DIM
RET_CHUNK = 128
RET_EPS = 1e-5
RET_COLS = 4 * RET_W

RWKV_HEADS = GROUP_W // HEAD_DIM
RWKV_W = RWKV_HEADS * HEAD_DIM
RWKV_DECAY_LORA = 64
RWKV_A_LORA = 64
RWKV_GATE_LORA = 128
RWKV_EPS = 64e-5
RWKV_COLS = 3 * RWKV_W + RWKV_DECAY_LORA + RWKV_A_LORA + RWKV_GATE_LORA

MLA_HEADS = GROUP_W // HEAD_DIM
MLA_Q_LORA = 256
MLA_KV_LORA = 128
MLA_NOPE = 64
MLA_ROPE = 32
MLA_V = HEAD_DIM
MLA_QK = MLA_NOPE + MLA_ROPE
MLA_BLOCK = 128
MLA_COLS = MLA_Q_LORA + MLA_KV_LORA + MLA_ROPE

GROUP_COLS = (SWA_COLS, RET_COLS, RWKV_COLS, MLA_COLS)
N_IN = SWA_COLS + RET_COLS + RWKV_COLS + MLA_COLS
D_FF = 4 * D_MODEL

kernel_name = 'hybrid_parallel_group_flow_block'


def split_sizes(t, sizes):
    idx = [int(i) for i in np.cumsum(sizes)[:-1]]
    return jnp.split(t, idx, axis=-1)


def rmsnorm(x, g, eps=EPS):
    xf = x.astype(jnp.float32)
    y = xf * lax.rsqrt(jnp.mean(xf * xf, axis=-1, keepdims=True) + eps)
    return y.astype(x.dtype) * g


def head_norm(o, gain, eps):
    mu = jnp.mean(o, axis=-1, keepdims=True)
    var = jnp.mean(jnp.square(o - mu), axis=-1, keepdims=True)
    y = (o - mu) * lax.rsqrt(var + eps)
    return y.reshape(o.shape[:-2] + (-1,)) * gain.astype(jnp.float32)


def modulate(h, shift, scale):
    return h * (1.0 + scale) + shift


def axial_rope_tables(n_tokens, d_rot):
    n_rows = n_tokens // GRID_W
    row, col = jnp.meshgrid(jnp.arange(n_rows, dtype=jnp.float32), jnp.arange(GRID_W, dtype=jnp.float32), indexing='ij')
    d_ax = d_rot // 2
    inv = ROPE_BASE ** (-jnp.arange(0, d_ax, 2, dtype=jnp.float32) / d_ax)
    ang = jnp.stack([row.reshape(-1)[:, None] * inv, col.reshape(-1)[:, None] * inv], axis=1)
    return jnp.cos(ang), jnp.sin(ang)


def rope(x, tables):
    cos, sin = tables
    nf = cos.shape[-1]
    shape = (1, cos.shape[0]) + (1,) * (x.ndim - 3) + (2, nf)
    c = cos.reshape(shape).astype(x.dtype)
    s = sin.reshape(shape).astype(x.dtype)
    xr = x.reshape(x.shape[:-1] + (2, 2, nf))
    x1, x2 = xr[..., 0, :], xr[..., 1, :]
    return jnp.stack([x1 * c - x2 * s, x2 * c + x1 * s], axis=-2).reshape(x.shape)


def sink_softmax(s, sink):
    m = jnp.maximum(jnp.max(s, axis=-1, keepdims=True), sink)
    p = jnp.exp(s - m)
    return p / (jnp.sum(p, axis=-1, keepdims=True) + jnp.exp(sink - m))


def swa_mixer(p, pc, sink, rope_t, emit_ctx):
    B, L, _ = p.shape
    G = SWA_HEADS // SWA_KV_HEADS
    dh = HEAD_DIM
    Bk = SWA_BLOCK
    NB = L // Bk
    scale = dh ** -0.5

    def heads(t):
        q, k, v = split_sizes(t, (SWA_Q, SWA_KV, SWA_KV))
        b, n = t.shape[:2]
        return (q.reshape(b, n, SWA_KV_HEADS, G, dh), k.reshape(b, n, SWA_KV_HEADS, dh),
                v.reshape(b, n, SWA_KV_HEADS, dh))

    q, k, v = heads(p)
    qc, kc, vc = heads(pc)
    q = rope(q, rope_t) * scale
    k = rope(k, rope_t)
    sink_hg = sink.astype(jnp.float32).reshape(SWA_KV_HEADS, G)

    pad = ((0, 0), (Bk, Bk), (0, 0), (0, 0))
    kp = jnp.pad(k, pad).reshape(B, NB + 2, Bk, SWA_KV_HEADS, dh)
    vp = jnp.pad(v, pad).reshape(B, NB + 2, Bk, SWA_KV_HEADS, dh)
    kb = jnp.concatenate([kp[:, :-2], kp[:, 1:-1], kp[:, 2:]], axis=2)
    vb = jnp.concatenate([vp[:, :-2], vp[:, 1:-1], vp[:, 2:]], axis=2)
    qb = q.reshape(B, NB, Bk, SWA_KV_HEADS, G, dh)
    s_loc = jnp.einsum('bnqhgd,bnkhd->bhgnqk', qb, kb).astype(jnp.float32)
    qpos = jnp.arange(NB)[:, None] * Bk + jnp.arange(Bk)[None, :]
    kpos = (jnp.arange(NB)[:, None] - 1) * Bk + jnp.arange(3 * Bk)[None, :]
    valid = ((jnp.abs(qpos[:, :, None] - kpos[:, None, :]) <= SWA_WINDOW)
             & (kpos[:, None, :] >= 0) & (kpos[:, None, :] < L))
    s_loc = jnp.where(valid, s_loc, -jnp.inf)
    s_ctx = jnp.einsum('bnqhgd,bchd->bhgnqc', qb, kc).astype(jnp.float32)
    probs = sink_softmax(jnp.concatenate([s_loc, s_ctx], axis=-1),
                         sink_hg[None, :, :, None, None, None]).astype(v.dtype)
    o = (jnp.einsum('bhgnqk,bnkhd->bnqhgd', probs[..., :3 * Bk], vb)
         + jnp.einsum('bhgnqc,bchd->bnqhgd', probs[..., 3 * Bk:], vc))
    y = o.reshape(B, L, SWA_Q)

    y_ctx = None
    if emit_ctx:
        qcs = qc * scale
        sc = jnp.einsum('bqhgd,bkhd->bhgqk', qcs, kc).astype(jnp.float32)
        pc_ = sink_softmax(sc, sink_hg[None, :, :, None, None]).astype(vc.dtype)
        y_ctx = jnp.einsum('bhgqk,bkhd->bqhgd', pc_, vc).reshape(B, pc.shape[1], SWA_Q)
    return y, y_ctx


def retention_scan(q, k, v, log_g, S0):
    B, H, L, d = q.shape
    C = RET_CHUNK
    N = L // C
    i = jnp.arange(C, dtype=jnp.float32)
    diff = i[:, None] - i[None, :]
    D = jnp.where(diff >= 0, jnp.exp(jnp.maximum(diff, 0.0)[None] * log_g[:, None, None]), 0.0)
    q_dec = jnp.exp((i + 1.0)[None, :] * log_g[:, None])
    k_dec = jnp.exp((C - 1.0 - i)[None, :] * log_g[:, None])
    c_dec = jnp.exp(C * log_g)
    xs = tuple(t.reshape(B, H, N, C, d).transpose(2, 0, 1, 3, 4) for t in (q, k, v))

    def step(S, qkv):
        qc, kc, vc = qkv
        s = jnp.einsum('bhid,bhjd->bhij', qc, kc) * D
        o = jnp.einsum('bhij,bhjd->bhid', s, vc) + jnp.einsum('bhid,bhde->bhie', qc * q_dec[:, :, None], S)
        S = S * c_dec[:, None, None] + jnp.einsum('bhjd,bhje->bhde', kc * k_dec[:, :, None], vc)
        return S, o

    S, o = lax.scan(step, S0, xs)
    return o.transpose(1, 2, 0, 3, 4).reshape(B, H, L, d), S


def retention_state(k, v, log_g):
    L = k.shape[2]
    dec = jnp.exp((L - 1.0 - jnp.arange(L, dtype=jnp.float32))[None, :] * log_g[:, None])
    return jnp.einsum('bhjd,bhje->bhde', k * dec[None, :, :, None], v)


def retention_mixer(p, pc, gn_g, emit_ctx):
    log_g = jnp.log1p(-jnp.exp2(-5.0 - jnp.arange(RET_HEADS, dtype=jnp.float32)))

    def heads(t):
        return t.reshape(t.shape[0], t.shape[1], RET_HEADS, HEAD_DIM).transpose(0, 2, 1, 3).astype(jnp.float32)

    def prep(t):
        q, k, v, g = split_sizes(t, (RET_W, RET_W, RET_W, RET_W))
        return heads(q), heads(k) * (HEAD_DIM ** -0.5), heads(v), g

    flip = lambda t: t[:, :, ::-1]
    q, k, v, g = prep(p)
    qc, kc, vc, gc = prep(pc)
    zeros = jnp.zeros((p.shape[0], RET_HEADS, HEAD_DIM, HEAD_DIM), jnp.float32)
    if emit_ctx:
        oc_f, s_f = retention_scan(qc, kc, vc, log_g, zeros)
        oc_b, s_b = retention_scan(flip(qc), flip(kc), flip(vc), log_g, zeros)
    else:
        s_f = retention_state(kc, vc, log_g)
        s_b = retention_state(flip(kc), flip(vc), log_g)
    o_f, _ = retention_scan(q, k, v, log_g, s_f)
    o_b, _ = retention_scan(flip(q), flip(k), flip(v), log_g, s_b)

    def finish(o, gate):
        y = head_norm(o.transpose(0, 2, 1, 3), gn_g, RET_EPS)
        return (y * jax.nn.silu(gate.astype(jnp.float32))).astype(gate.dtype)

    y = finish(o_f + flip(o_b), g)
    y_ctx = finish(oc_f + flip(oc_b), gc) if emit_ctx else None
    return y, y_ctx


def token_shift(z, mu):
    zp = jnp.pad(z, ((0, 0), (1, 1), (0, 0)))
    return z + mu * (0.5 * (zp[:, :-2] + zp[:, 2:]) - z)


def rwkv_scan(w, kh, a, v, kt, r, S0, emit):
    tm = lambda t: jnp.swapaxes(t, 0, 1)
    xs = (tm(w), tm(kh), tm(a), tm(v), tm(kt)) + ((tm(r),) if emit else ())

    def step(S, inp):
        w_t, kh_t, a_t, v_t, kt_t = inp[:5]
        S = (S * w_t[:, :, None, :]
             - jnp.einsum('bhvk,bhk->bhv', S, kh_t)[..., None] * (kh_t * a_t)[:, :, None, :]
             + v_t[..., None] * kt_t[:, :, None, :])
        o = jnp.einsum('bhvk,bhk->bhv', S, inp[5]) if emit else None
        return S, o

    S, o = lax.scan(step, S0, xs)
    return S, (tm(o) if emit else None)


def rwkv_mixer(p, pc, mu, w0, w_up, a0, a_up, g_up, kk_p, ka_p, rk, gn_g, emit_ctx):
    def heads(t):
        return t.reshape(t.shape[0], t.shape[1], RWKV_HEADS, HEAD_DIM).astype(jnp.float32)

    def prep(t):
        r, k, v, wd, ad, gd = split_sizes(token_shift(t, mu), (RWKV_W, RWKV_W, RWKV_W, RWKV_DECAY_LORA, RWKV_A_LORA, RWKV_GATE_LORA))
        kk = heads(k * kk_p)
        kh = kk * lax.rsqrt(jnp.maximum(jnp.sum(kk * kk, axis=-1, keepdims=True), 1e-12))
        dirs = []
        for d in range(2):
            w = jnp.exp(-math.exp(-0.5) * jax.nn.sigmoid((w0[d] + jnp.tanh(wd) @ w_up[d]).astype(jnp.float32)))
            a = jax.nn.sigmoid((a0[d] + ad @ a_up[d]).astype(jnp.float32))
            kt = heads(k) * (1.0 + (heads(a) - 1.0) * heads(ka_p[None, None, :]))
            dirs.append((heads(w), heads(a), kt))
        g = jax.nn.sigmoid(gd) @ g_up
        return heads(r), kh, heads(v), dirs, g

    flip = lambda t: t[:, ::-1]

    def run(r, kh, v, dirs, S0s, emit):
        outs, states = [], []
        for d in range(2):
            w, a, kt = dirs[d]
            seqs = (w, kh, a, v, kt, r)
            if d == 1:
                seqs = tuple(flip(t) for t in seqs)
            S, o = rwkv_scan(*seqs, S0s[d], emit)
            states.append(S)
            outs.append(flip(o) if (emit and d == 1) else o)
        return outs, states

    def finish(outs, r, v, dirs, g):
        y = head_norm(outs[0] + outs[1], gn_g, RWKV_EPS)
        bonus = sum(jnp.sum(r * dirs[d][2] * rk, axis=-1, keepdims=True) * v for d in range(2))
        y = y + bonus.reshape(y.shape)
        return (y * g.astype(jnp.float32)).astype(g.dtype)

    r_c, kh_c, v_c, dirs_c, g_c = prep(pc)
    zeros = jnp.zeros((p.shape[0], RWKV_HEADS, HEAD_DIM, HEAD_DIM), jnp.float32)
    outs_c, states_c = run(r_c, kh_c, v_c, dirs_c, (zeros, zeros), emit_ctx)
    r, kh, v, dirs, g = prep(p)
    outs, _ = run(r, kh, v, dirs, states_c, True)
    y = finish(outs, r, v, dirs, g)
    y_ctx = finish(outs_c, r_c, v_c, dirs_c, g_c) if emit_ctx else None
    return y, y_ctx


def mla_mixer(p, pc, q_norm_g, w_uq, kv_norm_g, w_ukv, rope_t, emit_ctx):
    H = MLA_HEADS
    scale = MLA_QK ** -0.5

    def project_q(t, rotate):
        qd = t[..., :MLA_Q_LORA]
        b, n = t.shape[:2]
        q = (rmsnorm(qd, q_norm_g) @ w_uq).reshape(b, n, H, MLA_QK)
        q_nope, q_pe = q[..., :MLA_NOPE], q[..., MLA_NOPE:]
        if rotate:
            q_pe = rope(q_pe, rope_t)
        return jnp.concatenate([q_nope, q_pe], axis=-1) * scale

    def project_kv(t, rotate):
        _, kvd, kpe = split_sizes(t, (MLA_Q_LORA, MLA_KV_LORA, MLA_ROPE))
        b, n = t.shape[:2]
        kv = (rmsnorm(kvd, kv_norm_g) @ w_ukv).reshape(b, n, H, MLA_NOPE + MLA_V)
        k_nope, v = kv[..., :MLA_NOPE], kv[..., MLA_NOPE:]
        kpe = kpe[:, :, None, :]
        if rotate:
            kpe = rope(kpe, rope_t)
        k = jnp.concatenate([k_nope, jnp.broadcast_to(kpe, (b, n, H, MLA_ROPE))], axis=-1)
        return k, v

    B, L, _ = p.shape
    q = project_q(p, True)
    k, v = project_kv(p, True)
    kc, vc = project_kv(pc, False)
    k_all = jnp.concatenate([kc, k], axis=1)
    v_all = jnp.concatenate([vc, v], axis=1)
    qb = q.reshape(B, L // MLA_BLOCK, MLA_BLOCK, H, MLA_QK).swapaxes(0, 1)

    def attend(qblk):
        s = jnp.einsum('bqhd,bkhd->bhqk', qblk, k_all).astype(jnp.float32)
        pr = jax.nn.softmax(s, axis=-1).astype(v_all.dtype)
        return jnp.einsum('bhqk,bkhd->bqhd', pr, v_all)

    y = lax.map(attend, qb).swapaxes(0, 1).reshape(B, L, H * MLA_V)
    y_ctx = None
    if emit_ctx:
        qc = project_q(pc, False)
        sc = jnp.einsum('bqhd,bkhd->bhqk', qc, kc).astype(jnp.float32)
        prc = jax.nn.softmax(sc, axis=-1).astype(vc.dtype)
        y_ctx = jnp.einsum('bhqk,bkhd->bqhd', prc, vc).reshape(B, pc.shape[1], H * MLA_V)
    return y, y_ctx


def sq_relu_mlp(h, w1, w2):
    return jnp.square(jax.nn.relu(h @ w1)) @ w2


def layer(x, xc, c, c_ctx, lp, rope_a, rope_d, emit_ctx):
    mod = jax.nn.silu(c) @ lp['ada_w'] + lp['ada_b']
    sh_m, sc_m, gt_m, sh_f, sc_f, gt_f = jnp.split(mod[:, None, :], 6, axis=-1)
    mod_c = jax.nn.silu(c_ctx) @ lp['ada_w'] + lp['ada_b']
    csh_m, csc_m, cgt_m, csh_f, csc_f, cgt_f = jnp.split(mod_c, 6, axis=-1)

    h = modulate(rmsnorm(x, lp['pre_mix_g']), sh_m, sc_m)
    hc = modulate(rmsnorm(xc, lp['pre_mix_g']), csh_m, csc_m)
    p_a, p_b, p_c, p_d = split_sizes(h @ lp['w_in'], GROUP_COLS)
    pc_a, pc_b, pc_c, pc_d = split_sizes(hc @ lp['w_in'], GROUP_COLS)

    y_a, yc_a = swa_mixer(p_a, pc_a, lp['swa_sink'], rope_a, emit_ctx)
    y_b, yc_b = retention_mixer(p_b, pc_b, lp['ret_gn_g'], emit_ctx)
    y_c, yc_c = rwkv_mixer(p_c, pc_c, lp['rwkv_mu'], lp['rwkv_w0'], lp['rwkv_w_up'], lp['rwkv_a0'], lp['rwkv_a_up'],
                           lp['rwkv_g_up'], lp['rwkv_kk'], lp['rwkv_ka'], lp['rwkv_rk'], lp['rwkv_gn_g'], emit_ctx)
    y_d, yc_d = mla_mixer(p_d, pc_d, lp['mla_q_norm_g'], lp['mla_w_uq'], lp['mla_kv_norm_g'], lp['mla_w_ukv'], rope_d, emit_ctx)

    y = jnp.concatenate([y_a, y_b, y_c, y_d], axis=-1) @ lp['w_out']
    x = x + gt_m * rmsnorm(y, lp['post_mix_g'])
    f = sq_relu_mlp(modulate(rmsnorm(x, lp['pre_mlp_g']), sh_f, sc_f), lp['mlp_w1'], lp['mlp_w2'])
    x = x + gt_f * rmsnorm(f, lp['post_mlp_g'])

    if emit_ctx:
        yc = jnp.concatenate([yc_a, yc_b, yc_c, yc_d], axis=-1) @ lp['w_out']
        xc = xc + cgt_m * rmsnorm(yc, lp['post_mix_g'])
        fc = sq_relu_mlp(modulate(rmsnorm(xc, lp['pre_mlp_g']), csh_f, csc_f), lp['mlp_w1'], lp['mlp_w2'])
        xc = xc + cgt_f * rmsnorm(fc, lp['post_mlp_g'])
    return x, xc


def setup_inputs(seed: int = 0) -> dict:
    key = jax.random.key(seed)
    ks = iter(jax.random.split(key, 40))
    nrm = lambda shape, s: jax.random.normal(next(ks), shape, jnp.float32) * s
    gain = lambda shape: 1.0 + nrm(shape, 0.05)
    return {
        'x': nrm((BATCH, SEQ, D_MODEL), 1.0),
        'c': nrm((BATCH, D_MODEL), 1.0),
        'ctx': nrm((BATCH, CTX_LEN, D_MODEL), 1.0),
        'c_ctx': nrm((D_MODEL,), 1.0),
        'ada_w': nrm((DEPTH, D_MODEL, 6 * D_MODEL), 0.5 * D_MODEL ** -0.5),
        'ada_b': nrm((DEPTH, 6 * D_MODEL), 0.01),
        'pre_mix_g': gain((DEPTH, D_MODEL)),
        'post_mix_g': gain((DEPTH, D_MODEL)),
        'pre_mlp_g': gain((DEPTH, D_MODEL)),
        'post_mlp_g': gain((DEPTH, D_MODEL)),
        'w_in': nrm((DEPTH, D_MODEL, N_IN), D_MODEL ** -0.5),
        'w_out': nrm((DEPTH, D_MIX, D_MODEL), D_MIX ** -0.5),
        'swa_sink': nrm((DEPTH, SWA_HEADS), 0.5),
        'ret_gn_g': gain((DEPTH, RET_W)),
        'rwkv_mu': jax.random.uniform(next(ks), (DEPTH, RWKV_COLS), jnp.float32),
        'rwkv_w0': nrm((DEPTH, 2, RWKV_W), 0.5),
        'rwkv_w_up': nrm((DEPTH, 2, RWKV_DECAY_LORA, RWKV_W), 0.1),
        'rwkv_a0': nrm((DEPTH, 2, RWKV_W), 0.5),
        'rwkv_a_up': nrm((DEPTH, 2, RWKV_A_LORA, RWKV_W), 0.1),
        'rwkv_g_up': nrm((DEPTH, RWKV_GATE_LORA, RWKV_W), RWKV_GATE_LORA ** -0.5),
        'rwkv_kk': gain((DEPTH, RWKV_W)),
        'rwkv_ka': gain((DEPTH, RWKV_W)),
        'rwkv_rk': nrm((DEPTH, RWKV_HEADS, HEAD_DIM), 0.1),
        'rwkv_gn_g': gain((DEPTH, RWKV_W)),
        'mla_q_norm_g': gain((DEPTH, MLA_Q_LORA)),
        'mla_w_uq': nrm((DEPTH, MLA_Q_LORA, MLA_HEADS * MLA_QK), MLA_Q_LORA ** -0.5),
        'mla_kv_norm_g': gain((DEPTH, MLA_KV_LORA)),
        'mla_w_ukv': nrm((DEPTH, MLA_KV_LORA, MLA_HEADS * (MLA_NOPE + MLA_V)), MLA_KV_LORA ** -0.5),
        'mlp_w1': nrm((DEPTH, D_MODEL, D_FF), D_MODEL ** -0.5),
        'mlp_w2': nrm((DEPTH, D_FF, D_MODEL), D_FF ** -0.5),
    }


def reference(x, c, ctx, c_ctx, ada_w, ada_b, pre_mix_g, post_mix_g, pre_mlp_g, post_mlp_g, w_in, w_out,
              swa_sink, ret_gn_g, rwkv_mu, rwkv_w0, rwkv_w_up, rwkv_a0, rwkv_a_up, rwkv_g_up, rwkv_kk, rwkv_ka,
              rwkv_rk, rwkv_gn_g, mla_q_norm_g, mla_w_uq, mla_kv_norm_g, mla_w_ukv, mlp_w1, mlp_w2):
    L = x.shape[1]
    rope_a = axial_rope_tables(L, HEAD_DIM)
    rope_d = axial_rope_tables(L, MLA_ROPE)
    xc = ctx
    for l in range(DEPTH):
        lp = dict(ada_w=ada_w[l], ada_b=ada_b[l], pre_mix_g=pre_mix_g[l], post_mix_g=post_mix_g[l],
                  pre_mlp_g=pre_mlp_g[l], post_mlp_g=post_mlp_g[l], w_in=w_in[l], w_out=w_out[l],
                  swa_sink=swa_sink[l], ret_gn_g=ret_gn_g[l], rwkv_mu=rwkv_mu[l], rwkv_w0=rwkv_w0[l],
                  rwkv_w_up=rwkv_w_up[l], rwkv_a0=rwkv_a0[l], rwkv_a_up=rwkv_a_up[l], rwkv_g_up=rwkv_g_up[l],
                  rwkv_kk=rwkv_kk[l], rwkv_ka=rwkv_ka[l], rwkv_rk=rwkv_rk[l], rwkv_gn_g=rwkv_gn_g[l],
                  mla_q_norm_g=mla_q_norm_g[l], mla_w_uq=mla_w_uq[l], mla_kv_norm_g=mla_kv_norm_g[l],
                  mla_w_ukv=mla_w_ukv[l], mlp_w1=mlp_w1[l], mlp_w2=mlp_w2[l])
        x, xc = layer(x, xc, c, c_ctx, lp, rope_a, rope_d, l < DEPTH - 1)
    return x
```

```python
import contextlib
import math
import numpy as np
import ml_dtypes
import concourse.bass as bass
import concourse.mybir as mybir
from concourse.bass_utils import run_bass_kernel_spmd

F32 = mybir.dt.float32
BF16 = mybir.dt.bfloat16
AF = mybir.ActivationFunctionType
ALU = mybir.AluOpType
AX = mybir.AxisListType

D = 1024
L = 4096
LC = 256
T = L + LC
NT = T // 128
NCT = LC // 128
DEPTH = 2
N_IN = 2976
DFF = 4096
EPS = 1e-6
NDMA = 6


class Ctx:
    ENG = ('pe', 'act', 'dve', 'pool', 'sp')

    def __init__(self, nc):
        self.nc = nc
        self.es = contextlib.ExitStack()
        self.eng = dict(pe=nc.tensor, act=nc.scalar, dve=nc.vector, pool=nc.gpsimd, sp=nc.sync)
        self.semh = {}
        self.cnt = {}
        for e in self.ENG:
            self.semh[e] = self.es.enter_context(nc.semaphore("sem_" + e))
            self.cnt[e] = 0
        self.known = {e: {} for e in self.ENG}
        self.dq = {}
        for q in ('sp', 'act', 'pool'):
            sems = []
            for i in range(NDMA):
                name = f"dma_{q}_{i}"
                self.semh[name] = self.es.enter_context(nc.semaphore(name))
                sems.append(name)
            self.dq[q] = dict(sems=sems, n=0)
        self.lastw = {}
        self.rd = {}
        self.subs = {}
        self.ninst = 0
        self.psum_names = set()
        self.bank_last = {}

    def sb(self, name, shape, dt=F32):
        self.uid = getattr(self, 'uid', 0) + 1
        return self.es.enter_context(self.nc.sbuf_tensor(f"{name}_{self.uid}", list(shape), dt))

    def ps(self, name, shape, dt=F32):
        self.uid = getattr(self, 'uid', 0) + 1
        nbytes = int(np.prod(shape[1:])) * (2 if dt == BF16 else 4)
        assert nbytes == 2048, "PSUM tensors must be exactly one bank (collision tracking is per tensor)"
        self.psum_names.add(f"{name}_{self.uid}")
        return self.es.enter_context(self.nc.psum_tensor(f"{name}_{self.uid}", list(shape), dt))

    @staticmethod
    def _key(x):
        if isinstance(x, tuple):
            a, sub = x
        else:
            a, sub = x, None
        name = a if isinstance(a, str) else getattr(a, 'tensor', a).name
        return (name, sub)

    def _dep_keys(self, key):
        name, sub = key
        if sub is None:
            return [(name, None)] + [(name, s) for s in self.subs.get(name, ())]
        return [(name, None), (name, sub)]

    def _wait(self, e, src, val):
        if self.known[e].get(src, 0) >= val:
            return
        self.eng[e].wait_ge(self.semh[src], val)
        self.known[e][src] = val

    def _sync(self, e, reads, writes):
        deps = {}

        def add(st):
            if st is not None:
                deps[st[0]] = max(deps.get(st[0], 0), st[1])
        same = 0
        for r in reads:
            for k in self._dep_keys(self._key(r)):
                st = self.lastw.get(k)
                add(st)
                if st is not None and st[0] == e:
                    same = max(same, st[1])
        for w in writes:
            for k in self._dep_keys(self._key(w)):
                st = self.lastw.get(k)
                add(st)
                if st is not None and st[0] == e:
                    same = max(same, st[1])
                for src, val in self.rd.get(k, {}).items():
                    add((src, val))
                    if src == e:
                        same = max(same, val)
        for src, val in deps.items():
            if src == e:
                continue
            self._wait(e, src, val)
        if same and e != 'pe':
            self._wait(e, e, same)

    def _record(self, reads, writes, stamp):
        for r in reads:
            k = self._key(r)
            d = self.rd.setdefault(k, {})
            d[stamp[0]] = max(d.get(stamp[0], 0), stamp[1])
        for w in writes:
            k = self._key(w)
            name, sub = k
            self.lastw[k] = stamp
            self.rd[k] = {}
            if sub is None:
                for s in self.subs.get(name, ()):
                    self.lastw[(name, s)] = stamp
                    self.rd[(name, s)] = {}
            else:
                self.subs.setdefault(name, set()).add(sub)

    def op(self, e, reads, writes, fn, rk=None, wk=None):
        if rk is not None:
            reads = list(rk)
        if wk is not None:
            writes = list(wk)
        self._sync(e, reads, writes)
        banks = set()
        for x in list(reads) + list(writes):
            nm = self._key(x)[0]
            if nm in self.psum_names:
                banks.add(nm)
        for nm in banks:
            for src, val in self.bank_last.get(nm, {}).items():
                if src != e:
                    self._wait(e, src, val)
        ins = fn(self.eng[e])
        self.cnt[e] += 1
        ins.then_inc(self.semh[e], 1)
        self._record(reads, writes, (e, self.cnt[e]))
        for nm in banks:
            self.bank_last.setdefault(nm, {})[e] = self.cnt[e]
        self.ninst += 1
        return ins

    def dma(self, q, out, in_, rk=None, wk=None, **kw):
        d = self.dq[q]
        i = d['n']
        src = d['sems'][i % NDMA]
        if i >= NDMA:
            self._wait(q, src, 16 * (i // NDMA))
        reads = rk if isinstance(rk, list) else [rk if rk is not None else in_]
        writes = wk if isinstance(wk, list) else [wk if wk is not None else out]
        self._sync(q, reads, writes)
        self.eng[q].dma_start(out=out, in_=in_, **kw).then_inc(self.semh[src], 16)
        d['n'] += 1
        self._record(reads, writes, (src, 16 * (i // NDMA + 1)))
        self.ninst += 1

    def barrier(self):
        targets = {}
        for q, d in self.dq.items():
            for j, src in enumerate(d['sems']):
                n = (d['n'] - j + NDMA - 1) // NDMA
                if n > 0:
                    targets[src] = 16 * n
        for e in self.ENG:
            if self.cnt[e] > 0:
                targets[e] = self.cnt[e]
        for e in self.ENG:
            for src, val in targets.items():
                self._wait(e, src, val)

    @contextlib.contextmanager
    def scope(self):
        outer = self.es
        with contextlib.ExitStack() as es:
            self.es = es
            try:
                yield
            finally:
                self.barrier()
                self.es = outer

    def finish(self):
        for q, d in self.dq.items():
            for j, src in enumerate(d['sems']):
                n = (d['n'] - j + NDMA - 1) // NDMA
                if n > 0:
                    self._wait('sp', src, 16 * n)
        for e in self.ENG:
            if e != 'sp' and self.cnt[e] > 0:
                self._wait('sp', e, self.cnt[e])

    def mm(self, out, lhsT, rhs, start=True, stop=True, r=(), w=(), rk=None, wk=None, **kw):
        return self.op('pe', [lhsT, rhs] + list(r), [out] + list(w),
                       lambda e: e.matmul(out, lhsT=lhsT, rhs=rhs, start=start, stop=stop, **kw), rk=rk, wk=wk)

    def tr(self, out, in_, ident, r=(), w=(), rk=None, wk=None):
        return self.op('pe', [in_, ident] + list(r), [out] + list(w),
                       lambda e: e.transpose(out, in_, ident), rk=rk, wk=wk)

    def act(self, out, in_, func, bias=None, scale=None, accum_out=None, r=(), w=(), rk=None, wk=None):
        kw = {}
        reads = [in_] + list(r)
        writes = [out] + list(w)
        if bias is not None:
            kw['bias'] = bias
            if not isinstance(bias, (int, float)):
                reads.append(bias)
        if scale is not None:
            kw['scale'] = scale
            if not isinstance(scale, (int, float)):
                reads.append(scale)
        if accum_out is not None:
            kw['accum_out'] = accum_out
            writes.append(accum_out)
        return self.op('act', reads, writes, lambda e: e.activation(out=out, in_=in_, func=func, **kw), rk=rk, wk=wk)

    def ts(self, e, out, in0, s1, s2, op0, op1=None, accum_out=None, r=(), w=(), rk=None, wk=None):
        reads = [in0] + list(r)
        writes = [out] + list(w)
        for s in (s1, s2):
            if s is not None and not isinstance(s, (int, float)):
                reads.append(s)
        kw = {}
        if op1 is not None:
            kw['op1'] = op1
        if accum_out is not None:
            kw['accum_out'] = accum_out
            writes.append(accum_out)
        return self.op(e, reads, writes,
                       lambda g: g.tensor_scalar(out=out, in0=in0, scalar1=s1, scalar2=s2, op0=op0, **kw), rk=rk, wk=wk)

    def tt(self, e, out, in0, in1, op, r=(), w=(), rk=None, wk=None):
        return self.op(e, [in0, in1] + list(r), [out] + list(w),
                       lambda g: g.tensor_tensor(out=out, in0=in0, in1=in1, op=op), rk=rk, wk=wk)

    def stt(self, out, in0, scalar, in1, op0, op1, r=(), w=(), rk=None, wk=None):
        reads = [in0, in1] + list(r)
        if not isinstance(scalar, (int, float)):
            reads.append(scalar)
        return self.op('dve', reads, [out] + list(w),
                       lambda g: g.scalar_tensor_tensor(out=out, in0=in0, scalar=scalar, in1=in1, op0=op0, op1=op1), rk=rk, wk=wk)

    def cp(self, e, out, in_, r=(), w=(), rk=None, wk=None):
        if e == 'act':
            return self.act(out, in_, AF.Copy, r=r, w=w, rk=rk, wk=wk)
        return self.op(e, [in_] + list(r), [out] + list(w), lambda g: g.tensor_copy(out=out, in_=in_), rk=rk, wk=wk)

    def memset(self, e, out, val, w=()):
        return self.op(e, [], [out] + list(w), lambda g: g.memset(out, val))

    def red(self, e, out, in_, op, axis=AX.X, r=(), w=()):
        return self.op(e, [in_] + list(r), [out] + list(w),
                       lambda g: g.tensor_reduce(out=out, in_=in_, axis=axis, op=op))


def load_cols(C, ident_f, items):
    with C.scope():
        st = C.sb("ldst", [128, 128], F32)
        ps = C.ps("ldps", [128, 512], F32)
        r = 0
        plan = []
        for dst, src in items:
            n = src.shape[0] // 128
            C.dma('sp', st[r:r + n, :], src.rearrange("(k p) -> k p", p=128))
            plan.append((dst, r, n))
            r += n
        assert r <= 128
        C.tr(ps[:, 0:r], st[0:r, :], ident_f[0:r, 0:r])
        for dst, r0, n in plan:
            C.cp('dve', dst, ps[:, r0:r0 + n])

def _rope_tables(n_tokens, d_rot, grid_w=64, base=10000.0):
    n_rows = n_tokens // grid_w
    row, col = np.meshgrid(np.arange(n_rows, dtype=np.float32), np.arange(grid_w, dtype=np.float32), indexing='ij')
    d_ax = d_rot // 2
    inv = (np.float32(base) ** (-np.arange(0, d_ax, 2, dtype=np.float32) / np.float32(d_ax))).astype(np.float32)
    ang = np.stack([row.reshape(-1)[:, None] * inv, col.reshape(-1)[:, None] * inv], axis=1).astype(np.float32)
    return np.cos(ang).astype(np.float32), np.sin(ang).astype(np.float32)


def make_consts():
    c = {}
    c['ident_f'] = np.eye(128, dtype=np.float32)
    c['ident_b'] = np.eye(128, dtype=np.float32).astype(ml_dtypes.bfloat16)
    lg = np.log1p(-np.exp2(-5.0 - np.arange(4, dtype=np.float32))).astype(np.float32)
    i = np.arange(128, dtype=np.float32)
    qft = np.zeros((128, 2, 128), np.float32)
    qbt = np.zeros((128, 2, 128), np.float32)
    gcc = np.zeros((128, 2), np.float32)
    for cc in range(2):
        for hh in range(2):
            h = 2 * cc + hh
            qft[hh * 64:(hh + 1) * 64, cc, :] = np.exp((i + 1.0) * lg[h])[None, :]
            qbt[hh * 64:(hh + 1) * 64, cc, :] = np.exp((128.0 - i) * lg[h])[None, :]
            gcc[hh * 64:(hh + 1) * 64, cc] = np.exp(np.float32(128.0) * lg[h])
    cos_a, sin_a = _rope_tables(L, 64)
    ct = np.zeros((128, L), np.float32)
    st = np.zeros((128, L), np.float32)
    pm = np.zeros((128, 128), np.float32)
    for blk in range(2):
        for ax in range(2):
            for half in range(2):
                for f in range(16):
                    p = blk * 64 + ax * 32 + half * 16 + f
                    ct[p, :] = cos_a[:, ax, f]
                    st[p, :] = sin_a[:, ax, f]
                    if half == 0:
                        pm[blk * 64 + ax * 32 + 16 + f, p] = -1.0
                    else:
                        pm[blk * 64 + ax * 32 + f, p] = 1.0
    c['swa_ct'] = ct
    c['swa_st'] = st
    c['swa_pm'] = pm.astype(ml_dtypes.bfloat16)
    qi = np.arange(128)[:, None]
    kj = np.arange(128)[None, :]
    NEG = np.float32(-30000.0)
    mk = np.zeros((128, 384), np.float32)
    mk[:, 0:128] = np.where(kj >= qi, 0.0, NEG)
    mk[:, 256:384] = np.where(kj <= qi, 0.0, NEG)
    c['swa_mask'] = mk
    rm = np.ones((128, 512), np.float32)
    rm[:, 0::128] = 0.0
    c['rw_resetm'] = rm
    blk = np.zeros((128, 128), np.float32)
    blk[0:64, 0:64] = 1.0
    blk[64:128, 64:128] = 1.0
    c['rw_blk'] = blk.astype(ml_dtypes.bfloat16)
    blk2 = np.zeros((128, 2), np.float32)
    blk2[0:64, 0] = 1.0
    blk2[64:128, 1] = 1.0
    c['rw_blk2'] = blk2.astype(ml_dtypes.bfloat16)
    si = np.arange(128)[:, None]
    ti = np.arange(128)[None, :]
    m4 = np.zeros((2, 128, 4, 128), np.float32)
    mnt = np.zeros((2, 128, 2, 128), np.float32)
    for dd in range(2):
        strict = (si < ti) if dd == 0 else (si > ti)
        incl = (si <= ti) if dd == 0 else (si >= ti)
        m4[dd, :, 0, :] = strict
        m4[dd, :, 1, :] = incl
        m4[dd, :, 2, :] = -1.0 * strict
        m4[dd, :, 3, :] = incl
        mnt[dd, :, 0, :] = -1.0 * strict.T
        mnt[dd, :, 1, :] = -1.0 * strict.T
    c['rw_mask4'] = m4.reshape(2, 128, 512)
    c['rw_masknt'] = mnt.reshape(2, 128, 256)
    cos_d, sin_d = _rope_tables(L, 32)
    c['mla_cos'] = np.ascontiguousarray(cos_d.reshape(L // 128, 128, 16).transpose(1, 0, 2).reshape(128, (L // 128) * 16))
    c['mla_sin'] = np.ascontiguousarray(sin_d.reshape(L // 128, 128, 16).transpose(1, 0, 2).reshape(128, (L // 128) * 16))
    c['ret_qft'] = qft
    c['ret_qbt'] = qbt
    c['ret_gcc'] = gcc
    kf = np.zeros((128, 4, 64), np.float32)
    kb = np.zeros((128, 4, 64), np.float32)
    dt = np.zeros((128, 4, 128), np.float32)
    for h in range(4):
        kf[:, h, :] = (np.exp((127.0 - i) * lg[h]) * 0.125)[:, None]
        kb[:, h, :] = (np.exp(i * lg[h]) * 0.125)[:, None]
        dt[:, h, :] = np.exp(np.abs(i[:, None] - i[None, :]) * lg[h]) * (1.0 + np.eye(128, dtype=np.float32))
    c['ret_kf'] = kf.reshape(128, 256)
    c['ret_kb'] = kb.reshape(128, 256)
    c['ret_dt'] = dt.reshape(128, 512)
    return c


CONST_SPECS = None


def build_program(debug=None, debug_mixers=None):
    nc = bass.Bass("TRN2", target_bir_lowering=False)
    C = Ctx(nc)
    dram_in = {}

    def din(name, shape, dt=F32):
        dram_in[name] = nc.dram_tensor(name, list(shape), dt, kind="ExternalInput").ap()
        return dram_in[name]

    x_in = din('x', [L, D])
    c_in = din('c', [D])
    ctx_in = din('ctx', [LC, D])
    cctx_in = din('c_ctx', [D])
    W = {}
    wshapes = dict(ada_w=[DEPTH, D, 6 * D], ada_b=[DEPTH, 6 * D], pre_mix_g=[DEPTH, D], post_mix_g=[DEPTH, D],
                   pre_mlp_g=[DEPTH, D], post_mlp_g=[DEPTH, D], w_in=[DEPTH, D, N_IN], w_out=[DEPTH, D, D],
                   swa_sink=[DEPTH, 4], ret_gn_g=[DEPTH, 256], rwkv_mu=[DEPTH, 1024], rwkv_w0=[DEPTH, 2, 256],
                   rwkv_w_up=[DEPTH, 2, 64, 256], rwkv_a0=[DEPTH, 2, 256], rwkv_a_up=[DEPTH, 2, 64, 256],
                   rwkv_g_up=[DEPTH, 128, 256], rwkv_kk=[DEPTH, 256], rwkv_ka=[DEPTH, 256], rwkv_rk=[DEPTH, 4, 64],
                   rwkv_gn_g=[DEPTH, 256], mla_q_norm_g=[DEPTH, 256], mla_w_uq=[DEPTH, 256, 384],
                   mla_kv_norm_g=[DEPTH, 128], mla_w_ukv=[DEPTH, 128, 512], mlp_w1=[DEPTH, D, DFF],
                   mlp_w2=[DEPTH, DFF, D])
    for k, s in wshapes.items():
        W[k] = din(k, s)
    K = {}
    consts = make_consts()
    for k, v in consts.items():
        K[k] = din('k_' + k, v.shape, BF16 if v.dtype == ml_dtypes.bfloat16 else F32)
    out = nc.dram_tensor("out", [L, D], F32, kind="ExternalOutput").ap()
    dbg = {}

    def dbg_out(name, shape, dt=F32):
        dbg[name] = nc.dram_tensor(name, list(shape), dt, kind="ExternalOutput").ap()
        return dbg[name]

    with C.es:
        def dscr(name, shape, dt=F32):
            return nc.dram_tensor(name, list(shape), dt).ap()
        wbf = {}
        for l in range(DEPTH):
            wbf[('w_in', l)] = dscr(f"wbf_in{l}", [D, N_IN], BF16)
            wbf[('w_out', l)] = dscr(f"wbf_out{l}", [D, D], BF16)
            wbf[('w1', l)] = dscr(f"wbf_w1{l}", [D, DFF], BF16)
            wbf[('w2', l)] = dscr(f"wbf_w2{l}", [DFF, D], BF16)
        if debug == 'p4':
            ycat = nc.dram_tensor("ycat_in", [T, D], F32, kind="ExternalInput").ap()
        elif debug == 'mix':
            ycat = dbg_out("d_ycat", [T, D])
        else:
            ycat = dscr("ycat", [T, D])
        xmid = dscr("xmid", [T, D])
        xres = dbg_out("d_xres", [T, D]) if debug == 'p4' else dscr("xres", [T, D])
        h2T_d = dscr("h2T_d", [128, 8, T], BF16)
        zraw = dscr("zraw", [8, 128, T])

        ident_f = C.sb("ident_f", [128, 128], F32)
        ident_b = C.sb("ident_b", [128, 128], BF16)
        C.dma('sp', ident_f[:], K['ident_f'][:, :])
        C.dma('sp', ident_b[:], K['ident_b'][:, :])
        neghalf = C.sb("neghalf", [128, 8], F32)
        C.memset('pool', neghalf[:], -0.5)
        modT = C.sb("modT", [128, 48, 2], F32)
        AB = C.sb("AB", [128, 4, 8, 2], F32)
        scT = C.sb("scT", [128, 8, 2], F32)
        gvec = C.sb("gvec", [128, 4, 8], F32)
        adab = C.sb("adab", [128, 48], F32)
        gv2 = C.sb("gv2", [128, 2, 8, 2], F32)
        Gb = C.sb("Gb", [128, 2, 2, D], F32)
        stat = [C.sb(f"stat{i}", [128, 8], F32) for i in range(2)]
        junk = C.sb("junk", [128, D], F32)

        def rstd_from_ss(st_, nsum, n):
            if nsum == 2:
                C.tt('pool', st_[:, 2:3], st_[:, 0:1], st_[:, 1:2], ALU.add)
                src = st_[:, 2:3]
            else:
                src = st_[:, 0:1]
            C.ts('pool', st_[:, 3:4], src, 1.0 / n, EPS, ALU.mult, ALU.add)
            C.tt('pool', st_[:, 4:5], st_[:, 3:4], neghalf[:, 0:1], ALU.pow)
            return st_[:, 4:5]

        if debug in (None, 'p4', 'mix'):
            with C.scope():
                cf = [C.sb(f"cf{i}", [128, 4096], F32) for i in range(2)]
                cb = [C.sb(f"cb{i}", [128, 4096], BF16) for i in range(2)]
                n = 0
                for l in range(DEPTH):
                    for wname, key, ncol in (('w_in', 'w_in', N_IN), ('w_out', 'w_out', D), ('mlp_w1', 'w1', DFF),
                                             ('mlp_w2', 'w2', DFF)):
                        if debug == 'p4' and (key == 'w_in' or l > 0):
                            continue
                        if debug == 'mix' and (key != 'w_in' or l > 0):
                            continue
                        if key == 'w2':
                            src2 = W[wname][l].rearrange("(a b) n -> a (b n)", b=4)
                            dst2 = wbf[(key, l)].rearrange("(a b) n -> a (b n)", b=4)
                        else:
                            src2 = W[wname][l]
                            dst2 = wbf[(key, l)]
                        for r8 in range(8):
                            f_, b_ = cf[n % 2], cb[n % 2]
                            C.dma('sp', f_[:, 0:ncol], src2[r8 * 128:(r8 + 1) * 128, :])
                            eng = ('dve', 'pool', 'act')[n % 3]
                            C.cp(eng, b_[:, 0:ncol], f_[:, 0:ncol])
                            C.dma('pool', dst2[r8 * 128:(r8 + 1) * 128, :], b_[:, 0:ncol], wk=[(dst2, r8)])
                            n += 1

        for l in range(DEPTH):
            last = (l == DEPTH - 1)
            tiles = list(range(NT))
            with C.scope():
                adaw = [C.sb(f"adaw{i}", [128, 8, 512], F32) for i in range(2)]
                pA = C.ps("pA", [128, 512], F32)
                items = [(adab[:], W['ada_b'][l])]
                for gi, gname in enumerate(('pre_mix_g', 'post_mix_g', 'pre_mlp_g', 'post_mlp_g')):
                    items.append((gvec[:, gi, :], W[gname][l]))
                if l == 0:
                    items += [(scT[:, :, 0], c_in), (scT[:, :, 1], cctx_in)]
                load_cols(C, ident_f, items)
                if l == 0:
                    C.act(junk[:, 0:16], scT[:].rearrange("p k s -> p (k s)"), AF.Sigmoid)
                    C.tt('dve', scT[:].rearrange("p k s -> p (k s)"), scT[:].rearrange("p k s -> p (k s)"),
                         junk[:, 0:16], ALU.mult)
                for s in range(12):
                    aw = adaw[s % 2]
                    C.dma('sp' if s % 2 == 0 else 'pool', aw[:],
                          W['ada_w'][l][:, s * 512:(s + 1) * 512].rearrange("(k p) n -> p k n", p=128))
                    for jj in range(4):
                        j = s * 4 + jj
                        for k in range(8):
                            C.mm(pA[:, 2 * j:2 * j + 2], aw[:, k, jj * 128:(jj + 1) * 128], scT[:, k, :],
                                 start=(k == 0), stop=(k == 7))
                C.tt('dve', modT[:], pA[:, 0:96].rearrange("p (j s) -> p j s", s=2),
                     adab[:].unsqueeze(2).to_broadcast([128, 48, 2]), ALU.add)
                for which, (gi, sci, shi) in enumerate(((0, 1, 0), (2, 4, 3))):
                    C.ts('dve', AB[:, 2 * which, :, :], modT[:, sci * 8:(sci + 1) * 8, :], 1.0, None, ALU.add)
                    C.tt('dve', AB[:, 2 * which, :, :], AB[:, 2 * which, :, :],
                         gvec[:, gi, :].unsqueeze(2).to_broadcast([128, 8, 2]), ALU.mult)
                    C.cp('dve', AB[:, 2 * which + 1, :, :], modT[:, shi * 8:(shi + 1) * 8, :])
                for w_, (gti, pgi) in enumerate(((2, 1), (5, 3))):
                    C.tt('dve', gv2[:, w_, :, :], modT[:, gti * 8:(gti + 1) * 8, :],
                         gvec[:, pgi, :].unsqueeze(2).to_broadcast([128, 8, 2]), ALU.mult)
                    for s_ in range(2):
                        for k in range(8):
                            C.mm(pA[:, (k % 4) * 128:(k % 4 + 1) * 128],
                                 gv2[:, w_, k, s_:s_ + 1].to_broadcast([128, 128]), ident_f[:])
                            if k % 4 == 3:
                                C.cp('act', Gb[:, w_, s_, (k - 3) * 128:(k + 1) * 128], pA[:, :])

            mix_scope = contextlib.ExitStack()
            if debug != 'p4':
              with C.scope():
                hT = C.sb("hT", [128, 8, T], BF16)
                with C.scope():
                    xt = [C.sb(f"xt{i}", [128, D], F32) for i in range(2)]
                    xn = [C.sb(f"xn{i}", [128, D], F32) for i in range(2)]
                    pA = C.ps("pA", [128, 512], F32)
                    pB = C.ps("pB", [128, 512], F32)
                    for i in range(NT):
                        s = 1 if i < NCT else 0
                        if l == 0:
                            src = ctx_in[i * 128:(i + 1) * 128, :] if i < NCT else x_in[(i - NCT) * 128:(i - NCT + 1) * 128, :]
                        else:
                            src = xres[i * 128:(i + 1) * 128, :]
                        xb_, xn_, st_ = xt[i % 2], xn[i % 2], stat[i % 2]
                        C.dma('sp' if i % 2 == 0 else 'pool', xb_[:], src, rk=(xres, i) if l > 0 else None)
                        C.act(junk[:], xb_[:], AF.Square, accum_out=st_[:, 0:1])
                        rs = rstd_from_ss(st_, 1, D)
                        C.ts('dve', xn_[:], xb_[:], rs, None, ALU.mult)
                        for half in range(2):
                            pp = pA if half == 0 else pB
                            for kk in range(4):
                                k = half * 4 + kk
                                C.tr(pp[:, kk * 128:(kk + 1) * 128], xn_[:, k * 128:(k + 1) * 128], ident_f[:])
                            for kk in range(4):
                                k = half * 4 + kk
                                if half == 0:
                                    C.act(hT[:, k, i * 128:(i + 1) * 128], pp[:, kk * 128:(kk + 1) * 128], AF.Identity,
                                          bias=AB[:, 1, k, s:s + 1], scale=AB[:, 0, k, s:s + 1], wk=[(hT, i)])
                                else:
                                    C.ts('dve', hT[:, k, i * 128:(i + 1) * 128], pp[:, kk * 128:(kk + 1) * 128],
                                         AB[:, 0, k, s:s + 1], AB[:, 1, k, s:s + 1], ALU.mult, ALU.add, wk=[(hT, i)])
                if debug == 'hT' and l == 0:
                    d1 = dbg_out("d_modT", [128, 96])
                    C.dma('sp', d1[:, :], modT[:].rearrange("p j s -> p (j s)"))
                    d2 = dbg_out("d_hT", [128, 8 * T], BF16)
                    C.dma('sp', d2[:, :], hT[:].rearrange("p k t -> p (k t)"))
                MIXERS(locals())
            if debug != 'p4' and (debug_mixers is None or 'rwkv' in debug_mixers):
                E2 = dict(locals())
                E2['emit'] = l < DEPTH - 1
                rwkv_main(E2)
            if debug in ('hT', 'mix'):
                break

            ptiles = [i for i in range(NT) if not (last and i < NCT)]
            with C.scope():
                wo = C.sb("wo", [128, 8, D], BF16)
                C.dma('sp', wo[:], wbf[('w_out', l)].rearrange("(k p) n -> p k n", p=128), rk=wbf[('w_out', l)])
                yt = [C.sb(f"yt{i}", [128, D], F32) for i in range(2)]
                ybf = [C.sb(f"ybf{i}", [128, D], BF16) for i in range(2)]
                yT = [C.sb(f"yT{i}", [128, 8, 128], BF16) for i in range(2)]
                xt = [C.sb(f"xt{i}", [128, D], F32) for i in range(2)]
                xm = [C.sb(f"xm{i}", [128, D], F32) for i in range(2)]
                xn = [C.sb(f"xn{i}", [128, D], F32) for i in range(2)]
                h2 = [C.sb(f"h2{i}", [128, 8, 128], BF16) for i in range(2)]
                pT = C.ps("pT", [128, D], BF16)
                pY = [C.ps(f"pY{i}", [128, 512], F32) for i in range(2)]
                pA = C.ps("pA", [128, 512], F32)
                pB = C.ps("pB", [128, 512], F32)
                for n_, i in enumerate(ptiles):
                    s = 1 if i < NCT else 0
                    b2 = n_ % 2
                    st_ = stat[b2]
                    C.dma('sp', yt[b2][:], ycat[i * 128:(i + 1) * 128, :], rk=(ycat, i))
                    if l == 0:
                        src = ctx_in[i * 128:(i + 1) * 128, :] if i < NCT else x_in[(i - NCT) * 128:(i - NCT + 1) * 128, :]
                    else:
                        src = xres[i * 128:(i + 1) * 128, :]
                    C.dma('pool', xt[b2][:], src, rk=(xres, i) if l > 0 else None)
                    C.cp('pool' if b2 else 'dve', ybf[b2][:], yt[b2][:])
                    for k in range(8):
                        C.tr(pT[:, k * 128:(k + 1) * 128], ybf[b2][:, k * 128:(k + 1) * 128], ident_b[:])
                    C.cp('act', yT[b2][:].rearrange("p k t -> p (k t)"), pT[:, :])
                    for nh in range(2):
                        for k in range(8):
                            C.mm(pY[nh][:, :], yT[b2][:, k, :], wo[:, k, nh * 512:(nh + 1) * 512],
                                 start=(k == 0), stop=(k == 7))
                    for nh in range(2):
                        C.act(junk[:, nh * 512:(nh + 1) * 512], pY[nh][:, :], AF.Square, accum_out=st_[:, nh:nh + 1])
                    rs = rstd_from_ss(st_, 2, D)
                    for nh in range(2):
                        sl = slice(nh * 512, (nh + 1) * 512)
                        C.stt(xm[b2][:, sl], pY[nh][:, :], rs, Gb[:, 0, s, sl], ALU.mult, ALU.mult)
                    C.tt('pool', xm[b2][:], xm[b2][:], xt[b2][:], ALU.add)
                    C.dma('sp', xmid[i * 128:(i + 1) * 128, :], xm[b2][:], wk=(xmid, i))
                    C.act(junk[:], xm[b2][:], AF.Square, accum_out=st_[:, 5:6])
                    C.ts('pool', st_[:, 6:7], st_[:, 5:6], 1.0 / D, EPS, ALU.mult, ALU.add)
                    C.tt('pool', st_[:, 7:8], st_[:, 6:7], neghalf[:, 0:1], ALU.pow)
                    C.ts('dve', xn[b2][:], xm[b2][:], st_[:, 7:8], None, ALU.mult)
                    for half in range(2):
                        pp = pA if half == 0 else pB
                        for kk in range(4):
                            k = half * 4 + kk
                            C.tr(pp[:, kk * 128:(kk + 1) * 128], xn[b2][:, k * 128:(k + 1) * 128], ident_f[:])
                        for kk in range(4):
                            k = half * 4 + kk
                            if half == 0:
                                C.act(h2[b2][:, k, :], pp[:, kk * 128:(kk + 1) * 128], AF.Identity,
                                      bias=AB[:, 3, k, s:s + 1], scale=AB[:, 2, k, s:s + 1])
                            else:
                                C.ts('dve', h2[b2][:, k, :], pp[:, kk * 128:(kk + 1) * 128],
                                     AB[:, 2, k, s:s + 1], AB[:, 3, k, s:s + 1], ALU.mult, ALU.add)
                    C.dma('pool', h2T_d[:, :, i * 128:(i + 1) * 128], h2[b2][:], wk=(h2T_d, i))

            with C.scope():
                w1s = C.sb("w1s", [128, 8, DFF], BF16)
                w2s = C.sb("w2s", [128, 32, D], BF16)
                for k in range(8):
                    C.dma('sp' if k % 2 == 0 else 'pool', w1s[:, k, :], wbf[('w1', l)][k * 128:(k + 1) * 128, :],
                          rk=wbf[('w1', l)], wk=(w1s, k))
                for j4 in range(8):
                    C.dma('sp' if j4 % 2 == 0 else 'pool', w2s[:, j4 * 4:(j4 + 1) * 4, :],
                          wbf[('w2', l)][j4 * 512:(j4 + 1) * 512, :].rearrange("(j p) n -> p j n", p=128),
                          rk=wbf[('w2', l)], wk=(w2s, j4))
                h2g = [C.sb(f"h2g{i}", [128, 8, 256], BF16) for i in range(2)]
                rl = [C.sb(f"rl{i}", [128, 256], BF16) for i in range(2)]
                uT = [C.sb(f"uT{i}", [128, 256], BF16) for i in range(3)]
                xm = [C.sb(f"xm{i}", [128, D], F32) for i in range(2)]
                xo = [C.sb(f"xo{i}", [128, D], F32) for i in range(2)]
                pH = [C.ps(f"pH{i}", [128, 512], F32) for i in range(2)]
                pF = [[C.ps(f"pF{t_}{h_}", [128, 512], F32) for h_ in range(2)] for t_ in range(2)]
                groups = [ptiles[a:a + 2] for a in range(0, len(ptiles), 2)]
                nu = 0
                for gi, grp in enumerate(groups):
                    i0 = grp[0]
                    hg = h2g[gi % 2]
                    C.dma('sp', hg[:], h2T_d[:, :, i0 * 128:(i0 + 2) * 128], rk=(h2T_d, None))
                    for j in range(32):
                        ph = pH[j % 2]
                        for k in range(8):
                            C.mm(ph[:, 0:256], w1s[:, k, j * 128:(j + 1) * 128], hg[:, k, :],
                                 start=(k == 0), stop=(k == 7), rk=[(w1s, k), hg])
                        r_ = rl[j % 2]
                        u_ = uT[nu % 3]
                        nu += 1
                        C.act(r_[:], ph[:, 0:256], AF.Relu)
                        C.tt('dve' if j % 2 == 0 else 'pool', u_[:], r_[:], r_[:], ALU.mult)
                        for t_ in range(2):
                            for nh in range(2):
                                C.mm(pF[t_][nh][:, :], u_[:, t_ * 128:(t_ + 1) * 128], w2s[:, j, nh * 512:(nh + 1) * 512],
                                     start=(j == 0), stop=(j == 31), rk=[u_, (w2s, j // 4)])
                    for t_, i in enumerate(grp):
                        s = 1 if i < NCT else 0
                        b2 = (gi * 2 + t_) % 2
                        st_ = stat[b2]
                        C.dma('pool', xm[b2][:], xmid[i * 128:(i + 1) * 128, :], rk=(xmid, i))
                        for nh in range(2):
                            C.act(junk[:, nh * 512:(nh + 1) * 512], pF[t_][nh][:, :], AF.Square,
                                  accum_out=st_[:, nh:nh + 1])
                        rs = rstd_from_ss(st_, 2, D)
                        for nh in range(2):
                            sl = slice(nh * 512, (nh + 1) * 512)
                            C.stt(xo[b2][:, sl], pF[t_][nh][:, :], rs, Gb[:, 1, s, sl], ALU.mult, ALU.mult)
                        C.tt('pool', xo[b2][:], xo[b2][:], xm[b2][:], ALU.add)
                        if last:
                            C.dma('sp', out[(i - NCT) * 128:(i - NCT + 1) * 128, :], xo[b2][:])
                        else:
                            C.dma('sp', xres[i * 128:(i + 1) * 128, :], xo[b2][:], wk=(xres, i))
            if debug == 'p4':
                break
        C.finish()
    return nc, list(dbg.keys())


def mixer_ret(E):
    C, l, hT, wbf, W, K, ycat = E['C'], E['l'], E['hT'], E['wbf'], E['W'], E['K'], E['ycat']
    stat, neghalf, emit = E['stat'], E['neghalf'], E['emit']
    with C.scope():
        wB = C.sb("wB", [128, 8, 1024], BF16)
        C.dma('sp', wB[:], wbf[('w_in', l)][:, 512:1536].rearrange("(k p) n -> p k n", p=128), rk=wbf[('w_in', l)])
        tabs = {}
        for nm, shp in (('ret_qft', [128, 256]), ('ret_qbt', [128, 256]), ('ret_gcc', [128, 2]), ('ret_kf', [128, 256]),
                        ('ret_kb', [128, 256]), ('ret_dt', [128, 512])):
            tabs[nm] = C.sb(nm, shp, F32)
            src = K[nm]
            if len(src.shape) == 3:
                src = src.rearrange("p a b -> p (a b)")
            C.dma('pool', tabs[nm][:], src)
        gng = C.sb("ret_gng", [128, 256], F32)
        C.dma('pool', gng[:], W['ret_gn_g'][l].partition_broadcast(128))
        Gs = [C.sb(f"retG{d}", [128, NT, 128], F32) for d in range(2)]
        Ss = [C.sb(f"retS{d}", [128, NT, 128], BF16) for d in range(2)]
        cur = [C.sb(f"retcur{d}", [128, 128], F32) for d in range(2)]
        import os
        if int(os.environ.get('RET_STOP', '99')) <= 0:
            return
        with C.scope():
            pKV = [C.ps(f"pKV{i}", [128, 512], F32) for i in range(2)]
            pV1 = [C.ps(f"pV1{i}", [128, 512], F32) for i in range(2)]
            pGb = [C.ps(f"pG{i}", [128, 512], F32) for i in range(2)]
            pG = [[pGb[i][:, d * 128:(d + 1) * 128] for d in range(2)] for i in range(2)]
            kd = [[C.sb(f"kd{i}{d}", [128, 256], BF16) for d in range(2)] for i in range(2)]
            vv = [C.sb(f"vv{i}", [128, 256], BF16) for i in range(2)]
            for n in range(NT):
                b2 = n % 2
                for k in range(8):
                    C.mm(pKV[b2][:, 0:256], hT[:, k, n * 128:(n + 1) * 128], wB[:, k, 256:512], start=(k == 0), stop=(k == 7),
                         rk=[(hT, n), wB])
                for k in range(8):
                    C.mm(pV1[b2][:, 256:512], hT[:, k, n * 128:(n + 1) * 128], wB[:, k, 512:768], start=(k == 0), stop=(k == 7),
                         rk=[(hT, n), wB])
                p1 = os.environ.get('RET_P1', '')
                if 'nodve' not in p1:
                    C.tt('dve', kd[b2][0][:], pKV[b2][:, 0:256], tabs['ret_kf'][:], ALU.mult)
                    C.tt('dve', kd[b2][1][:], pKV[b2][:, 0:256], tabs['ret_kb'][:], ALU.mult)
                if 'noact' not in p1:
                    C.cp('act', vv[b2][:], pV1[b2][:, 256:512])
                for d in range(2):
                    if 'nog' in os.environ.get('RET_P1', ''):
                        continue
                    for h in range(4):
                        if os.environ.get('RET_P1') == 'h0' and h != 0:
                            continue
                        hh, cc = h % 2, h // 2
                        C.mm(pGb[b2][hh * 64:(hh + 1) * 64, d * 128 + cc * 64:d * 128 + (cc + 1) * 64], kd[b2][d][:, h * 64:(h + 1) * 64],
                             vv[b2][:, h * 64:(h + 1) * 64])
                    C.cp('act', Gs[d][:, n, :], pG[b2][d], wk=[(Gs[d], n)])
        import os
        stop = int(os.environ.get('RET_STOP', '99'))
        if stop <= 1:
            return
        gcc = tabs['ret_gcc']
        for d in range(2):
            order = list(range(NT)) if d == 0 else [1, 0] + list(range(NT - 1, 1, -1))
            C.memset('dve', cur[d][:], 0.0)
            for n in order:
                C.cp('act', Ss[d][:, n, :], cur[d][:], wk=[(Ss[d], n)])
                for cc in range(2):
                    C.stt(cur[d][:, cc * 64:(cc + 1) * 64], cur[d][:, cc * 64:(cc + 1) * 64], gcc[:, cc:cc + 1],
                          Gs[d][:, n, cc * 64:(cc + 1) * 64], ALU.mult, ALU.add, rk=[cur[d], gcc, (Gs[d], n)])
        if stop <= 2:
            return
        with C.scope():
            pQ = C.ps("pQ", [128, 512], F32)
            pK = C.ps("pK", [128, 512], F32)
            pVG = C.ps("pVG", [128, 512], F32)
            pS = [C.ps(f"pS{i}", [128, 512], F32) for i in range(2)]
            pOb = [C.ps(f"pO{i}", [128, 512], F32) for i in range(2)]
            qT = [C.sb(f"rqT{i}", [128, 2, 128], BF16) for i in range(2)]
            qTf = [C.sb(f"rqTf{i}", [128, 2, 128], BF16) for i in range(2)]
            qTb = [C.sb(f"rqTb{i}", [128, 2, 128], BF16) for i in range(2)]
            kT = [C.sb(f"rkT{i}", [128, 2, 128], BF16) for i in range(2)]
            vv = [C.sb(f"rvv{i}", [128, 256], BF16) for i in range(2)]
            sg = [C.sb(f"rsg{i}", [128, 256], F32) for i in range(2)]
            sTm = [C.sb(f"rsTm{i}", [128, 4, 128], BF16) for i in range(2)]
            oc = [C.sb(f"roc{i}", [128, 256], F32) for i in range(2)]
            osq = [C.sb(f"rosq{i}", [128, 256], F32) for i in range(2)]
            yo = [C.sb(f"ryo{i}", [128, 256], F32) for i in range(2)]
            st4 = [C.sb(f"rst{i}", [128, 8, 4], F32) for i in range(2)]
            nh4 = C.sb("nh4", [128, 4], F32)
            C.memset('pool', nh4[:], -0.5)
            tiles = [n for n in range(NT) if emit or n >= NCT]
            for n in tiles:
                b2 = n % 2
                for j in range(4):
                    col = (j % 2) * 128 + (0 if j < 2 else 256)
                    dst = (pQ if j < 2 else pK)[:, (j % 2) * 128:(j % 2 + 1) * 128]
                    for k in range(8):
                        C.mm(dst, wB[:, k, col:col + 128], hT[:, k, n * 128:(n + 1) * 128],
                             start=(k == 0), stop=(k == 7), rk=[(hT, n), wB])
                for k in range(8):
                    C.mm(pVG[:, :], hT[:, k, n * 128:(n + 1) * 128], wB[:, k, 512:1024], start=(k == 0), stop=(k == 7),
                         rk=[(hT, n), wB])
                C.cp('dve', qT[b2][:].rearrange("p c t -> p (c t)"), pQ[:, 0:256])
                C.tt('dve', qTf[b2][:].rearrange("p c t -> p (c t)"), pQ[:, 0:256], tabs['ret_qft'][:], ALU.mult)
                C.tt('dve', qTb[b2][:].rearrange("p c t -> p (c t)"), pQ[:, 0:256], tabs['ret_qbt'][:], ALU.mult)
                C.act(kT[b2][:].rearrange("p c t -> p (c t)"), pK[:, 0:256], AF.Copy, scale=0.125)
                C.cp('act', vv[b2][:], pVG[:, 0:256])
                C.act(sg[b2][:], pVG[:, 256:512], AF.Silu)
                for h in range(4):
                    hh, cc = h % 2, h // 2
                    C.mm(pS[hh][:, cc * 128:(cc + 1) * 128], kT[b2][hh * 64:(hh + 1) * 64, cc, :],
                         qT[b2][hh * 64:(hh + 1) * 64, cc, :])
                dt4 = tabs['ret_dt'][:].rearrange("p (c hh t) -> p c hh t", hh=2, t=128)
                sT4 = sTm[b2][:].rearrange("p (c hh) t -> p c hh t", hh=2)
                for hh in range(2):
                    C.tt('dve', sT4[:, :, hh, :], pS[hh][:, 0:256].rearrange("p (c t) -> p c t", t=128), dt4[:, :, hh, :],
                         ALU.mult)
                for h in range(4):
                    hh, cc = h % 2, h // 2
                    o_ = pOb[hh][:, cc * 64:(cc + 1) * 64]
                    C.mm(o_, sTm[b2][:, h, :], vv[b2][:, h * 64:(h + 1) * 64], start=True, stop=False)
                    C.mm(o_, qTf[b2][hh * 64:(hh + 1) * 64, cc, :], Ss[0][hh * 64:(hh + 1) * 64, n, cc * 64:(cc + 1) * 64],
                         start=False, stop=False, rk=[qTf[b2], (Ss[0], n)])
                    C.mm(o_, qTb[b2][hh * 64:(hh + 1) * 64, cc, :], Ss[1][hh * 64:(hh + 1) * 64, n, cc * 64:(cc + 1) * 64],
                         start=False, stop=True, rk=[qTb[b2], (Ss[1], n)])
                s4 = st4[b2]
                oc4 = oc[b2][:].rearrange("p (c hh e) -> p c hh e", hh=2, e=64)
                for hh in range(2):
                    C.cp('act', oc4[:, :, hh, :], pOb[hh][:, 0:128].rearrange("p (c e) -> p c e", e=64))
                C.act(osq[b2][:], oc[b2][:], AF.Square)
                o3 = oc[b2][:].rearrange("p (h e) -> p h e", e=64)
                C.red('dve', s4[:, 0, :], o3, ALU.add)
                C.red('dve', s4[:, 1, :], osq[b2][:].rearrange("p (h e) -> p h e", e=64), ALU.add)
                C.ts('pool', s4[:, 2, :], s4[:, 0, :], 1.0 / 64, None, ALU.mult)
                C.tt('pool', s4[:, 3, :], s4[:, 2, :], s4[:, 2, :], ALU.mult)
                C.ts('pool', s4[:, 4, :], s4[:, 1, :], 1.0 / 64, 1e-5, ALU.mult, ALU.add)
                C.tt('pool', s4[:, 4, :], s4[:, 4, :], s4[:, 3, :], ALU.subtract)
                C.tt('pool', s4[:, 5, :], s4[:, 4, :], nh4[:], ALU.pow)
                y3 = yo[b2][:].rearrange("p (h e) -> p h e", e=64)
                C.tt('dve', y3, o3, s4[:, 2, :].unsqueeze(2).to_broadcast([128, 4, 64]), ALU.subtract)
                C.tt('dve', y3, y3, s4[:, 5, :].unsqueeze(2).to_broadcast([128, 4, 64]), ALU.mult)
                C.tt('pool', yo[b2][:], yo[b2][:], gng[:], ALU.mult)
                C.tt('pool', yo[b2][:], yo[b2][:], sg[b2][:], ALU.mult)
                C.dma('sp', ycat[n * 128:(n + 1) * 128, 256:512], yo[b2][:], wk=(ycat, n))


def mixer_swa(E):
    C, l, hT, wbf, W, K, ycat = E['C'], E['l'], E['hT'], E['wbf'], E['W'], E['K'], E['ycat']
    emit, ident_b = E['emit'], E['ident_b']
    with C.scope():
        wA = C.sb("wA", [128, 8, 512], BF16)
        C.dma('sp', wA[:], wbf[('w_in', l)][:, 0:512].rearrange("(k p) n -> p k n", p=128), rk=wbf[('w_in', l)])
        wK2 = C.sb("wK2", [128, 8, 2, 128], BF16)
        for hk in range(2):
            for dup in range(2):
                C.dma('pool', wK2[:, :, hk, dup * 64:(dup + 1) * 64],
                      wbf[('w_in', l)][:, 256 + hk * 64:256 + (hk + 1) * 64].rearrange("(k p) n -> p k n", p=128),
                      rk=wbf[('w_in', l)])
        pm = C.sb("swa_pm", [128, 128], BF16)
        C.dma('pool', pm[:], K['swa_pm'][:, :])
        mask = C.sb("swa_mask", [128, 384], F32)
        C.dma('pool', mask[:], K['swa_mask'][:, :])
        sinkb = C.sb("sinkb", [128, 4], F32)
        C.dma('pool', sinkb[:], W['swa_sink'][l].partition_broadcast(128))
        qTr = C.sb("qTr", [128, 2, T], BF16)
        kTz = [C.sb(f"kTz{g}", [128, 2, T], BF16) for g in range(2)]
        vtok = C.sb("vtok", [128, NT, 128], BF16)
        C.memset('pool', kTz[0][64:128, :, :], 0.0)
        C.memset('pool', kTz[1][0:64, :, :], 0.0)
        with C.scope():
            pQ = C.ps("pQ", [128, 512], F32)
            pP = C.ps("pP", [128, 512], F32)
            pV = C.ps("pV", [128, 512], F32)
            ctg = [C.sb(f"ctg{i}", [128, 512], F32) for i in range(2)]
            stg = [C.sb(f"stg{i}", [128, 512], F32) for i in range(2)]
            xq = [C.sb(f"xq{i}", [128, 512], BF16) for i in range(2)]
            t1 = [C.sb(f"t1{i}", [128, 512], F32) for i in range(2)]
            t2 = [C.sb(f"t2{i}", [128, 512], F32) for i in range(2)]
            groups = [(0, LC)] + [(LC + 512 * i, 512) for i in range(8)]
            nx = 0
            for gi, (t0, n) in enumerate(groups):
                rope = t0 >= LC
                if rope:
                    C.dma('sp', ctg[gi % 2][:], K['swa_ct'][:, t0 - LC:t0 - LC + n])
                    C.dma('sp', stg[gi % 2][:], K['swa_st'][:, t0 - LC:t0 - LC + n])
                for j in range(4):
                    hk = j % 2
                    for k in range(8):
                        lhsT = wA[:, k, hk * 128:(hk + 1) * 128] if j < 2 else wK2[:, k, hk, :]
                        C.mm(pQ[:, 0:n], lhsT, hT[:, k, t0:t0 + n], start=(k == 0), stop=(k == 7), rk=[hT, wA, wK2])
                    x_ = xq[nx % 2]
                    nx += 1
                    C.cp('act', x_[:, 0:n], pQ[:, 0:n])
                    if rope:
                        C.mm(pP[:, 0:n], pm[:], x_[:, 0:n])
                        a_, b_ = t1[nx % 2], t2[nx % 2]
                        C.tt('pool', a_[:, 0:n], x_[:, 0:n], ctg[gi % 2][:, 0:n], ALU.mult)
                        C.tt('dve', b_[:, 0:n], pP[:, 0:n], stg[gi % 2][:, 0:n], ALU.mult)
                        if j < 2:
                            C.tt('pool', qTr[:, hk, t0:t0 + n], a_[:, 0:n], b_[:, 0:n], ALU.add, wk=[(qTr, gi)])
                        else:
                            for g in range(2):
                                C.tt('pool', kTz[g][g * 64:(g + 1) * 64, hk, t0:t0 + n], a_[g * 64:(g + 1) * 64, 0:n],
                                     b_[g * 64:(g + 1) * 64, 0:n], ALU.add, wk=[(kTz[g], gi)])
                    else:
                        if j < 2:
                            C.cp('pool', qTr[:, hk, t0:t0 + n], x_[:, 0:n], wk=[(qTr, gi)])
                        else:
                            for g in range(2):
                                C.cp('pool', kTz[g][g * 64:(g + 1) * 64, hk, t0:t0 + n], x_[g * 64:(g + 1) * 64, 0:n],
                                     wk=[(kTz[g], gi)])
                for ti in range(t0 // 128, (t0 + n) // 128):
                    for k in range(8):
                        C.mm(pV[:, 0:128], hT[:, k, ti * 128:(ti + 1) * 128], wA[:, k, 384:512], start=(k == 0),
                             stop=(k == 7), rk=[hT, wA])
                    C.cp('act', vtok[:, ti, :], pV[:, 0:128], wk=[(vtok, ti)])
        with C.scope():
            pSa = [C.ps(f"pSa{i}", [128, 512], F32) for i in range(2)]
            pSb = [C.ps(f"pSb{i}", [128, 512], F32) for i in range(2)]
            pT = [C.ps(f"pTs{i}", [128, 1024], BF16) for i in range(2)]
            pO = C.ps("pOs", [128, 512], F32)
            sc = [C.sb(f"sc{i}", [128, 640], F32) for i in range(2)]
            pb = [C.sb(f"pb{i}", [128, 640], BF16) for i in range(2)]
            pTs = [C.sb(f"pTsb{i}", [128, 640], BF16) for i in range(2)]
            sm = [C.sb(f"sm{i}", [128, 8], F32) for i in range(2)]
            ya = [C.sb(f"ya{i}", [128, 256], F32) for i in range(2)]
            rden = [C.sb(f"rden{i}", [128, 4], F32) for i in range(2)]
            qblocks = [n for n in range(NT) if emit or n >= NCT]
            nh = 0
            for bi, n in enumerate(qblocks):
                isx = n >= NCT
                if isx:
                    lo, hi = max(n - 1, NCT), min(n + 1, NT - 1)
                    ktiles = list(range(lo, hi + 1)) + [0, 1]
                    nloc = hi - lo + 1
                    moff = (lo - (n - 1)) * 128
                else:
                    lo, hi = 0, 1
                    ktiles = [0, 1]
                    nloc = 2
                ncol = len(ktiles) * 128
                for h in range(4):
                    hk, g = h // 2, h % 2
                    b2 = nh % 2
                    nh += 1
                    C.mm(pSa[b2][:, 0:nloc * 128], qTr[:, hk, n * 128:(n + 1) * 128], kTz[g][:, hk, lo * 128:(hi + 1) * 128],
                         rk=[qTr, kTz[g]])
                    if isx:
                        C.mm(pSb[b2][:, 0:256], qTr[:, hk, n * 128:(n + 1) * 128], kTz[g][:, hk, 0:256], rk=[qTr, kTz[g]])
                        C.tt('dve', sc[b2][:, 0:nloc * 128], pSa[b2][:, 0:nloc * 128], mask[:, moff:moff + nloc * 128], ALU.add)
                        C.cp('dve', sc[b2][:, nloc * 128:ncol], pSb[b2][:, 0:256])
                    else:
                        C.cp('dve', sc[b2][:, 0:ncol], pSa[b2][:, 0:ncol])
                    s_ = sm[b2]
                    C.red('dve', s_[:, 0:1], sc[b2][:, 0:ncol], ALU.max)
                    C.ts('dve', s_[:, 1:2], s_[:, 0:1], 0.125, sinkb[:, h:h + 1], ALU.mult, ALU.max)
                    C.ts('dve', s_[:, 2:3], s_[:, 1:2], -1.0, None, ALU.mult)
                    C.act(pb[b2][:, 0:ncol], sc[b2][:, 0:ncol], AF.Exp, bias=s_[:, 2:3], scale=0.125, accum_out=s_[:, 3:4])
                    C.act(s_[:, 4:5], sinkb[:, h:h + 1], AF.Exp, bias=s_[:, 2:3], scale=1.0)
                    C.tt('dve', s_[:, 5:6], s_[:, 3:4], s_[:, 4:5], ALU.add)
                    C.op('dve', [s_], [rden[bi % 2]], lambda e_, o_=rden[bi % 2][:, h:h + 1], i_=s_[:, 5:6]: e_.reciprocal(out=o_, in_=i_))
                    for j in range(len(ktiles)):
                        C.tr(pT[b2][:, j * 128:(j + 1) * 128], pb[b2][:, j * 128:(j + 1) * 128], ident_b[:])
                    C.cp('act', pTs[b2][:, 0:ncol], pT[b2][:, 0:ncol])
                    for j, kt in enumerate(ktiles):
                        C.mm(pO[:, h * 64:(h + 1) * 64], pTs[b2][:, j * 128:(j + 1) * 128], vtok[:, kt, hk * 64:(hk + 1) * 64],
                             start=(j == 0), stop=(j == len(ktiles) - 1), rk=[pTs[b2], vtok])
                for h in range(4):
                    C.act(ya[bi % 2][:, h * 64:(h + 1) * 64], pO[:, h * 64:(h + 1) * 64], AF.Copy, scale=rden[bi % 2][:, h:h + 1])
                C.dma('sp', ycat[n * 128:(n + 1) * 128, 0:256], ya[bi % 2][:], wk=(ycat, n))


def mixer_mla(E):
    C, l, hT, wbf, W, K, ycat = E['C'], E['l'], E['hT'], E['wbf'], E['W'], E['K'], E['ycat']
    emit, ident_b, ident_f, neghalf = E['emit'], E['ident_b'], E['ident_f'], E['neghalf']
    SCALE = 96.0 ** -0.5
    with C.scope():
        wD = C.sb("wD", [128, 8, 416], BF16)
        C.dma('sp', wD[:], wbf[('w_in', l)][:, 2560:2976].rearrange("(k p) n -> p k n", p=128), rk=wbf[('w_in', l)])
        wf = C.sb("mla_wf", [128, 1280], F32)
        wuq = C.sb("wuq", [128, 2, 384], BF16)
        wukv = C.sb("wukv", [128, 512], BF16)
        C.dma('sp', wf[:, 0:768].rearrange("p (c n) -> p c n", c=2), W['mla_w_uq'][l].rearrange("(c p) n -> p c n", p=128))
        C.dma('sp', wf[:, 768:1280], W['mla_w_ukv'][l])
        C.cp('dve', wuq[:].rearrange("p c n -> p (c n)"), wf[:, 0:768])
        C.cp('dve', wukv[:], wf[:, 768:1280])
        gq = C.sb("mla_gq", [128, 2], F32)
        gkv = C.sb("mla_gkv", [128, 1], F32)
        load_cols(C, ident_f, [(gq[:], W['mla_q_norm_g'][l]), (gkv[:], W['mla_kv_norm_g'][l])])
        cosT = C.sb("mla_cos", [128, L // 128, 16], F32)
        sinT = C.sb("mla_sin", [128, L // 128, 16], F32)
        C.dma('pool', cosT[:].rearrange("p n f -> p (n f)"), K['mla_cos'][:, :])
        C.dma('pool', sinT[:].rearrange("p n f -> p (n f)"), K['mla_sin'][:, :])
        ones_b = C.sb("mla_ones", [128, 2], BF16)
        C.memset('pool', ones_b[:], 1.0)
        ones_f = C.sb("mla_onesf", [1, 128], F32)
        C.memset('pool', ones_f[:], 1.0)
        half = C.sb("mla_half", [128, 4], F32)
        C.memset('pool', half[:], 0.5)
        KaT = C.sb("KaT", [128, 4, T], BF16)
        Va = C.sb("Va", [128, NT, 4, 65], BF16)
        C.memset('pool', Va[:, :, :, 64:65], 1.0)
        kmx = C.sb("kmx", [128, 4], F32)
        C.memset('dve', kmx[:], 0.0)
        kmaxb = C.sb("kmaxb", [128, 2], F32)
        aug = [C.sb(f"aug{i}", [128, 4, 128], BF16) for i in range(2)]
        for i in range(2):
            C.memset('pool', aug[i][:], 0.0)
        fT = [C.sb(f"mfT{i}", [128, 2, 128], BF16) for i in range(2)]
        sqT = [C.sb(f"msq{i}", [128, 2, 128], BF16) for i in range(2)]
        qf = [C.sb(f"mqf{i}", [128, 512], F32) for i in range(2)]
        tmp = [C.sb(f"mtmp{i}", [128, 384], F32) for i in range(2)]
        rp = [C.sb(f"mrp{i}", [128, 6, 64], F32) for i in range(2)]
        st_ = [C.sb(f"mst{i}", [128, 16], F32) for i in range(2)]

        def rope_tm(out1, out2, x1, x2, c_, s_, r_, shape):
            n = int(np.prod(shape[1:]))
            v = [r_[:, i, 0:n].rearrange("p (a b) -> p a b", b=shape[-1]) if len(shape) == 3 else
                 r_[:, i, 0:n].rearrange("p (h a b) -> p h a b", a=shape[2], b=shape[3]) for i in range(4)]
            C.tt('dve', v[0], x1, c_, ALU.mult)
            C.tt('dve', v[1], x2, s_, ALU.mult)
            C.tt('dve', v[2], x2, c_, ALU.mult)
            C.tt('dve', v[3], x1, s_, ALU.mult)
            C.tt('dve', out1, v[0], v[1], ALU.subtract)
            C.tt('dve', out2, v[2], v[3], ALU.add)

        import os
        mstop = int(os.environ.get('MLA_STOP', '99'))
        if mstop <= 0:
            return
        with C.scope():
            pA = C.ps("mpA", [128, 512], F32)
            pB = C.ps("mpB", [128, 512], F32)
            pB2 = C.ps("mpB2", [128, 512], F32)
            pT = C.ps("mpT", [128, 1024], BF16)
            for n in range(NT):
                b2 = n % 2
                a_, s4 = aug[b2], st_[b2]
                for k in range(8):
                    C.mm(pA[:, 0:128], wD[:, k, 256:384], hT[:, k, n * 128:(n + 1) * 128], start=(k == 0), stop=(k == 7),
                         rk=[(hT, n), wD])
                C.act(fT[b2][:, 0, :], pA[:, 0:128], AF.Copy, scale=gkv[:, 0:1])
                C.act(sqT[b2][:, 0, :], pA[:, 0:128], AF.Square)
                C.mm(pB[:, :], fT[b2][:, 0, :], wukv[:])
                C.mm(pB2[:, 0:2], sqT[b2][:, 0, :], ones_b[:, 0:2])
                for k in range(8):
                    C.mm(pB2[:, 32:64], hT[:, k, n * 128:(n + 1) * 128], wD[:, k, 384:416], start=(k == 0), stop=(k == 7),
                         rk=[(hT, n), wD])
                C.ts('dve', s4[:, 0:1], pB2[:, 0:1], 1.0 / 128, EPS, ALU.mult, ALU.add)
                C.tt('pool', s4[:, 1:2], s4[:, 0:1], neghalf[:, 0:1], ALU.pow)
                kv4 = pB[:, :].rearrange("p (h e) -> p h e", e=128)
                C.ts('dve', a_[:, :, 0:64], kv4[:, :, 0:64], s4[:, 1:2], None, ALU.mult)
                C.ts('dve', Va[:, n, :, 0:64], kv4[:, :, 64:128], s4[:, 1:2], None, ALU.mult, wk=[(Va, n)])
                if n >= NCT:
                    x4 = pB2[:, 32:64].rearrange("p (a b f) -> p a b f", a=2, b=2)
                    c_ = cosT[:, n - NCT, :].rearrange("p (a f) -> p a f", a=2)
                    s_ = sinT[:, n - NCT, :].rearrange("p (a f) -> p a f", a=2)
                    o4 = rp[b2][:, 4, 0:32].rearrange("p (a b f) -> p a b f", a=2, b=2)
                    rope_tm(o4[:, :, 0, :], o4[:, :, 1, :], x4[:, :, 0, :], x4[:, :, 1, :], c_, s_, rp[b2], [128, 2, 8])
                else:
                    C.cp('dve', rp[b2][:, 4, 0:32], pB2[:, 32:64])
                C.cp('dve', a_[:, :, 64:96], rp[b2][:, 4, 0:32].unsqueeze(1).to_broadcast([128, 4, 32]))
                C.memset('pool', a_[:, :, 96:97], 1.0)
                t3 = tmp[b2][:].rearrange("p (h e) -> p h e", e=96)
                C.tt('pool', t3, a_[:, :, 0:96], a_[:, :, 0:96], ALU.mult)
                C.red('dve', s4[:, 4:8], t3, ALU.add)
                C.tt('dve', kmx[:], kmx[:], s4[:, 4:8], ALU.max)
                for h in range(4):
                    C.tr(pT[:, h * 128:(h + 1) * 128], a_[:, h, :], ident_b[:])
                C.cp('act', KaT[:, :, n * 128:(n + 1) * 128], pT[:, 0:512].rearrange("p (h t) -> p h t", t=128),
                     wk=[(KaT, n)])
            if mstop <= 1:
                return
            C.red('dve', st_[0][:, 8:9], kmx[:], ALU.max)
            C.tr(pB[:, 0:128].bitcast(F32)[0:1, 0:128], st_[0][:, 8:9], ident_f[:])
            C.red('dve', st_[0][0:1, 9:10], pB[0:1, 0:128], ALU.max)
            C.tt('pool', st_[0][0:1, 10:11], st_[0][0:1, 9:10], half[0:1, 0:1], ALU.pow)
            C.mm(pB2[:, 0:1], ones_f[0:1, :], st_[0][0:1, 10:11])
            C.ts('dve', kmaxb[:, 0:1], pB2[:, 0:1], -1.0, None, ALU.mult)

        if mstop <= 2:
            return
        with C.scope():
            pA = C.ps("mqA", [128, 512], F32)
            pB = C.ps("mqB", [128, 512], F32)
            pT = C.ps("mqT", [128, 1024], BF16)
            pS = [C.ps(f"mpS{i}", [128, 512], F32) for i in range(2)]
            pO = [C.ps(f"mpO{h}", [128, 512], F32) for h in range(3)]
            QaT = [C.sb(f"QaT{i}", [128, 4, 512], BF16) for i in range(2)]
            PT = [C.sb(f"PTs{i}", [128, 512], BF16) for i in range(3)]
            yd = [C.sb(f"yd{i}", [128, 256], F32) for i in range(2)]
            rd_ = [C.sb(f"mrd{i}", [128, 4], F32) for i in range(2)]
            groups = ([(0, 2, [0, 1])] if emit else []) + [(NCT + 4 * i, 4, list(range(NT))) for i in range(8)]
            nps = 0
            ny = 0
            for gi, (n0, nt, ktiles) in enumerate(groups):
                Qg = QaT[gi % 2]
                for j in range(nt):
                    n = n0 + j
                    b2 = n % 2
                    a_, s4 = aug[b2], st_[b2]
                    for c in range(2):
                        for k in range(8):
                            C.mm(pA[:, c * 128:(c + 1) * 128], wD[:, k, c * 128:(c + 1) * 128], hT[:, k, n * 128:(n + 1) * 128],
                                 start=(k == 0), stop=(k == 7), rk=[(hT, n), wD])
                    for c in range(2):
                        C.act(fT[b2][:, c, :], pA[:, c * 128:(c + 1) * 128], AF.Copy, scale=gq[:, c:c + 1])
                        C.act(sqT[b2][:, c, :], pA[:, c * 128:(c + 1) * 128], AF.Square)
                    for c in range(2):
                        C.mm(pB[:, 0:384], fT[b2][:, c, :], wuq[:, c, :], start=(c == 0), stop=(c == 1))
                    for c in range(2):
                        C.mm(pB[:, 384 + 2 * c:386 + 2 * c], sqT[b2][:, c, :], ones_b[:, 0:2])
                    C.cp('dve', s4[:, 12:16], pB[:, 384:388])
                    C.tt('dve', s4[:, 0:1], s4[:, 12:13], s4[:, 14:15], ALU.add)
                    C.ts('dve', s4[:, 0:1], s4[:, 0:1], 1.0 / 256, EPS, ALU.mult, ALU.add)
                    C.tt('pool', s4[:, 1:2], s4[:, 0:1], neghalf[:, 0:1], ALU.pow)
                    C.ts('dve', qf[b2][:, 0:384], pB[:, 0:384], s4[:, 1:2], None, ALU.mult)
                    q3 = qf[b2][:, 0:384].rearrange("p (h e) -> p h e", e=96)
                    C.cp('dve', a_[:, :, 0:64], q3[:, :, 0:64])
                    if n >= NCT and os.environ.get('MLA_VAR') != 'norope':
                        x5 = q3[:, :, 64:96].rearrange("p h (a b f) -> p h a b f", a=2, b=2)
                        o5 = a_[:, :, 64:96].rearrange("p h (a b f) -> p h a b f", a=2, b=2)
                        c_ = cosT[:, n - NCT, :].rearrange("p (a f) -> p a f", a=2).unsqueeze(1).to_broadcast([128, 4, 2, 8])
                        s_ = sinT[:, n - NCT, :].rearrange("p (a f) -> p a f", a=2).unsqueeze(1).to_broadcast([128, 4, 2, 8])
                        rope_tm(o5[:, :, :, 0, :], o5[:, :, :, 1, :], x5[:, :, :, 0, :], x5[:, :, :, 1, :], c_, s_, rp[b2],
                                [128, 4, 2, 8])
                    else:
                        C.cp('dve', a_[:, :, 64:96], q3[:, :, 64:96])
                    t3 = tmp[b2][:].rearrange("p (h e) -> p h e", e=96)
                    C.tt('pool', t3, a_[:, :, 0:96], a_[:, :, 0:96], ALU.mult)
                    C.red('dve', s4[:, 4:8], t3, ALU.add)
                    C.tt('pool', s4[:, 8:12], s4[:, 4:8], half[:, 0:4], ALU.pow)
                    C.ts('dve', a_[:, :, 96:97], s4[:, 8:12].unsqueeze(2), kmaxb[:, 0:1], None, ALU.mult)
                    for h in range(4):
                        C.tr(pT[:, h * 128:(h + 1) * 128], a_[:, h, :], ident_b[:])
                    C.cp('act', Qg[:, :, j * 128:(j + 1) * 128], pT[:, 0:512].rearrange("p (h t) -> p h t", t=128))
                nq = nt * 128
                if mstop <= 3:
                    continue
                for h in range(4):
                    po = pO[h % 3]
                    for ki, kt in enumerate(ktiles):
                        ps_ = pS[nps % 2]
                        pt_ = PT[nps % 3]
                        nps += 1
                        C.mm(ps_[:, 0:nq], KaT[0:97, h, kt * 128:(kt + 1) * 128], Qg[0:97, h, 0:nq], rk=[(KaT, kt), Qg])
                        C.act(pt_[:, 0:nq], ps_[:, 0:nq], AF.Exp, scale=SCALE)
                        for j in range(nt):
                            C.mm(po[:, j * 65:(j + 1) * 65], pt_[:, j * 128:(j + 1) * 128], Va[:, kt, h, :],
                                 start=(ki == 0 and j == 0), stop=(ki == len(ktiles) - 1), skip_group_check=True,
                                 rk=[pt_, (Va, kt)])
                    r4 = rd_[h % 2]
                    o3 = po[:, 0:nt * 65].rearrange("p (j e) -> p j e", e=65)
                    C.op('dve', [po], [r4], lambda e_, o_=r4[:, 0:nt], i_=o3[:, :, 64]: e_.reciprocal(out=o_, in_=i_))
                    for j in range(nt):
                        C.ts('dve', E['ydg'][j][:, h * 64:(h + 1) * 64], o3[:, j, 0:64], r4[:, j:j + 1], None, ALU.mult)
                for j in range(nt):
                    n = n0 + j
                    C.dma('sp', ycat[n * 128:(n + 1) * 128, 768:1024], E['ydg'][j][:], wk=(ycat, n))


def rwkv_project(E):
    C, l, hT, wbf, zraw = E['C'], E['l'], E['hT'], E['wbf'], E['zraw']
    with C.scope():
        wC = C.sb("wC", [128, 8, 1024], BF16)
        C.dma('sp', wC[:], wbf[('w_in', l)][:, 1536:2560].rearrange("(k p) n -> p k n", p=128), rk=wbf[('w_in', l)])
        pZ = [C.ps(f"pZ{i}", [128, 512], F32) for i in range(2)]
        stg = [C.sb(f"zstg{i}", [128, 512], F32) for i in range(2)]
        groups = [(512 * i, 512) for i in range(8)] + [(4096, 256)]
        i = 0
        for c in range(8):
            for (t0, n) in groups:
                for k in range(8):
                    C.mm(pZ[i % 2][:, 0:n], wC[:, k, c * 128:(c + 1) * 128], hT[:, k, t0:t0 + n], start=(k == 0), stop=(k == 7),
                         rk=[hT, wC])
                C.cp('act', stg[i % 2][:, 0:n], pZ[i % 2][:, 0:n])
                C.dma('sp' if i % 2 == 0 else 'pool', zraw[c, :, t0:t0 + n], stg[i % 2][:, 0:n], wk=(zraw, c))
                i += 1


def rwkv_main(E):
    C, l, W, K, ycat, zraw, emit, ident_b = E['C'], E['l'], E['W'], E['K'], E['ycat'], E['zraw'], E['emit'], E['ident_b']
    LAM = -math.exp(-0.5)
    WP = T + 4
    with C.scope():
        rkT = C.sb("rkT", [128, 4, T], BF16)
        twz = C.sb("twz", [128, T], BF16)
        sgd = C.sb("sgd", [128, T], BF16)
        vtok = C.sb("rw_vtok", [128, 2, NT, 128], BF16)
        muT = C.sb("muT", [128, 8], F32)
        pp = C.sb("rw_pp", [128, 4, 2], F32)
        w0T = C.sb("rw_w0", [128, 2, 2], F32)
        a0T = C.sb("rw_a0", [128, 2, 2], F32)
        items = [(muT[:], W['rwkv_mu'][l]), (pp[:, 0, :], W['rwkv_kk'][l]), (pp[:, 1, :], W['rwkv_ka'][l]),
                 (pp[:, 3, :], W['rwkv_rk'][l].rearrange("h k -> (h k)"))]
        for d in range(2):
            items += [(w0T[:, d, :], W['rwkv_w0'][l][d]), (a0T[:, d, :], W['rwkv_a0'][l][d])]
        load_cols(C, E['ident_f'], items)
        C.ts('dve', pp[:, 2, :], pp[:, 1, :], -1.0, 1.0, ALU.mult, ALU.add)
        lstage = C.sb("rw_lst", [128, 2, 2, 256], F32)
        C.memset('pool', lstage[:], 0.0)
        wupz = C.sb("wupz", [128, 2, 256], BF16)
        aupz = C.sb("aupz", [128, 2, 256], BF16)
        for d in range(2):
            C.dma('sp', lstage[0:64, 0, d, :], W['rwkv_w_up'][l][d])
            C.dma('sp', lstage[64:128, 1, d, :], W['rwkv_a_up'][l][d])
        C.cp('dve', wupz[:], lstage[:, 0, :, :])
        C.cp('dve', aupz[:], lstage[:, 1, :, :])
        gst = C.sb("rw_gst", [128, 256], F32)
        gup = C.sb("rw_gup", [128, 256], BF16)
        C.dma('sp', gst[:], W['rwkv_g_up'][l])
        C.cp('dve', gup[:], gst[:])
        gng = C.sb("rw_gng", [128, 256], F32)
        C.dma('pool', gng[:], W['rwkv_gn_g'][l].partition_broadcast(128))
        resetm = C.sb("rw_resetm", [128, 512], F32)
        C.dma('pool', resetm[:], K['rw_resetm'][:, :])
        blk = C.sb("rw_blk", [128, 128], BF16)
        C.dma('pool', blk[:], K['rw_blk'][:, :])
        blk2 = C.sb("rw_blk2", [128, 2], BF16)
        C.dma('pool', blk2[:], K['rw_blk2'][:, :])
        mask4 = C.sb("rw_mask4", [128, 2, 512], F32)
        masknt = C.sb("rw_masknt", [128, 2, 256], F32)
        for d in range(2):
            C.dma('pool', mask4[:, d, :], K['rw_mask4'][d])
            C.dma('pool', masknt[:, d, :], K['rw_masknt'][d])
        negh = C.sb("rw_negh", [128, 512], F32)
        C.memset('pool', negh[:], -0.5)
        nh2 = C.sb("rw_nh2", [128, 2], F32)
        C.memset('pool', nh2[:], -0.5)

        with C.scope():
            zr = [C.sb(f"zr{i}", [128, WP], F32) for i in range(2)]
            tA = [C.sb(f"tA{i}", [128, WP], F32) for i in range(2)]
            vb = C.sb("rw_vb", [128, T], BF16)
            pT = C.ps("rw_pT", [128, 1024], BF16)
            for i, c in enumerate([6, 7, 4, 5, 0, 1, 2, 3]):
                z_, t_ = zr[i % 2], tA[i % 2]
                C.memset('pool', z_[:, 0:1], 0.0)
                C.memset('pool', z_[:, 257:259], 0.0)
                C.memset('pool', z_[:, WP - 1:WP], 0.0)
                C.dma('sp', z_[:, 1:257], zraw[c, :, 0:LC], rk=(zraw, c))
                C.dma('pool', z_[:, 259:259 + L], zraw[c, :, LC:T], rk=(zraw, c))
                mid = slice(1, WP - 1)
                C.tt('pool', t_[:, mid], z_[:, 0:WP - 2], z_[:, 2:WP], ALU.add)
                C.stt(t_[:, mid], t_[:, mid], 0.5, z_[:, mid], ALU.mult, ALU.subtract)
                C.stt(t_[:, mid], t_[:, mid], muT[:, c:c + 1], z_[:, mid], ALU.mult, ALU.add)
                segs = ((slice(1, 257), slice(0, LC)), (slice(259, 259 + L), slice(LC, T)))
                for (src, dst) in segs:
                    if c == 6:
                        C.act(twz[0:64, dst], t_[0:64, src], AF.Tanh)
                        C.cp('act', twz[64:128, dst], t_[64:128, src])
                    elif c == 7:
                        C.act(sgd[:, dst], t_[:, src], AF.Sigmoid)
                    elif c in (4, 5):
                        C.cp('act', vb[:, dst], t_[:, src])
                    else:
                        C.cp('act', rkT[:, c, dst], t_[:, src])
                if c in (4, 5):
                    for n0 in range(0, NT, 8):
                        nn = min(8, NT - n0)
                        for j in range(nn):
                            C.tr(pT[:, j * 128:(j + 1) * 128], vb[:, (n0 + j) * 128:(n0 + j + 1) * 128], ident_b[:])
                        C.cp('act', vtok[:, c - 4, n0:n0 + nn, :], pT[:, 0:nn * 128].rearrange("p (j f) -> p j f", f=128))

        import os
        rstop = int(os.environ.get('RW_STOP', '99'))
        if rstop <= 0:
            return
        Oacc = [C.sb(f"rw_O{d}", [128, NT, 128], F32) for d in range(2)]
        bsum = [C.sb(f"rw_bs{d}", [128, NT, 2], F32) for d in range(2)]
        ktI = [C.sb(f"ktI{i}", [128, 512], BF16) for i in range(2)]
        bI = [C.sb(f"bI{i}", [128, 512], BF16) for i in range(2)]
        KR = [[C.sb(f"KR{i}{hh}", [128, 4, 2, 128], BF16) for hh in range(2)] for i in range(2)]
        for i in range(2):
            for hh in range(2):
                C.memset('pool', KR[i][hh][:], 0.0)
        KBg = [C.sb(f"KBg{i}", [128, 4, 2, 128], BF16) for i in range(2)]
        gam = [C.sb(f"gam{i}", [128, 4], F32) for i in range(2)]
        Mst = C.sb("rw_M", [128, 64], F32)
        Mb = [C.sb(f"rw_Mb{i}", [128, 64], BF16) for i in range(2)]
        f32t = {nm: C.sb("rwt_" + nm, [128, 512], F32) for nm in
                ('sw', 'a', 'ci', 'ei', 'ee', 'e1', 'e2', 'e3', 'kt', 'b', 'tK', 'tB')}
        f32t.update(kk=f32t['sw'], rs=f32t['ci'], kh=f32t['ei'], tk=f32t['ee'])
        b16t = {nm: C.sb("rwb_" + nm, [128, 512], BF16) for nm in ('ksq', 'kgT', 'bgT', 'prod')}
        Am = [[C.sb(f"Am{i}{hh}", [128, 4, 128], BF16) for hh in range(2)] for i in range(2)]
        XXi = [C.sb(f"XXi{i}", [128, 2, 128], BF16) for i in range(2)]
        XX = [[C.sb(f"XX{i}{j}", [128, 2, 2, 128], BF16) for j in range(2)] for i in range(2)]
        Pb = [[C.sb(f"Pb{i}{j}", [128, 2, 128], BF16) for j in range(2)] for i in range(2)]
        Xs = [C.sb(f"rwXs{i}", [128, 128], BF16) for i in range(2)]
        Un = [C.sb(f"rwUn{i}", [128, 128], BF16) for i in range(2)]
        fin = {nm: C.sb("rwf_" + nm, [128, 128], F32) for nm in ('o', 'osq', 'y', 'yo')}
        fst = C.sb("rwf_st", [128, 8, 2], F32)
        pL = C.ps("rw_pL", [128, 512], F32)
        pA = [C.ps(f"rw_pA{hh}", [128, 512], F32) for hh in range(2)]
        pI1 = C.ps("rw_pI1", [128, 512], F32)
        pI2 = C.ps("rw_pI2", [128, 512], F32)
        pTb = C.ps("rw_pTb", [128, 1024], BF16)
        pR1 = C.ps("rw_pR1", [128, 512], F32)
        pR2 = C.ps("rw_pR2", [128, 512], F32)

        xgroups = [(NCT + 4 * i, 4) for i in range(8)]
        for fc in range(2):
            for d in range(2):
                glist = [(0, NCT)] + (xgroups if d == 0 else xgroups[::-1])
                C.memset('dve', Mst[:], 0.0)
                C.memset('pool', Mb[0][:], 0.0)
                cur = 0
                for gi, (c0, ncn) in enumerate(glist):
                    pb = gi % 2
                    t0, n = c0 * 128, ncn * 128
                    t = f32t
                    rT_ = rkT[:, fc, t0:t0 + n]
                    kT_ = rkT[:, 2 + fc, t0:t0 + n]
                    sl = slice(0, n)
                    C.mm(pL[:, sl], wupz[:, d, fc * 128:(fc + 1) * 128], twz[:, t0:t0 + n])
                    C.act(t['sw'][:, sl], pL[:, sl], AF.Sigmoid, bias=w0T[:, d, fc:fc + 1])
                    C.mm(pL[:, sl], aupz[:, d, fc * 128:(fc + 1) * 128], twz[:, t0:t0 + n])
                    C.act(t['a'][:, sl], pL[:, sl], AF.Sigmoid, bias=a0T[:, d, fc:fc + 1])
                    C.op('dve', [resetm, t['sw']], [t['ci']],
                         lambda e_, o_=t['ci'][:, sl], a_=resetm[:, sl], b_=t['sw'][:, sl]:
                         e_.tensor_tensor_scan(out=o_, data0=a_, data1=b_, initial=0.0, op0=ALU.mult, op1=ALU.add))
                    ci3 = t['ci'][:, sl].rearrange("p (c t) -> p c t", t=128)
                    tot = ci3[:, :, 127:128]
                    if d == 1:
                        ei3 = t['ei'][:, sl].rearrange("p (c t) -> p c t", t=128)
                        C.tt('dve', ei3, tot.to_broadcast([128, ncn, 128]), ci3, ALU.subtract)
                        C.tt('dve', t['ei'][:, sl], t['ei'][:, sl], t['sw'][:, sl], ALU.add)
                        ei = t['ei']
                    else:
                        ei = t['ci']
                    C.tt('pool', t['ee'][:, sl], ei[:, sl], t['sw'][:, sl], ALU.subtract)
                    C.act(t['e1'][:, sl], t['ee'][:, sl], AF.Exp, scale=LAM)
                    C.act(t['e2'][:, sl], ei[:, sl], AF.Exp, scale=-LAM)
                    C.act(t['e3'][:, sl], ei[:, sl], AF.Exp, scale=LAM)
                    C.act(gam[pb][:, 0:ncn], ci3[:, :, 127], AF.Exp, scale=LAM)
                    C.ts('dve', t['kk'][:, sl], kT_, pp[:, 0, fc:fc + 1], None, ALU.mult)
                    C.tt('pool', b16t['ksq'][:, sl], t['kk'][:, sl], t['kk'][:, sl], ALU.mult)
                    C.mm(pL[:, sl], blk[:], b16t['ksq'][:, sl])
                    C.cp('act', t['rs'][:, sl], pL[:, sl])
                    C.ts('dve', t['rs'][:, sl], t['rs'][:, sl], 1e-12, None, ALU.max)
                    C.tt('pool', t['rs'][:, sl], t['rs'][:, sl], negh[:, sl], ALU.pow)
                    C.tt('dve', t['kh'][:, sl], t['kk'][:, sl], t['rs'][:, sl], ALU.mult)
                    C.ts('dve', t['tk'][:, sl], t['a'][:, sl], pp[:, 1, fc:fc + 1], pp[:, 2, fc:fc + 1], ALU.mult, ALU.add)
                    C.tt('pool', t['kt'][:, sl], t['tk'][:, sl], kT_, ALU.mult)
                    C.tt('pool', t['b'][:, sl], t['kh'][:, sl], t['a'][:, sl], ALU.mult)
                    for hh in range(2):
                        hs = slice(hh * 64, (hh + 1) * 64)
                        C.tt('dve', KR[pb][hh][hs, 0:ncn, 0, :], t['kh'][hs, sl].rearrange("p (c t) -> p c t", t=128),
                             t['e1'][hs, sl].rearrange("p (c t) -> p c t", t=128), ALU.mult, wk=[KR[pb][hh]])
                        C.tt('pool', KR[pb][hh][hs, 0:ncn, 1, :], rT_[hs, :].rearrange("p (c t) -> p c t", t=128),
                             t['e3'][hs, sl].rearrange("p (c t) -> p c t", t=128), ALU.mult, wk=[KR[pb][hh]])
                    C.tt('dve', t['tK'][:, sl], t['kt'][:, sl], t['e2'][:, sl], ALU.mult)
                    C.tt('pool', t['tB'][:, sl], t['b'][:, sl], t['e2'][:, sl], ALU.mult)
                    C.cp('act', ktI[pb][:, sl], t['tK'][:, sl])
                    C.cp('act', bI[pb][:, sl], t['tB'][:, sl])
                    gb = gam[pb][:, 0:ncn].unsqueeze(2).to_broadcast([128, ncn, 128])
                    C.tt('dve', b16t['kgT'][:, sl].rearrange("p (c t) -> p c t", t=128),
                         t['tK'][:, sl].rearrange("p (c t) -> p c t", t=128), gb, ALU.mult)
                    C.tt('pool', b16t['bgT'][:, sl].rearrange("p (c t) -> p c t", t=128),
                         t['tB'][:, sl].rearrange("p (c t) -> p c t", t=128), gb, ALU.mult)
                    for j in range(ncn):
                        C.tr(pTb[:, (2 * j) * 128:(2 * j + 1) * 128], b16t['kgT'][:, j * 128:(j + 1) * 128], ident_b[:])
                        C.tr(pTb[:, (2 * j + 1) * 128:(2 * j + 2) * 128], b16t['bgT'][:, j * 128:(j + 1) * 128], ident_b[:])
                    C.cp('act', KBg[pb][:, 0:ncn, :, :].rearrange("p c k f -> p (c k f)"), pTb[:, 0:ncn * 256])
                    C.stt(b16t['prod'][:, sl], rT_, pp[:, 3, fc:fc + 1], t['kt'][:, sl], ALU.mult, ALU.mult)
                    for j in range(ncn):
                        C.mm(pL[:, 2 * j:2 * j + 2], b16t['prod'][:, j * 128:(j + 1) * 128], blk2[:])
                    C.cp('act', bsum[d][:, c0:c0 + ncn, :].rearrange("p c k -> p (c k)"), pL[:, 0:2 * ncn])

                    if rstop <= 1:
                        continue
                    chunks = list(range(c0, c0 + ncn)) if d == 0 else list(range(c0 + ncn - 1, c0 - 1, -1))
                    for n_ in chunks:
                        j = n_ - c0
                        q2 = n_ % 2
                        cs = slice(j * 128, (j + 1) * 128)
                        for hh in range(2):
                            kr = KR[pb][hh][:, j, :, :].rearrange("p a t -> p (a t)")
                            C.mm(pA[hh][:, 0:256], ktI[pb][:, cs], kr, rk=[ktI[pb], KR[pb][hh]])
                            C.mm(pA[hh][:, 256:512], bI[pb][:, cs], kr, rk=[bI[pb], KR[pb][hh]])
                            C.mm(pI2[:, 256 + hh * 128:256 + (hh + 1) * 128], KR[pb][hh][:, j, 0, :], bI[pb][:, cs], rk=[bI[pb], KR[pb][hh]])
                            C.tt('dve', Am[q2][hh][:].rearrange("p a t -> p (a t)"), pA[hh][:, :], mask4[:, d, :], ALU.mult)
                        C.tt('dve', XXi[q2][:].rearrange("p a t -> p (a t)"), pI2[:, 256:512], masknt[:, d, :], ALU.mult)
                        for hh in range(2):
                            C.tt('pool', Pb[q2][0][:, hh, :], Am[q2][hh][:, 2, :], ident_b[:], ALU.add, wk=[Pb[q2][0]])
                        Xc = [Am[q2][0][:, 2, :], Am[q2][1][:, 2, :]]
                        Xk = [Am[q2][0], Am[q2][1]]
                        XTc = [XXi[q2][:, 0, :], XXi[q2][:, 1, :]]
                        XTk = [XXi[q2], XXi[q2]]
                        Pc = Pb[q2][0]
                        for lvl in range(6):
                            lastl = (lvl == 5)
                            nxt = XX[q2][lvl % 2]
                            for hh in range(2):
                                if not lastl:
                                    C.mm(pI1[:, hh * 128:(hh + 1) * 128], XTc[hh], Xc[hh], rk=[XTk[hh], Xk[hh]])
                                C.mm(pI1[:, 256 + hh * 128:256 + (hh + 1) * 128], Xc[hh], XTc[hh], rk=[XTk[hh], Xk[hh]])
                            if lastl:
                                C.cp('act', nxt[:, 1, :, :].rearrange("p h t -> p (h t)"), pI1[:, 256:512])
                            else:
                                C.cp('act', nxt[:].rearrange("p a h t -> p (a h t)"), pI1[:, :])
                            for hh in range(2):
                                C.mm(pI2[:, hh * 128:(hh + 1) * 128], nxt[:, 1, hh, :], Pc[:, hh, :], rk=[nxt, Pc])
                            Pn = Pb[q2][(lvl + 1) % 2]
                            C.tt('dve', Pn[:].rearrange("p h t -> p (h t)"), pI2[:, 0:256], Pc[:].rearrange("p h t -> p (h t)"), ALU.add)
                            Xc = [nxt[:, 0, 0, :], nxt[:, 0, 1, :]]
                            XTc = [nxt[:, 1, 0, :], nxt[:, 1, 1, :]]
                            Xk = [nxt, nxt]
                            XTk = [nxt, nxt]
                            Pc = Pn
                        TT = Pc
                        if rstop <= 2:
                            continue
                        Mold = Mb[cur]
                        V = vtok[:, fc, n_, :]
                        for hh in range(2):
                            vs = slice(hh * 64, (hh + 1) * 64)
                            C.mm(pR1[:, vs], KR[pb][hh][:, j, 0, :], Mold[:], start=True, stop=False, rk=[KR[pb][hh], Mold])
                            C.mm(pR1[:, vs], Am[q2][hh][:, 0, :], V[:, vs], start=False, stop=True, rk=[Am[q2][hh], vtok])
                        C.cp('act', Xs[q2][:], pR1[:, 0:128])
                        for hh in range(2):
                            vs = slice(hh * 64, (hh + 1) * 64)
                            C.mm(pR2[:, vs], TT[:, hh, :], Xs[q2][:, vs], rk=[TT, Xs[q2]])
                        C.ts('dve', Un[q2][:], pR2[:, 0:128], -1.0, None, ALU.mult)
                        for hh in range(2):
                            vs = slice(hh * 64, (hh + 1) * 64)
                            C.mm(pR2[vs, 128:192], KBg[pb][:, j, 0, vs], V[:, vs], start=True, stop=False, rk=[KBg[pb], vtok])
                            C.mm(pR2[vs, 128:192], KBg[pb][:, j, 1, vs], Un[q2][:, vs], start=False, stop=True, rk=[KBg[pb], Un[q2]])
                        want_o = emit or n_ >= NCT
                        if want_o:
                            for hh in range(2):
                                vs = slice(128 + hh * 64, 128 + (hh + 1) * 64)
                                v2 = slice(hh * 64, (hh + 1) * 64)
                                C.mm(pR1[:, vs], KR[pb][hh][:, j, 1, :], Mold[:], start=True, stop=False, rk=[KR[pb][hh], Mold])
                                C.mm(pR1[:, vs], Am[q2][hh][:, 1, :], V[:, v2], start=False, stop=False, rk=[Am[q2][hh], vtok])
                                C.mm(pR1[:, vs], Am[q2][hh][:, 3, :], Un[q2][:, v2], start=False, stop=True, rk=[Am[q2][hh], Un[q2]])
                            C.cp('act', Oacc[d][:, n_, :], pR1[:, 128:256], wk=[(Oacc[d], n_)])
                        C.stt(Mst[:], Mst[:], gam[pb][:, j:j + 1], pR2[:, 128:192], ALU.mult, ALU.add)
                        C.cp('act', Mb[1 - cur][:], Mst[:])
                        cur = 1 - cur

            for n_ in range(NT):
                if not (emit or n_ >= NCT) or rstop <= 3:
                    continue
                f = fin
                C.tt('pool', f['o'][:], Oacc[0][:, n_, :], Oacc[1][:, n_, :], ALU.add, rk=[(Oacc[0], n_), (Oacc[1], n_)])
                C.tt('pool', f['osq'][:], f['o'][:], f['o'][:], ALU.mult)
                o3 = f['o'][:].rearrange("p (h e) -> p h e", e=64)
                C.red('dve', fst[:, 0, :], o3, ALU.add)
                C.red('dve', fst[:, 1, :], f['osq'][:].rearrange("p (h e) -> p h e", e=64), ALU.add)
                C.ts('pool', fst[:, 2, :], fst[:, 0, :], 1.0 / 64, None, ALU.mult)
                C.tt('pool', fst[:, 3, :], fst[:, 2, :], fst[:, 2, :], ALU.mult)
                C.ts('pool', fst[:, 4, :], fst[:, 1, :], 1.0 / 64, 64e-5, ALU.mult, ALU.add)
                C.tt('pool', fst[:, 4, :], fst[:, 4, :], fst[:, 3, :], ALU.subtract)
                C.tt('pool', fst[:, 5, :], fst[:, 4, :], nh2[:], ALU.pow)
                C.tt('pool', fst[:, 6, :], bsum[0][:, n_, :], bsum[1][:, n_, :], ALU.add)
                y3 = f['y'][:].rearrange("p (h e) -> p h e", e=64)
                C.tt('dve', y3, o3, fst[:, 2, :].unsqueeze(2).to_broadcast([128, 2, 64]), ALU.subtract)
                C.tt('dve', y3, y3, fst[:, 5, :].unsqueeze(2).to_broadcast([128, 2, 64]), ALU.mult)
                C.tt('pool', f['y'][:], f['y'][:], gng[:, fc * 128:(fc + 1) * 128], ALU.mult)
                for hh in range(2):
                    vs = slice(hh * 64, (hh + 1) * 64)
                    C.stt(f['y'][:, vs], vtok[:, fc, n_, vs], fst[:, 6, hh:hh + 1], f['y'][:, vs], ALU.mult, ALU.add)
                C.mm(pR2[:, 0:128], sgd[:, n_ * 128:(n_ + 1) * 128], gup[:, fc * 128:(fc + 1) * 128])
                C.tt('dve', f['yo'][:], f['y'][:], pR2[:, 0:128], ALU.mult)
                C.dma('sp', ycat[n_ * 128:(n_ + 1) * 128, 512 + fc * 128:512 + (fc + 1) * 128], f['yo'][:], wk=(ycat, n_))


def MIXERS(env):
    E = dict(env)
    E['emit'] = env['l'] < DEPTH - 1
    which = env['debug_mixers'] if env.get('debug_mixers') else ('swa', 'ret', 'rwkv', 'mla')
    if 'swa' in which:
        mixer_swa(E)
    if 'ret' in which:
        mixer_ret(E)
    if 'rwkv' in which:
        rwkv_project(E)
    if 'mla' in which:
        with E['C'].scope():
            E['ydg'] = [E['C'].sb(f"ydg{j}", [128, 256], F32) for j in range(4)]
            mixer_mla(E)


_PROG = {}


def kernel(**inputs):
    if 'nc' not in _PROG:
        _PROG['nc'] = build_program()[0]
    nc = _PROG['nc']
    consts = make_consts()
    f32 = lambda a: np.ascontiguousarray(np.asarray(a, dtype=np.float32))
    shared = {k: f32(v) for k, v in inputs.items() if k not in ('x', 'c', 'ctx')}
    shared.update({'k_' + k: v for k, v in consts.items()})
    x, c, ctx = f32(inputs['x']), f32(inputs['c']), f32(inputs['ctx'])
    in_maps = []
    for b in range(8):
        m = dict(shared)
        m['x'] = x[b]
        m['c'] = c[b]
        m['ctx'] = ctx[b]
        in_maps.append(m)
    res = run_bass_kernel_spmd(nc, in_maps, core_ids=list(range(8)))
    return np.stack([np.asarray(r['out'], dtype=np.float32) for r in res.results], axis=0)
```

```python
import contextlib
import math
import numpy as np
import ml_dtypes
import concourse.bass as bass
import concourse.mybir as mybir
from concourse.bass_utils import run_bass_kernel_spmd

F32 = mybir.dt.float32
BF16 = mybir.dt.bfloat16
AF = mybir.ActivationFunctionType
ALU = mybir.AluOpType
AX = mybir.AxisListType

D = 1024
L = 4096
LC = 256
T = L + LC
NT = T // 128
NCT = LC // 128
DEPTH = 2
N_IN = 2976
DFF = 4096
EPS = 1e-6
NDMA = 6


class Ctx:
    ENG = ('pe', 'act', 'dve', 'pool', 'sp')

    def __init__(self, nc):
        self.nc = nc
        self.es = contextlib.ExitStack()
        self.eng = dict(pe=nc.tensor, act=nc.scalar, dve=nc.vector, pool=nc.gpsimd, sp=nc.sync)
        self.semh = {}
        self.cnt = {}
        for e in self.ENG:
            self.semh[e] = self.es.enter_context(nc.semaphore("sem_" + e))
            self.cnt[e] = 0
        self.known = {e: {} for e in self.ENG}
        self.dq = {}
        for q in ('sp', 'act', 'pool'):
            sems = []
            for i in range(NDMA):
                name = f"dma_{q}_{i}"
                self.semh[name] = self.es.enter_context(nc.semaphore(name))
                sems.append(name)
            self.dq[q] = dict(sems=sems, n=0)
        self.lastw = {}
        self.rd = {}
        self.subs = {}
        self.ninst = 0
        self.psum_names = set()
        self.bank_last = {}

    def sb(self, name, shape, dt=F32):
        self.uid = getattr(self, 'uid', 0) + 1
        return self.es.enter_context(self.nc.sbuf_tensor(f"{name}_{self.uid}", list(shape), dt))

    def ps(self, name, shape, dt=F32):
        self.uid = getattr(self, 'uid', 0) + 1
        nbytes = int(np.prod(shape[1:])) * (2 if dt == BF16 else 4)
        assert nbytes == 2048, "PSUM tensors must be exactly one bank (collision tracking is per tensor)"
        self.psum_names.add(f"{name}_{self.uid}")
        return self.es.enter_context(self.nc.psum_tensor(f"{name}_{self.uid}", list(shape), dt))

    @staticmethod
    def _key(x):
        if isinstance(x, tuple):
            a, sub = x
        else:
            a, sub = x, None
        name = a if isinstance(a, str) else getattr(a, 'tensor', a).name
        return (name, sub)

    def _dep_keys(self, key):
        name, sub = key
        if sub is None:
            return [(name, None)] + [(name, s) for s in self.subs.get(name, ())]
        return [(name, None), (name, sub)]

    def _wait(self, e, src, val):
        if self.known[e].get(src, 0) >= val:
            return
        self.eng[e].wait_ge(self.semh[src], val)
        self.known[e][src] = val

    def _sync(self, e, reads, writes):
        deps = {}

        def add(st):
            if st is not None:
                deps[st[0]] = max(deps.get(st[0], 0), st[1])
        same = 0
        for r in reads:
            for k in self._dep_keys(self._key(r)):
                st = self.lastw.get(k)
                add(st)
                if st is not None and st[0] == e:
                    same = max(same, st[1])
        for w in writes:
            for k in self._dep_keys(self._key(w)):
                st = self.lastw.get(k)
                add(st)
                if st is not None and st[0] == e:
                    same = max(same, st[1])
                for src, val in self.rd.get(k, {}).items():
                    add((src, val))
                    if src == e:
                        same = max(same, val)
        for src, val in deps.items():
            if src == e:
                continue
            self._wait(e, src, val)
        if same and e != 'pe':
            self._wait(e, e, same)

    def _record(self, reads, writes, stamp):
        for r in reads:
            k = self._key(r)
            d = self.rd.setdefault(k, {})
            d[stamp[0]] = max(d.get(stamp[0], 0), stamp[1])
        for w in writes:
            k = self._key(w)
            name, sub = k
            self.lastw[k] = stamp
            self.rd[k] = {}
            if sub is None:
                for s in self.subs.get(name, ()):
                    self.lastw[(name, s)] = stamp
                    self.rd[(name, s)] = {}
            else:
                self.subs.setdefault(name, set()).add(sub)

    def op(self, e, reads, writes, fn, rk=None, wk=None):
        if rk is not None:
            reads = list(rk)
        if wk is not None:
            writes = list(wk)
        self._sync(e, reads, writes)
        banks = set()
        for x in list(reads) + list(writes):
            nm = self._key(x)[0]
            if nm in self.psum_names:
                banks.add(nm)
        for nm in banks:
            for src, val in self.bank_last.get(nm, {}).items():
                if src != e:
                    self._wait(e, src, val)
        ins = fn(self.eng[e])
        self.cnt[e] += 1
        ins.then_inc(self.semh[e], 1)
        self._record(reads, writes, (e, self.cnt[e]))
        for nm in banks:
            self.bank_last.setdefault(nm, {})[e] = self.cnt[e]
        self.ninst += 1
        return ins

    def dma(self, q, out, in_, rk=None, wk=None, **kw):
        d = self.dq[q]
        i = d['n']
        src = d['sems'][i % NDMA]
        if i >= NDMA:
            self._wait(q, src, 16 * (i // NDMA))
        reads = rk if isinstance(rk, list) else [rk if rk is not None else in_]
        writes = wk if isinstance(wk, list) else [wk if wk is not None else out]
        self._sync(q, reads, writes)
        self.eng[q].dma_start(out=out, in_=in_, **kw).then_inc(self.semh[src], 16)
        d['n'] += 1
        self._record(reads, writes, (src, 16 * (i // NDMA + 1)))
        self.ninst += 1

    def barrier(self):
        targets = {}
        for q, d in self.dq.items():
            for j, src in enumerate(d['sems']):
                n = (d['n'] - j + NDMA - 1) // NDMA
                if n > 0:
                    targets[src] = 16 * n
        for e in self.ENG:
            if self.cnt[e] > 0:
                targets[e] = self.cnt[e]
        for e in self.ENG:
            for src, val in targets.items():
                self._wait(e, src, val)

    @contextlib.contextmanager
    def scope(self):
        outer = self.es
        with contextlib.ExitStack() as es:
            self.es = es
            try:
                yield
            finally:
                self.barrier()
                self.es = outer

    def finish(self):
        for q, d in self.dq.items():
            for j, src in enumerate(d['sems']):
                n = (d['n'] - j + NDMA - 1) // NDMA
                if n > 0:
                    self._wait('sp', src, 16 * n)
        for e in self.ENG:
            if e != 'sp' and self.cnt[e] > 0:
                self._wait('sp', e, self.cnt[e])

    def mm(self, out, lhsT, rhs, start=True, stop=True, r=(), w=(), rk=None, wk=None, **kw):
        return self.op('pe', [lhsT, rhs] + list(r), [out] + list(w),
                       lambda e: e.matmul(out, lhsT=lhsT, rhs=rhs, start=start, stop=stop, **kw), rk=rk, wk=wk)

    def tr(self, out, in_, ident, r=(), w=(), rk=None, wk=None):
        return self.op('pe', [in_, ident] + list(r), [out] + list(w),
                       lambda e: e.transpose(out, in_, ident), rk=rk, wk=wk)

    def act(self, out, in_, func, bias=None, scale=None, accum_out=None, r=(), w=(), rk=None, wk=None):
        kw = {}
        reads = [in_] + list(r)
        writes = [out] + list(w)
        if bias is not None:
            kw['bias'] = bias
            if not isinstance(bias, (int, float)):
                reads.append(bias)
        if scale is not None:
            kw['scale'] = scale
            if not isinstance(scale, (int, float)):
                reads.append(scale)
        if accum_out is not None:
            kw['accum_out'] = accum_out
            writes.append(accum_out)
        return self.op('act', reads, writes, lambda e: e.activation(out=out, in_=in_, func=func, **kw), rk=rk, wk=wk)

    def ts(self, e, out, in0, s1, s2, op0, op1=None, accum_out=None, r=(), w=(), rk=None, wk=None):
        reads = [in0] + list(r)
        writes = [out] + list(w)
        for s in (s1, s2):
            if s is not None and not isinstance(s, (int, float)):
                reads.append(s)
        kw = {}
        if op1 is not None:
            kw['op1'] = op1
        if accum_out is not None:
            kw['accum_out'] = accum_out
            writes.append(accum_out)
        return self.op(e, reads, writes,
                       lambda g: g.tensor_scalar(out=out, in0=in0, scalar1=s1, scalar2=s2, op0=op0, **kw), rk=rk, wk=wk)

    def tt(self, e, out, in0, in1, op, r=(), w=(), rk=None, wk=None):
        return self.op(e, [in0, in1] + list(r), [out] + list(w),
                       lambda g: g.tensor_tensor(out=out, in0=in0, in1=in1, op=op), rk=rk, wk=wk)

    def stt(self, out, in0, scalar, in1, op0, op1, r=(), w=(), rk=None, wk=None):
        reads = [in0, in1] + list(r)
        if not isinstance(scalar, (int, float)):
            reads.append(scalar)
        return self.op('dve', reads, [out] + list(w),
                       lambda g: g.scalar_tensor_tensor(out=out, in0=in0, scalar=scalar, in1=in1, op0=op0, op1=op1), rk=rk, wk=wk)

    def cp(self, e, out, in_, r=(), w=(), rk=None, wk=None):
        if e == 'act':
            return self.act(out, in_, AF.Copy, r=r, w=w, rk=rk, wk=wk)
        return self.op(e, [in_] + list(r), [out] + list(w), lambda g: g.tensor_copy(out=out, in_=in_), rk=rk, wk=wk)

    def memset(self, e, out, val, w=()):
        return self.op(e, [], [out] + list(w), lambda g: g.memset(out, val))

    def red(self, e, out, in_, op, axis=AX.X, r=(), w=()):
        return self.op(e, [in_] + list(r), [out] + list(w),
                       lambda g: g.tensor_reduce(out=out, in_=in_, axis=axis, op=op))


def load_cols(C, ident_f, items):
    with C.scope():
        st = C.sb("ldst", [128, 128], F32)
        ps = C.ps("ldps", [128, 512], F32)
        r = 0
        plan = []
        for dst, src in items:
            n = src.shape[0] // 128
            C.dma('sp', st[r:r + n, :], src.rearrange("(k p) -> k p", p=128))
            plan.append((dst, r, n))
            r += n
        assert r <= 128
        C.tr(ps[:, 0:r], st[0:r, :], ident_f[0:r, 0:r])
        for dst, r0, n in plan:
            C.cp('dve', dst, ps[:, r0:r0 + n])

def _rope_tables(n_tokens, d_rot, grid_w=64, base=10000.0):
    n_rows = n_tokens // grid_w
    row, col = np.meshgrid(np.arange(n_rows, dtype=np.float32), np.arange(grid_w, dtype=np.float32), indexing='ij')
    d_ax = d_rot // 2
    inv = (np.float32(base) ** (-np.arange(0, d_ax, 2, dtype=np.float32) / np.float32(d_ax))).astype(np.float32)
    ang = np.stack([row.reshape(-1)[:, None] * inv, col.reshape(-1)[:, None] * inv], axis=1).astype(np.float32)
    return np.cos(ang).astype(np.float32), np.sin(ang).astype(np.float32)


def make_consts():
    c = {}
    c['ident_f'] = np.eye(128, dtype=np.float32)
    c['ident_b'] = np.eye(128, dtype=np.float32).astype(ml_dtypes.bfloat16)
    lg = np.log1p(-np.exp2(-5.0 - np.arange(4, dtype=np.float32))).astype(np.float32)
    i = np.arange(128, dtype=np.float32)
    qft = np.zeros((128, 2, 128), np.float32)
    qbt = np.zeros((128, 2, 128), np.float32)
    gcc = np.zeros((128, 2), np.float32)
    for cc in range(2):
        for hh in range(2):
            h = 2 * cc + hh
            qft[hh * 64:(hh + 1) * 64, cc, :] = np.exp((i + 1.0) * lg[h])[None, :]
            qbt[hh * 64:(hh + 1) * 64, cc, :] = np.exp((128.0 - i) * lg[h])[None, :]
            gcc[hh * 64:(hh + 1) * 64, cc] = np.exp(np.float32(128.0) * lg[h])
    cos_a, sin_a = _rope_tables(L, 64)
    ct = np.zeros((128, L), np.float32)
    st = np.zeros((128, L), np.float32)
    pm = np.zeros((128, 128), np.float32)
    for blk in range(2):
        for ax in range(2):
            for half in range(2):
                for f in range(16):
                    p = blk * 64 + ax * 32 + half * 16 + f
                    ct[p, :] = cos_a[:, ax, f]
                    st[p, :] = sin_a[:, ax, f]
                    if half == 0:
                        pm[blk * 64 + ax * 32 + 16 + f, p] = -1.0
                    else:
                        pm[blk * 64 + ax * 32 + f, p] = 1.0
    c['swa_ct'] = ct
    c['swa_st'] = st
    c['swa_pm'] = pm.astype(ml_dtypes.bfloat16)
    qi = np.arange(128)[:, None]
    kj = np.arange(128)[None, :]
    NEG = np.float32(-30000.0)
    mk = np.zeros((128, 384), np.float32)
    mk[:, 0:128] = np.where(kj >= qi, 0.0, NEG)
    mk[:, 256:384] = np.where(kj <= qi, 0.0, NEG)
    c['swa_mask'] = mk
    rm = np.ones((128, 512), np.float32)
    rm[:, 0::128] = 0.0
    c['rw_resetm'] = rm
    blk = np.zeros((128, 128), np.float32)
    blk[0:64, 0:64] = 1.0
    blk[64:128, 64:128] = 1.0
    c['rw_blk'] = blk.astype(ml_dtypes.bfloat16)
    blk2 = np.zeros((128, 2), np.float32)
    blk2[0:64, 0] = 1.0
    blk2[64:128, 1] = 1.0
    c['rw_blk2'] = blk2.astype(ml_dtypes.bfloat16)
    si = np.arange(128)[:, None]
    ti = np.arange(128)[None, :]
    m4 = np.zeros((2, 128, 4, 128), np.float32)
    mnt = np.zeros((2, 128, 2, 128), np.float32)
    for dd in range(2):
        strict = (si < ti) if dd == 0 else (si > ti)
        incl = (si <= ti) if dd == 0 else (si >= ti)
        m4[dd, :, 0, :] = strict
        m4[dd, :, 1, :] = incl
        m4[dd, :, 2, :] = -1.0 * strict
        m4[dd, :, 3, :] = incl
        mnt[dd, :, 0, :] = -1.0 * strict.T
        mnt[dd, :, 1, :] = -1.0 * strict.T
    c['rw_mask4'] = m4.reshape(2, 128, 512)
    c['rw_masknt'] = mnt.reshape(2, 128, 256)
    cos_d, sin_d = _rope_tables(L, 32)
    c['mla_cos'] = np.ascontiguousarray(cos_d.reshape(L // 128, 128, 16).transpose(1, 0, 2).reshape(128, (L // 128) * 16))
    c['mla_sin'] = np.ascontiguousarray(sin_d.reshape(L // 128, 128, 16).transpose(1, 0, 2).reshape(128, (L // 128) * 16))
    c['ret_qft'] = qft
    c['ret_qbt'] = qbt
    c['ret_gcc'] = gcc
    kf = np.zeros((128, 4, 64), np.float32)
    kb = np.zeros((128, 4, 64), np.float32)
    dt = np.zeros((128, 4, 128), np.float32)
    for h in range(4):
        kf[:, h, :] = (np.exp((127.0 - i) * lg[h]) * 0.125)[:, None]
        kb[:, h, :] = (np.exp(i * lg[h]) * 0.125)[:, None]
        dt[:, h, :] = np.exp(np.abs(i[:, None] - i[None, :]) * lg[h]) * (1.0 + np.eye(128, dtype=np.float32))
    c['ret_kf'] = kf.reshape(128, 256)
    c['ret_kb'] = kb.reshape(128, 256)
    c['ret_dt'] = dt.reshape(128, 512)
    return c


CONST_SPECS = None


def build_program(debug=None, debug_mixers=None):
    nc = bass.Bass("TRN2", target_bir_lowering=False)
    C = Ctx(nc)
    dram_in = {}

    def din(name, shape, dt=F32):
        dram_in[name] = nc.dram_tensor(name, list(shape), dt, kind="ExternalInput").ap()
        return dram_in[name]

    x_in = din('x', [L, D])
    c_in = din('c', [D])
    ctx_in = din('ctx', [LC, D])
    cctx_in = din('c_ctx', [D])
    W = {}
    wshapes = dict(ada_w=[DEPTH, D, 6 * D], ada_b=[DEPTH, 6 * D], pre_mix_g=[DEPTH, D], post_mix_g=[DEPTH, D],
                   pre_mlp_g=[DEPTH, D], post_mlp_g=[DEPTH, D], w_in=[DEPTH, D, N_IN], w_out=[DEPTH, D, D],
                   swa_sink=[DEPTH, 4], ret_gn_g=[DEPTH, 256], rwkv_mu=[DEPTH, 1024], rwkv_w0=[DEPTH, 2, 256],
                   rwkv_w_up=[DEPTH, 2, 64, 256], rwkv_a0=[DEPTH, 2, 256], rwkv_a_up=[DEPTH, 2, 64, 256],
                   rwkv_g_up=[DEPTH, 128, 256], rwkv_kk=[DEPTH, 256], rwkv_ka=[DEPTH, 256], rwkv_rk=[DEPTH, 4, 64],
                   rwkv_gn_g=[DEPTH, 256], mla_q_norm_g=[DEPTH, 256], mla_w_uq=[DEPTH, 256, 384],
                   mla_kv_norm_g=[DEPTH, 128], mla_w_ukv=[DEPTH, 128, 512], mlp_w1=[DEPTH, D, DFF],
                   mlp_w2=[DEPTH, DFF, D])
    for k, s in wshapes.items():
        W[k] = din(k, s)
    K = {}
    consts = make_consts()
    for k, v in consts.items():
        K[k] = din('k_' + k, v.shape, BF16 if v.dtype == ml_dtypes.bfloat16 else F32)
    out = nc.dram_tensor("out", [L, D], F32, kind="ExternalOutput").ap()
    dbg = {}

    def dbg_out(name, shape, dt=F32):
        dbg[name] = nc.dram_tensor(name, list(shape), dt, kind="ExternalOutput").ap()
        return dbg[name]

    with C.es:
        def dscr(name, shape, dt=F32):
            return nc.dram_tensor(name, list(shape), dt).ap()
        wbf = {}
        for l in range(DEPTH):
            wbf[('w_in', l)] = dscr(f"wbf_in{l}", [D, N_IN], BF16)
            wbf[('w_out', l)] = dscr(f"wbf_out{l}", [D, D], BF16)
            wbf[('w1', l)] = dscr(f"wbf_w1{l}", [D, DFF], BF16)
            wbf[('w2', l)] = dscr(f"wbf_w2{l}", [DFF, D], BF16)
        if debug == 'p4':
            ycat = nc.dram_tensor("ycat_in", [T, D], F32, kind="ExternalInput").ap()
        elif debug == 'mix':
            ycat = dbg_out("d_ycat", [T, D])
        else:
            ycat = dscr("ycat", [T, D])
        xmid = dscr("xmid", [T, D])
        xres = dbg_out("d_xres", [T, D]) if debug == 'p4' else dscr("xres", [T, D])
        h2T_d = dscr("h2T_d", [128, 8, T], BF16)
        zraw = dscr("zraw", [8, 128, T])

        ident_f = C.sb("ident_f", [128, 128], F32)
        ident_b = C.sb("ident_b", [128, 128], BF16)
        C.dma('sp', ident_f[:], K['ident_f'][:, :])
        C.dma('sp', ident_b[:], K['ident_b'][:, :])
        neghalf = C.sb("neghalf", [128, 8], F32)
        C.memset('pool', neghalf[:], -0.5)
        modT = C.sb("modT", [128, 48, 2], F32)
        AB = C.sb("AB", [128, 4, 8, 2], F32)
        scT = C.sb("scT", [128, 8, 2], F32)
        gvec = C.sb("gvec", [128, 4, 8], F32)
        adab = C.sb("adab", [128, 48], F32)
        gv2 = C.sb("gv2", [128, 2, 8, 2], F32)
        Gb = C.sb("Gb", [128, 2, 2, D], F32)
        stat = [C.sb(f"stat{i}", [128, 8], F32) for i in range(2)]
        junk = C.sb("junk", [128, D], F32)

        def rstd_from_ss(st_, nsum, n):
            if nsum == 2:
                C.tt('pool', st_[:, 2:3], st_[:, 0:1], st_[:, 1:2], ALU.add)
                src = st_[:, 2:3]
            else:
                src = st_[:, 0:1]
            C.ts('pool', st_[:, 3:4], src, 1.0 / n, EPS, ALU.mult, ALU.add)
            C.tt('pool', st_[:, 4:5], st_[:, 3:4], neghalf[:, 0:1], ALU.pow)
            return st_[:, 4:5]

        if debug in (None, 'p4', 'mix'):
            with C.scope():
                cf = [C.sb(f"cf{i}", [128, 4096], F32) for i in range(2)]
                cb = [C.sb(f"cb{i}", [128, 4096], BF16) for i in range(2)]
                n = 0
                for l in range(DEPTH):
                    for wname, key, ncol in (('w_in', 'w_in', N_IN), ('w_out', 'w_out', D), ('mlp_w1', 'w1', DFF),
                                             ('mlp_w2', 'w2', DFF)):
                        if debug == 'p4' and (key == 'w_in' or l > 0):
                            continue
                        if debug == 'mix' and (key != 'w_in' or l > 0):
                            continue
                        if key == 'w2':
                            src2 = W[wname][l].rearrange("(a b) n -> a (b n)", b=4)
                            dst2 = wbf[(key, l)].rearrange("(a b) n -> a (b n)", b=4)
                        else:
                            src2 = W[wname][l]
                            dst2 = wbf[(key, l)]
                        for r8 in range(8):
                            f_, b_ = cf[n % 2], cb[n % 2]
                            C.dma('sp', f_[:, 0:ncol], src2[r8 * 128:(r8 + 1) * 128, :])
                            eng = ('dve', 'pool', 'act')[n % 3]
                            C.cp(eng, b_[:, 0:ncol], f_[:, 0:ncol])
                            C.dma('pool', dst2[r8 * 128:(r8 + 1) * 128, :], b_[:, 0:ncol], wk=[(dst2, r8)])
                            n += 1

        for l in range(DEPTH):
            last = (l == DEPTH - 1)
            tiles = list(range(NT))
            with C.scope():
                adaw = [C.sb(f"adaw{i}", [128, 8, 512], F32) for i in range(2)]
                pA = C.ps("pA", [128, 512], F32)
                items = [(adab[:], W['ada_b'][l])]
                for gi, gname in enumerate(('pre_mix_g', 'post_mix_g', 'pre_mlp_g', 'post_mlp_g')):
                    items.append((gvec[:, gi, :], W[gname][l]))
                if l == 0:
                    items += [(scT[:, :, 0], c_in), (scT[:, :, 1], cctx_in)]
                load_cols(C, ident_f, items)
                if l == 0:
                    C.act(junk[:, 0:16], scT[:].rearrange("p k s -> p (k s)"), AF.Sigmoid)
                    C.tt('dve', scT[:].rearrange("p k s -> p (k s)"), scT[:].rearrange("p k s -> p (k s)"),
                         junk[:, 0:16], ALU.mult)
                for s in range(12):
                    aw = adaw[s % 2]
                    C.dma('sp' if s % 2 == 0 else 'pool', aw[:],
                          W['ada_w'][l][:, s * 512:(s + 1) * 512].rearrange("(k p) n -> p k n", p=128))
                    for jj in range(4):
                        j = s * 4 + jj
                        for k in range(8):
                            C.mm(pA[:, 2 * j:2 * j + 2], aw[:, k, jj * 128:(jj + 1) * 128], scT[:, k, :],
                                 start=(k == 0), stop=(k == 7))
                C.tt('dve', modT[:], pA[:, 0:96].rearrange("p (j s) -> p j s", s=2),
                     adab[:].unsqueeze(2).to_broadcast([128, 48, 2]), ALU.add)
                for which, (gi, sci, shi) in enumerate(((0, 1, 0), (2, 4, 3))):
                    C.ts('dve', AB[:, 2 * which, :, :], modT[:, sci * 8:(sci + 1) * 8, :], 1.0, None, ALU.add)
                    C.tt('dve', AB[:, 2 * which, :, :], AB[:, 2 * which, :, :],
                         gvec[:, gi, :].unsqueeze(2).to_broadcast([128, 8, 2]), ALU.mult)
                    C.cp('dve', AB[:, 2 * which + 1, :, :], modT[:, shi * 8:(shi + 1) * 8, :])
                for w_, (gti, pgi) in enumerate(((2, 1), (5, 3))):
                    C.tt('dve', gv2[:, w_, :, :], modT[:, gti * 8:(gti + 1) * 8, :],
                         gvec[:, pgi, :].unsqueeze(2).to_broadcast([128, 8, 2]), ALU.mult)
                    for s_ in range(2):
                        for k in range(8):
                            C.mm(pA[:, (k % 4) * 128:(k % 4 + 1) * 128],
                                 gv2[:, w_, k, s_:s_ + 1].to_broadcast([128, 128]), ident_f[:])
                            if k % 4 == 3:
                                C.cp('act', Gb[:, w_, s_, (k - 3) * 128:(k + 1) * 128], pA[:, :])

            mix_scope = contextlib.ExitStack()
            if debug != 'p4':
              with C.scope():
                hT = C.sb("hT", [128, 8, T], BF16)
                with C.scope():
                    xt = [C.sb(f"xt{i}", [128, D], F32) for i in range(2)]
                    xn = [C.sb(f"xn{i}", [128, D], F32) for i in range(2)]
                    pA = C.ps("pA", [128, 512], F32)
                    pB = C.ps("pB", [128, 512], F32)
                    for i in range(NT):
                        s = 1 if i < NCT else 0
                        if l == 0:
                            src = ctx_in[i * 128:(i + 1) * 128, :] if i < NCT else x_in[(i - NCT) * 128:(i - NCT + 1) * 128, :]
                        else:
                            src = xres[i * 128:(i + 1) * 128, :]
                        xb_, xn_, st_ = xt[i % 2], xn[i % 2], stat[i % 2]
                        C.dma('sp' if i % 2 == 0 else 'pool', xb_[:], src, rk=(xres, i) if l > 0 else None)
                        C.act(junk[:], xb_[:], AF.Square, accum_out=st_[:, 0:1])
                        rs = rstd_from_ss(st_, 1, D)
                        C.ts('dve', xn_[:], xb_[:], rs, None, ALU.mult)
                        for half in range(2):
                            pp = pA if half == 0 else pB
                            for kk in range(4):
                                k = half * 4 + kk
                                C.tr(pp[:, kk * 128:(kk + 1) * 128], xn_[:, k * 128:(k + 1) * 128], ident_f[:])
                            for kk in range(4):
                                k = half * 4 + kk
                                if half == 0:
                                    C.act(hT[:, k, i * 128:(i + 1) * 128], pp[:, kk * 128:(kk + 1) * 128], AF.Identity,
                                          bias=AB[:, 1, k, s:s + 1], scale=AB[:, 0, k, s:s + 1], wk=[(hT, i)])
                                else:
                                    C.ts('dve', hT[:, k, i * 128:(i + 1) * 128], pp[:, kk * 128:(kk + 1) * 128],
                                         AB[:, 0, k, s:s + 1], AB[:, 1, k, s:s + 1], ALU.mult, ALU.add, wk=[(hT, i)])
                if debug == 'hT' and l == 0:
                    d1 = dbg_out("d_modT", [128, 96])
                    C.dma('sp', d1[:, :], modT[:].rearrange("p j s -> p (j s)"))
                    d2 = dbg_out("d_hT", [128, 8 * T], BF16)
                    C.dma('sp', d2[:, :], hT[:].rearrange("p k t -> p (k t)"))
                MIXERS(locals())
            if debug != 'p4' and (debug_mixers is None or 'rwkv' in debug_mixers):
                E2 = dict(locals())
                E2['emit'] = l < DEPTH - 1
                rwkv_main(E2)
            if debug in ('hT', 'mix'):
                break

            ptiles = [i for i in range(NT) if not (last and i < NCT)]
            with C.scope():
                wo = C.sb("wo", [128, 8, D], BF16)
                C.dma('sp', wo[:], wbf[('w_out', l)].rearrange("(k p) n -> p k n", p=128), rk=wbf[('w_out', l)])
                yt = [C.sb(f"yt{i}", [128, D], F32) for i in range(2)]
                ybf = [C.sb(f"ybf{i}", [128, D], BF16) for i in range(2)]
                yT = [C.sb(f"yT{i}", [128, 8, 128], BF16) for i in range(2)]
                xt = [C.sb(f"xt{i}", [128, D], F32) for i in range(2)]
                xm = [C.sb(f"xm{i}", [128, D], F32) for i in range(2)]
                xn = [C.sb(f"xn{i}", [128, D], F32) for i in range(2)]
                h2 = [C.sb(f"h2{i}", [128, 8, 128], BF16) for i in range(2)]
                pT = C.ps("pT", [128, D], BF16)
                pY = [C.ps(f"pY{i}", [128, 512], F32) for i in range(2)]
                pA = C.ps("pA", [128, 512], F32)
                pB = C.ps("pB", [128, 512], F32)
                for n_, i in enumerate(ptiles):
                    s = 1 if i < NCT else 0
                    b2 = n_ % 2
                    st_ = stat[b2]
                    C.dma('sp', yt[b2][:], ycat[i * 128:(i + 1) * 128, :], rk=(ycat, i))
                    if l == 0:
                        src = ctx_in[i * 128:(i + 1) * 128, :] if i < NCT else x_in[(i - NCT) * 128:(i - NCT + 1) * 128, :]
                    else:
                        src = xres[i * 128:(i + 1) * 128, :]
                    C.dma('pool', xt[b2][:], src, rk=(xres, i) if l > 0 else None)
                    C.cp('pool' if b2 else 'dve', ybf[b2][:], yt[b2][:])
                    for k in range(8):
                        C.tr(pT[:, k * 128:(k + 1) * 128], ybf[b2][:, k * 128:(k + 1) * 128], ident_b[:])
                    C.cp('act', yT[b2][:].rearrange("p k t -> p (k t)"), pT[:, :])
                    for nh in range(2):
                        for k in range(8):
                            C.mm(pY[nh][:, :], yT[b2][:, k, :], wo[:, k, nh * 512:(nh + 1) * 512],
                                 start=(k == 0), stop=(k == 7))
                    for nh in range(2):
                        C.act(junk[:, nh * 512:(nh + 1) * 512], pY[nh][:, :], AF.Square, accum_out=st_[:, nh:nh + 1])
                    rs = rstd_from_ss(st_, 2, D)
                    for nh in range(2):
                        sl = slice(nh * 512, (nh + 1) * 512)
                        C.stt(xm[b2][:, sl], pY[nh][:, :], rs, Gb[:, 0, s, sl], ALU.mult, ALU.mult)
                    C.tt('pool', xm[b2][:], xm[b2][:], xt[b2][:], ALU.add)
                    C.dma('sp', xmid[i * 128:(i + 1) * 128, :], xm[b2][:], wk=(xmid, i))
                    C.act(junk[:], xm[b2][:], AF.Square, accum_out=st_[:, 5:6])
                    C.ts('pool', st_[:, 6:7], st_[:, 5:6], 1.0 / D, EPS, ALU.mult, ALU.add)
                    C.tt('pool', st_[:, 7:8], st_[:, 6:7], neghalf[:, 0:1], ALU.pow)
                    C.ts('dve', xn[b2][:], xm[b2][:], st_[:, 7:8], None, ALU.mult)
                    for half in range(2):
                        pp = pA if half == 0 else pB
                        for kk in range(4):
                            k = half * 4 + kk
                            C.tr(pp[:, kk * 128:(kk + 1) * 128], xn[b2][:, k * 128:(k + 1) * 128], ident_f[:])
                        for kk in range(4):
                            k = half * 4 + kk
                            if half == 0:
                                C.act(h2[b2][:, k, :], pp[:, kk * 128:(kk + 1) * 128], AF.Identity,
                                      bias=AB[:, 3, k, s:s + 1], scale=AB[:, 2, k, s:s + 1])
                            else:
                                C.ts('dve', h2[b2][:, k, :], pp[:, kk * 128:(kk + 1) * 128],
                                     AB[:, 2, k, s:s + 1], AB[:, 3, k, s:s + 1], ALU.mult, ALU.add)
                    C.dma('pool', h2T_d[:, :, i * 128:(i + 1) * 128], h2[b2][:], wk=(h2T_d, i))

            with C.scope():
                w1s = C.sb("w1s", [128, 8, DFF], BF16)
                w2s = C.sb("w2s", [128, 32, D], BF16)
                for k in range(8):
                    C.dma('sp' if k % 2 == 0 else 'pool', w1s[:, k, :], wbf[('w1', l)][k * 128:(k + 1) * 128, :],
                          rk=wbf[('w1', l)], wk=(w1s, k))
                for j4 in range(8):
                    C.dma('sp' if j4 % 2 == 0 else 'pool', w2s[:, j4 * 4:(j4 + 1) * 4, :],
                          wbf[('w2', l)][j4 * 512:(j4 + 1) * 512, :].rearrange("(j p) n -> p j n", p=128),
                          rk=wbf[('w2', l)], wk=(w2s, j4))
                h2g = [C.sb(f"h2g{i}", [128, 8, 256], BF16) for i in range(2)]
                rl = [C.sb(f"rl{i}", [128, 256], BF16) for i in range(2)]
                uT = [C.sb(f"uT{i}", [128, 256], BF16) for i in range(3)]
                xm = [C.sb(f"xm{i}", [128, D], F32) for i in range(2)]
                xo = [C.sb(f"xo{i}", [128, D], F32) for i in range(2)]
                pH = [C.ps(f"pH{i}", [128, 512], F32) for i in range(2)]
                pF = [[C.ps(f"pF{t_}{h_}", [128, 512], F32) for h_ in range(2)] for t_ in range(2)]
                groups = [ptiles[a:a + 2] for a in range(0, len(ptiles), 2)]
                nu = 0
                for gi, grp in enumerate(groups):
                    i0 = grp[0]
                    hg = h2g[gi % 2]
                    C.dma('sp', hg[:], h2T_d[:, :, i0 * 128:(i0 + 2) * 128], rk=(h2T_d, None))
                    for j in range(32):
                        ph = pH[j % 2]
                        for k in range(8):
                            C.mm(ph[:, 0:256], w1s[:, k, j * 128:(j + 1) * 128], hg[:, k, :],
                                 start=(k == 0), stop=(k == 7), rk=[(w1s, k), hg])
                        r_ = rl[j % 2]
                        u_ = uT[nu % 3]
                        nu += 1
                        C.act(r_[:], ph[:, 0:256], AF.Relu)
                        C.tt('dve' if j % 2 == 0 else 'pool', u_[:], r_[:], r_[:], ALU.mult)
                        for t_ in range(2):
                            for nh in range(2):
                                C.mm(pF[t_][nh][:, :], u_[:, t_ * 128:(t_ + 1) * 128], w2s[:, j, nh * 512:(nh + 1) * 512],
                                     start=(j == 0), stop=(j == 31), rk=[u_, (w2s, j // 4)])
                    for t_, i in enumerate(grp):
                        s = 1 if i < NCT else 0
                        b2 = (gi * 2 + t_) % 2
                        st_ = stat[b2]
                        C.dma('pool', xm[b2][:], xmid[i * 128:(i + 1) * 128, :], rk=(xmid, i))
                        for nh in range(2):
                            C.act(junk[:, nh * 512:(nh + 1) * 512], pF[t_][nh][:, :], AF.Square,
                                  accum_out=st_[:, nh:nh + 1])
                        rs = rstd_from_ss(st_, 2, D)
                        for nh in range(2):
                            sl = slice(nh * 512, (nh + 1) * 512)
                            C.stt(xo[b2][:, sl], pF[t_][nh][:, :], rs, Gb[:, 1, s, sl], ALU.mult, ALU.mult)
                        C.tt('pool', xo[b2][:], xo[b2][:], xm[b2][:], ALU.add)
                        if last:
                            C.dma('sp', out[(i - NCT) * 128:(i - NCT + 1) * 128, :], xo[b2][:])
                        else:
                            C.dma('sp', xres[i * 128:(i + 1) * 128, :], xo[b2][:], wk=(xres, i))
            if debug == 'p4':
                break
        C.finish()
    return nc, list(dbg.keys())


def mixer_ret(E):
    C, l, hT, wbf, W, K, ycat = E['C'], E['l'], E['hT'], E['wbf'], E['W'], E['K'], E['ycat']
    stat, neghalf, emit = E['stat'], E['neghalf'], E['emit']
    with C.scope():
        wB = C.sb("wB", [128, 8, 1024], BF16)
        C.dma('sp', wB[:], wbf[('w_in', l)][:, 512:1536].rearrange("(k p) n -> p k n", p=128), rk=wbf[('w_in', l)])
        tabs = {}
        for nm, shp in (('ret_qft', [128, 256]), ('ret_qbt', [128, 256]), ('ret_gcc', [128, 2]), ('ret_kf', [128, 256]),
                        ('ret_kb', [128, 256]), ('ret_dt', [128, 512])):
            tabs[nm] = C.sb(nm, shp, F32)
            src = K[nm]
            if len(src.shape) == 3:
                src = src.rearrange("p a b -> p (a b)")
            C.dma('pool', tabs[nm][:], src)
        gng = C.sb("ret_gng", [128, 256], F32)
        C.dma('pool', gng[:], W['ret_gn_g'][l].partition_broadcast(128))
        Gs = [C.sb(f"retG{d}", [128, NT, 128], F32) for d in range(2)]
        Ss = [C.sb(f"retS{d}", [128, NT, 128], BF16) for d in range(2)]
        cur = [C.sb(f"retcur{d}", [128, 128], F32) for d in range(2)]
        import os
        if int(os.environ.get('RET_STOP', '99')) <= 0:
            return
        with C.scope():
            pKV = [C.ps(f"pKV{i}", [128, 512], F32) for i in range(2)]
            pV1 = [C.ps(f"pV1{i}", [128, 512], F32) for i in range(2)]
            pGb = [C.ps(f"pG{i}", [128, 512], F32) for i in range(2)]
            pG = [[pGb[i][:, d * 128:(d + 1) * 128] for d in range(2)] for i in range(2)]
            kd = [[C.sb(f"kd{i}{d}", [128, 256], BF16) for d in range(2)] for i in range(2)]
            vv = [C.sb(f"vv{i}", [128, 256], BF16) for i in range(2)]
            for n in range(NT):
                b2 = n % 2
                for k in range(8):
                    C.mm(pKV[b2][:, 0:256], hT[:, k, n * 128:(n + 1) * 128], wB[:, k, 256:512], start=(k == 0), stop=(k == 7),
                         rk=[(hT, n), wB])
                for k in range(8):
                    C.mm(pV1[b2][:, 256:512], hT[:, k, n * 128:(n + 1) * 128], wB[:, k, 512:768], start=(k == 0), stop=(k == 7),
                         rk=[(hT, n), wB])
                p1 = os.environ.get('RET_P1', '')
                if 'nodve' not in p1:
                    C.tt('dve', kd[b2][0][:], pKV[b2][:, 0:256], tabs['ret_kf'][:], ALU.mult)
                    C.tt('dve', kd[b2][1][:], pKV[b2][:, 0:256], tabs['ret_kb'][:], ALU.mult)
                if 'noact' not in p1:
                    C.cp('act', vv[b2][:], pV1[b2][:, 256:512])
                for d in range(2):
                    if 'nog' in os.environ.get('RET_P1', ''):
                        continue
                    for h in range(4):
                        if os.environ.get('RET_P1') == 'h0' and h != 0:
                            continue
                        hh, cc = h % 2, h // 2
                        C.mm(pGb[b2][hh * 64:(hh + 1) * 64, d * 128 + cc * 64:d * 128 + (cc + 1) * 64], kd[b2][d][:, h * 64:(h + 1) * 64],
                             vv[b2][:, h * 64:(h + 1) * 64])
                    C.cp('act', Gs[d][:, n, :], pG[b2][d], wk=[(Gs[d], n)])
        import os
        stop = int(os.environ.get('RET_STOP', '99'))
        if stop <= 1:
            return
        gcc = tabs['ret_gcc']
        for d in range(2):
            order = list(range(NT)) if d == 0 else [1, 0] + list(range(NT - 1, 1, -1))
            C.memset('dve', cur[d][:], 0.0)
            for n in order:
                C.cp('act', Ss[d][:, n, :], cur[d][:], wk=[(Ss[d], n)])
                for cc in range(2):
                    C.stt(cur[d][:, cc * 64:(cc + 1) * 64], cur[d][:, cc * 64:(cc + 1) * 64], gcc[:, cc:cc + 1],
                          Gs[d][:, n, cc * 64:(cc + 1) * 64], ALU.mult, ALU.add, rk=[cur[d], gcc, (Gs[d], n)])
        if stop <= 2:
            return
        with C.scope():
            pQ = C.ps("pQ", [128, 512], F32)
            pK = C.ps("pK", [128, 512], F32)
            pVG = C.ps("pVG", [128, 512], F32)
            pS = [C.ps(f"pS{i}", [128, 512], F32) for i in range(2)]
            pOb = [C.ps(f"pO{i}", [128, 512], F32) for i in range(2)]
            qT = [C.sb(f"rqT{i}", [128, 2, 128], BF16) for i in range(2)]
            qTf = [C.sb(f"rqTf{i}", [128, 2, 128], BF16) for i in range(2)]
            qTb = [C.sb(f"rqTb{i}", [128, 2, 128], BF16) for i in range(2)]
            kT = [C.sb(f"rkT{i}", [128, 2, 128], BF16) for i in range(2)]
            vv = [C.sb(f"rvv{i}", [128, 256], BF16) for i in range(2)]
            sg = [C.sb(f"rsg{i}", [128, 256], F32) for i in range(2)]
            sTm = [C.sb(f"rsTm{i}", [128, 4, 128], BF16) for i in range(2)]
            oc = [C.sb(f"roc{i}", [128, 256], F32) for i in range(2)]
            osq = [C.sb(f"rosq{i}", [128, 256], F32) for i in range(2)]
            yo = [C.sb(f"ryo{i}", [128, 256], F32) for i in range(2)]
            st4 = [C.sb(f"rst{i}", [128, 8, 4], F32) for i in range(2)]
            nh4 = C.sb("nh4", [128, 4], F32)
            C.memset('pool', nh4[:], -0.5)
            tiles = [n for n in range(NT) if emit or n >= NCT]
            for n in tiles:
                b2 = n % 2
                for j in range(4):
                    col = (j % 2) * 128 + (0 if j < 2 else 256)
                    dst = (pQ if j < 2 else pK)[:, (j % 2) * 128:(j % 2 + 1) * 128]
                    for k in range(8):
                        C.mm(dst, wB[:, k, col:col + 128], hT[:, k, n * 128:(n + 1) * 128],
                             start=(k == 0), stop=(k == 7), rk=[(hT, n), wB])
                for k in range(8):
                    C.mm(pVG[:, :], hT[:, k, n * 128:(n + 1) * 128], wB[:, k, 512:1024], start=(k == 0), stop=(k == 7),
                         rk=[(hT, n), wB])
                C.cp('dve', qT[b2][:].rearrange("p c t -> p (c t)"), pQ[:, 0:256])
                C.tt('dve', qTf[b2][:].rearrange("p c t -> p (c t)"), pQ[:, 0:256], tabs['ret_qft'][:], ALU.mult)
                C.tt('dve', qTb[b2][:].rearrange("p c t -> p (c t)"), pQ[:, 0:256], tabs['ret_qbt'][:], ALU.mult)
                C.act(kT[b2][:].rearrange("p c t -> p (c t)"), pK[:, 0:256], AF.Copy, scale=0.125)
                C.cp('act', vv[b2][:], pVG[:, 0:256])
                C.act(sg[b2][:], pVG[:, 256:512], AF.Silu)
                for h in range(4):
                    hh, cc = h % 2, h // 2
                    C.mm(pS[hh][:, cc * 128:(cc + 1) * 128], kT[b2][hh * 64:(hh + 1) * 64, cc, :],
                         qT[b2][hh * 64:(hh + 1) * 64, cc, :])
                dt4 = tabs['ret_dt'][:].rearrange("p (c hh t) -> p c hh t", hh=2, t=128)
                sT4 = sTm[b2][:].rearrange("p (c hh) t -> p c hh t", hh=2)
                for hh in range(2):
                    C.tt('dve', sT4[:, :, hh, :], pS[hh][:, 0:256].rearrange("p (c t) -> p c t", t=128), dt4[:, :, hh, :],
                         ALU.mult)
                for h in range(4):
                    hh, cc = h % 2, h // 2
                    o_ = pOb[hh][:, cc * 64:(cc + 1) * 64]
                    C.mm(o_, sTm[b2][:, h, :], vv[b2][:, h * 64:(h + 1) * 64], start=True, stop=False)
                    C.mm(o_, qTf[b2][hh * 64:(hh + 1) * 64, cc, :], Ss[0][hh * 64:(hh + 1) * 64, n, cc * 64:(cc + 1) * 64],
                         start=False, stop=False, rk=[qTf[b2], (Ss[0], n)])
                    C.mm(o_, qTb[b2][hh * 64:(hh + 1) * 64, cc, :], Ss[1][hh * 64:(hh + 1) * 64, n, cc * 64:(cc + 1) * 64],
                         start=False, stop=True, rk=[qTb[b2], (Ss[1], n)])
                s4 = st4[b2]
                oc4 = oc[b2][:].rearrange("p (c hh e) -> p c hh e", hh=2, e=64)
                for hh in range(2):
                    C.cp('act', oc4[:, :, hh, :], pOb[hh][:, 0:128].rearrange("p (c e) -> p c e", e=64))
                C.act(osq[b2][:], oc[b2][:], AF.Square)
                o3 = oc[b2][:].rearrange("p (h e) -> p h e", e=64)
                C.red('dve', s4[:, 0, :], o3, ALU.add)
                C.red('dve', s4[:, 1, :], osq[b2][:].rearrange("p (h e) -> p h e", e=64), ALU.add)
                C.ts('pool', s4[:, 2, :], s4[:, 0, :], 1.0 / 64, None, ALU.mult)
                C.tt('pool', s4[:, 3, :], s4[:, 2, :], s4[:, 2, :], ALU.mult)
                C.ts('pool', s4[:, 4, :], s4[:, 1, :], 1.0 / 64, 1e-5, ALU.mult, ALU.add)
                C.tt('pool', s4[:, 4, :], s4[:, 4, :], s4[:, 3, :], ALU.subtract)
                C.tt('pool', s4[:, 5, :], s4[:, 4, :], nh4[:], ALU.pow)
                y3 = yo[b2][:].rearrange("p (h e) -> p h e", e=64)
                C.tt('dve', y3, o3, s4[:, 2, :].unsqueeze(2).to_broadcast([128, 4, 64]), ALU.subtract)
                C.tt('dve', y3, y3, s4[:, 5, :].unsqueeze(2).to_broadcast([128, 4, 64]), ALU.mult)
                C.tt('pool', yo[b2][:], yo[b2][:], gng[:], ALU.mult)
                C.tt('pool', yo[b2][:], yo[b2][:], sg[b2][:], ALU.mult)
                C.dma('sp', ycat[n * 128:(n + 1) * 128, 256:512], yo[b2][:], wk=(ycat, n))


def mixer_swa(E):
    C, l, hT, wbf, W, K, ycat = E['C'], E['l'], E['hT'], E['wbf'], E['W'], E['K'], E['ycat']
    emit, ident_b = E['emit'], E['ident_b']
    with C.scope():
        wA = C.sb("wA", [128, 8, 512], BF16)
        C.dma('sp', wA[:], wbf[('w_in', l)][:, 0:512].rearrange("(k p) n -> p k n", p=128), rk=wbf[('w_in', l)])
        wK2 = C.sb("wK2", [128, 8, 2, 128], BF16)
        for hk in range(2):
            for dup in range(2):
                C.dma('pool', wK2[:, :, hk, dup * 64:(dup + 1) * 64],
                      wbf[('w_in', l)][:, 256 + hk * 64:256 + (hk + 1) * 64].rearrange("(k p) n -> p k n", p=128),
                      rk=wbf[('w_in', l)])
        pm = C.sb("swa_pm", [128, 128], BF16)
        C.dma('pool', pm[:], K['swa_pm'][:, :])
        mask = C.sb("swa_mask", [128, 384], F32)
        C.dma('pool', mask[:], K['swa_mask'][:, :])
        sinkb = C.sb("sinkb", [128, 4], F32)
        C.dma('pool', sinkb[:], W['swa_sink'][l].partition_broadcast(128))
        qTr = C.sb("qTr", [128, 2, T], BF16)
        kTz = [C.sb(f"kTz{g}", [128, 2, T], BF16) for g in range(2)]
        vtok = C.sb("vtok", [128, NT, 128], BF16)
        C.memset('pool', kTz[0][64:128, :, :], 0.0)
        C.memset('pool', kTz[1][0:64, :, :], 0.0)
        with C.scope():
            pQ = C.ps("pQ", [128, 512], F32)
            pP = C.ps("pP", [128, 512], F32)
            pV = C.ps("pV", [128, 512], F32)
            ctg = [C.sb(f"ctg{i}", [128, 512], F32) for i in range(2)]
            stg = [C.sb(f"stg{i}", [128, 512], F32) for i in range(2)]
            xq = [C.sb(f"xq{i}", [128, 512], BF16) for i in range(2)]
            t1 = [C.sb(f"t1{i}", [128, 512], F32) for i in range(2)]
            t2 = [C.sb(f"t2{i}", [128, 512], F32) for i in range(2)]
            groups = [(0, LC)] + [(LC + 512 * i, 512) for i in range(8)]
            nx = 0
            for gi, (t0, n) in enumerate(groups):
                rope = t0 >= LC
                if rope:
                    C.dma('sp', ctg[gi % 2][:], K['swa_ct'][:, t0 - LC:t0 - LC + n])
                    C.dma('sp', stg[gi % 2][:], K['swa_st'][:, t0 - LC:t0 - LC + n])
                for j in range(4):
                    hk = j % 2
                    for k in range(8):
                        lhsT = wA[:, k, hk * 128:(hk + 1) * 128] if j < 2 else wK2[:, k, hk, :]
                        C.mm(pQ[:, 0:n], lhsT, hT[:, k, t0:t0 + n], start=(k == 0), stop=(k == 7), rk=[hT, wA, wK2])
                    x_ = xq[nx % 2]
                    nx += 1
                    C.cp('act', x_[:, 0:n], pQ[:, 0:n])
                    if rope:
                        C.mm(pP[:, 0:n], pm[:], x_[:, 0:n])
                        a_, b_ = t1[nx % 2], t2[nx % 2]
                        C.tt('pool', a_[:, 0:n], x_[:, 0:n], ctg[gi % 2][:, 0:n], ALU.mult)
                        C.tt('dve', b_[:, 0:n], pP[:, 0:n], stg[gi % 2][:, 0:n], ALU.mult)
                        if j < 2:
                            C.tt('pool', qTr[:, hk, t0:t0 + n], a_[:, 0:n], b_[:, 0:n], ALU.add, wk=[(qTr, gi)])
                        else:
                            for g in range(2):
                                C.tt('pool', kTz[g][g * 64:(g + 1) * 64, hk, t0:t0 + n], a_[g * 64:(g + 1) * 64, 0:n],
                                     b_[g * 64:(g + 1) * 64, 0:n], ALU.add, wk=[(kTz[g], gi)])
                    else:
                        if j < 2:
                            C.cp('pool', qTr[:, hk, t0:t0 + n], x_[:, 0:n], wk=[(qTr, gi)])
                        else:
                            for g in range(2):
                                C.cp('pool', kTz[g][g * 64:(g + 1) * 64, hk, t0:t0 + n], x_[g * 64:(g + 1) * 64, 0:n],
                                     wk=[(kTz[g], gi)])
                for ti in range(t0 // 128, (t0 + n) // 128):
                    for k in range(8):
                        C.mm(pV[:, 0:128], hT[:, k, ti * 128:(ti + 1) * 128], wA[:, k, 384:512], start=(k == 0),
                             stop=(k == 7), rk=[hT, wA])
                    C.cp('act', vtok[:, ti, :], pV[:, 0:128], wk=[(vtok, ti)])
        with C.scope():
            pSa = [C.ps(f"pSa{i}", [128, 512], F32) for i in range(2)]
            pSb = [C.ps(f"pSb{i}", [128, 512], F32) for i in range(2)]
            pT = [C.ps(f"pTs{i}", [128, 1024], BF16) for i in range(2)]
            pO = C.ps("pOs", [128, 512], F32)
            sc = [C.sb(f"sc{i}", [128, 640], F32) for i in range(2)]
            pb = [C.sb(f"pb{i}", [128, 640], BF16) for i in range(2)]
            pTs = [C.sb(f"pTsb{i}", [128, 640], BF16) for i in range(2)]
            sm = [C.sb(f"sm{i}", [128, 8], F32) for i in range(2)]
            ya = [C.sb(f"ya{i}", [128, 256], F32) for i in range(2)]
            rden = [C.sb(f"rden{i}", [128, 4], F32) for i in range(2)]
            qblocks = [n for n in range(NT) if emit or n >= NCT]
            nh = 0
            for bi, n in enumerate(qblocks):
                isx = n >= NCT
                if isx:
                    lo, hi = max(n - 1, NCT), min(n + 1, NT - 1)
                    ktiles = list(range(lo, hi + 1)) + [0, 1]
                    nloc = hi - lo + 1
                    moff = (lo - (n - 1)) * 128
                else:
                    lo, hi = 0, 1
                    ktiles = [0, 1]
                    nloc = 2
                ncol = len(ktiles) * 128
                for h in range(4):
                    hk, g = h // 2, h % 2
                    b2 = nh % 2
                    nh += 1
                    C.mm(pSa[b2][:, 0:nloc * 128], qTr[:, hk, n * 128:(n + 1) * 128], kTz[g][:, hk, lo * 128:(hi + 1) * 128],
                         rk=[qTr, kTz[g]])
                    if isx:
                        C.mm(pSb[b2][:, 0:256], qTr[:, hk, n * 128:(n + 1) * 128], kTz[g][:, hk, 0:256], rk=[qTr, kTz[g]])
                        C.tt('dve', sc[b2][:, 0:nloc * 128], pSa[b2][:, 0:nloc * 128], mask[:, moff:moff + nloc * 128], ALU.add)
                        C.cp('dve', sc[b2][:, nloc * 128:ncol], pSb[b2][:, 0:256])
                    else:
                        C.cp('dve', sc[b2][:, 0:ncol], pSa[b2][:, 0:ncol])
                    s_ = sm[b2]
                    C.red('dve', s_[:, 0:1], sc[b2][:, 0:ncol], ALU.max)
                    C.ts('dve', s_[:, 1:2], s_[:, 0:1], 0.125, sinkb[:, h:h + 1], ALU.mult, ALU.max)
                    C.ts('dve', s_[:, 2:3], s_[:, 1:2], -1.0, None, ALU.mult)
                    C.act(pb[b2][:, 0:ncol], sc[b2][:, 0:ncol], AF.Exp, bias=s_[:, 2:3], scale=0.125, accum_out=s_[:, 3:4])
                    C.act(s_[:, 4:5], sinkb[:, h:h + 1], AF.Exp, bias=s_[:, 2:3], scale=1.0)
                    C.tt('dve', s_[:, 5:6], s_[:, 3:4], s_[:, 4:5], ALU.add)
                    C.op('dve', [s_], [rden[bi % 2]], lambda e_, o_=rden[bi % 2][:, h:h + 1], i_=s_[:, 5:6]: e_.reciprocal(out=o_, in_=i_))
                    for j in range(len(ktiles)):
                        C.tr(pT[b2][:, j * 128:(j + 1) * 128], pb[b2][:, j * 128:(j + 1) * 128], ident_b[:])
                    C.cp('act', pTs[b2][:, 0:ncol], pT[b2][:, 0:ncol])
                    for j, kt in enumerate(ktiles):
                        C.mm(pO[:, h * 64:(h + 1) * 64], pTs[b2][:, j * 128:(j + 1) * 128], vtok[:, kt, hk * 64:(hk + 1) * 64],
                             start=(j == 0), stop=(j == len(ktiles) - 1), rk=[pTs[b2], vtok])
                for h in range(4):
                    C.act(ya[bi % 2][:, h * 64:(h + 1) * 64], pO[:, h * 64:(h + 1) * 64], AF.Copy, scale=rden[bi % 2][:, h:h + 1])
                C.dma('sp', ycat[n * 128:(n + 1) * 128, 0:256], ya[bi % 2][:], wk=(ycat, n))


def mixer_mla(E):
    C, l, hT, wbf, W, K, ycat = E['C'], E['l'], E['hT'], E['wbf'], E['W'], E['K'], E['ycat']
    emit, ident_b, ident_f, neghalf = E['emit'], E['ident_b'], E['ident_f'], E['neghalf']
    SCALE = 96.0 ** -0.5
    with C.scope():
        wD = C.sb("wD", [128, 8, 416], BF16)
        C.dma('sp', wD[:], wbf[('w_in', l)][:, 2560:2976].rearrange("(k p) n -> p k n", p=128), rk=wbf[('w_in', l)])
        wf = C.sb("mla_wf", [128, 1280], F32)
        wuq = C.sb("wuq", [128, 2, 384], BF16)
        wukv = C.sb("wukv", [128, 512], BF16)
        C.dma('sp', wf[:, 0:768].rearrange("p (c n) -> p c n", c=2), W['mla_w_uq'][l].rearrange("(c p) n -> p c n", p=128))
        C.dma('sp', wf[:, 768:1280], W['mla_w_ukv'][l])
        C.cp('dve', wuq[:].rearrange("p c n -> p (c n)"), wf[:, 0:768])
        C.cp('dve', wukv[:], wf[:, 768:1280])
        gq = C.sb("mla_gq", [128, 2], F32)
        gkv = C.sb("mla_gkv", [128, 1], F32)
        load_cols(C, ident_f, [(gq[:], W['mla_q_norm_g'][l]), (gkv[:], W['mla_kv_norm_g'][l])])
        cosT = C.sb("mla_cos", [128, L // 128, 16], F32)
        sinT = C.sb("mla_sin", [128, L // 128, 16], F32)
        C.dma('pool', cosT[:].rearrange("p n f -> p (n f)"), K['mla_cos'][:, :])
        C.dma('pool', sinT[:].rearrange("p n f -> p (n f)"), K['mla_sin'][:, :])
        ones_b = C.sb("mla_ones", [128, 2], BF16)
        C.memset('pool', ones_b[:], 1.0)
        ones_f = C.sb("mla_onesf", [1, 128], F32)
        C.memset('pool', ones_f[:], 1.0)
        half = C.sb("mla_half", [128, 4], F32)
        C.memset('pool', half[:], 0.5)
        KaT = C.sb("KaT", [128, 4, T], BF16)
        Va = C.sb("Va", [128, NT, 4, 65], BF16)
        C.memset('pool', Va[:, :, :, 64:65], 1.0)
        kmx = C.sb("kmx", [128, 4], F32)
        C.memset('dve', kmx[:], 0.0)
        kmaxb = C.sb("kmaxb", [128, 2], F32)
        aug = [C.sb(f"aug{i}", [128, 4, 128], BF16) for i in range(2)]
        for i in range(2):
            C.memset('pool', aug[i][:], 0.0)
        fT = [C.sb(f"mfT{i}", [128, 2, 128], BF16) for i in range(2)]
        sqT = [C.sb(f"msq{i}", [128, 2, 128], BF16) for i in range(2)]
        qf = [C.sb(f"mqf{i}", [128, 512], F32) for i in range(2)]
        tmp = [C.sb(f"mtmp{i}", [128, 384], F32) for i in range(2)]
        rp = [C.sb(f"mrp{i}", [128, 6, 64], F32) for i in range(2)]
        st_ = [C.sb(f"mst{i}", [128, 16], F32) for i in range(2)]

        def rope_tm(out1, out2, x1, x2, c_, s_, r_, shape):
            n = int(np.prod(shape[1:]))
            v = [r_[:, i, 0:n].rearrange("p (a b) -> p a b", b=shape[-1]) if len(shape) == 3 else
                 r_[:, i, 0:n].rearrange("p (h a b) -> p h a b", a=shape[2], b=shape[3]) for i in range(4)]
            C.tt('dve', v[0], x1, c_, ALU.mult)
            C.tt('dve', v[1], x2, s_, ALU.mult)
            C.tt('dve', v[2], x2, c_, ALU.mult)
            C.tt('dve', v[3], x1, s_, ALU.mult)
            C.tt('dve', out1, v[0], v[1], ALU.subtract)
            C.tt('dve', out2, v[2], v[3], ALU.add)

        import os
        mstop = int(os.environ.get('MLA_STOP', '99'))
        if mstop <= 0:
            return
        with C.scope():
            pA = C.ps("mpA", [128, 512], F32)
            pB = C.ps("mpB", [128, 512], F32)
            pB2 = C.ps("mpB2", [128, 512], F32)
            pT = C.ps("mpT", [128, 1024], BF16)
            for n in range(NT):
                b2 = n % 2
                a_, s4 = aug[b2], st_[b2]
                for k in range(8):
                    C.mm(pA[:, 0:128], wD[:, k, 256:384], hT[:, k, n * 128:(n + 1) * 128], start=(k == 0), stop=(k == 7),
                         rk=[(hT, n), wD])
                C.act(fT[b2][:, 0, :], pA[:, 0:128], AF.Copy, scale=gkv[:, 0:1])
                C.act(sqT[b2][:, 0, :], pA[:, 0:128], AF.Square)
                C.mm(pB[:, :], fT[b2][:, 0, :], wukv[:])
                C.mm(pB2[:, 0:2], sqT[b2][:, 0, :], ones_b[:, 0:2])
                for k in range(8):
                    C.mm(pB2[:, 32:64], hT[:, k, n * 128:(n + 1) * 128], wD[:, k, 384:416], start=(k == 0), stop=(k == 7),
                         rk=[(hT, n), wD])
                C.ts('dve', s4[:, 0:1], pB2[:, 0:1], 1.0 / 128, EPS, ALU.mult, ALU.add)
                C.tt('pool', s4[:, 1:2], s4[:, 0:1], neghalf[:, 0:1], ALU.pow)
                kv4 = pB[:, :].rearrange("p (h e) -> p h e", e=128)
                C.ts('dve', a_[:, :, 0:64], kv4[:, :, 0:64], s4[:, 1:2], None, ALU.mult)
                C.ts('dve', Va[:, n, :, 0:64], kv4[:, :, 64:128], s4[:, 1:2], None, ALU.mult, wk=[(Va, n)])
                if n >= NCT:
                    x4 = pB2[:, 32:64].rearrange("p (a b f) -> p a b f", a=2, b=2)
                    c_ = cosT[:, n - NCT, :].rearrange("p (a f) -> p a f", a=2)
                    s_ = sinT[:, n - NCT, :].rearrange("p (a f) -> p a f", a=2)
                    o4 = rp[b2][:, 4, 0:32].rearrange("p (a b f) -> p a b f", a=2, b=2)
                    rope_tm(o4[:, :, 0, :], o4[:, :, 1, :], x4[:, :, 0, :], x4[:, :, 1, :], c_, s_, rp[b2], [128, 2, 8])
                else:
                    C.cp('dve', rp[b2][:, 4, 0:32], pB2[:, 32:64])
                C.cp('dve', a_[:, :, 64:96], rp[b2][:, 4, 0:32].unsqueeze(1).to_broadcast([128, 4, 32]))
                C.memset('pool', a_[:, :, 96:97], 1.0)
                t3 = tmp[b2][:].rearrange("p (h e) -> p h e", e=96)
                C.tt('pool', t3, a_[:, :, 0:96], a_[:, :, 0:96], ALU.mult)
                C.red('dve', s4[:, 4:8], t3, ALU.add)
                C.tt('dve', kmx[:], kmx[:], s4[:, 4:8], ALU.max)
                for h in range(4):
                    C.tr(pT[:, h * 128:(h + 1) * 128], a_[:, h, :], ident_b[:])
                C.cp('act', KaT[:, :, n * 128:(n + 1) * 128], pT[:, 0:512].rearrange("p (h t) -> p h t", t=128),
                     wk=[(KaT, n)])
            if mstop <= 1:
                return
            C.red('dve', st_[0][:, 8:9], kmx[:], ALU.max)
            C.tr(pB[:, 0:128].bitcast(F32)[0:1, 0:128], st_[0][:, 8:9], ident_f[:])
            C.red('dve', st_[0][0:1, 9:10], pB[0:1, 0:128], ALU.max)
            C.tt('pool', st_[0][0:1, 10:11], st_[0][0:1, 9:10], half[0:1, 0:1], ALU.pow)
            C.mm(pB2[:, 0:1], ones_f[0:1, :], st_[0][0:1, 10:11])
            C.ts('dve', kmaxb[:, 0:1], pB2[:, 0:1], -1.0, None, ALU.mult)

        if mstop <= 2:
            return
        with C.scope():
            pA = C.ps("mqA", [128, 512], F32)
            pB = C.ps("mqB", [128, 512], F32)
            pT = C.ps("mqT", [128, 1024], BF16)
            pS = [C.ps(f"mpS{i}", [128, 512], F32) for i in range(2)]
            pO = [C.ps(f"mpO{h}", [128, 512], F32) for h in range(3)]
            QaT = [C.sb(f"QaT{i}", [128, 4, 512], BF16) for i in range(2)]
            PT = [C.sb(f"PTs{i}", [128, 512], BF16) for i in range(3)]
            yd = [C.sb(f"yd{i}", [128, 256], F32) for i in range(2)]
            rd_ = [C.sb(f"mrd{i}", [128, 4], F32) for i in range(2)]
            groups = ([(0, 2, [0, 1])] if emit else []) + [(NCT + 4 * i, 4, list(range(NT))) for i in range(8)]
            nps = 0
            ny = 0
            for gi, (n0, nt, ktiles) in enumerate(groups):
                Qg = QaT[gi % 2]
                for j in range(nt):
                    n = n0 + j
                    b2 = n % 2
                    a_, s4 = aug[b2], st_[b2]
                    for c in range(2):
                        for k in range(8):
                            C.mm(pA[:, c * 128:(c + 1) * 128], wD[:, k, c * 128:(c + 1) * 128], hT[:, k, n * 128:(n + 1) * 128],
                                 start=(k == 0), stop=(k == 7), rk=[(hT, n), wD])
                    for c in range(2):
                        C.act(fT[b2][:, c, :], pA[:, c * 128:(c + 1) * 128], AF.Copy, scale=gq[:, c:c + 1])
                        C.act(sqT[b2][:, c, :], pA[:, c * 128:(c + 1) * 128], AF.Square)
                    for c in range(2):
                        C.mm(pB[:, 0:384], fT[b2][:, c, :], wuq[:, c, :], start=(c == 0), stop=(c == 1))
                    for c in range(2):
                        C.mm(pB[:, 384 + 2 * c:386 + 2 * c], sqT[b2][:, c, :], ones_b[:, 0:2])
                    C.cp('dve', s4[:, 12:16], pB[:, 384:388])
                    C.tt('dve', s4[:, 0:1], s4[:, 12:13], s4[:, 14:15], ALU.add)
                    C.ts('dve', s4[:, 0:1], s4[:, 0:1], 1.0 / 256, EPS, ALU.mult, ALU.add)
                    C.tt('pool', s4[:, 1:2], s4[:, 0:1], neghalf[:, 0:1], ALU.pow)
                    C.ts('dve', qf[b2][:, 0:384], pB[:, 0:384], s4[:, 1:2], None, ALU.mult)
                    q3 = qf[b2][:, 0:384].rearrange("p (h e) -> p h e", e=96)
                    C.cp('dve', a_[:, :, 0:64], q3[:, :, 0:64])
                    if n >= NCT and os.environ.get('MLA_VAR') != 'norope':
                        x5 = q3[:, :, 64:96].rearrange("p h (a b f) -> p h a b f", a=2, b=2)
                        o5 = a_[:, :, 64:96].rearrange("p h (a b f) -> p h a b f", a=2, b=2)
                        c_ = cosT[:, n - NCT, :].rearrange("p (a f) -> p a f", a=2).unsqueeze(1).to_broadcast([128, 4, 2, 8])
                        s_ = sinT[:, n - NCT, :].rearrange("p (a f) -> p a f", a=2).unsqueeze(1).to_broadcast([128, 4, 2, 8])
                        rope_tm(o5[:, :, :, 0, :], o5[:, :, :, 1, :], x5[:, :, :, 0, :], x5[:, :, :, 1, :], c_, s_, rp[b2],
                                [128, 4, 2, 8])
                    else:
                        C.cp('dve', a_[:, :, 64:96], q3[:, :, 64:96])
                    t3 = tmp[b2][:].rearrange("p (h e) -> p h e", e=96)
                    C.tt('pool', t3, a_[:, :, 0:96], a_[:, :, 0:96], ALU.mult)
                    C.red('dve', s4[:, 4:8], t3, ALU.add)
                    C.tt('pool', s4[:, 8:12], s4[:, 4:8], half[:, 0:4], ALU.pow)
                    C.ts('dve', a_[:, :, 96:97], s4[:, 8:12].unsqueeze(2), kmaxb[:, 0:1], None, ALU.mult)
                    for h in range(4):
                        C.tr(pT[:, h * 128:(h + 1) * 128], a_[:, h, :], ident_b[:])
                    C.cp('act', Qg[:, :, j * 128:(j + 1) * 128], pT[:, 0:512].rearrange("p (h t) -> p h t", t=128))
                nq = nt * 128
                if mstop <= 3:
                    continue
                for h in range(4):
                    po = pO[h % 3]
                    for ki, kt in enumerate(ktiles):
                        ps_ = pS[nps % 2]
                        pt_ = PT[nps % 3]
                        nps += 1
                        C.mm(ps_[:, 0:nq], KaT[0:97, h, kt * 128:(kt + 1) * 128], Qg[0:97, h, 0:nq], rk=[(KaT, kt), Qg])
                        C.act(pt_[:, 0:nq], ps_[:, 0:nq], AF.Exp, scale=SCALE)
                        for j in range(nt):
                            C.mm(po[:, j * 65:(j + 1) * 65], pt_[:, j * 128:(j + 1) * 128], Va[:, kt, h, :],
                                 start=(ki == 0 and j == 0), stop=(ki == len(ktiles) - 1), skip_group_check=True,
                                 rk=[pt_, (Va, kt)])
                    r4 = rd_[h % 2]
                    o3 = po[:, 0:nt * 65].rearrange("p (j e) -> p j e", e=65)
                    C.op('dve', [po], [r4], lambda e_, o_=r4[:, 0:nt], i_=o3[:, :, 64]: e_.reciprocal(out=o_, in_=i_))
                    for j in range(nt):
                        C.ts('dve', E['ydg'][j][:, h * 64:(h + 1) * 64], o3[:, j, 0:64], r4[:, j:j + 1], None, ALU.mult)
                for j in range(nt):
                    n = n0 + j
                    C.dma('sp', ycat[n * 128:(n + 1) * 128, 768:1024], E['ydg'][j][:], wk=(ycat, n))


def rwkv_project(E):
    C, l, hT, wbf, zraw = E['C'], E['l'], E['hT'], E['wbf'], E['zraw']
    with C.scope():
        wC = C.sb("wC", [128, 8, 1024], BF16)
        C.dma('sp', wC[:], wbf[('w_in', l)][:, 1536:2560].rearrange("(k p) n -> p k n", p=128), rk=wbf[('w_in', l)])
        pZ = [C.ps(f"pZ{i}", [128, 512], F32) for i in range(2)]
        stg = [C.sb(f"zstg{i}", [128, 512], F32) for i in range(2)]
        groups = [(512 * i, 512) for i in range(8)] + [(4096, 256)]
        i = 0
        for c in range(8):
            for (t0, n) in groups:
                for k in range(8):
                    C.mm(pZ[i % 2][:, 0:n], wC[:, k, c * 128:(c + 1) * 128], hT[:, k, t0:t0 + n], start=(k == 0), stop=(k == 7),
                         rk=[hT, wC])
                C.cp('act', stg[i % 2][:, 0:n], pZ[i % 2][:, 0:n])
                C.dma('sp' if i % 2 == 0 else 'pool', zraw[c, :, t0:t0 + n], stg[i % 2][:, 0:n], wk=(zraw, c))
                i += 1


def rwkv_main(E):
    C, l, W, K, ycat, zraw, emit, ident_b = E['C'], E['l'], E['W'], E['K'], E['ycat'], E['zraw'], E['emit'], E['ident_b']
    LAM = -math.exp(-0.5)
    WP = T + 4
    with C.scope():
        rkT = C.sb("rkT", [128, 4, T], BF16)
        twz = C.sb("twz", [128, T], BF16)
        sgd = C.sb("sgd", [128, T], BF16)
        vtok = C.sb("rw_vtok", [128, 2, NT, 128], BF16)
        muT = C.sb("muT", [128, 8], F32)
        pp = C.sb("rw_pp", [128, 4, 2], F32)
        w0T = C.sb("rw_w0", [128, 2, 2], F32)
        a0T = C.sb("rw_a0", [128, 2, 2], F32)
        items = [(muT[:], W['rwkv_mu'][l]), (pp[:, 0, :], W['rwkv_kk'][l]), (pp[:, 1, :], W['rwkv_ka'][l]),
                 (pp[:, 3, :], W['rwkv_rk'][l].rearrange("h k -> (h k)"))]
        for d in range(2):
            items += [(w0T[:, d, :], W['rwkv_w0'][l][d]), (a0T[:, d, :], W['rwkv_a0'][l][d])]
        load_cols(C, E['ident_f'], items)
        C.ts('dve', pp[:, 2, :], pp[:, 1, :], -1.0, 1.0, ALU.mult, ALU.add)
        lstage = C.sb("rw_lst", [128, 2, 2, 256], F32)
        C.memset('pool', lstage[:], 0.0)
        wupz = C.sb("wupz", [128, 2, 256], BF16)
        aupz = C.sb("aupz", [128, 2, 256], BF16)
        for d in range(2):
            C.dma('sp', lstage[0:64, 0, d, :], W['rwkv_w_up'][l][d])
            C.dma('sp', lstage[64:128, 1, d, :], W['rwkv_a_up'][l][d])
        C.cp('dve', wupz[:], lstage[:, 0, :, :])
        C.cp('dve', aupz[:], lstage[:, 1, :, :])
        gst = C.sb("rw_gst", [128, 256], F32)
        gup = C.sb("rw_gup", [128, 256], BF16)
        C.dma('sp', gst[:], W['rwkv_g_up'][l])
        C.cp('dve', gup[:], gst[:])
        gng = C.sb("rw_gng", [128, 256], F32)
        C.dma('pool', gng[:], W['rwkv_gn_g'][l].partition_broadcast(128))
        resetm = C.sb("rw_resetm", [128, 512], F32)
        C.dma('pool', resetm[:], K['rw_resetm'][:, :])
        blk = C.sb("rw_blk", [128, 128], BF16)
        C.dma('pool', blk[:], K['rw_blk'][:, :])
        blk2 = C.sb("rw_blk2", [128, 2], BF16)
        C.dma('pool', blk2[:], K['rw_blk2'][:, :])
        mask4 = C.sb("rw_mask4", [128, 2, 512], F32)
        masknt = C.sb("rw_masknt", [128, 2, 256], F32)
        for d in range(2):
            C.dma('pool', mask4[:, d, :], K['rw_mask4'][d])
            C.dma('pool', masknt[:, d, :], K['rw_masknt'][d])
        negh = C.sb("rw_negh", [128, 512], F32)
        C.memset('pool', negh[:], -0.5)
        nh2 = C.sb("rw_nh2", [128, 2], F32)
        C.memset('pool', nh2[:], -0.5)

        with C.scope():
            zr = [C.sb(f"zr{i}", [128, WP], F32) for i in range(2)]
            tA = [C.sb(f"tA{i}", [128, WP], F32) for i in range(2)]
            vb = C.sb("rw_vb", [128, T], BF16)
            pT = C.ps("rw_pT", [128, 1024], BF16)
            for i, c in enumerate([6, 7, 4, 5, 0, 1, 2, 3]):
                z_, t_ = zr[i % 2], tA[i % 2]
                C.memset('pool', z_[:, 0:1], 0.0)
                C.memset('pool', z_[:, 257:259], 0.0)
                C.memset('pool', z_[:, WP - 1:WP], 0.0)
                C.dma('sp', z_[:, 1:257], zraw[c, :, 0:LC], rk=(zraw, c))
                C.dma('pool', z_[:, 259:259 + L], zraw[c, :, LC:T], rk=(zraw, c))
                mid = slice(1, WP - 1)
                C.tt('pool', t_[:, mid], z_[:, 0:WP - 2], z_[:, 2:WP], ALU.add)
                C.stt(t_[:, mid], t_[:, mid], 0.5, z_[:, mid], ALU.mult, ALU.subtract)
                C.stt(t_[:, mid], t_[:, mid], muT[:, c:c + 1], z_[:, mid], ALU.mult, ALU.add)
                segs = ((slice(1, 257), slice(0, LC)), (slice(259, 259 + L), slice(LC, T)))
                for (src, dst) in segs:
                    if c == 6:
                        C.act(twz[0:64, dst], t_[0:64, src], AF.Tanh)
                        C.cp('act', twz[64:128, dst], t_[64:128, src])
                    elif c == 7:
                        C.act(sgd[:, dst], t_[:, src], AF.Sigmoid)
                    elif c in (4, 5):
                        C.cp('act', vb[:, dst], t_[:, src])
                    else:
                        C.cp('act', rkT[:, c, dst], t_[:, src])
                if c in (4, 5):
                    for n0 in range(0, NT, 8):
                        nn = min(8, NT - n0)
                        for j in range(nn):
                            C.tr(pT[:, j * 128:(j + 1) * 128], vb[:, (n0 + j) * 128:(n0 + j + 1) * 128], ident_b[:])
                        C.cp('act', vtok[:, c - 4, n0:n0 + nn, :], pT[:, 0:nn * 128].rearrange("p (j f) -> p j f", f=128))

        import os
        rstop = int(os.environ.get('RW_STOP', '99'))
        if rstop <= 0:
            return
        Oacc = [C.sb(f"rw_O{d}", [128, NT, 128], F32) for d in range(2)]
        bsum = [C.sb(f"rw_bs{d}", [128, NT, 2], F32) for d in range(2)]
        ktI = [C.sb(f"ktI{i}", [128, 512], BF16) for i in range(2)]
        bI = [C.sb(f"bI{i}", [128, 512], BF16) for i in range(2)]
        KR = [[C.sb(f"KR{i}{hh}", [128, 4, 2, 128], BF16) for hh in range(2)] for i in range(2)]
        for i in range(2):
            for hh in range(2):
                C.memset('pool', KR[i][hh][:], 0.0)
        KBg = [C.sb(f"KBg{i}", [128, 4, 2, 128], BF16) for i in range(2)]
        gam = [C.sb(f"gam{i}", [128, 4], F32) for i in range(2)]
        Mst = C.sb("rw_M", [128, 64], F32)
        Mb = [C.sb(f"rw_Mb{i}", [128, 64], BF16) for i in range(2)]
        f32t = {nm: C.sb("rwt_" + nm, [128, 512], F32) for nm in
                ('sw', 'a', 'ci', 'ei', 'ee', 'e1', 'e2', 'e3', 'kt', 'b', 'tK', 'tB')}
        f32t.update(kk=f32t['sw'], rs=f32t['ci'], kh=f32t['ei'], tk=f32t['ee'])
        b16t = {nm: C.sb("rwb_" + nm, [128, 512], BF16) for nm in ('ksq', 'kgT', 'bgT', 'prod')}
        Am = [[C.sb(f"Am{i}{hh}", [128, 4, 128], BF16) for hh in range(2)] for i in range(3)]
        XXi = [C.sb(f"XXi{i}", [128, 2, 128], BF16) for i in range(3)]
        XX = [[C.sb(f"XX{i}{j}", [128, 2, 2, 128], BF16) for j in range(2)] for i in range(3)]
        Pb = [[C.sb(f"Pb{i}{j}", [128, 2, 128], BF16) for j in range(2)] for i in range(3)]
        Xs = [C.sb(f"rwXs{i}", [128, 128], BF16) for i in range(2)]
        Un = [C.sb(f"rwUn{i}", [128, 128], BF16) for i in range(2)]
        fin = {nm: C.sb("rwf_" + nm, [128, 128], F32) for nm in ('o', 'osq', 'y', 'yo')}
        fst = C.sb("rwf_st", [128, 8, 2], F32)
        pL = C.ps("rw_pL", [128, 512], F32)
        pA = [C.ps(f"rw_pA{hh}", [128, 512], F32) for hh in range(2)]
        pI1 = C.ps("rw_pI1", [128, 512], F32)
        pI2 = C.ps("rw_pI2", [128, 512], F32)
        pTb = C.ps("rw_pTb", [128, 1024], BF16)
        pR1 = C.ps("rw_pR1", [128, 512], F32)
        pR2 = C.ps("rw_pR2", [128, 512], F32)

        xgroups = [(NCT + 4 * i, 4) for i in range(8)]
        for fc in range(2):
            for d in range(2):
                glist = [(0, NCT)] + (xgroups if d == 0 else xgroups[::-1])
                C.memset('dve', Mst[:], 0.0)
                C.memset('pool', Mb[0][:], 0.0)
                def do_prep(gi, c0, ncn):
                    pb = gi % 2
                    t0, n = c0 * 128, ncn * 128
                    t = f32t
                    rT_ = rkT[:, fc, t0:t0 + n]
                    kT_ = rkT[:, 2 + fc, t0:t0 + n]
                    sl = slice(0, n)
                    C.mm(pL[:, sl], wupz[:, d, fc * 128:(fc + 1) * 128], twz[:, t0:t0 + n])
                    C.act(t['sw'][:, sl], pL[:, sl], AF.Sigmoid, bias=w0T[:, d, fc:fc + 1])
                    C.mm(pL[:, sl], aupz[:, d, fc * 128:(fc + 1) * 128], twz[:, t0:t0 + n])
                    C.act(t['a'][:, sl], pL[:, sl], AF.Sigmoid, bias=a0T[:, d, fc:fc + 1])
                    C.op('dve', [resetm, t['sw']], [t['ci']],
                         lambda e_, o_=t['ci'][:, sl], a_=resetm[:, sl], b_=t['sw'][:, sl]:
                         e_.tensor_tensor_scan(out=o_, data0=a_, data1=b_, initial=0.0, op0=ALU.mult, op1=ALU.add))
                    ci3 = t['ci'][:, sl].rearrange("p (c t) -> p c t", t=128)
                    tot = ci3[:, :, 127:128]
                    if d == 1:
                        ei3 = t['ei'][:, sl].rearrange("p (c t) -> p c t", t=128)
                        C.tt('dve', ei3, tot.to_broadcast([128, ncn, 128]), ci3, ALU.subtract)
                        C.tt('dve', t['ei'][:, sl], t['ei'][:, sl], t['sw'][:, sl], ALU.add)
                        ei = t['ei']
                    else:
                        ei = t['ci']
                    C.tt('pool', t['ee'][:, sl], ei[:, sl], t['sw'][:, sl], ALU.subtract)
                    C.act(t['e1'][:, sl], t['ee'][:, sl], AF.Exp, scale=LAM)
                    C.act(t['e2'][:, sl], ei[:, sl], AF.Exp, scale=-LAM)
                    C.act(t['e3'][:, sl], ei[:, sl], AF.Exp, scale=LAM)
                    C.act(gam[pb][:, 0:ncn], ci3[:, :, 127], AF.Exp, scale=LAM)
                    C.ts('dve', t['kk'][:, sl], kT_, pp[:, 0, fc:fc + 1], None, ALU.mult)
                    C.tt('pool', b16t['ksq'][:, sl], t['kk'][:, sl], t['kk'][:, sl], ALU.mult)
                    C.mm(pL[:, sl], blk[:], b16t['ksq'][:, sl])
                    C.cp('act', t['rs'][:, sl], pL[:, sl])
                    C.ts('dve', t['rs'][:, sl], t['rs'][:, sl], 1e-12, None, ALU.max)
                    C.tt('pool', t['rs'][:, sl], t['rs'][:, sl], negh[:, sl], ALU.pow)
                    C.tt('dve', t['kh'][:, sl], t['kk'][:, sl], t['rs'][:, sl], ALU.mult)
                    C.ts('dve', t['tk'][:, sl], t['a'][:, sl], pp[:, 1, fc:fc + 1], pp[:, 2, fc:fc + 1], ALU.mult, ALU.add)
                    C.tt('pool', t['kt'][:, sl], t['tk'][:, sl], kT_, ALU.mult)
                    C.tt('pool', t['b'][:, sl], t['kh'][:, sl], t['a'][:, sl], ALU.mult)
                    for hh in range(2):
                        hs = slice(hh * 64, (hh + 1) * 64)
                        C.tt('dve', KR[pb][hh][hs, 0:ncn, 0, :], t['kh'][hs, sl].rearrange("p (c t) -> p c t", t=128),
                             t['e1'][hs, sl].rearrange("p (c t) -> p c t", t=128), ALU.mult, wk=[KR[pb][hh]])
                        C.tt('pool', KR[pb][hh][hs, 0:ncn, 1, :], rT_[hs, :].rearrange("p (c t) -> p c t", t=128),
                             t['e3'][hs, sl].rearrange("p (c t) -> p c t", t=128), ALU.mult, wk=[KR[pb][hh]])
                    C.tt('dve', t['tK'][:, sl], t['kt'][:, sl], t['e2'][:, sl], ALU.mult)
                    C.tt('pool', t['tB'][:, sl], t['b'][:, sl], t['e2'][:, sl], ALU.mult)
                    C.cp('act', ktI[pb][:, sl], t['tK'][:, sl])
                    C.cp('act', bI[pb][:, sl], t['tB'][:, sl])
                    gb = gam[pb][:, 0:ncn].unsqueeze(2).to_broadcast([128, ncn, 128])
                    C.tt('dve', b16t['kgT'][:, sl].rearrange("p (c t) -> p c t", t=128),
                         t['tK'][:, sl].rearrange("p (c t) -> p c t", t=128), gb, ALU.mult)
                    C.tt('pool', b16t['bgT'][:, sl].rearrange("p (c t) -> p c t", t=128),
                         t['tB'][:, sl].rearrange("p (c t) -> p c t", t=128), gb, ALU.mult)
                    for j in range(ncn):
                        C.tr(pTb[:, (2 * j) * 128:(2 * j + 1) * 128], b16t['kgT'][:, j * 128:(j + 1) * 128], ident_b[:])
                        C.tr(pTb[:, (2 * j + 1) * 128:(2 * j + 2) * 128], b16t['bgT'][:, j * 128:(j + 1) * 128], ident_b[:])
                    C.cp('act', KBg[pb][:, 0:ncn, :, :].rearrange("p c k f -> p (c k f)"), pTb[:, 0:ncn * 256])
                    C.stt(b16t['prod'][:, sl], rT_, pp[:, 3, fc:fc + 1], t['kt'][:, sl], ALU.mult, ALU.mult)
                    for j in range(ncn):
                        C.mm(pL[:, 2 * j:2 * j + 2], b16t['prod'][:, j * 128:(j + 1) * 128], blk2[:])
                    C.cp('act', bsum[d][:, c0:c0 + ncn, :].rearrange("p c k -> p (c k)"), pL[:, 0:2 * ncn])


                def gen_inv(idx, n_, gi, c0):
                    pb = gi % 2
                    j = n_ - c0
                    q3 = idx % 3
                    cs = slice(j * 128, (j + 1) * 128)
                    if True:
                        for hh in range(2):
                            kr = KR[pb][hh][:, j, :, :].rearrange("p a t -> p (a t)")
                            C.mm(pA[hh][:, 0:256], ktI[pb][:, cs], kr, rk=[ktI[pb], KR[pb][hh]])
                            C.mm(pA[hh][:, 256:512], bI[pb][:, cs], kr, rk=[bI[pb], KR[pb][hh]])
                            C.mm(pI2[:, 256 + hh * 128:256 + (hh + 1) * 128], KR[pb][hh][:, j, 0, :], bI[pb][:, cs], rk=[bI[pb], KR[pb][hh]])
                            C.tt('dve', Am[q3][hh][:].rearrange("p a t -> p (a t)"), pA[hh][:, :], mask4[:, d, :], ALU.mult)
                        C.tt('dve', XXi[q3][:].rearrange("p a t -> p (a t)"), pI2[:, 256:512], masknt[:, d, :], ALU.mult)
                        for hh in range(2):
                            C.tt('pool', Pb[q3][0][:, hh, :], Am[q3][hh][:, 2, :], ident_b[:], ALU.add, wk=[Pb[q3][0]])
                        yield
                        Xc = [Am[q3][0][:, 2, :], Am[q3][1][:, 2, :]]
                        Xk = [Am[q3][0], Am[q3][1]]
                        XTc = [XXi[q3][:, 0, :], XXi[q3][:, 1, :]]
                        XTk = [XXi[q3], XXi[q3]]
                        Pc = Pb[q3][0]
                        for lvl in range(6):
                            lastl = (lvl == 5)
                            nxt = XX[q3][lvl % 2]
                            for hh in range(2):
                                if not lastl:
                                    C.mm(pI1[:, hh * 128:(hh + 1) * 128], XTc[hh], Xc[hh], rk=[XTk[hh], Xk[hh]])
                                C.mm(pI1[:, 256 + hh * 128:256 + (hh + 1) * 128], Xc[hh], XTc[hh], rk=[XTk[hh], Xk[hh]])
                            if lastl:
                                C.cp('act', nxt[:, 1, :, :].rearrange("p h t -> p (h t)"), pI1[:, 256:512])
                            else:
                                C.cp('act', nxt[:].rearrange("p a h t -> p (a h t)"), pI1[:, :])
                            yield
                            for hh in range(2):
                                C.mm(pI2[:, hh * 128:(hh + 1) * 128], nxt[:, 1, hh, :], Pc[:, hh, :], rk=[nxt, Pc])
                            Pn = Pb[q3][(lvl + 1) % 2]
                            C.tt('dve', Pn[:].rearrange("p h t -> p (h t)"), pI2[:, 0:256], Pc[:].rearrange("p h t -> p (h t)"), ALU.add)
                            yield
                            Xc = [nxt[:, 0, 0, :], nxt[:, 0, 1, :]]
                            XTc = [nxt[:, 1, 0, :], nxt[:, 1, 1, :]]
                            Xk = [nxt, nxt]
                            XTk = [nxt, nxt]
                            Pc = Pn
                def gen_rec(idx, n_, gi, c0):
                    pb = gi % 2
                    j = n_ - c0
                    q3 = idx % 3
                    q2 = idx % 2
                    if True:
                        TT = Pb[q3][0]
                        cur = stt_['cur']
                        Mold = Mb[cur]
                        V = vtok[:, fc, n_, :]
                        for hh in range(2):
                            vs = slice(hh * 64, (hh + 1) * 64)
                            C.mm(pR1[:, vs], KR[pb][hh][:, j, 0, :], Mold[:], start=True, stop=False, rk=[KR[pb][hh], Mold])
                            C.mm(pR1[:, vs], Am[q3][hh][:, 0, :], V[:, vs], start=False, stop=True, rk=[Am[q3][hh], vtok])
                        C.cp('act', Xs[q2][:], pR1[:, 0:128])
                        yield
                        for hh in range(2):
                            vs = slice(hh * 64, (hh + 1) * 64)
                            C.mm(pR2[:, vs], TT[:, hh, :], Xs[q2][:, vs], rk=[TT, Xs[q2]])
                        C.ts('dve', Un[q2][:], pR2[:, 0:128], -1.0, None, ALU.mult)
                        yield
                        for hh in range(2):
                            vs = slice(hh * 64, (hh + 1) * 64)
                            C.mm(pR2[vs, 128:192], KBg[pb][:, j, 0, vs], V[:, vs], start=True, stop=False, rk=[KBg[pb], vtok])
                            C.mm(pR2[vs, 128:192], KBg[pb][:, j, 1, vs], Un[q2][:, vs], start=False, stop=True, rk=[KBg[pb], Un[q2]])
                        want_o = emit or n_ >= NCT
                        if want_o:
                            for hh in range(2):
                                vs = slice(128 + hh * 64, 128 + (hh + 1) * 64)
                                v2 = slice(hh * 64, (hh + 1) * 64)
                                C.mm(pR1[:, vs], KR[pb][hh][:, j, 1, :], Mold[:], start=True, stop=False, rk=[KR[pb][hh], Mold])
                                C.mm(pR1[:, vs], Am[q3][hh][:, 1, :], V[:, v2], start=False, stop=False, rk=[Am[q3][hh], vtok])
                                C.mm(pR1[:, vs], Am[q3][hh][:, 3, :], Un[q2][:, v2], start=False, stop=True, rk=[Am[q3][hh], Un[q2]])
                            C.cp('act', Oacc[d][:, n_, :], pR1[:, 128:256], wk=[(Oacc[d], n_)])
                        C.stt(Mst[:], Mst[:], gam[pb][:, j:j + 1], pR2[:, 128:192], ALU.mult, ALU.add)
                        C.cp('act', Mb[1 - cur][:], Mst[:])
                        stt_['cur'] = 1 - cur
                        yield

                seq = []
                for gi, (c0, ncn) in enumerate(glist):
                    chunks = list(range(c0, c0 + ncn)) if d == 0 else list(range(c0 + ncn - 1, c0 - 1, -1))
                    for n_ in chunks:
                        seq.append((n_, gi, c0, ncn))
                stt_ = {'cur': 0}
                prep_done = set()
                inv_started, inv_done, rec_i, rec_gen, active = 0, set(), 0, None, []
                while rec_i < len(seq):
                    while len(active) < 2 and inv_started < len(seq) and inv_started < rec_i + 3:
                        n_, gi, c0, ncn = seq[inv_started]
                        if gi not in prep_done:
                            do_prep(gi, c0, ncn)
                            prep_done.add(gi)
                        active.append((inv_started, gen_inv(inv_started, n_, gi, c0)))
                        inv_started += 1
                    for item in list(active):
                        try:
                            next(item[1])
                        except StopIteration:
                            inv_done.add(item[0])
                            active.remove(item)
                    if rec_gen is None and rec_i in inv_done:
                        n_, gi, c0, ncn = seq[rec_i]
                        rec_gen = gen_rec(rec_i, n_, gi, c0)
                    if rec_gen is not None:
                        try:
                            next(rec_gen)
                        except StopIteration:
                            rec_gen = None
                            rec_i += 1
            for n_ in range(NT):
                if not (emit or n_ >= NCT) or rstop <= 3:
                    continue
                f = fin
                C.tt('pool', f['o'][:], Oacc[0][:, n_, :], Oacc[1][:, n_, :], ALU.add, rk=[(Oacc[0], n_), (Oacc[1], n_)])
                C.tt('pool', f['osq'][:], f['o'][:], f['o'][:], ALU.mult)
                o3 = f['o'][:].rearrange("p (h e) -> p h e", e=64)
                C.red('dve', fst[:, 0, :], o3, ALU.add)
                C.red('dve', fst[:, 1, :], f['osq'][:].rearrange("p (h e) -> p h e", e=64), ALU.add)
                C.ts('pool', fst[:, 2, :], fst[:, 0, :], 1.0 / 64, None, ALU.mult)
                C.tt('pool', fst[:, 3, :], fst[:, 2, :], fst[:, 2, :], ALU.mult)
                C.ts('pool', fst[:, 4, :], fst[:, 1, :], 1.0 / 64, 64e-5, ALU.mult, ALU.add)
                C.tt('pool', fst[:, 4, :], fst[:, 4, :], fst[:, 3, :], ALU.subtract)
                C.tt('pool', fst[:, 5, :], fst[:, 4, :], nh2[:], ALU.pow)
                C.tt('pool', fst[:, 6, :], bsum[0][:, n_, :], bsum[1][:, n_, :], ALU.add)
                y3 = f['y'][:].rearrange("p (h e) -> p h e", e=64)
                C.tt('dve', y3, o3, fst[:, 2, :].unsqueeze(2).to_broadcast([128, 2, 64]), ALU.subtract)
                C.tt('dve', y3, y3, fst[:, 5, :].unsqueeze(2).to_broadcast([128, 2, 64]), ALU.mult)
                C.tt('pool', f['y'][:], f['y'][:], gng[:, fc * 128:(fc + 1) * 128], ALU.mult)
                for hh in range(2):
                    vs = slice(hh * 64, (hh + 1) * 64)
                    C.stt(f['y'][:, vs], vtok[:, fc, n_, vs], fst[:, 6, hh:hh + 1], f['y'][:, vs], ALU.mult, ALU.add)
                C.mm(pR2[:, 0:128], sgd[:, n_ * 128:(n_ + 1) * 128], gup[:, fc * 128:(fc + 1) * 128])
                C.tt('dve', f['yo'][:], f['y'][:], pR2[:, 0:128], ALU.mult)
                C.dma('sp', ycat[n_ * 128:(n_ + 1) * 128, 512 + fc * 128:512 + (fc + 1) * 128], f['yo'][:], wk=(ycat, n_))


def MIXERS(env):
    E = dict(env)
    E['emit'] = env['l'] < DEPTH - 1
    which = env['debug_mixers'] if env.get('debug_mixers') else ('swa', 'ret', 'rwkv', 'mla')
    if 'swa' in which:
        mixer_swa(E)
    if 'ret' in which:
        mixer_ret(E)
    if 'rwkv' in which:
        rwkv_project(E)
    if 'mla' in which:
        with E['C'].scope():
            E['ydg'] = [E['C'].sb(f"ydg{j}", [128, 256], F32) for j in range(4)]
            mixer_mla(E)


_PROG = {}


def kernel(**inputs):
    if 'nc' not in _PROG:
        _PROG['nc'] = build_program()[0]
    nc = _PROG['nc']
    consts = make_consts()
    f32 = lambda a: np.ascontiguousarray(np.asarray(a, dtype=np.float32))
    shared = {k: f32(v) for k, v in inputs.items() if k not in ('x', 'c', 'ctx')}
    shared.update({'k_' + k: v for k, v in consts.items()})
    x, c, ctx = f32(inputs['x']), f32(inputs['c']), f32(inputs['ctx'])
    in_maps = []
    for b in range(8):
        m = dict(shared)
        m['x'] = x[b]
        m['c'] = c[b]
        m['ctx'] = ctx[b]
        in_maps.append(m)
    res = run_bass_kernel_spmd(nc, in_maps, core_ids=list(range(8)))
    return np.stack([np.asarray(r['out'], dtype=np.float32) for r in res.results], axis=0)
```

```python
import contextlib
import math
import numpy as np
import ml_dtypes
import concourse.bass as bass
import concourse.mybir as mybir
from concourse.bass_utils import run_bass_kernel_spmd

F32 = mybir.dt.float32
BF16 = mybir.dt.bfloat16
AF = mybir.ActivationFunctionType
ALU = mybir.AluOpType
AX = mybir.AxisListType

D = 1024
L = 4096
LC = 256
T = L + LC
NT = T // 128
NCT = LC // 128
DEPTH = 2
N_IN = 2976
DFF = 4096
EPS = 1e-6
NDMA = 6


class Ctx:
    ENG = ('pe', 'act', 'dve', 'pool', 'sp')

    def __init__(self, nc):
        self.nc = nc
        self.es = contextlib.ExitStack()
        self.eng = dict(pe=nc.tensor, act=nc.scalar, dve=nc.vector, pool=nc.gpsimd, sp=nc.sync)
        self.semh = {}
        self.cnt = {}
        for e in self.ENG:
            self.semh[e] = self.es.enter_context(nc.semaphore("sem_" + e))
            self.cnt[e] = 0
        self.known = {e: {} for e in self.ENG}
        self.dq = {}
        for q in ('sp', 'act', 'pool'):
            sems = []
            for i in range(NDMA):
                name = f"dma_{q}_{i}"
                self.semh[name] = self.es.enter_context(nc.semaphore(name))
                sems.append(name)
            self.dq[q] = dict(sems=sems, n=0)
        self.lastw = {}
        self.rd = {}
        self.subs = {}
        self.ninst = 0
        self.psum_names = set()
        self.bank_last = {}

    def sb(self, name, shape, dt=F32):
        self.uid = getattr(self, 'uid', 0) + 1
        return self.es.enter_context(self.nc.sbuf_tensor(f"{name}_{self.uid}", list(shape), dt))

    def ps(self, name, shape, dt=F32):
        self.uid = getattr(self, 'uid', 0) + 1
        nbytes = int(np.prod(shape[1:])) * (2 if dt == BF16 else 4)
        assert nbytes == 2048, "PSUM tensors must be exactly one bank (collision tracking is per tensor)"
        self.psum_names.add(f"{name}_{self.uid}")
        return self.es.enter_context(self.nc.psum_tensor(f"{name}_{self.uid}", list(shape), dt))

    @staticmethod
    def _key(x):
        if isinstance(x, tuple):
            a, sub = x
        else:
            a, sub = x, None
        name = a if isinstance(a, str) else getattr(a, 'tensor', a).name
        return (name, sub)

    def _dep_keys(self, key):
        name, sub = key
        if sub is None:
            return [(name, None)] + [(name, s) for s in self.subs.get(name, ())]
        return [(name, None), (name, sub)]

    def _wait(self, e, src, val):
        if self.known[e].get(src, 0) >= val:
            return
        self.eng[e].wait_ge(self.semh[src], val)
        self.known[e][src] = val

    def _sync(self, e, reads, writes):
        deps = {}

        def add(st):
            if st is not None:
                deps[st[0]] = max(deps.get(st[0], 0), st[1])
        same = 0
        for r in reads:
            for k in self._dep_keys(self._key(r)):
                st = self.lastw.get(k)
                add(st)
                if st is not None and st[0] == e:
                    same = max(same, st[1])
        for w in writes:
            for k in self._dep_keys(self._key(w)):
                st = self.lastw.get(k)
                add(st)
                if st is not None and st[0] == e:
                    same = max(same, st[1])
                for src, val in self.rd.get(k, {}).items():
                    add((src, val))
                    if src == e:
                        same = max(same, val)
        for src, val in deps.items():
            if src == e:
                continue
            self._wait(e, src, val)
        if same and e != 'pe':
            self._wait(e, e, same)

    def _record(self, reads, writes, stamp):
        for r in reads:
            k = self._key(r)
            d = self.rd.setdefault(k, {})
            d[stamp[0]] = max(d.get(stamp[0], 0), stamp[1])
        for w in writes:
            k = self._key(w)
            name, sub = k
            self.lastw[k] = stamp
            self.rd[k] = {}
            if sub is None:
                for s in self.subs.get(name, ()):
                    self.lastw[(name, s)] = stamp
                    self.rd[(name, s)] = {}
            else:
                self.subs.setdefault(name, set()).add(sub)

    def op(self, e, reads, writes, fn, rk=None, wk=None):
        if rk is not None:
            reads = list(rk)
        if wk is not None:
            writes = list(wk)
        self._sync(e, reads, writes)
        banks = set()
        for x in list(reads) + list(writes):
            nm = self._key(x)[0]
            if nm in self.psum_names:
                banks.add(nm)
        for nm in banks:
            for src, val in self.bank_last.get(nm, {}).items():
                if src != e:
                    self._wait(e, src, val)
        ins = fn(self.eng[e])
        self.cnt[e] += 1
        ins.then_inc(self.semh[e], 1)
        self._record(reads, writes, (e, self.cnt[e]))
        for nm in banks:
            self.bank_last.setdefault(nm, {})[e] = self.cnt[e]
        self.ninst += 1
        return ins

    def dma(self, q, out, in_, rk=None, wk=None, **kw):
        d = self.dq[q]
        i = d['n']
        src = d['sems'][i % NDMA]
        if i >= NDMA:
            self._wait(q, src, 16 * (i // NDMA))
        reads = rk if isinstance(rk, list) else [rk if rk is not None else in_]
        writes = wk if isinstance(wk, list) else [wk if wk is not None else out]
        self._sync(q, reads, writes)
        self.eng[q].dma_start(out=out, in_=in_, **kw).then_inc(self.semh[src], 16)
        d['n'] += 1
        self._record(reads, writes, (src, 16 * (i // NDMA + 1)))
        self.ninst += 1

    def barrier(self):
        targets = {}
        for q, d in self.dq.items():
            for j, src in enumerate(d['sems']):
                n = (d['n'] - j + NDMA - 1) // NDMA
                if n > 0:
                    targets[src] = 16 * n
        for e in self.ENG:
            if self.cnt[e] > 0:
                targets[e] = self.cnt[e]
        for e in self.ENG:
            for src, val in targets.items():
                self._wait(e, src, val)

    @contextlib.contextmanager
    def scope(self):
        outer = self.es
        with contextlib.ExitStack() as es:
            self.es = es
            try:
                yield
            finally:
                self.barrier()
                self.es = outer

    def finish(self):
        for q, d in self.dq.items():
            for j, src in enumerate(d['sems']):
                n = (d['n'] - j + NDMA - 1) // NDMA
                if n > 0:
                    self._wait('sp', src, 16 * n)
        for e in self.ENG:
            if e != 'sp' and self.cnt[e] > 0:
                self._wait('sp', e, self.cnt[e])

    def mm(self, out, lhsT, rhs, start=True, stop=True, r=(), w=(), rk=None, wk=None, **kw):
        return self.op('pe', [lhsT, rhs] + list(r), [out] + list(w),
                       lambda e: e.matmul(out, lhsT=lhsT, rhs=rhs, start=start, stop=stop, **kw), rk=rk, wk=wk)

    def tr(self, out, in_, ident, r=(), w=(), rk=None, wk=None):
        return self.op('pe', [in_, ident] + list(r), [out] + list(w),
                       lambda e: e.transpose(out, in_, ident), rk=rk, wk=wk)

    def act(self, out, in_, func, bias=None, scale=None, accum_out=None, r=(), w=(), rk=None, wk=None):
        kw = {}
        reads = [in_] + list(r)
        writes = [out] + list(w)
        if bias is not None:
            kw['bias'] = bias
            if not isinstance(bias, (int, float)):
                reads.append(bias)
        if scale is not None:
            kw['scale'] = scale
            if not isinstance(scale, (int, float)):
                reads.append(scale)
        if accum_out is not None:
            kw['accum_out'] = accum_out
            writes.append(accum_out)
        return self.op('act', reads, writes, lambda e: e.activation(out=out, in_=in_, func=func, **kw), rk=rk, wk=wk)

    def ts(self, e, out, in0, s1, s2, op0, op1=None, accum_out=None, r=(), w=(), rk=None, wk=None):
        reads = [in0] + list(r)
        writes = [out] + list(w)
        for s in (s1, s2):
            if s is not None and not isinstance(s, (int, float)):
                reads.append(s)
        kw = {}
        if op1 is not None:
            kw['op1'] = op1
        if accum_out is not None:
            kw['accum_out'] = accum_out
            writes.append(accum_out)
        return self.op(e, reads, writes,
                       lambda g: g.tensor_scalar(out=out, in0=in0, scalar1=s1, scalar2=s2, op0=op0, **kw), rk=rk, wk=wk)

    def tt(self, e, out, in0, in1, op, r=(), w=(), rk=None, wk=None):
        return self.op(e, [in0, in1] + list(r), [out] + list(w),
                       lambda g: g.tensor_tensor(out=out, in0=in0, in1=in1, op=op), rk=rk, wk=wk)

    def stt(self, out, in0, scalar, in1, op0, op1, r=(), w=(), rk=None, wk=None):
        reads = [in0, in1] + list(r)
        if not isinstance(scalar, (int, float)):
            reads.append(scalar)
        return self.op('dve', reads, [out] + list(w),
                       lambda g: g.scalar_tensor_tensor(out=out, in0=in0, scalar=scalar, in1=in1, op0=op0, op1=op1), rk=rk, wk=wk)

    def cp(self, e, out, in_, r=(), w=(), rk=None, wk=None):
        if e == 'act':
            return self.act(out, in_, AF.Copy, r=r, w=w, rk=rk, wk=wk)
        return self.op(e, [in_] + list(r), [out] + list(w), lambda g: g.tensor_copy(out=out, in_=in_), rk=rk, wk=wk)

    def rpow(self, out, in_, expo):
        self.act(out, in_, AF.Ln)
        self.act(out, out, AF.Exp, scale=float(expo))

    def memset(self, e, out, val, w=()):
        return self.op(e, [], [out] + list(w), lambda g: g.memset(out, val))

    def red(self, e, out, in_, op, axis=AX.X, r=(), w=()):
        return self.op(e, [in_] + list(r), [out] + list(w),
                       lambda g: g.tensor_reduce(out=out, in_=in_, axis=axis, op=op))


def load_cols(C, ident_f, items):
    with C.scope():
        st = C.sb("ldst", [128, 128], F32)
        ps = C.ps("ldps", [128, 512], F32)
        r = 0
        plan = []
        for dst, src in items:
            n = src.shape[0] // 128
            C.dma('sp', st[r:r + n, :], src.rearrange("(k p) -> k p", p=128))
            plan.append((dst, r, n))
            r += n
        assert r <= 128
        C.tr(ps[:, 0:r], st[0:r, :], ident_f[0:r, 0:r])
        for dst, r0, n in plan:
            C.cp('dve', dst, ps[:, r0:r0 + n])

def _rope_tables(n_tokens, d_rot, grid_w=64, base=10000.0):
    n_rows = n_tokens // grid_w
    row, col = np.meshgrid(np.arange(n_rows, dtype=np.float32), np.arange(grid_w, dtype=np.float32), indexing='ij')
    d_ax = d_rot // 2
    inv = (np.float32(base) ** (-np.arange(0, d_ax, 2, dtype=np.float32) / np.float32(d_ax))).astype(np.float32)
    ang = np.stack([row.reshape(-1)[:, None] * inv, col.reshape(-1)[:, None] * inv], axis=1).astype(np.float32)
    return np.cos(ang).astype(np.float32), np.sin(ang).astype(np.float32)


def make_consts():
    c = {}
    c['ident_f'] = np.eye(128, dtype=np.float32)
    c['ident_b'] = np.eye(128, dtype=np.float32).astype(ml_dtypes.bfloat16)
    lg = np.log1p(-np.exp2(-5.0 - np.arange(4, dtype=np.float32))).astype(np.float32)
    i = np.arange(128, dtype=np.float32)
    qft = np.zeros((128, 2, 128), np.float32)
    qbt = np.zeros((128, 2, 128), np.float32)
    gcc = np.zeros((128, 2), np.float32)
    for cc in range(2):
        for hh in range(2):
            h = 2 * cc + hh
            qft[hh * 64:(hh + 1) * 64, cc, :] = np.exp((i + 1.0) * lg[h])[None, :]
            qbt[hh * 64:(hh + 1) * 64, cc, :] = np.exp((128.0 - i) * lg[h])[None, :]
            gcc[hh * 64:(hh + 1) * 64, cc] = np.exp(np.float32(128.0) * lg[h])
    cos_a, sin_a = _rope_tables(L, 64)
    ct = np.zeros((128, L), np.float32)
    st = np.zeros((128, L), np.float32)
    pm = np.zeros((128, 128), np.float32)
    for blk in range(2):
        for ax in range(2):
            for half in range(2):
                for f in range(16):
                    p = blk * 64 + ax * 32 + half * 16 + f
                    ct[p, :] = cos_a[:, ax, f]
                    st[p, :] = sin_a[:, ax, f]
                    if half == 0:
                        pm[blk * 64 + ax * 32 + 16 + f, p] = -1.0
                    else:
                        pm[blk * 64 + ax * 32 + f, p] = 1.0
    c['swa_ct'] = ct
    c['swa_st'] = st
    c['swa_pm'] = pm.astype(ml_dtypes.bfloat16)
    qi = np.arange(128)[:, None]
    kj = np.arange(128)[None, :]
    NEG = np.float32(-30000.0)
    mk = np.zeros((128, 384), np.float32)
    mk[:, 0:128] = np.where(kj >= qi, 0.0, NEG)
    mk[:, 256:384] = np.where(kj <= qi, 0.0, NEG)
    c['swa_mask'] = mk
    rm = np.ones((128, 512), np.float32)
    rm[:, 0::128] = 0.0
    c['rw_resetm'] = rm
    blk = np.zeros((128, 128), np.float32)
    blk[0:64, 0:64] = 1.0
    blk[64:128, 64:128] = 1.0
    c['rw_blk'] = blk.astype(ml_dtypes.bfloat16)
    blk2 = np.zeros((128, 2), np.float32)
    blk2[0:64, 0] = 1.0
    blk2[64:128, 1] = 1.0
    c['rw_blk2'] = blk2.astype(ml_dtypes.bfloat16)
    si = np.arange(128)[:, None]
    ti = np.arange(128)[None, :]
    m4 = np.zeros((2, 128, 4, 128), np.float32)
    mnt = np.zeros((2, 128, 2, 128), np.float32)
    for dd in range(2):
        strict = (si < ti) if dd == 0 else (si > ti)
        incl = (si <= ti) if dd == 0 else (si >= ti)
        m4[dd, :, 0, :] = strict
        m4[dd, :, 1, :] = incl
        m4[dd, :, 2, :] = -1.0 * strict
        m4[dd, :, 3, :] = incl
        mnt[dd, :, 0, :] = -1.0 * strict.T
        mnt[dd, :, 1, :] = -1.0 * strict.T
    c['rw_mask4'] = m4.reshape(2, 128, 512)
    c['rw_masknt'] = mnt.reshape(2, 128, 256)
    cos_d, sin_d = _rope_tables(L, 32)
    c['mla_cos'] = np.ascontiguousarray(cos_d.reshape(L // 128, 128, 16).transpose(1, 0, 2).reshape(128, (L // 128) * 16))
    c['mla_sin'] = np.ascontiguousarray(sin_d.reshape(L // 128, 128, 16).transpose(1, 0, 2).reshape(128, (L // 128) * 16))
    c['ret_qft'] = qft
    c['ret_qbt'] = qbt
    c['ret_gcc'] = gcc
    kf = np.zeros((128, 4, 64), np.float32)
    kb = np.zeros((128, 4, 64), np.float32)
    dt = np.zeros((128, 4, 128), np.float32)
    for h in range(4):
        kf[:, h, :] = (np.exp((127.0 - i) * lg[h]) * 0.125)[:, None]
        kb[:, h, :] = (np.exp(i * lg[h]) * 0.125)[:, None]
        dt[:, h, :] = np.exp(np.abs(i[:, None] - i[None, :]) * lg[h]) * (1.0 + np.eye(128, dtype=np.float32))
    c['ret_kf'] = kf.reshape(128, 256)
    c['ret_kb'] = kb.reshape(128, 256)
    c['ret_dt'] = dt.reshape(128, 512)
    return c


CONST_SPECS = None


def build_program(debug=None, debug_mixers=None):
    nc = bass.Bass("TRN2", target_bir_lowering=False)
    C = Ctx(nc)
    dram_in = {}

    def din(name, shape, dt=F32):
        dram_in[name] = nc.dram_tensor(name, list(shape), dt, kind="ExternalInput").ap()
        return dram_in[name]

    x_in = din('x', [L, D])
    c_in = din('c', [D])
    ctx_in = din('ctx', [LC, D])
    cctx_in = din('c_ctx', [D])
    W = {}
    wshapes = dict(ada_w=[DEPTH, D, 6 * D], ada_b=[DEPTH, 6 * D], pre_mix_g=[DEPTH, D], post_mix_g=[DEPTH, D],
                   pre_mlp_g=[DEPTH, D], post_mlp_g=[DEPTH, D], w_in=[DEPTH, D, N_IN], w_out=[DEPTH, D, D],
                   swa_sink=[DEPTH, 4], ret_gn_g=[DEPTH, 256], rwkv_mu=[DEPTH, 1024], rwkv_w0=[DEPTH, 2, 256],
                   rwkv_w_up=[DEPTH, 2, 64, 256], rwkv_a0=[DEPTH, 2, 256], rwkv_a_up=[DEPTH, 2, 64, 256],
                   rwkv_g_up=[DEPTH, 128, 256], rwkv_kk=[DEPTH, 256], rwkv_ka=[DEPTH, 256], rwkv_rk=[DEPTH, 4, 64],
                   rwkv_gn_g=[DEPTH, 256], mla_q_norm_g=[DEPTH, 256], mla_w_uq=[DEPTH, 256, 384],
                   mla_kv_norm_g=[DEPTH, 128], mla_w_ukv=[DEPTH, 128, 512], mlp_w1=[DEPTH, D, DFF],
                   mlp_w2=[DEPTH, DFF, D])
    for k, s in wshapes.items():
        W[k] = din(k, s)
    K = {}
    consts = make_consts()
    for k, v in consts.items():
        K[k] = din('k_' + k, v.shape, BF16 if v.dtype == ml_dtypes.bfloat16 else F32)
    out = nc.dram_tensor("out", [L, D], F32, kind="ExternalOutput").ap()
    dbg = {}

    def dbg_out(name, shape, dt=F32):
        dbg[name] = nc.dram_tensor(name, list(shape), dt, kind="ExternalOutput").ap()
        return dbg[name]

    with C.es:
        def dscr(name, shape, dt=F32):
            return nc.dram_tensor(name, list(shape), dt).ap()
        wbf = {}
        for l in range(DEPTH):
            wbf[('w_in', l)] = dscr(f"wbf_in{l}", [D, N_IN], BF16)
            wbf[('w_out', l)] = dscr(f"wbf_out{l}", [D, D], BF16)
            wbf[('w1', l)] = dscr(f"wbf_w1{l}", [D, DFF], BF16)
            wbf[('w2', l)] = dscr(f"wbf_w2{l}", [DFF, D], BF16)
        if debug == 'p4':
            ycat = nc.dram_tensor("ycat_in", [T, D], F32, kind="ExternalInput").ap()
        elif debug == 'mix':
            ycat = dbg_out("d_ycat", [T, D])
        else:
            ycat = dscr("ycat", [T, D])
        xmid = dscr("xmid", [T, D])
        xres = dbg_out("d_xres", [T, D]) if debug == 'p4' else dscr("xres", [T, D])
        h2T_d = dscr("h2T_d", [128, 8, T], BF16)
        zraw = dscr("zraw", [8, 128, T])

        ident_f = C.sb("ident_f", [128, 128], F32)
        ident_b = C.sb("ident_b", [128, 128], BF16)
        C.dma('sp', ident_f[:], K['ident_f'][:, :])
        C.dma('sp', ident_b[:], K['ident_b'][:, :])
        neghalf = C.sb("neghalf", [128, 8], F32)
        C.memset('pool', neghalf[:], -0.5)
        modT = C.sb("modT", [128, 48, 2], F32)
        AB = C.sb("AB", [128, 4, 8, 2], F32)
        scT = C.sb("scT", [128, 8, 2], F32)
        gvec = C.sb("gvec", [128, 4, 8], F32)
        adab = C.sb("adab", [128, 48], F32)
        gv2 = C.sb("gv2", [128, 2, 8, 2], F32)
        Gb = C.sb("Gb", [128, 2, 2, D], F32)
        stat = [C.sb(f"stat{i}", [128, 8], F32) for i in range(2)]
        junk = C.sb("junk", [128, D], F32)

        def rstd_from_ss(st_, nsum, n):
            if nsum == 2:
                C.tt('pool', st_[:, 2:3], st_[:, 0:1], st_[:, 1:2], ALU.add)
                src = st_[:, 2:3]
            else:
                src = st_[:, 0:1]
            C.ts('pool', st_[:, 3:4], src, 1.0 / n, EPS, ALU.mult, ALU.add)
            C.rpow(st_[:, 4:5], st_[:, 3:4], -0.5)
            return st_[:, 4:5]

        if debug in (None, 'p4', 'mix'):
            with C.scope():
                cf = [C.sb(f"cf{i}", [128, 4096], F32) for i in range(2)]
                cb = [C.sb(f"cb{i}", [128, 4096], BF16) for i in range(2)]
                n = 0
                for l in range(DEPTH):
                    for wname, key, ncol in (('w_in', 'w_in', N_IN), ('w_out', 'w_out', D), ('mlp_w1', 'w1', DFF),
                                             ('mlp_w2', 'w2', DFF)):
                        if debug == 'p4' and (key == 'w_in' or l > 0):
                            continue
                        if debug == 'mix' and (key != 'w_in' or l > 0):
                            continue
                        if key == 'w2':
                            src2 = W[wname][l].rearrange("(a b) n -> a (b n)", b=4)
                            dst2 = wbf[(key, l)].rearrange("(a b) n -> a (b n)", b=4)
                        else:
                            src2 = W[wname][l]
                            dst2 = wbf[(key, l)]
                        for r8 in range(8):
                            f_, b_ = cf[n % 2], cb[n % 2]
                            C.dma('sp', f_[:, 0:ncol], src2[r8 * 128:(r8 + 1) * 128, :])
                            eng = ('dve', 'pool', 'act')[n % 3]
                            C.cp(eng, b_[:, 0:ncol], f_[:, 0:ncol])
                            C.dma('pool', dst2[r8 * 128:(r8 + 1) * 128, :], b_[:, 0:ncol], wk=[(dst2, r8)])
                            n += 1

        for l in range(DEPTH):
            last = (l == DEPTH - 1)
            tiles = list(range(NT))
            with C.scope():
                adaw = [C.sb(f"adaw{i}", [128, 8, 512], F32) for i in range(2)]
                pA = C.ps("pA", [128, 512], F32)
                items = [(adab[:], W['ada_b'][l])]
                for gi, gname in enumerate(('pre_mix_g', 'post_mix_g', 'pre_mlp_g', 'post_mlp_g')):
                    items.append((gvec[:, gi, :], W[gname][l]))
                if l == 0:
                    items += [(scT[:, :, 0], c_in), (scT[:, :, 1], cctx_in)]
                load_cols(C, ident_f, items)
                if l == 0:
                    C.act(junk[:, 0:16], scT[:].rearrange("p k s -> p (k s)"), AF.Sigmoid)
                    C.tt('dve', scT[:].rearrange("p k s -> p (k s)"), scT[:].rearrange("p k s -> p (k s)"),
                         junk[:, 0:16], ALU.mult)
                for s in range(12):
                    aw = adaw[s % 2]
                    C.dma('sp' if s % 2 == 0 else 'pool', aw[:],
                          W['ada_w'][l][:, s * 512:(s + 1) * 512].rearrange("(k p) n -> p k n", p=128))
                    for jj in range(4):
                        j = s * 4 + jj
                        for k in range(8):
                            C.mm(pA[:, 2 * j:2 * j + 2], aw[:, k, jj * 128:(jj + 1) * 128], scT[:, k, :],
                                 start=(k == 0), stop=(k == 7))
                C.tt('dve', modT[:], pA[:, 0:96].rearrange("p (j s) -> p j s", s=2),
                     adab[:].unsqueeze(2).to_broadcast([128, 48, 2]), ALU.add)
                for which, (gi, sci, shi) in enumerate(((0, 1, 0), (2, 4, 3))):
                    C.ts('dve', AB[:, 2 * which, :, :], modT[:, sci * 8:(sci + 1) * 8, :], 1.0, None, ALU.add)
                    C.tt('dve', AB[:, 2 * which, :, :], AB[:, 2 * which, :, :],
                         gvec[:, gi, :].unsqueeze(2).to_broadcast([128, 8, 2]), ALU.mult)
                    C.cp('dve', AB[:, 2 * which + 1, :, :], modT[:, shi * 8:(shi + 1) * 8, :])
                for w_, (gti, pgi) in enumerate(((2, 1), (5, 3))):
                    C.tt('dve', gv2[:, w_, :, :], modT[:, gti * 8:(gti + 1) * 8, :],
                         gvec[:, pgi, :].unsqueeze(2).to_broadcast([128, 8, 2]), ALU.mult)
                    for s_ in range(2):
                        for k in range(8):
                            C.mm(pA[:, (k % 4) * 128:(k % 4 + 1) * 128],
                                 gv2[:, w_, k, s_:s_ + 1].to_broadcast([128, 128]), ident_f[:])
                            if k % 4 == 3:
                                C.cp('act', Gb[:, w_, s_, (k - 3) * 128:(k + 1) * 128], pA[:, :])

            mix_scope = contextlib.ExitStack()
            if debug != 'p4':
              with C.scope():
                hT = C.sb("hT", [128, 8, T], BF16)
                with C.scope():
                    xt = [C.sb(f"xt{i}", [128, D], F32) for i in range(2)]
                    xn = [C.sb(f"xn{i}", [128, D], F32) for i in range(2)]
                    pA = C.ps("pA", [128, 512], F32)
                    pB = C.ps("pB", [128, 512], F32)
                    for i in range(NT):
                        s = 1 if i < NCT else 0
                        if l == 0:
                            src = ctx_in[i * 128:(i + 1) * 128, :] if i < NCT else x_in[(i - NCT) * 128:(i - NCT + 1) * 128, :]
                        else:
                            src = xres[i * 128:(i + 1) * 128, :]
                        xb_, xn_, st_ = xt[i % 2], xn[i % 2], stat[i % 2]
                        C.dma('sp' if i % 2 == 0 else 'pool', xb_[:], src, rk=(xres, i) if l > 0 else None)
                        C.act(junk[:], xb_[:], AF.Square, accum_out=st_[:, 0:1])
                        rs = rstd_from_ss(st_, 1, D)
                        C.ts('dve', xn_[:], xb_[:], rs, None, ALU.mult)
                        for half in range(2):
                            pp = pA if half == 0 else pB
                            for kk in range(4):
                                k = half * 4 + kk
                                C.tr(pp[:, kk * 128:(kk + 1) * 128], xn_[:, k * 128:(k + 1) * 128], ident_f[:])
                            for kk in range(4):
                                k = half * 4 + kk
                                if half == 0:
                                    C.act(hT[:, k, i * 128:(i + 1) * 128], pp[:, kk * 128:(kk + 1) * 128], AF.Identity,
                                          bias=AB[:, 1, k, s:s + 1], scale=AB[:, 0, k, s:s + 1], wk=[(hT, i)])
                                else:
                                    C.ts('dve', hT[:, k, i * 128:(i + 1) * 128], pp[:, kk * 128:(kk + 1) * 128],
                                         AB[:, 0, k, s:s + 1], AB[:, 1, k, s:s + 1], ALU.mult, ALU.add, wk=[(hT, i)])
                if debug == 'hT' and l == 0:
                    d1 = dbg_out("d_modT", [128, 96])
                    C.dma('sp', d1[:, :], modT[:].rearrange("p j s -> p (j s)"))
                    d2 = dbg_out("d_hT", [128, 8 * T], BF16)
                    C.dma('sp', d2[:, :], hT[:].rearrange("p k t -> p (k t)"))
                MIXERS(locals())
            if debug != 'p4' and (debug_mixers is None or 'rwkv' in debug_mixers):
                E2 = dict(locals())
                E2['emit'] = l < DEPTH - 1
                rwkv_main(E2)
            if debug in ('hT', 'mix'):
                break

            ptiles = [i for i in range(NT) if not (last and i < NCT)]
            with C.scope():
                wo = C.sb("wo", [128, 8, D], BF16)
                C.dma('sp', wo[:], wbf[('w_out', l)].rearrange("(k p) n -> p k n", p=128), rk=wbf[('w_out', l)])
                yt = [C.sb(f"yt{i}", [128, D], F32) for i in range(2)]
                ybf = [C.sb(f"ybf{i}", [128, D], BF16) for i in range(2)]
                yT = [C.sb(f"yT{i}", [128, 8, 128], BF16) for i in range(2)]
                xt = [C.sb(f"xt{i}", [128, D], F32) for i in range(2)]
                xm = [C.sb(f"xm{i}", [128, D], F32) for i in range(2)]
                xn = [C.sb(f"xn{i}", [128, D], F32) for i in range(2)]
                h2 = [C.sb(f"h2{i}", [128, 8, 128], BF16) for i in range(2)]
                pT = C.ps("pT", [128, D], BF16)
                pY = [C.ps(f"pY{i}", [128, 512], F32) for i in range(2)]
                pA = C.ps("pA", [128, 512], F32)
                pB = C.ps("pB", [128, 512], F32)
                for n_, i in enumerate(ptiles):
                    s = 1 if i < NCT else 0
                    b2 = n_ % 2
                    st_ = stat[b2]
                    C.dma('sp', yt[b2][:], ycat[i * 128:(i + 1) * 128, :], rk=(ycat, i))
                    if l == 0:
                        src = ctx_in[i * 128:(i + 1) * 128, :] if i < NCT else x_in[(i - NCT) * 128:(i - NCT + 1) * 128, :]
                    else:
                        src = xres[i * 128:(i + 1) * 128, :]
                    C.dma('pool', xt[b2][:], src, rk=(xres, i) if l > 0 else None)
                    C.cp('pool' if b2 else 'dve', ybf[b2][:], yt[b2][:])
                    for k in range(8):
                        C.tr(pT[:, k * 128:(k + 1) * 128], ybf[b2][:, k * 128:(k + 1) * 128], ident_b[:])
                    C.cp('act', yT[b2][:].rearrange("p k t -> p (k t)"), pT[:, :])
                    for nh in range(2):
                        for k in range(8):
                            C.mm(pY[nh][:, :], yT[b2][:, k, :], wo[:, k, nh * 512:(nh + 1) * 512],
                                 start=(k == 0), stop=(k == 7))
                    for nh in range(2):
                        C.act(junk[:, nh * 512:(nh + 1) * 512], pY[nh][:, :], AF.Square, accum_out=st_[:, nh:nh + 1])
                    rs = rstd_from_ss(st_, 2, D)
                    for nh in range(2):
                        sl = slice(nh * 512, (nh + 1) * 512)
                        C.stt(xm[b2][:, sl], pY[nh][:, :], rs, Gb[:, 0, s, sl], ALU.mult, ALU.mult)
                    C.tt('pool', xm[b2][:], xm[b2][:], xt[b2][:], ALU.add)
                    C.dma('sp', xmid[i * 128:(i + 1) * 128, :], xm[b2][:], wk=(xmid, i))
                    C.act(junk[:], xm[b2][:], AF.Square, accum_out=st_[:, 5:6])
                    C.ts('pool', st_[:, 6:7], st_[:, 5:6], 1.0 / D, EPS, ALU.mult, ALU.add)
                    C.rpow(st_[:, 7:8], st_[:, 6:7], -0.5)
                    C.ts('dve', xn[b2][:], xm[b2][:], st_[:, 7:8], None, ALU.mult)
                    for half in range(2):
                        pp = pA if half == 0 else pB
                        for kk in range(4):
                            k = half * 4 + kk
                            C.tr(pp[:, kk * 128:(kk + 1) * 128], xn[b2][:, k * 128:(k + 1) * 128], ident_f[:])
                        for kk in range(4):
                            k = half * 4 + kk
                            if half == 0:
                                C.act(h2[b2][:, k, :], pp[:, kk * 128:(kk + 1) * 128], AF.Identity,
                                      bias=AB[:, 3, k, s:s + 1], scale=AB[:, 2, k, s:s + 1])
                            else:
                                C.ts('dve', h2[b2][:, k, :], pp[:, kk * 128:(kk + 1) * 128],
                                     AB[:, 2, k, s:s + 1], AB[:, 3, k, s:s + 1], ALU.mult, ALU.add)
                    C.dma('pool', h2T_d[:, :, i * 128:(i + 1) * 128], h2[b2][:], wk=(h2T_d, i))

            with C.scope():
                w1s = C.sb("w1s", [128, 8, DFF], BF16)
                w2s = C.sb("w2s", [128, 32, D], BF16)
                for k in range(8):
                    C.dma('sp' if k % 2 == 0 else 'pool', w1s[:, k, :], wbf[('w1', l)][k * 128:(k + 1) * 128, :],
                          rk=wbf[('w1', l)], wk=(w1s, k))
                for j4 in range(8):
                    C.dma('sp' if j4 % 2 == 0 else 'pool', w2s[:, j4 * 4:(j4 + 1) * 4, :],
                          wbf[('w2', l)][j4 * 512:(j4 + 1) * 512, :].rearrange("(j p) n -> p j n", p=128),
                          rk=wbf[('w2', l)], wk=(w2s, j4))
                h2g = [C.sb(f"h2g{i}", [128, 8, 256], BF16) for i in range(2)]
                rl = [C.sb(f"rl{i}", [128, 256], BF16) for i in range(2)]
                uT = [C.sb(f"uT{i}", [128, 256], BF16) for i in range(3)]
                xm = [C.sb(f"xm{i}", [128, D], F32) for i in range(2)]
                xo = [C.sb(f"xo{i}", [128, D], F32) for i in range(2)]
                pH = [C.ps(f"pH{i}", [128, 512], F32) for i in range(2)]
                pF = [[C.ps(f"pF{t_}{h_}", [128, 512], F32) for h_ in range(2)] for t_ in range(2)]
                groups = [ptiles[a:a + 2] for a in range(0, len(ptiles), 2)]
                nu = 0
                for gi, grp in enumerate(groups):
                    i0 = grp[0]
                    hg = h2g[gi % 2]
                    C.dma('sp', hg[:], h2T_d[:, :, i0 * 128:(i0 + 2) * 128], rk=(h2T_d, None))
                    for j in range(32):
                        ph = pH[j % 2]
                        for k in range(8):
                            C.mm(ph[:, 0:256], w1s[:, k, j * 128:(j + 1) * 128], hg[:, k, :],
                                 start=(k == 0), stop=(k == 7), rk=[(w1s, k), hg])
                        r_ = rl[j % 2]
                        u_ = uT[nu % 3]
                        nu += 1
                        C.act(r_[:], ph[:, 0:256], AF.Relu)
                        C.tt('dve' if j % 2 == 0 else 'pool', u_[:], r_[:], r_[:], ALU.mult)
                        for t_ in range(2):
                            for nh in range(2):
                                C.mm(pF[t_][nh][:, :], u_[:, t_ * 128:(t_ + 1) * 128], w2s[:, j, nh * 512:(nh + 1) * 512],
                                     start=(j == 0), stop=(j == 31), rk=[u_, (w2s, j // 4)])
                    for t_, i in enumerate(grp):
                        s = 1 if i < NCT else 0
                        b2 = (gi * 2 + t_) % 2
                        st_ = stat[b2]
                        C.dma('pool', xm[b2][:], xmid[i * 128:(i + 1) * 128, :], rk=(xmid, i))
                        for nh in range(2):
                            C.act(junk[:, nh * 512:(nh + 1) * 512], pF[t_][nh][:, :], AF.Square,
                                  accum_out=st_[:, nh:nh + 1])
                        rs = rstd_from_ss(st_, 2, D)
                        for nh in range(2):
                            sl = slice(nh * 512, (nh + 1) * 512)
                            C.stt(xo[b2][:, sl], pF[t_][nh][:, :], rs, Gb[:, 1, s, sl], ALU.mult, ALU.mult)
                        C.tt('pool', xo[b2][:], xo[b2][:], xm[b2][:], ALU.add)
                        if last:
                            C.dma('sp', out[(i - NCT) * 128:(i - NCT + 1) * 128, :], xo[b2][:])
                        else:
                            C.dma('sp', xres[i * 128:(i + 1) * 128, :], xo[b2][:], wk=(xres, i))
            if debug == 'p4':
                break
        C.finish()
    return nc, list(dbg.keys())


def mixer_ret(E):
    C, l, hT, wbf, W, K, ycat = E['C'], E['l'], E['hT'], E['wbf'], E['W'], E['K'], E['ycat']
    stat, neghalf, emit = E['stat'], E['neghalf'], E['emit']
    with C.scope():
        wB = C.sb("wB", [128, 8, 1024], BF16)
        C.dma('sp', wB[:], wbf[('w_in', l)][:, 512:1536].rearrange("(k p) n -> p k n", p=128), rk=wbf[('w_in', l)])
        tabs = {}
        for nm, shp in (('ret_qft', [128, 256]), ('ret_qbt', [128, 256]), ('ret_gcc', [128, 2]), ('ret_kf', [128, 256]),
                        ('ret_kb', [128, 256]), ('ret_dt', [128, 512])):
            tabs[nm] = C.sb(nm, shp, F32)
            src = K[nm]
            if len(src.shape) == 3:
                src = src.rearrange("p a b -> p (a b)")
            C.dma('pool', tabs[nm][:], src)
        gng = C.sb("ret_gng", [128, 256], F32)
        C.dma('pool', gng[:], W['ret_gn_g'][l].partition_broadcast(128))
        Gs = [C.sb(f"retG{d}", [128, NT, 128], F32) for d in range(2)]
        Ss = [C.sb(f"retS{d}", [128, NT, 128], BF16) for d in range(2)]
        cur = [C.sb(f"retcur{d}", [128, 128], F32) for d in range(2)]
        import os
        if int(os.environ.get('RET_STOP', '99')) <= 0:
            return
        with C.scope():
            pKV = [C.ps(f"pKV{i}", [128, 512], F32) for i in range(2)]
            pV1 = [C.ps(f"pV1{i}", [128, 512], F32) for i in range(2)]
            pGb = [C.ps(f"pG{i}", [128, 512], F32) for i in range(2)]
            pG = [[pGb[i][:, d * 128:(d + 1) * 128] for d in range(2)] for i in range(2)]
            kd = [[C.sb(f"kd{i}{d}", [128, 256], BF16) for d in range(2)] for i in range(2)]
            vv = [C.sb(f"vv{i}", [128, 256], BF16) for i in range(2)]
            for n in range(NT):
                b2 = n % 2
                for k in range(8):
                    C.mm(pKV[b2][:, 0:256], hT[:, k, n * 128:(n + 1) * 128], wB[:, k, 256:512], start=(k == 0), stop=(k == 7),
                         rk=[(hT, n), wB])
                for k in range(8):
                    C.mm(pV1[b2][:, 256:512], hT[:, k, n * 128:(n + 1) * 128], wB[:, k, 512:768], start=(k == 0), stop=(k == 7),
                         rk=[(hT, n), wB])
                p1 = os.environ.get('RET_P1', '')
                if 'nodve' not in p1:
                    C.tt('dve', kd[b2][0][:], pKV[b2][:, 0:256], tabs['ret_kf'][:], ALU.mult)
                    C.tt('dve', kd[b2][1][:], pKV[b2][:, 0:256], tabs['ret_kb'][:], ALU.mult)
                if 'noact' not in p1:
                    C.cp('act', vv[b2][:], pV1[b2][:, 256:512])
                for d in range(2):
                    if 'nog' in os.environ.get('RET_P1', ''):
                        continue
                    for h in range(4):
                        if os.environ.get('RET_P1') == 'h0' and h != 0:
                            continue
                        hh, cc = h % 2, h // 2
                        C.mm(pGb[b2][hh * 64:(hh + 1) * 64, d * 128 + cc * 64:d * 128 + (cc + 1) * 64], kd[b2][d][:, h * 64:(h + 1) * 64],
                             vv[b2][:, h * 64:(h + 1) * 64])
                    C.cp('act', Gs[d][:, n, :], pG[b2][d], wk=[(Gs[d], n)])
        import os
        stop = int(os.environ.get('RET_STOP', '99'))
        if stop <= 1:
            return
        gcc = tabs['ret_gcc']
        for d in range(2):
            order = list(range(NT)) if d == 0 else [1, 0] + list(range(NT - 1, 1, -1))
            C.memset('dve', cur[d][:], 0.0)
            for n in order:
                C.cp('act', Ss[d][:, n, :], cur[d][:], wk=[(Ss[d], n)])
                for cc in range(2):
                    C.stt(cur[d][:, cc * 64:(cc + 1) * 64], cur[d][:, cc * 64:(cc + 1) * 64], gcc[:, cc:cc + 1],
                          Gs[d][:, n, cc * 64:(cc + 1) * 64], ALU.mult, ALU.add, rk=[cur[d], gcc, (Gs[d], n)])
        if stop <= 2:
            return
        with C.scope():
            pQ = C.ps("pQ", [128, 512], F32)
            pK = C.ps("pK", [128, 512], F32)
            pVG = C.ps("pVG", [128, 512], F32)
            pS = [C.ps(f"pS{i}", [128, 512], F32) for i in range(2)]
            pOb = [C.ps(f"pO{i}", [128, 512], F32) for i in range(2)]
            qT = [C.sb(f"rqT{i}", [128, 2, 128], BF16) for i in range(2)]
            qTf = [C.sb(f"rqTf{i}", [128, 2, 128], BF16) for i in range(2)]
            qTb = [C.sb(f"rqTb{i}", [128, 2, 128], BF16) for i in range(2)]
            kT = [C.sb(f"rkT{i}", [128, 2, 128], BF16) for i in range(2)]
            vv = [C.sb(f"rvv{i}", [128, 256], BF16) for i in range(2)]
            sg = [C.sb(f"rsg{i}", [128, 256], F32) for i in range(2)]
            sTm = [C.sb(f"rsTm{i}", [128, 4, 128], BF16) for i in range(2)]
            oc = [C.sb(f"roc{i}", [128, 256], F32) for i in range(2)]
            osq = [C.sb(f"rosq{i}", [128, 256], F32) for i in range(2)]
            yo = [C.sb(f"ryo{i}", [128, 256], F32) for i in range(2)]
            st4 = [C.sb(f"rst{i}", [128, 8, 4], F32) for i in range(2)]
            nh4 = C.sb("nh4", [128, 4], F32)
            C.memset('pool', nh4[:], -0.5)
            tiles = [n for n in range(NT) if emit or n >= NCT]
            for n in tiles:
                b2 = n % 2
                for j in range(4):
                    col = (j % 2) * 128 + (0 if j < 2 else 256)
                    dst = (pQ if j < 2 else pK)[:, (j % 2) * 128:(j % 2 + 1) * 128]
                    for k in range(8):
                        C.mm(dst, wB[:, k, col:col + 128], hT[:, k, n * 128:(n + 1) * 128],
                             start=(k == 0), stop=(k == 7), rk=[(hT, n), wB])
                for k in range(8):
                    C.mm(pVG[:, :], hT[:, k, n * 128:(n + 1) * 128], wB[:, k, 512:1024], start=(k == 0), stop=(k == 7),
                         rk=[(hT, n), wB])
                C.cp('dve', qT[b2][:].rearrange("p c t -> p (c t)"), pQ[:, 0:256])
                C.tt('dve', qTf[b2][:].rearrange("p c t -> p (c t)"), pQ[:, 0:256], tabs['ret_qft'][:], ALU.mult)
                C.tt('dve', qTb[b2][:].rearrange("p c t -> p (c t)"), pQ[:, 0:256], tabs['ret_qbt'][:], ALU.mult)
                C.act(kT[b2][:].rearrange("p c t -> p (c t)"), pK[:, 0:256], AF.Copy, scale=0.125)
                C.cp('act', vv[b2][:], pVG[:, 0:256])
                C.act(sg[b2][:], pVG[:, 256:512], AF.Silu)
                for h in range(4):
                    hh, cc = h % 2, h // 2
                    C.mm(pS[hh][:, cc * 128:(cc + 1) * 128], kT[b2][hh * 64:(hh + 1) * 64, cc, :],
                         qT[b2][hh * 64:(hh + 1) * 64, cc, :])
                dt4 = tabs['ret_dt'][:].rearrange("p (c hh t) -> p c hh t", hh=2, t=128)
                sT4 = sTm[b2][:].rearrange("p (c hh) t -> p c hh t", hh=2)
                for hh in range(2):
                    C.tt('dve', sT4[:, :, hh, :], pS[hh][:, 0:256].rearrange("p (c t) -> p c t", t=128), dt4[:, :, hh, :],
                         ALU.mult)
                for h in range(4):
                    hh, cc = h % 2, h // 2
                    o_ = pOb[hh][:, cc * 64:(cc + 1) * 64]
                    C.mm(o_, sTm[b2][:, h, :], vv[b2][:, h * 64:(h + 1) * 64], start=True, stop=False)
                    C.mm(o_, qTf[b2][hh * 64:(hh + 1) * 64, cc, :], Ss[0][hh * 64:(hh + 1) * 64, n, cc * 64:(cc + 1) * 64],
                         start=False, stop=False, rk=[qTf[b2], (Ss[0], n)])
                    C.mm(o_, qTb[b2][hh * 64:(hh + 1) * 64, cc, :], Ss[1][hh * 64:(hh + 1) * 64, n, cc * 64:(cc + 1) * 64],
                         start=False, stop=True, rk=[qTb[b2], (Ss[1], n)])
                s4 = st4[b2]
                oc4 = oc[b2][:].rearrange("p (c hh e) -> p c hh e", hh=2, e=64)
                for hh in range(2):
                    C.cp('act', oc4[:, :, hh, :], pOb[hh][:, 0:128].rearrange("p (c e) -> p c e", e=64))
                C.act(osq[b2][:], oc[b2][:], AF.Square)
                o3 = oc[b2][:].rearrange("p (h e) -> p h e", e=64)
                C.red('dve', s4[:, 0, :], o3, ALU.add)
                C.red('dve', s4[:, 1, :], osq[b2][:].rearrange("p (h e) -> p h e", e=64), ALU.add)
                C.ts('pool', s4[:, 2, :], s4[:, 0, :], 1.0 / 64, None, ALU.mult)
                C.tt('pool', s4[:, 3, :], s4[:, 2, :], s4[:, 2, :], ALU.mult)
                C.ts('pool', s4[:, 4, :], s4[:, 1, :], 1.0 / 64, 1e-5, ALU.mult, ALU.add)
                C.tt('pool', s4[:, 4, :], s4[:, 4, :], s4[:, 3, :], ALU.subtract)
                C.rpow(s4[:, 5, :], s4[:, 4, :], -0.5)
                y3 = yo[b2][:].rearrange("p (h e) -> p h e", e=64)
                C.tt('dve', y3, o3, s4[:, 2, :].unsqueeze(2).to_broadcast([128, 4, 64]), ALU.subtract)
                C.tt('dve', y3, y3, s4[:, 5, :].unsqueeze(2).to_broadcast([128, 4, 64]), ALU.mult)
                C.tt('pool', yo[b2][:], yo[b2][:], gng[:], ALU.mult)
                C.tt('pool', yo[b2][:], yo[b2][:], sg[b2][:], ALU.mult)
                C.dma('sp', ycat[n * 128:(n + 1) * 128, 256:512], yo[b2][:], wk=(ycat, n))


def mixer_swa(E):
    C, l, hT, wbf, W, K, ycat = E['C'], E['l'], E['hT'], E['wbf'], E['W'], E['K'], E['ycat']
    emit, ident_b = E['emit'], E['ident_b']
    with C.scope():
        wA = C.sb("wA", [128, 8, 512], BF16)
        C.dma('sp', wA[:], wbf[('w_in', l)][:, 0:512].rearrange("(k p) n -> p k n", p=128), rk=wbf[('w_in', l)])
        wK2 = C.sb("wK2", [128, 8, 2, 128], BF16)
        for hk in range(2):
            for dup in range(2):
                C.dma('pool', wK2[:, :, hk, dup * 64:(dup + 1) * 64],
                      wbf[('w_in', l)][:, 256 + hk * 64:256 + (hk + 1) * 64].rearrange("(k p) n -> p k n", p=128),
                      rk=wbf[('w_in', l)])
        pm = C.sb("swa_pm", [128, 128], BF16)
        C.dma('pool', pm[:], K['swa_pm'][:, :])
        mask = C.sb("swa_mask", [128, 384], F32)
        C.dma('pool', mask[:], K['swa_mask'][:, :])
        sinkb = C.sb("sinkb", [128, 4], F32)
        C.dma('pool', sinkb[:], W['swa_sink'][l].partition_broadcast(128))
        qTr = C.sb("qTr", [128, 2, T], BF16)
        kTz = [C.sb(f"kTz{g}", [128, 2, T], BF16) for g in range(2)]
        vtok = C.sb("vtok", [128, NT, 128], BF16)
        C.memset('pool', kTz[0][64:128, :, :], 0.0)
        C.memset('pool', kTz[1][0:64, :, :], 0.0)
        with C.scope():
            pQ = C.ps("pQ", [128, 512], F32)
            pP = C.ps("pP", [128, 512], F32)
            pV = C.ps("pV", [128, 512], F32)
            ctg = [C.sb(f"ctg{i}", [128, 512], F32) for i in range(2)]
            stg = [C.sb(f"stg{i}", [128, 512], F32) for i in range(2)]
            xq = [C.sb(f"xq{i}", [128, 512], BF16) for i in range(2)]
            t1 = [C.sb(f"t1{i}", [128, 512], F32) for i in range(2)]
            t2 = [C.sb(f"t2{i}", [128, 512], F32) for i in range(2)]
            groups = [(0, LC)] + [(LC + 512 * i, 512) for i in range(8)]
            nx = 0
            for gi, (t0, n) in enumerate(groups):
                rope = t0 >= LC
                if rope:
                    C.dma('sp', ctg[gi % 2][:], K['swa_ct'][:, t0 - LC:t0 - LC + n])
                    C.dma('sp', stg[gi % 2][:], K['swa_st'][:, t0 - LC:t0 - LC + n])
                for j in range(4):
                    hk = j % 2
                    for k in range(8):
                        lhsT = wA[:, k, hk * 128:(hk + 1) * 128] if j < 2 else wK2[:, k, hk, :]
                        C.mm(pQ[:, 0:n], lhsT, hT[:, k, t0:t0 + n], start=(k == 0), stop=(k == 7), rk=[hT, wA, wK2])
                    x_ = xq[nx % 2]
                    nx += 1
                    C.cp('act', x_[:, 0:n], pQ[:, 0:n])
                    if rope:
                        C.mm(pP[:, 0:n], pm[:], x_[:, 0:n])
                        a_, b_ = t1[nx % 2], t2[nx % 2]
                        C.tt('pool', a_[:, 0:n], x_[:, 0:n], ctg[gi % 2][:, 0:n], ALU.mult)
                        C.tt('dve', b_[:, 0:n], pP[:, 0:n], stg[gi % 2][:, 0:n], ALU.mult)
                        if j < 2:
                            C.tt('pool', qTr[:, hk, t0:t0 + n], a_[:, 0:n], b_[:, 0:n], ALU.add, wk=[(qTr, gi)])
                        else:
                            for g in range(2):
                                C.tt('pool', kTz[g][g * 64:(g + 1) * 64, hk, t0:t0 + n], a_[g * 64:(g + 1) * 64, 0:n],
                                     b_[g * 64:(g + 1) * 64, 0:n], ALU.add, wk=[(kTz[g], gi)])
                    else:
                        if j < 2:
                            C.cp('pool', qTr[:, hk, t0:t0 + n], x_[:, 0:n], wk=[(qTr, gi)])
                        else:
                            for g in range(2):
                                C.cp('pool', kTz[g][g * 64:(g + 1) * 64, hk, t0:t0 + n], x_[g * 64:(g + 1) * 64, 0:n],
                                     wk=[(kTz[g], gi)])
                for ti in range(t0 // 128, (t0 + n) // 128):
                    for k in range(8):
                        C.mm(pV[:, 0:128], hT[:, k, ti * 128:(ti + 1) * 128], wA[:, k, 384:512], start=(k == 0),
                             stop=(k == 7), rk=[hT, wA])
                    C.cp('act', vtok[:, ti, :], pV[:, 0:128], wk=[(vtok, ti)])
        with C.scope():
            pSa = [C.ps(f"pSa{i}", [128, 512], F32) for i in range(2)]
            pSb = [C.ps(f"pSb{i}", [128, 512], F32) for i in range(2)]
            pT = [C.ps(f"pTs{i}", [128, 1024], BF16) for i in range(2)]
            pO = C.ps("pOs", [128, 512], F32)
            sc = [C.sb(f"sc{i}", [128, 640], F32) for i in range(2)]
            pb = [C.sb(f"pb{i}", [128, 640], BF16) for i in range(2)]
            pTs = [C.sb(f"pTsb{i}", [128, 640], BF16) for i in range(2)]
            sm = [C.sb(f"sm{i}", [128, 8], F32) for i in range(2)]
            ya = [C.sb(f"ya{i}", [128, 256], F32) for i in range(2)]
            rden = [C.sb(f"rden{i}", [128, 4], F32) for i in range(2)]
            qblocks = [n for n in range(NT) if emit or n >= NCT]
            nh = 0
            for bi, n in enumerate(qblocks):
                isx = n >= NCT
                if isx:
                    lo, hi = max(n - 1, NCT), min(n + 1, NT - 1)
                    ktiles = list(range(lo, hi + 1)) + [0, 1]
                    nloc = hi - lo + 1
                    moff = (lo - (n - 1)) * 128
                else:
                    lo, hi = 0, 1
                    ktiles = [0, 1]
                    nloc = 2
                ncol = len(ktiles) * 128
                for h in range(4):
                    hk, g = h // 2, h % 2
                    b2 = nh % 2
                    nh += 1
                    C.mm(pSa[b2][:, 0:nloc * 128], qTr[:, hk, n * 128:(n + 1) * 128], kTz[g][:, hk, lo * 128:(hi + 1) * 128],
                         rk=[qTr, kTz[g]])
                    if isx:
                        C.mm(pSb[b2][:, 0:256], qTr[:, hk, n * 128:(n + 1) * 128], kTz[g][:, hk, 0:256], rk=[qTr, kTz[g]])
                        C.tt('dve', sc[b2][:, 0:nloc * 128], pSa[b2][:, 0:nloc * 128], mask[:, moff:moff + nloc * 128], ALU.add)
                        C.cp('dve', sc[b2][:, nloc * 128:ncol], pSb[b2][:, 0:256])
                    else:
                        C.cp('dve', sc[b2][:, 0:ncol], pSa[b2][:, 0:ncol])
                    s_ = sm[b2]
                    C.red('dve', s_[:, 0:1], sc[b2][:, 0:ncol], ALU.max)
                    C.ts('dve', s_[:, 1:2], s_[:, 0:1], 0.125, sinkb[:, h:h + 1], ALU.mult, ALU.max)
                    C.ts('dve', s_[:, 2:3], s_[:, 1:2], -1.0, None, ALU.mult)
                    C.act(pb[b2][:, 0:ncol], sc[b2][:, 0:ncol], AF.Exp, bias=s_[:, 2:3], scale=0.125, accum_out=s_[:, 3:4])
                    C.act(s_[:, 4:5], sinkb[:, h:h + 1], AF.Exp, bias=s_[:, 2:3], scale=1.0)
                    C.tt('dve', s_[:, 5:6], s_[:, 3:4], s_[:, 4:5], ALU.add)
                    C.op('dve', [s_], [rden[bi % 2]], lambda e_, o_=rden[bi % 2][:, h:h + 1], i_=s_[:, 5:6]: e_.reciprocal(out=o_, in_=i_))
                    for j in range(len(ktiles)):
                        C.tr(pT[b2][:, j * 128:(j + 1) * 128], pb[b2][:, j * 128:(j + 1) * 128], ident_b[:])
                    C.cp('act', pTs[b2][:, 0:ncol], pT[b2][:, 0:ncol])
                    for j, kt in enumerate(ktiles):
                        C.mm(pO[:, h * 64:(h + 1) * 64], pTs[b2][:, j * 128:(j + 1) * 128], vtok[:, kt, hk * 64:(hk + 1) * 64],
                             start=(j == 0), stop=(j == len(ktiles) - 1), rk=[pTs[b2], vtok])
                for h in range(4):
                    C.act(ya[bi % 2][:, h * 64:(h + 1) * 64], pO[:, h * 64:(h + 1) * 64], AF.Copy, scale=rden[bi % 2][:, h:h + 1])
                C.dma('sp', ycat[n * 128:(n + 1) * 128, 0:256], ya[bi % 2][:], wk=(ycat, n))


def mixer_mla(E):
    C, l, hT, wbf, W, K, ycat = E['C'], E['l'], E['hT'], E['wbf'], E['W'], E['K'], E['ycat']
    emit, ident_b, ident_f, neghalf = E['emit'], E['ident_b'], E['ident_f'], E['neghalf']
    SCALE = 96.0 ** -0.5
    with C.scope():
        wD = C.sb("wD", [128, 8, 416], BF16)
        C.dma('sp', wD[:], wbf[('w_in', l)][:, 2560:2976].rearrange("(k p) n -> p k n", p=128), rk=wbf[('w_in', l)])
        wf = C.sb("mla_wf", [128, 1280], F32)
        wuq = C.sb("wuq", [128, 2, 384], BF16)
        wukv = C.sb("wukv", [128, 512], BF16)
        C.dma('sp', wf[:, 0:768].rearrange("p (c n) -> p c n", c=2), W['mla_w_uq'][l].rearrange("(c p) n -> p c n", p=128))
        C.dma('sp', wf[:, 768:1280], W['mla_w_ukv'][l])
        C.cp('dve', wuq[:].rearrange("p c n -> p (c n)"), wf[:, 0:768])
        C.cp('dve', wukv[:], wf[:, 768:1280])
        gq = C.sb("mla_gq", [128, 2], F32)
        gkv = C.sb("mla_gkv", [128, 1], F32)
        load_cols(C, ident_f, [(gq[:], W['mla_q_norm_g'][l]), (gkv[:], W['mla_kv_norm_g'][l])])
        cosT = C.sb("mla_cos", [128, L // 128, 16], F32)
        sinT = C.sb("mla_sin", [128, L // 128, 16], F32)
        C.dma('pool', cosT[:].rearrange("p n f -> p (n f)"), K['mla_cos'][:, :])
        C.dma('pool', sinT[:].rearrange("p n f -> p (n f)"), K['mla_sin'][:, :])
        ones_b = C.sb("mla_ones", [128, 2], BF16)
        C.memset('pool', ones_b[:], 1.0)
        ones_f = C.sb("mla_onesf", [1, 128], F32)
        C.memset('pool', ones_f[:], 1.0)
        half = C.sb("mla_half", [128, 4], F32)
        C.memset('pool', half[:], 0.5)
        KaT = C.sb("KaT", [128, 4, T], BF16)
        Va = C.sb("Va", [128, NT, 4, 65], BF16)
        C.memset('pool', Va[:, :, :, 64:65], 1.0)
        kmx = C.sb("kmx", [128, 4], F32)
        C.memset('dve', kmx[:], 0.0)
        kmaxb = C.sb("kmaxb", [128, 2], F32)
        aug = [C.sb(f"aug{i}", [128, 4, 128], BF16) for i in range(2)]
        for i in range(2):
            C.memset('pool', aug[i][:], 0.0)
        fT = [C.sb(f"mfT{i}", [128, 2, 128], BF16) for i in range(2)]
        sqT = [C.sb(f"msq{i}", [128, 2, 128], BF16) for i in range(2)]
        qf = [C.sb(f"mqf{i}", [128, 512], F32) for i in range(2)]
        tmp = [C.sb(f"mtmp{i}", [128, 384], F32) for i in range(2)]
        rp = [C.sb(f"mrp{i}", [128, 6, 64], F32) for i in range(2)]
        st_ = [C.sb(f"mst{i}", [128, 16], F32) for i in range(2)]

        def rope_tm(out1, out2, x1, x2, c_, s_, r_, shape):
            n = int(np.prod(shape[1:]))
            v = [r_[:, i, 0:n].rearrange("p (a b) -> p a b", b=shape[-1]) if len(shape) == 3 else
                 r_[:, i, 0:n].rearrange("p (h a b) -> p h a b", a=shape[2], b=shape[3]) for i in range(4)]
            C.tt('dve', v[0], x1, c_, ALU.mult)
            C.tt('dve', v[1], x2, s_, ALU.mult)
            C.tt('dve', v[2], x2, c_, ALU.mult)
            C.tt('dve', v[3], x1, s_, ALU.mult)
            C.tt('dve', out1, v[0], v[1], ALU.subtract)
            C.tt('dve', out2, v[2], v[3], ALU.add)

        import os
        mstop = int(os.environ.get('MLA_STOP', '99'))
        if mstop <= 0:
            return
        with C.scope():
            pA = C.ps("mpA", [128, 512], F32)
            pB = C.ps("mpB", [128, 512], F32)
            pB2 = C.ps("mpB2", [128, 512], F32)
            pT = C.ps("mpT", [128, 1024], BF16)
            for n in range(NT):
                b2 = n % 2
                a_, s4 = aug[b2], st_[b2]
                for k in range(8):
                    C.mm(pA[:, 0:128], wD[:, k, 256:384], hT[:, k, n * 128:(n + 1) * 128], start=(k == 0), stop=(k == 7),
                         rk=[(hT, n), wD])
                C.act(fT[b2][:, 0, :], pA[:, 0:128], AF.Copy, scale=gkv[:, 0:1])
                C.act(sqT[b2][:, 0, :], pA[:, 0:128], AF.Square)
                C.mm(pB[:, :], fT[b2][:, 0, :], wukv[:])
                C.mm(pB2[:, 0:2], sqT[b2][:, 0, :], ones_b[:, 0:2])
                for k in range(8):
                    C.mm(pB2[:, 32:64], hT[:, k, n * 128:(n + 1) * 128], wD[:, k, 384:416], start=(k == 0), stop=(k == 7),
                         rk=[(hT, n), wD])
                C.ts('dve', s4[:, 0:1], pB2[:, 0:1], 1.0 / 128, EPS, ALU.mult, ALU.add)
                C.rpow(s4[:, 1:2], s4[:, 0:1], -0.5)
                kv4 = pB[:, :].rearrange("p (h e) -> p h e", e=128)
                C.ts('dve', a_[:, :, 0:64], kv4[:, :, 0:64], s4[:, 1:2], None, ALU.mult)
                C.ts('dve', Va[:, n, :, 0:64], kv4[:, :, 64:128], s4[:, 1:2], None, ALU.mult, wk=[(Va, n)])
                if n >= NCT:
                    x4 = pB2[:, 32:64].rearrange("p (a b f) -> p a b f", a=2, b=2)
                    c_ = cosT[:, n - NCT, :].rearrange("p (a f) -> p a f", a=2)
                    s_ = sinT[:, n - NCT, :].rearrange("p (a f) -> p a f", a=2)
                    o4 = rp[b2][:, 4, 0:32].rearrange("p (a b f) -> p a b f", a=2, b=2)
                    rope_tm(o4[:, :, 0, :], o4[:, :, 1, :], x4[:, :, 0, :], x4[:, :, 1, :], c_, s_, rp[b2], [128, 2, 8])
                else:
                    C.cp('dve', rp[b2][:, 4, 0:32], pB2[:, 32:64])
                C.cp('dve', a_[:, :, 64:96], rp[b2][:, 4, 0:32].unsqueeze(1).to_broadcast([128, 4, 32]))
                C.memset('pool', a_[:, :, 96:97], 1.0)
                t3 = tmp[b2][:].rearrange("p (h e) -> p h e", e=96)
                C.tt('pool', t3, a_[:, :, 0:96], a_[:, :, 0:96], ALU.mult)
                C.red('dve', s4[:, 4:8], t3, ALU.add)
                C.tt('dve', kmx[:], kmx[:], s4[:, 4:8], ALU.max)
                for h in range(4):
                    C.tr(pT[:, h * 128:(h + 1) * 128], a_[:, h, :], ident_b[:])
                C.cp('act', KaT[:, :, n * 128:(n + 1) * 128], pT[:, 0:512].rearrange("p (h t) -> p h t", t=128),
                     wk=[(KaT, n)])
            if mstop <= 1:
                return
            C.red('dve', st_[0][:, 8:9], kmx[:], ALU.max)
            C.tr(pB[:, 0:128].bitcast(F32)[0:1, 0:128], st_[0][:, 8:9], ident_f[:])
            C.red('dve', st_[0][0:1, 9:10], pB[0:1, 0:128], ALU.max)
            C.rpow(st_[0][0:1, 10:11], st_[0][0:1, 9:10], 0.5)
            C.mm(pB2[:, 0:1], ones_f[0:1, :], st_[0][0:1, 10:11])
            C.ts('dve', kmaxb[:, 0:1], pB2[:, 0:1], -1.0, None, ALU.mult)

        if mstop <= 2:
            return
        with C.scope():
            pA = C.ps("mqA", [128, 512], F32)
            pB = C.ps("mqB", [128, 512], F32)
            pT = C.ps("mqT", [128, 1024], BF16)
            pS = [C.ps(f"mpS{i}", [128, 512], F32) for i in range(2)]
            pO = [C.ps(f"mpO{h}", [128, 512], F32) for h in range(3)]
            QaT = [C.sb(f"QaT{i}", [128, 4, 512], BF16) for i in range(2)]
            PT = [C.sb(f"PTs{i}", [128, 512], BF16) for i in range(3)]
            yd = [C.sb(f"yd{i}", [128, 256], F32) for i in range(2)]
            rd_ = [C.sb(f"mrd{i}", [128, 4], F32) for i in range(2)]
            groups = ([(0, 2, [0, 1])] if emit else []) + [(NCT + 4 * i, 4, list(range(NT))) for i in range(8)]
            nps = 0
            ny = 0
            for gi, (n0, nt, ktiles) in enumerate(groups):
                Qg = QaT[gi % 2]
                for j in range(nt):
                    n = n0 + j
                    b2 = n % 2
                    a_, s4 = aug[b2], st_[b2]
                    for c in range(2):
                        for k in range(8):
                            C.mm(pA[:, c * 128:(c + 1) * 128], wD[:, k, c * 128:(c + 1) * 128], hT[:, k, n * 128:(n + 1) * 128],
                                 start=(k == 0), stop=(k == 7), rk=[(hT, n), wD])
                    for c in range(2):
                        C.act(fT[b2][:, c, :], pA[:, c * 128:(c + 1) * 128], AF.Copy, scale=gq[:, c:c + 1])
                        C.act(sqT[b2][:, c, :], pA[:, c * 128:(c + 1) * 128], AF.Square)
                    for c in range(2):
                        C.mm(pB[:, 0:384], fT[b2][:, c, :], wuq[:, c, :], start=(c == 0), stop=(c == 1))
                    for c in range(2):
                        C.mm(pB[:, 384 + 2 * c:386 + 2 * c], sqT[b2][:, c, :], ones_b[:, 0:2])
                    C.cp('dve', s4[:, 12:16], pB[:, 384:388])
                    C.tt('dve', s4[:, 0:1], s4[:, 12:13], s4[:, 14:15], ALU.add)
                    C.ts('dve', s4[:, 0:1], s4[:, 0:1], 1.0 / 256, EPS, ALU.mult, ALU.add)
                    C.rpow(s4[:, 1:2], s4[:, 0:1], -0.5)
                    C.ts('dve', qf[b2][:, 0:384], pB[:, 0:384], s4[:, 1:2], None, ALU.mult)
                    q3 = qf[b2][:, 0:384].rearrange("p (h e) -> p h e", e=96)
                    C.cp('dve', a_[:, :, 0:64], q3[:, :, 0:64])
                    if n >= NCT and os.environ.get('MLA_VAR') != 'norope':
                        x5 = q3[:, :, 64:96].rearrange("p h (a b f) -> p h a b f", a=2, b=2)
                        o5 = a_[:, :, 64:96].rearrange("p h (a b f) -> p h a b f", a=2, b=2)
                        c_ = cosT[:, n - NCT, :].rearrange("p (a f) -> p a f", a=2).unsqueeze(1).to_broadcast([128, 4, 2, 8])
                        s_ = sinT[:, n - NCT, :].rearrange("p (a f) -> p a f", a=2).unsqueeze(1).to_broadcast([128, 4, 2, 8])
                        rope_tm(o5[:, :, :, 0, :], o5[:, :, :, 1, :], x5[:, :, :, 0, :], x5[:, :, :, 1, :], c_, s_, rp[b2],
                                [128, 4, 2, 8])
                    else:
                        C.cp('dve', a_[:, :, 64:96], q3[:, :, 64:96])
                    t3 = tmp[b2][:].rearrange("p (h e) -> p h e", e=96)
                    C.tt('pool', t3, a_[:, :, 0:96], a_[:, :, 0:96], ALU.mult)
                    C.red('dve', s4[:, 4:8], t3, ALU.add)
                    C.rpow(s4[:, 8:12], s4[:, 4:8], 0.5)
                    C.ts('dve', a_[:, :, 96:97], s4[:, 8:12].unsqueeze(2), kmaxb[:, 0:1], None, ALU.mult)
                    for h in range(4):
                        C.tr(pT[:, h * 128:(h + 1) * 128], a_[:, h, :], ident_b[:])
                    C.cp('act', Qg[:, :, j * 128:(j + 1) * 128], pT[:, 0:512].rearrange("p (h t) -> p h t", t=128))
                nq = nt * 128
                if mstop <= 3:
                    continue
                for h in range(4):
                    po = pO[h % 3]
                    for ki, kt in enumerate(ktiles):
                        ps_ = pS[nps % 2]
                        pt_ = PT[nps % 3]
                        nps += 1
                        C.mm(ps_[:, 0:nq], KaT[0:97, h, kt * 128:(kt + 1) * 128], Qg[0:97, h, 0:nq], rk=[(KaT, kt), Qg])
                        C.act(pt_[:, 0:nq], ps_[:, 0:nq], AF.Exp, scale=SCALE)
                        for j in range(nt):
                            C.mm(po[:, j * 65:(j + 1) * 65], pt_[:, j * 128:(j + 1) * 128], Va[:, kt, h, :],
                                 start=(ki == 0 and j == 0), stop=(ki == len(ktiles) - 1), skip_group_check=True,
                                 rk=[pt_, (Va, kt)])
                    r4 = rd_[h % 2]
                    o3 = po[:, 0:nt * 65].rearrange("p (j e) -> p j e", e=65)
                    C.op('dve', [po], [r4], lambda e_, o_=r4[:, 0:nt], i_=o3[:, :, 64]: e_.reciprocal(out=o_, in_=i_))
                    for j in range(nt):
                        C.ts('dve', E['ydg'][j][:, h * 64:(h + 1) * 64], o3[:, j, 0:64], r4[:, j:j + 1], None, ALU.mult)
                for j in range(nt):
                    n = n0 + j
                    C.dma('sp', ycat[n * 128:(n + 1) * 128, 768:1024], E['ydg'][j][:], wk=(ycat, n))


def rwkv_project(E):
    C, l, hT, wbf, zraw = E['C'], E['l'], E['hT'], E['wbf'], E['zraw']
    with C.scope():
        wC = C.sb("wC", [128, 8, 1024], BF16)
        C.dma('sp', wC[:], wbf[('w_in', l)][:, 1536:2560].rearrange("(k p) n -> p k n", p=128), rk=wbf[('w_in', l)])
        pZ = [C.ps(f"pZ{i}", [128, 512], F32) for i in range(2)]
        stg = [C.sb(f"zstg{i}", [128, 512], F32) for i in range(2)]
        groups = [(512 * i, 512) for i in range(8)] + [(4096, 256)]
        i = 0
        for c in range(8):
            for (t0, n) in groups:
                for k in range(8):
                    C.mm(pZ[i % 2][:, 0:n], wC[:, k, c * 128:(c + 1) * 128], hT[:, k, t0:t0 + n], start=(k == 0), stop=(k == 7),
                         rk=[hT, wC])
                C.cp('act', stg[i % 2][:, 0:n], pZ[i % 2][:, 0:n])
                C.dma('sp' if i % 2 == 0 else 'pool', zraw[c, :, t0:t0 + n], stg[i % 2][:, 0:n], wk=(zraw, c))
                i += 1


def rwkv_main(E):
    C, l, W, K, ycat, zraw, emit, ident_b = E['C'], E['l'], E['W'], E['K'], E['ycat'], E['zraw'], E['emit'], E['ident_b']
    LAM = -math.exp(-0.5)
    WP = T + 4
    with C.scope():
        rkT = C.sb("rkT", [128, 4, T], BF16)
        twz = C.sb("twz", [128, T], BF16)
        sgd = C.sb("sgd", [128, T], BF16)
        vtok = C.sb("rw_vtok", [128, 2, NT, 128], BF16)
        muT = C.sb("muT", [128, 8], F32)
        pp = C.sb("rw_pp", [128, 4, 2], F32)
        w0T = C.sb("rw_w0", [128, 2, 2], F32)
        a0T = C.sb("rw_a0", [128, 2, 2], F32)
        items = [(muT[:], W['rwkv_mu'][l]), (pp[:, 0, :], W['rwkv_kk'][l]), (pp[:, 1, :], W['rwkv_ka'][l]),
                 (pp[:, 3, :], W['rwkv_rk'][l].rearrange("h k -> (h k)"))]
        for d in range(2):
            items += [(w0T[:, d, :], W['rwkv_w0'][l][d]), (a0T[:, d, :], W['rwkv_a0'][l][d])]
        load_cols(C, E['ident_f'], items)
        C.ts('dve', pp[:, 2, :], pp[:, 1, :], -1.0, 1.0, ALU.mult, ALU.add)
        lstage = C.sb("rw_lst", [128, 2, 2, 256], F32)
        C.memset('pool', lstage[:], 0.0)
        wupz = C.sb("wupz", [128, 2, 256], BF16)
        aupz = C.sb("aupz", [128, 2, 256], BF16)
        for d in range(2):
            C.dma('sp', lstage[0:64, 0, d, :], W['rwkv_w_up'][l][d])
            C.dma('sp', lstage[64:128, 1, d, :], W['rwkv_a_up'][l][d])
        C.cp('dve', wupz[:], lstage[:, 0, :, :])
        C.cp('dve', aupz[:], lstage[:, 1, :, :])
        gst = C.sb("rw_gst", [128, 256], F32)
        gup = C.sb("rw_gup", [128, 256], BF16)
        C.dma('sp', gst[:], W['rwkv_g_up'][l])
        C.cp('dve', gup[:], gst[:])
        gng = C.sb("rw_gng", [128, 256], F32)
        C.dma('pool', gng[:], W['rwkv_gn_g'][l].partition_broadcast(128))
        resetm = C.sb("rw_resetm", [128, 512], F32)
        C.dma('pool', resetm[:], K['rw_resetm'][:, :])
        blk = C.sb("rw_blk", [128, 128], BF16)
        C.dma('pool', blk[:], K['rw_blk'][:, :])
        blk2 = C.sb("rw_blk2", [128, 2], BF16)
        C.dma('pool', blk2[:], K['rw_blk2'][:, :])
        mask4 = C.sb("rw_mask4", [128, 2, 512], F32)
        masknt = C.sb("rw_masknt", [128, 2, 256], F32)
        for d in range(2):
            C.dma('pool', mask4[:, d, :], K['rw_mask4'][d])
            C.dma('pool', masknt[:, d, :], K['rw_masknt'][d])
        negh = C.sb("rw_negh", [128, 512], F32)
        C.memset('pool', negh[:], -0.5)
        nh2 = C.sb("rw_nh2", [128, 2], F32)
        C.memset('pool', nh2[:], -0.5)

        with C.scope():
            zr = [C.sb(f"zr{i}", [128, WP], F32) for i in range(2)]
            tA = [C.sb(f"tA{i}", [128, WP], F32) for i in range(2)]
            vb = C.sb("rw_vb", [128, T], BF16)
            pT = C.ps("rw_pT", [128, 1024], BF16)
            for i, c in enumerate([6, 7, 4, 5, 0, 1, 2, 3]):
                z_, t_ = zr[i % 2], tA[i % 2]
                C.memset('pool', z_[:, 0:1], 0.0)
                C.memset('pool', z_[:, 257:259], 0.0)
                C.memset('pool', z_[:, WP - 1:WP], 0.0)
                C.dma('sp', z_[:, 1:257], zraw[c, :, 0:LC], rk=(zraw, c))
                C.dma('pool', z_[:, 259:259 + L], zraw[c, :, LC:T], rk=(zraw, c))
                mid = slice(1, WP - 1)
                C.tt('pool', t_[:, mid], z_[:, 0:WP - 2], z_[:, 2:WP], ALU.add)
                C.stt(t_[:, mid], t_[:, mid], 0.5, z_[:, mid], ALU.mult, ALU.subtract)
                C.stt(t_[:, mid], t_[:, mid], muT[:, c:c + 1], z_[:, mid], ALU.mult, ALU.add)
                segs = ((slice(1, 257), slice(0, LC)), (slice(259, 259 + L), slice(LC, T)))
                for (src, dst) in segs:
                    if c == 6:
                        C.act(twz[0:64, dst], t_[0:64, src], AF.Tanh)
                        C.cp('act', twz[64:128, dst], t_[64:128, src])
                    elif c == 7:
                        C.act(sgd[:, dst], t_[:, src], AF.Sigmoid)
                    elif c in (4, 5):
                        C.cp('act', vb[:, dst], t_[:, src])
                    else:
                        C.cp('act', rkT[:, c, dst], t_[:, src])
                if c in (4, 5):
                    for n0 in range(0, NT, 8):
                        nn = min(8, NT - n0)
                        for j in range(nn):
                            C.tr(pT[:, j * 128:(j + 1) * 128], vb[:, (n0 + j) * 128:(n0 + j + 1) * 128], ident_b[:])
                        C.cp('act', vtok[:, c - 4, n0:n0 + nn, :], pT[:, 0:nn * 128].rearrange("p (j f) -> p j f", f=128))

        import os
        rstop = int(os.environ.get('RW_STOP', '99'))
        if rstop <= 0:
            return
        Oacc = [C.sb(f"rw_O{d}", [128, NT, 128], F32) for d in range(2)]
        bsum = [C.sb(f"rw_bs{d}", [128, NT, 2], F32) for d in range(2)]
        ktI = [C.sb(f"ktI{i}", [128, 512], BF16) for i in range(2)]
        bI = [C.sb(f"bI{i}", [128, 512], BF16) for i in range(2)]
        KR = [[C.sb(f"KR{i}{hh}", [128, 4, 2, 128], BF16) for hh in range(2)] for i in range(2)]
        for i in range(2):
            for hh in range(2):
                C.memset('pool', KR[i][hh][:], 0.0)
        KBg = [C.sb(f"KBg{i}", [128, 4, 2, 128], BF16) for i in range(2)]
        gam = [C.sb(f"gam{i}", [128, 4], F32) for i in range(2)]
        Mst = C.sb("rw_M", [128, 64], F32)
        Mb = [C.sb(f"rw_Mb{i}", [128, 64], BF16) for i in range(2)]
        f32t = {nm: C.sb("rwt_" + nm, [128, 512], F32) for nm in
                ('sw', 'a', 'ci', 'ei', 'ee', 'e1', 'e2', 'e3', 'kt', 'b', 'tK', 'tB')}
        f32t.update(kk=f32t['sw'], rs=f32t['ci'], kh=f32t['ei'], tk=f32t['ee'])
        b16t = {nm: C.sb("rwb_" + nm, [128, 512], BF16) for nm in ('ksq', 'kgT', 'bgT', 'prod')}
        Am = [[C.sb(f"Am{i}{hh}", [128, 4, 128], BF16) for hh in range(2)] for i in range(3)]
        XXi = [C.sb(f"XXi{i}", [128, 2, 128], BF16) for i in range(3)]
        XX = [[C.sb(f"XX{i}{j}", [128, 2, 2, 128], BF16) for j in range(2)] for i in range(3)]
        Pb = [[C.sb(f"Pb{i}{j}", [128, 2, 128], BF16) for j in range(2)] for i in range(3)]
        Xs = [C.sb(f"rwXs{i}", [128, 128], BF16) for i in range(2)]
        Un = [C.sb(f"rwUn{i}", [128, 128], BF16) for i in range(2)]
        fin = {nm: C.sb("rwf_" + nm, [128, 128], F32) for nm in ('o', 'osq', 'y', 'yo')}
        fst = C.sb("rwf_st", [128, 8, 2], F32)
        pL = C.ps("rw_pL", [128, 512], F32)
        pA = [C.ps(f"rw_pA{hh}", [128, 512], F32) for hh in range(2)]
        pI1 = C.ps("rw_pI1", [128, 512], F32)
        pI2 = C.ps("rw_pI2", [128, 512], F32)
        pTb = C.ps("rw_pTb", [128, 1024], BF16)
        pR1 = C.ps("rw_pR1", [128, 512], F32)
        pR2 = C.ps("rw_pR2", [128, 512], F32)

        xgroups = [(NCT + 4 * i, 4) for i in range(8)]
        for fc in range(2):
            for d in range(2):
                glist = [(0, NCT)] + (xgroups if d == 0 else xgroups[::-1])
                C.memset('dve', Mst[:], 0.0)
                C.memset('pool', Mb[0][:], 0.0)
                def do_prep(gi, c0, ncn):
                    pb = gi % 2
                    t0, n = c0 * 128, ncn * 128
                    t = f32t
                    rT_ = rkT[:, fc, t0:t0 + n]
                    kT_ = rkT[:, 2 + fc, t0:t0 + n]
                    sl = slice(0, n)
                    C.mm(pL[:, sl], wupz[:, d, fc * 128:(fc + 1) * 128], twz[:, t0:t0 + n])
                    C.act(t['sw'][:, sl], pL[:, sl], AF.Sigmoid, bias=w0T[:, d, fc:fc + 1])
                    C.mm(pL[:, sl], aupz[:, d, fc * 128:(fc + 1) * 128], twz[:, t0:t0 + n])
                    C.act(t['a'][:, sl], pL[:, sl], AF.Sigmoid, bias=a0T[:, d, fc:fc + 1])
                    C.op('dve', [resetm, t['sw']], [t['ci']],
                         lambda e_, o_=t['ci'][:, sl], a_=resetm[:, sl], b_=t['sw'][:, sl]:
                         e_.tensor_tensor_scan(out=o_, data0=a_, data1=b_, initial=0.0, op0=ALU.mult, op1=ALU.add))
                    ci3 = t['ci'][:, sl].rearrange("p (c t) -> p c t", t=128)
                    tot = ci3[:, :, 127:128]
                    if d == 1:
                        ei3 = t['ei'][:, sl].rearrange("p (c t) -> p c t", t=128)
                        C.tt('dve', ei3, tot.to_broadcast([128, ncn, 128]), ci3, ALU.subtract)
                        C.tt('dve', t['ei'][:, sl], t['ei'][:, sl], t['sw'][:, sl], ALU.add)
                        ei = t['ei']
                    else:
                        ei = t['ci']
                    C.tt('pool', t['ee'][:, sl], ei[:, sl], t['sw'][:, sl], ALU.subtract)
                    C.act(t['e1'][:, sl], t['ee'][:, sl], AF.Exp, scale=LAM)
                    C.act(t['e2'][:, sl], ei[:, sl], AF.Exp, scale=-LAM)
                    C.act(t['e3'][:, sl], ei[:, sl], AF.Exp, scale=LAM)
                    C.act(gam[pb][:, 0:ncn], ci3[:, :, 127], AF.Exp, scale=LAM)
                    C.ts('dve', t['kk'][:, sl], kT_, pp[:, 0, fc:fc + 1], None, ALU.mult)
                    C.tt('pool', b16t['ksq'][:, sl], t['kk'][:, sl], t['kk'][:, sl], ALU.mult)
                    C.mm(pL[:, sl], blk[:], b16t['ksq'][:, sl])
                    C.cp('act', t['rs'][:, sl], pL[:, sl])
                    C.ts('dve', t['rs'][:, sl], t['rs'][:, sl], 1e-12, None, ALU.max)
                    C.rpow(t['rs'][:, sl], t['rs'][:, sl], -0.5)
                    C.tt('dve', t['kh'][:, sl], t['kk'][:, sl], t['rs'][:, sl], ALU.mult)
                    C.ts('dve', t['tk'][:, sl], t['a'][:, sl], pp[:, 1, fc:fc + 1], pp[:, 2, fc:fc + 1], ALU.mult, ALU.add)
                    C.tt('pool', t['kt'][:, sl], t['tk'][:, sl], kT_, ALU.mult)
                    C.tt('pool', t['b'][:, sl], t['kh'][:, sl], t['a'][:, sl], ALU.mult)
                    for hh in range(2):
                        hs = slice(hh * 64, (hh + 1) * 64)
                        C.tt('dve', KR[pb][hh][hs, 0:ncn, 0, :], t['kh'][hs, sl].rearrange("p (c t) -> p c t", t=128),
                             t['e1'][hs, sl].rearrange("p (c t) -> p c t", t=128), ALU.mult, wk=[KR[pb][hh]])
                        C.tt('pool', KR[pb][hh][hs, 0:ncn, 1, :], rT_[hs, :].rearrange("p (c t) -> p c t", t=128),
                             t['e3'][hs, sl].rearrange("p (c t) -> p c t", t=128), ALU.mult, wk=[KR[pb][hh]])
                    C.tt('dve', t['tK'][:, sl], t['kt'][:, sl], t['e2'][:, sl], ALU.mult)
                    C.tt('pool', t['tB'][:, sl], t['b'][:, sl], t['e2'][:, sl], ALU.mult)
                    C.cp('act', ktI[pb][:, sl], t['tK'][:, sl])
                    C.cp('act', bI[pb][:, sl], t['tB'][:, sl])
                    gb = gam[pb][:, 0:ncn].unsqueeze(2).to_broadcast([128, ncn, 128])
                    C.tt('dve', b16t['kgT'][:, sl].rearrange("p (c t) -> p c t", t=128),
                         t['tK'][:, sl].rearrange("p (c t) -> p c t", t=128), gb, ALU.mult)
                    C.tt('pool', b16t['bgT'][:, sl].rearrange("p (c t) -> p c t", t=128),
                         t['tB'][:, sl].rearrange("p (c t) -> p c t", t=128), gb, ALU.mult)
                    for j in range(ncn):
                        C.tr(pTb[:, (2 * j) * 128:(2 * j + 1) * 128], b16t['kgT'][:, j * 128:(j + 1) * 128], ident_b[:])
                        C.tr(pTb[:, (2 * j + 1) * 128:(2 * j + 2) * 128], b16t['bgT'][:, j * 128:(j + 1) * 128], ident_b[:])
                    C.cp('act', KBg[pb][:, 0:ncn, :, :].rearrange("p c k f -> p (c k f)"), pTb[:, 0:ncn * 256])
                    C.stt(b16t['prod'][:, sl], rT_, pp[:, 3, fc:fc + 1], t['kt'][:, sl], ALU.mult, ALU.mult)
                    for j in range(ncn):
                        C.mm(pL[:, 2 * j:2 * j + 2], b16t['prod'][:, j * 128:(j + 1) * 128], blk2[:])
                    C.cp('act', bsum[d][:, c0:c0 + ncn, :].rearrange("p c k -> p (c k)"), pL[:, 0:2 * ncn])


                def gen_inv(idx, n_, gi, c0):
                    pb = gi % 2
                    j = n_ - c0
                    q3 = idx % 3
                    cs = slice(j * 128, (j + 1) * 128)
                    if True:
                        for hh in range(2):
                            kr = KR[pb][hh][:, j, :, :].rearrange("p a t -> p (a t)")
                            C.mm(pA[hh][:, 0:256], ktI[pb][:, cs], kr, rk=[ktI[pb], KR[pb][hh]])
                            C.mm(pA[hh][:, 256:512], bI[pb][:, cs], kr, rk=[bI[pb], KR[pb][hh]])
                            C.mm(pI2[:, 256 + hh * 128:256 + (hh + 1) * 128], KR[pb][hh][:, j, 0, :], bI[pb][:, cs], rk=[bI[pb], KR[pb][hh]])
                            C.tt('dve', Am[q3][hh][:].rearrange("p a t -> p (a t)"), pA[hh][:, :], mask4[:, d, :], ALU.mult)
                        C.tt('dve', XXi[q3][:].rearrange("p a t -> p (a t)"), pI2[:, 256:512], masknt[:, d, :], ALU.mult)
                        for hh in range(2):
                            C.tt('pool', Pb[q3][0][:, hh, :], Am[q3][hh][:, 2, :], ident_b[:], ALU.add, wk=[Pb[q3][0]])
                        yield
                        Xc = [Am[q3][0][:, 2, :], Am[q3][1][:, 2, :]]
                        Xk = [Am[q3][0], Am[q3][1]]
                        XTc = [XXi[q3][:, 0, :], XXi[q3][:, 1, :]]
                        XTk = [XXi[q3], XXi[q3]]
                        Pc = Pb[q3][0]
                        for lvl in range(6):
                            lastl = (lvl == 5)
                            nxt = XX[q3][lvl % 2]
                            for hh in range(2):
                                if not lastl:
                                    C.mm(pI1[:, hh * 128:(hh + 1) * 128], XTc[hh], Xc[hh], rk=[XTk[hh], Xk[hh]])
                                C.mm(pI1[:, 256 + hh * 128:256 + (hh + 1) * 128], Xc[hh], XTc[hh], rk=[XTk[hh], Xk[hh]])
                            if lastl:
                                C.cp('act', nxt[:, 1, :, :].rearrange("p h t -> p (h t)"), pI1[:, 256:512])
                            else:
                                C.cp('act', nxt[:].rearrange("p a h t -> p (a h t)"), pI1[:, :])
                            yield
                            for hh in range(2):
                                C.mm(pI2[:, hh * 128:(hh + 1) * 128], nxt[:, 1, hh, :], Pc[:, hh, :], rk=[nxt, Pc])
                            Pn = Pb[q3][(lvl + 1) % 2]
                            C.tt('dve', Pn[:].rearrange("p h t -> p (h t)"), pI2[:, 0:256], Pc[:].rearrange("p h t -> p (h t)"), ALU.add)
                            yield
                            Xc = [nxt[:, 0, 0, :], nxt[:, 0, 1, :]]
                            XTc = [nxt[:, 1, 0, :], nxt[:, 1, 1, :]]
                            Xk = [nxt, nxt]
                            XTk = [nxt, nxt]
                            Pc = Pn
                def gen_rec(idx, n_, gi, c0):
                    pb = gi % 2
                    j = n_ - c0
                    q3 = idx % 3
                    q2 = idx % 2
                    if True:
                        TT = Pb[q3][0]
                        cur = stt_['cur']
                        Mold = Mb[cur]
                        V = vtok[:, fc, n_, :]
                        for hh in range(2):
                            vs = slice(hh * 64, (hh + 1) * 64)
                            C.mm(pR1[:, vs], KR[pb][hh][:, j, 0, :], Mold[:], start=True, stop=False, rk=[KR[pb][hh], Mold])
                            C.mm(pR1[:, vs], Am[q3][hh][:, 0, :], V[:, vs], start=False, stop=True, rk=[Am[q3][hh], vtok])
                        C.cp('act', Xs[q2][:], pR1[:, 0:128])
                        yield
                        for hh in range(2):
                            vs = slice(hh * 64, (hh + 1) * 64)
                            C.mm(pR2[:, vs], TT[:, hh, :], Xs[q2][:, vs], rk=[TT, Xs[q2]])
                        C.ts('dve', Un[q2][:], pR2[:, 0:128], -1.0, None, ALU.mult)
                        yield
                        for hh in range(2):
                            vs = slice(hh * 64, (hh + 1) * 64)
                            C.mm(pR2[vs, 128:192], KBg[pb][:, j, 0, vs], V[:, vs], start=True, stop=False, rk=[KBg[pb], vtok])
                            C.mm(pR2[vs, 128:192], KBg[pb][:, j, 1, vs], Un[q2][:, vs], start=False, stop=True, rk=[KBg[pb], Un[q2]])
                        want_o = emit or n_ >= NCT
                        if want_o:
                            for hh in range(2):
                                vs = slice(128 + hh * 64, 128 + (hh + 1) * 64)
                                v2 = slice(hh * 64, (hh + 1) * 64)
                                C.mm(pR1[:, vs], KR[pb][hh][:, j, 1, :], Mold[:], start=True, stop=False, rk=[KR[pb][hh], Mold])
                                C.mm(pR1[:, vs], Am[q3][hh][:, 1, :], V[:, v2], start=False, stop=False, rk=[Am[q3][hh], vtok])
                                C.mm(pR1[:, vs], Am[q3][hh][:, 3, :], Un[q2][:, v2], start=False, stop=True, rk=[Am[q3][hh], Un[q2]])
                            C.cp('act', Oacc[d][:, n_, :], pR1[:, 128:256], wk=[(Oacc[d], n_)])
                        C.stt(Mst[:], Mst[:], gam[pb][:, j:j + 1], pR2[:, 128:192], ALU.mult, ALU.add)
                        C.cp('act', Mb[1 - cur][:], Mst[:])
                        stt_['cur'] = 1 - cur
                        yield

                seq = []
                for gi, (c0, ncn) in enumerate(glist):
                    chunks = list(range(c0, c0 + ncn)) if d == 0 else list(range(c0 + ncn - 1, c0 - 1, -1))
                    for n_ in chunks:
                        seq.append((n_, gi, c0, ncn))
                stt_ = {'cur': 0}
                prep_done = set()
                inv_started, inv_done, rec_i, rec_gen, active = 0, set(), 0, None, []
                while rec_i < len(seq):
                    while len(active) < 2 and inv_started < len(seq) and inv_started < rec_i + 3:
                        n_, gi, c0, ncn = seq[inv_started]
                        if gi not in prep_done:
                            do_prep(gi, c0, ncn)
                            prep_done.add(gi)
                        active.append((inv_started, gen_inv(inv_started, n_, gi, c0)))
                        inv_started += 1
                    for item in list(active):
                        try:
                            next(item[1])
                        except StopIteration:
                            inv_done.add(item[0])
                            active.remove(item)
                    if rec_gen is None and rec_i in inv_done:
                        n_, gi, c0, ncn = seq[rec_i]
                        rec_gen = gen_rec(rec_i, n_, gi, c0)
                    if rec_gen is not None:
                        try:
                            next(rec_gen)
                        except StopIteration:
                            rec_gen = None
                            rec_i += 1
            for n_ in range(NT):
                if not (emit or n_ >= NCT) or rstop <= 3:
                    continue
                f = fin
                C.tt('pool', f['o'][:], Oacc[0][:, n_, :], Oacc[1][:, n_, :], ALU.add, rk=[(Oacc[0], n_), (Oacc[1], n_)])
                C.tt('pool', f['osq'][:], f['o'][:], f['o'][:], ALU.mult)
                o3 = f['o'][:].rearrange("p (h e) -> p h e", e=64)
                C.red('dve', fst[:, 0, :], o3, ALU.add)
                C.red('dve', fst[:, 1, :], f['osq'][:].rearrange("p (h e) -> p h e", e=64), ALU.add)
                C.ts('pool', fst[:, 2, :], fst[:, 0, :], 1.0 / 64, None, ALU.mult)
                C.tt('pool', fst[:, 3, :], fst[:, 2, :], fst[:, 2, :], ALU.mult)
                C.ts('pool', fst[:, 4, :], fst[:, 1, :], 1.0 / 64, 64e-5, ALU.mult, ALU.add)
                C.tt('pool', fst[:, 4, :], fst[:, 4, :], fst[:, 3, :], ALU.subtract)
                C.rpow(fst[:, 5, :], fst[:, 4, :], -0.5)
                C.tt('pool', fst[:, 6, :], bsum[0][:, n_, :], bsum[1][:, n_, :], ALU.add)
                y3 = f['y'][:].rearrange("p (h e) -> p h e", e=64)
                C.tt('dve', y3, o3, fst[:, 2, :].unsqueeze(2).to_broadcast([128, 2, 64]), ALU.subtract)
                C.tt('dve', y3, y3, fst[:, 5, :].unsqueeze(2).to_broadcast([128, 2, 64]), ALU.mult)
                C.tt('pool', f['y'][:], f['y'][:], gng[:, fc * 128:(fc + 1) * 128], ALU.mult)
                for hh in range(2):
                    vs = slice(hh * 64, (hh + 1) * 64)
                    C.stt(f['y'][:, vs], vtok[:, fc, n_, vs], fst[:, 6, hh:hh + 1], f['y'][:, vs], ALU.mult, ALU.add)
                C.mm(pR2[:, 0:128], sgd[:, n_ * 128:(n_ + 1) * 128], gup[:, fc * 128:(fc + 1) * 128])
                C.tt('dve', f['yo'][:], f['y'][:], pR2[:, 0:128], ALU.mult)
                C.dma('sp', ycat[n_ * 128:(n_ + 1) * 128, 512 + fc * 128:512 + (fc + 1) * 128], f['yo'][:], wk=(ycat, n_))


def MIXERS(env):
    E = dict(env)
    E['emit'] = env['l'] < DEPTH - 1
    which = env['debug_mixers'] if env.get('debug_mixers') else ('swa', 'ret', 'rwkv', 'mla')
    if 'swa' in which:
        mixer_swa(E)
    if 'ret' in which:
        mixer_ret(E)
    if 'rwkv' in which:
        rwkv_project(E)
    if 'mla' in which:
        with E['C'].scope():
            E['ydg'] = [E['C'].sb(f"ydg{j}", [128, 256], F32) for j in range(4)]
            mixer_mla(E)


_PROG = {}


def kernel(**inputs):
    if 'nc' not in _PROG:
        _PROG['nc'] = build_program()[0]
    nc = _PROG['nc']
    consts = make_consts()
    f32 = lambda a: np.ascontiguousarray(np.asarray(a, dtype=np.float32))
    shared = {k: f32(v) for k, v in inputs.items() if k not in ('x', 'c', 'ctx')}
    shared.update({'k_' + k: v for k, v in consts.items()})
    x, c, ctx = f32(inputs['x']), f32(inputs['c']), f32(inputs['ctx'])
    in_maps = []
    for b in range(8):
        m = dict(shared)
        m['x'] = x[b]
        m['c'] = c[b]
        m['ctx'] = ctx[b]
        in_maps.append(m)
    res = run_bass_kernel_spmd(nc, in_maps, core_ids=list(range(8)))
    return np.stack([np.asarray(r['out'], dtype=np.float32) for r in res.results], axis=0)
```

```python
import contextlib
import math
import numpy as np
import ml_dtypes
import concourse.bass as bass
import concourse.mybir as mybir
from concourse.bass_utils import run_bass_kernel_spmd

F32 = mybir.dt.float32
BF16 = mybir.dt.bfloat16
AF = mybir.ActivationFunctionType
ALU = mybir.AluOpType
AX = mybir.AxisListType

D = 1024
L = 4096
LC = 256
T = L + LC
NT = T // 128
NCT = LC // 128
DEPTH = 2
N_IN = 2976
DFF = 4096
EPS = 1e-6
NDMA = 6


class Ctx:
    ENG = ('pe', 'act', 'dve', 'pool', 'sp')

    def __init__(self, nc):
        self.nc = nc
        self.es = contextlib.ExitStack()
        self.eng = dict(pe=nc.tensor, act=nc.scalar, dve=nc.vector, pool=nc.gpsimd, sp=nc.sync)
        self.semh = {}
        self.cnt = {}
        for e in self.ENG:
            self.semh[e] = self.es.enter_context(nc.semaphore("sem_" + e))
            self.cnt[e] = 0
        self.known = {e: {} for e in self.ENG}
        self.dq = {}
        for q in ('sp', 'act', 'pool'):
            sems = []
            for i in range(NDMA):
                name = f"dma_{q}_{i}"
                self.semh[name] = self.es.enter_context(nc.semaphore(name))
                sems.append(name)
            self.dq[q] = dict(sems=sems, n=0)
        self.lastw = {}
        self.rd = {}
        self.subs = {}
        self.ninst = 0
        self.psum_names = set()
        self.bank_last = {}

    def sb(self, name, shape, dt=F32):
        self.uid = getattr(self, 'uid', 0) + 1
        return self.es.enter_context(self.nc.sbuf_tensor(f"{name}_{self.uid}", list(shape), dt))

    def ps(self, name, shape, dt=F32):
        self.uid = getattr(self, 'uid', 0) + 1
        nbytes = int(np.prod(shape[1:])) * (2 if dt == BF16 else 4)
        assert nbytes == 2048, "PSUM tensors must be exactly one bank (collision tracking is per tensor)"
        self.psum_names.add(f"{name}_{self.uid}")
        return self.es.enter_context(self.nc.psum_tensor(f"{name}_{self.uid}", list(shape), dt))

    @staticmethod
    def _key(x):
        if isinstance(x, tuple):
            a, sub = x
        else:
            a, sub = x, None
        name = a if isinstance(a, str) else getattr(a, 'tensor', a).name
        return (name, sub)

    def _dep_keys(self, key):
        name, sub = key
        if sub is None:
            return [(name, None)] + [(name, s) for s in self.subs.get(name, ())]
        return [(name, None), (name, sub)]

    def _wait(self, e, src, val):
        if self.known[e].get(src, 0) >= val:
            return
        self.eng[e].wait_ge(self.semh[src], val)
        self.known[e][src] = val

    def _sync(self, e, reads, writes):
        deps = {}

        def add(st):
            if st is not None:
                deps[st[0]] = max(deps.get(st[0], 0), st[1])
        same = 0
        for r in reads:
            for k in self._dep_keys(self._key(r)):
                st = self.lastw.get(k)
                add(st)
                if st is not None and st[0] == e:
                    same = max(same, st[1])
        for w in writes:
            for k in self._dep_keys(self._key(w)):
                st = self.lastw.get(k)
                add(st)
                if st is not None and st[0] == e:
                    same = max(same, st[1])
                for src, val in self.rd.get(k, {}).items():
                    add((src, val))
                    if src == e:
                        same = max(same, val)
        for src, val in deps.items():
            if src == e:
                continue
            self._wait(e, src, val)
        if same and e != 'pe':
            self._wait(e, e, same)

    def _record(self, reads, writes, stamp):
        for r in reads:
            k = self._key(r)
            d = self.rd.setdefault(k, {})
            d[stamp[0]] = max(d.get(stamp[0], 0), stamp[1])
        for w in writes:
            k = self._key(w)
            name, sub = k
            self.lastw[k] = stamp
            self.rd[k] = {}
            if sub is None:
                for s in self.subs.get(name, ()):
                    self.lastw[(name, s)] = stamp
                    self.rd[(name, s)] = {}
            else:
                self.subs.setdefault(name, set()).add(sub)

    def op(self, e, reads, writes, fn, rk=None, wk=None):
        if rk is not None:
            reads = list(rk)
        if wk is not None:
            writes = list(wk)
        self._sync(e, reads, writes)
        banks = set()
        for x in list(reads) + list(writes):
            nm = self._key(x)[0]
            if nm in self.psum_names:
                banks.add(nm)
        for nm in banks:
            for src, val in self.bank_last.get(nm, {}).items():
                if src != e:
                    self._wait(e, src, val)
        ins = fn(self.eng[e])
        self.cnt[e] += 1
        ins.then_inc(self.semh[e], 1)
        self._record(reads, writes, (e, self.cnt[e]))
        for nm in banks:
            self.bank_last.setdefault(nm, {})[e] = self.cnt[e]
        self.ninst += 1
        return ins

    def dma(self, q, out, in_, rk=None, wk=None, **kw):
        d = self.dq[q]
        i = d['n']
        src = d['sems'][i % NDMA]
        if i >= NDMA:
            self._wait(q, src, 16 * (i // NDMA))
        reads = rk if isinstance(rk, list) else [rk if rk is not None else in_]
        writes = wk if isinstance(wk, list) else [wk if wk is not None else out]
        self._sync(q, reads, writes)
        self.eng[q].dma_start(out=out, in_=in_, **kw).then_inc(self.semh[src], 16)
        d['n'] += 1
        self._record(reads, writes, (src, 16 * (i // NDMA + 1)))
        self.ninst += 1

    def barrier(self):
        targets = {}
        for q, d in self.dq.items():
            for j, src in enumerate(d['sems']):
                n = (d['n'] - j + NDMA - 1) // NDMA
                if n > 0:
                    targets[src] = 16 * n
        for e in self.ENG:
            if self.cnt[e] > 0:
                targets[e] = self.cnt[e]
        for e in self.ENG:
            for src, val in targets.items():
                self._wait(e, src, val)

    @contextlib.contextmanager
    def scope(self):
        outer = self.es
        with contextlib.ExitStack() as es:
            self.es = es
            try:
                yield
            finally:
                self.barrier()
                self.es = outer

    def finish(self):
        for q, d in self.dq.items():
            for j, src in enumerate(d['sems']):
                n = (d['n'] - j + NDMA - 1) // NDMA
                if n > 0:
                    self._wait('sp', src, 16 * n)
        for e in self.ENG:
            if e != 'sp' and self.cnt[e] > 0:
                self._wait('sp', e, self.cnt[e])

    def mm(self, out, lhsT, rhs, start=True, stop=True, r=(), w=(), rk=None, wk=None, **kw):
        return self.op('pe', [lhsT, rhs] + list(r), [out] + list(w),
                       lambda e: e.matmul(out, lhsT=lhsT, rhs=rhs, start=start, stop=stop, **kw), rk=rk, wk=wk)

    def tr(self, out, in_, ident, r=(), w=(), rk=None, wk=None):
        return self.op('pe', [in_, ident] + list(r), [out] + list(w),
                       lambda e: e.transpose(out, in_, ident), rk=rk, wk=wk)

    def act(self, out, in_, func, bias=None, scale=None, accum_out=None, r=(), w=(), rk=None, wk=None):
        kw = {}
        reads = [in_] + list(r)
        writes = [out] + list(w)
        if bias is not None:
            kw['bias'] = bias
            if not isinstance(bias, (int, float)):
                reads.append(bias)
        if scale is not None:
            kw['scale'] = scale
            if not isinstance(scale, (int, float)):
                reads.append(scale)
        if accum_out is not None:
            kw['accum_out'] = accum_out
            writes.append(accum_out)
        return self.op('act', reads, writes, lambda e: e.activation(out=out, in_=in_, func=func, **kw), rk=rk, wk=wk)

    def ts(self, e, out, in0, s1, s2, op0, op1=None, accum_out=None, r=(), w=(), rk=None, wk=None):
        reads = [in0] + list(r)
        writes = [out] + list(w)
        for s in (s1, s2):
            if s is not None and not isinstance(s, (int, float)):
                reads.append(s)
        kw = {}
        if op1 is not None:
            kw['op1'] = op1
        if accum_out is not None:
            kw['accum_out'] = accum_out
            writes.append(accum_out)
        return self.op(e, reads, writes,
                       lambda g: g.tensor_scalar(out=out, in0=in0, scalar1=s1, scalar2=s2, op0=op0, **kw), rk=rk, wk=wk)

    def tt(self, e, out, in0, in1, op, r=(), w=(), rk=None, wk=None):
        return self.op(e, [in0, in1] + list(r), [out] + list(w),
                       lambda g: g.tensor_tensor(out=out, in0=in0, in1=in1, op=op), rk=rk, wk=wk)

    def stt(self, out, in0, scalar, in1, op0, op1, r=(), w=(), rk=None, wk=None):
        reads = [in0, in1] + list(r)
        if not isinstance(scalar, (int, float)):
            reads.append(scalar)
        return self.op('dve', reads, [out] + list(w),
                       lambda g: g.scalar_tensor_tensor(out=out, in0=in0, scalar=scalar, in1=in1, op0=op0, op1=op1), rk=rk, wk=wk)

    def cp(self, e, out, in_, r=(), w=(), rk=None, wk=None):
        if e == 'act':
            return self.act(out, in_, AF.Copy, r=r, w=w, rk=rk, wk=wk)
        return self.op(e, [in_] + list(r), [out] + list(w), lambda g: g.tensor_copy(out=out, in_=in_), rk=rk, wk=wk)

    def rpow(self, out, in_, expo):
        self.act(out, in_, AF.Ln)
        self.act(out, out, AF.Exp, scale=float(expo))

    def memset(self, e, out, val, w=()):
        return self.op(e, [], [out] + list(w), lambda g: g.memset(out, val))

    def red(self, e, out, in_, op, axis=AX.X, r=(), w=()):
        return self.op(e, [in_] + list(r), [out] + list(w),
                       lambda g: g.tensor_reduce(out=out, in_=in_, axis=axis, op=op))


def load_cols(C, ident_f, items):
    with C.scope():
        st = C.sb("ldst", [128, 128], F32)
        ps = C.ps("ldps", [128, 512], F32)
        r = 0
        plan = []
        for dst, src in items:
            n = src.shape[0] // 128
            C.dma('sp', st[r:r + n, :], src.rearrange("(k p) -> k p", p=128))
            plan.append((dst, r, n))
            r += n
        assert r <= 128
        C.tr(ps[:, 0:r], st[0:r, :], ident_f[0:r, 0:r])
        for dst, r0, n in plan:
            C.cp('dve', dst, ps[:, r0:r0 + n])

def _rope_tables(n_tokens, d_rot, grid_w=64, base=10000.0):
    n_rows = n_tokens // grid_w
    row, col = np.meshgrid(np.arange(n_rows, dtype=np.float32), np.arange(grid_w, dtype=np.float32), indexing='ij')
    d_ax = d_rot // 2
    inv = (np.float32(base) ** (-np.arange(0, d_ax, 2, dtype=np.float32) / np.float32(d_ax))).astype(np.float32)
    ang = np.stack([row.reshape(-1)[:, None] * inv, col.reshape(-1)[:, None] * inv], axis=1).astype(np.float32)
    return np.cos(ang).astype(np.float32), np.sin(ang).astype(np.float32)


def make_consts():
    c = {}
    c['ident_f'] = np.eye(128, dtype=np.float32)
    c['ident_b'] = np.eye(128, dtype=np.float32).astype(ml_dtypes.bfloat16)
    lg = np.log1p(-np.exp2(-5.0 - np.arange(4, dtype=np.float32))).astype(np.float32)
    i = np.arange(128, dtype=np.float32)
    qft = np.zeros((128, 2, 128), np.float32)
    qbt = np.zeros((128, 2, 128), np.float32)
    gcc = np.zeros((128, 2), np.float32)
    for cc in range(2):
        for hh in range(2):
            h = 2 * cc + hh
            qft[hh * 64:(hh + 1) * 64, cc, :] = np.exp((i + 1.0) * lg[h])[None, :]
            qbt[hh * 64:(hh + 1) * 64, cc, :] = np.exp((128.0 - i) * lg[h])[None, :]
            gcc[hh * 64:(hh + 1) * 64, cc] = np.exp(np.float32(128.0) * lg[h])
    cos_a, sin_a = _rope_tables(L, 64)
    ct = np.zeros((128, L), np.float32)
    st = np.zeros((128, L), np.float32)
    pm = np.zeros((128, 128), np.float32)
    for blk in range(2):
        for ax in range(2):
            for half in range(2):
                for f in range(16):
                    p = blk * 64 + ax * 32 + half * 16 + f
                    ct[p, :] = cos_a[:, ax, f]
                    st[p, :] = sin_a[:, ax, f]
                    if half == 0:
                        pm[blk * 64 + ax * 32 + 16 + f, p] = -1.0
                    else:
                        pm[blk * 64 + ax * 32 + f, p] = 1.0
    c['swa_ct'] = ct
    c['swa_st'] = st
    c['swa_pm'] = pm.astype(ml_dtypes.bfloat16)
    qi = np.arange(128)[:, None]
    kj = np.arange(128)[None, :]
    NEG = np.float32(-30000.0)
    mk = np.zeros((128, 384), np.float32)
    mk[:, 0:128] = np.where(kj >= qi, 0.0, NEG)
    mk[:, 256:384] = np.where(kj <= qi, 0.0, NEG)
    c['swa_mask'] = mk
    rm = np.ones((128, 512), np.float32)
    rm[:, 0::128] = 0.0
    c['rw_resetm'] = rm
    blk = np.zeros((128, 128), np.float32)
    blk[0:64, 0:64] = 1.0
    blk[64:128, 64:128] = 1.0
    c['rw_blk'] = blk.astype(ml_dtypes.bfloat16)
    blk2 = np.zeros((128, 2), np.float32)
    blk2[0:64, 0] = 1.0
    blk2[64:128, 1] = 1.0
    c['rw_blk2'] = blk2.astype(ml_dtypes.bfloat16)
    si = np.arange(128)[:, None]
    ti = np.arange(128)[None, :]
    m4 = np.zeros((2, 128, 4, 128), np.float32)
    mnt = np.zeros((2, 128, 2, 128), np.float32)
    for dd in range(2):
        strict = (si < ti) if dd == 0 else (si > ti)
        incl = (si <= ti) if dd == 0 else (si >= ti)
        m4[dd, :, 0, :] = strict
        m4[dd, :, 1, :] = incl
        m4[dd, :, 2, :] = -1.0 * strict
        m4[dd, :, 3, :] = incl
        mnt[dd, :, 0, :] = -1.0 * strict.T
        mnt[dd, :, 1, :] = -1.0 * strict.T
    c['rw_mask4'] = m4.reshape(2, 128, 512)
    c['rw_masknt'] = mnt.reshape(2, 128, 256)
    cos_d, sin_d = _rope_tables(L, 32)
    c['mla_cos'] = np.ascontiguousarray(cos_d.reshape(L // 128, 128, 16).transpose(1, 0, 2).reshape(128, (L // 128) * 16))
    c['mla_sin'] = np.ascontiguousarray(sin_d.reshape(L // 128, 128, 16).transpose(1, 0, 2).reshape(128, (L // 128) * 16))
    c['ret_qft'] = qft
    c['ret_qbt'] = qbt
    c['ret_gcc'] = gcc
    kf = np.zeros((128, 4, 64), np.float32)
    kb = np.zeros((128, 4, 64), np.float32)
    dt = np.zeros((128, 4, 128), np.float32)
    for h in range(4):
        kf[:, h, :] = (np.exp((127.0 - i) * lg[h]) * 0.125)[:, None]
        kb[:, h, :] = (np.exp(i * lg[h]) * 0.125)[:, None]
        dt[:, h, :] = np.exp(np.abs(i[:, None] - i[None, :]) * lg[h]) * (1.0 + np.eye(128, dtype=np.float32))
    c['ret_kf'] = kf.reshape(128, 256)
    c['ret_kb'] = kb.reshape(128, 256)
    c['ret_dt'] = dt.reshape(128, 512)
    return c


CONST_SPECS = None


def build_program(debug=None, debug_mixers=None):
    nc = bass.Bass("TRN2", target_bir_lowering=False)
    C = Ctx(nc)
    dram_in = {}

    def din(name, shape, dt=F32):
        dram_in[name] = nc.dram_tensor(name, list(shape), dt, kind="ExternalInput").ap()
        return dram_in[name]

    x_in = din('x', [L, D])
    c_in = din('c', [D])
    ctx_in = din('ctx', [LC, D])
    cctx_in = din('c_ctx', [D])
    W = {}
    wshapes = dict(ada_w=[DEPTH, D, 6 * D], ada_b=[DEPTH, 6 * D], pre_mix_g=[DEPTH, D], post_mix_g=[DEPTH, D],
                   pre_mlp_g=[DEPTH, D], post_mlp_g=[DEPTH, D], w_in=[DEPTH, D, N_IN], w_out=[DEPTH, D, D],
                   swa_sink=[DEPTH, 4], ret_gn_g=[DEPTH, 256], rwkv_mu=[DEPTH, 1024], rwkv_w0=[DEPTH, 2, 256],
                   rwkv_w_up=[DEPTH, 2, 64, 256], rwkv_a0=[DEPTH, 2, 256], rwkv_a_up=[DEPTH, 2, 64, 256],
                   rwkv_g_up=[DEPTH, 128, 256], rwkv_kk=[DEPTH, 256], rwkv_ka=[DEPTH, 256], rwkv_rk=[DEPTH, 4, 64],
                   rwkv_gn_g=[DEPTH, 256], mla_q_norm_g=[DEPTH, 256], mla_w_uq=[DEPTH, 256, 384],
                   mla_kv_norm_g=[DEPTH, 128], mla_w_ukv=[DEPTH, 128, 512], mlp_w1=[DEPTH, D, DFF],
                   mlp_w2=[DEPTH, DFF, D])
    for k, s in wshapes.items():
        W[k] = din(k, s)
    K = {}
    consts = make_consts()
    for k, v in consts.items():
        K[k] = din('k_' + k, v.shape, BF16 if v.dtype == ml_dtypes.bfloat16 else F32)
    out = nc.dram_tensor("out", [L, D], F32, kind="ExternalOutput").ap()
    dbg = {}

    def dbg_out(name, shape, dt=F32):
        dbg[name] = nc.dram_tensor(name, list(shape), dt, kind="ExternalOutput").ap()
        return dbg[name]

    with C.es:
        def dscr(name, shape, dt=F32):
            return nc.dram_tensor(name, list(shape), dt).ap()
        wbf = {}
        for l in range(DEPTH):
            wbf[('w_in', l)] = dscr(f"wbf_in{l}", [D, N_IN], BF16)
            wbf[('w_out', l)] = dscr(f"wbf_out{l}", [D, D], BF16)
            wbf[('w1', l)] = dscr(f"wbf_w1{l}", [D, DFF], BF16)
            wbf[('w2', l)] = dscr(f"wbf_w2{l}", [DFF, D], BF16)
        if debug == 'p4':
            ycat = nc.dram_tensor("ycat_in", [T, D], F32, kind="ExternalInput").ap()
        elif debug == 'mix':
            ycat = dbg_out("d_ycat", [T, D])
        else:
            ycat = dscr("ycat", [T, D])
        xmid = dscr("xmid", [T, D])
        xres = dbg_out("d_xres", [T, D]) if debug == 'p4' else dscr("xres", [T, D])
        h2T_d = dscr("h2T_d", [128, 8, T], BF16)
        zraw = dscr("zraw", [8, 128, T])

        ident_f = C.sb("ident_f", [128, 128], F32)
        ident_b = C.sb("ident_b", [128, 128], BF16)
        C.dma('sp', ident_f[:], K['ident_f'][:, :])
        C.dma('sp', ident_b[:], K['ident_b'][:, :])
        neghalf = C.sb("neghalf", [128, 8], F32)
        C.memset('pool', neghalf[:], -0.5)
        modT = C.sb("modT", [128, 48, 2], F32)
        AB = C.sb("AB", [128, 4, 8, 2], F32)
        scT = C.sb("scT", [128, 8, 2], F32)
        gvec = C.sb("gvec", [128, 4, 8], F32)
        adab = C.sb("adab", [128, 48], F32)
        gv2 = C.sb("gv2", [128, 2, 8, 2], F32)
        Gb = C.sb("Gb", [128, 2, 2, D], F32)
        stat = [C.sb(f"stat{i}", [128, 8], F32) for i in range(2)]
        junk = C.sb("junk", [128, D], F32)

        def rstd_from_ss(st_, nsum, n):
            if nsum == 2:
                C.tt('pool', st_[:, 2:3], st_[:, 0:1], st_[:, 1:2], ALU.add)
                src = st_[:, 2:3]
            else:
                src = st_[:, 0:1]
            C.ts('pool', st_[:, 3:4], src, 1.0 / n, EPS, ALU.mult, ALU.add)
            C.rpow(st_[:, 4:5], st_[:, 3:4], -0.5)
            return st_[:, 4:5]

        if debug in (None, 'p4', 'mix'):
            with C.scope():
                cf = [C.sb(f"cf{i}", [128, 4096], F32) for i in range(2)]
                cb = [C.sb(f"cb{i}", [128, 4096], BF16) for i in range(2)]
                n = 0
                for l in range(DEPTH):
                    for wname, key, ncol in (('w_in', 'w_in', N_IN), ('w_out', 'w_out', D), ('mlp_w1', 'w1', DFF),
                                             ('mlp_w2', 'w2', DFF)):
                        if debug == 'p4' and (key == 'w_in' or l > 0):
                            continue
                        if debug == 'mix' and (key != 'w_in' or l > 0):
                            continue
                        if key == 'w2':
                            src2 = W[wname][l].rearrange("(a b) n -> a (b n)", b=4)
                            dst2 = wbf[(key, l)].rearrange("(a b) n -> a (b n)", b=4)
                        else:
                            src2 = W[wname][l]
                            dst2 = wbf[(key, l)]
                        for r8 in range(8):
                            f_, b_ = cf[n % 2], cb[n % 2]
                            C.dma('sp', f_[:, 0:ncol], src2[r8 * 128:(r8 + 1) * 128, :])
                            eng = ('dve', 'pool', 'act')[n % 3]
                            C.cp(eng, b_[:, 0:ncol], f_[:, 0:ncol])
                            C.dma('pool', dst2[r8 * 128:(r8 + 1) * 128, :], b_[:, 0:ncol], wk=[(dst2, r8)])
                            n += 1

        for l in range(DEPTH):
            last = (l == DEPTH - 1)
            tiles = list(range(NT))
            with C.scope():
                adaw = [C.sb(f"adaw{i}", [128, 8, 512], F32) for i in range(2)]
                pA = C.ps("pA", [128, 512], F32)
                items = [(adab[:], W['ada_b'][l])]
                for gi, gname in enumerate(('pre_mix_g', 'post_mix_g', 'pre_mlp_g', 'post_mlp_g')):
                    items.append((gvec[:, gi, :], W[gname][l]))
                if l == 0:
                    items += [(scT[:, :, 0], c_in), (scT[:, :, 1], cctx_in)]
                load_cols(C, ident_f, items)
                if l == 0:
                    C.act(junk[:, 0:16], scT[:].rearrange("p k s -> p (k s)"), AF.Sigmoid)
                    C.tt('dve', scT[:].rearrange("p k s -> p (k s)"), scT[:].rearrange("p k s -> p (k s)"),
                         junk[:, 0:16], ALU.mult)
                for s in range(12):
                    aw = adaw[s % 2]
                    C.dma('sp' if s % 2 == 0 else 'pool', aw[:],
                          W['ada_w'][l][:, s * 512:(s + 1) * 512].rearrange("(k p) n -> p k n", p=128))
                    for jj in range(4):
                        j = s * 4 + jj
                        for k in range(8):
                            C.mm(pA[:, 2 * j:2 * j + 2], aw[:, k, jj * 128:(jj + 1) * 128], scT[:, k, :],
                                 start=(k == 0), stop=(k == 7))
                C.tt('dve', modT[:], pA[:, 0:96].rearrange("p (j s) -> p j s", s=2),
                     adab[:].unsqueeze(2).to_broadcast([128, 48, 2]), ALU.add)
                for which, (gi, sci, shi) in enumerate(((0, 1, 0), (2, 4, 3))):
                    C.ts('dve', AB[:, 2 * which, :, :], modT[:, sci * 8:(sci + 1) * 8, :], 1.0, None, ALU.add)
                    C.tt('dve', AB[:, 2 * which, :, :], AB[:, 2 * which, :, :],
                         gvec[:, gi, :].unsqueeze(2).to_broadcast([128, 8, 2]), ALU.mult)
                    C.cp('dve', AB[:, 2 * which + 1, :, :], modT[:, shi * 8:(shi + 1) * 8, :])
                for w_, (gti, pgi) in enumerate(((2, 1), (5, 3))):
                    C.tt('dve', gv2[:, w_, :, :], modT[:, gti * 8:(gti + 1) * 8, :],
                         gvec[:, pgi, :].unsqueeze(2).to_broadcast([128, 8, 2]), ALU.mult)
                    for s_ in range(2):
                        for k in range(8):
                            C.mm(pA[:, (k % 4) * 128:(k % 4 + 1) * 128],
                                 gv2[:, w_, k, s_:s_ + 1].to_broadcast([128, 128]), ident_f[:])
                            if k % 4 == 3:
                                C.cp('act', Gb[:, w_, s_, (k - 3) * 128:(k + 1) * 128], pA[:, :])

            mix_scope = contextlib.ExitStack()
            if debug != 'p4':
              with C.scope():
                hT = C.sb("hT", [128, 8, T], BF16)
                with C.scope():
                    xt = [C.sb(f"xt{i}", [128, D], F32) for i in range(2)]
                    xn = [C.sb(f"xn{i}", [128, D], F32) for i in range(2)]
                    pA = C.ps("pA", [128, 512], F32)
                    pB = C.ps("pB", [128, 512], F32)
                    for i in range(NT):
                        s = 1 if i < NCT else 0
                        if l == 0:
                            src = ctx_in[i * 128:(i + 1) * 128, :] if i < NCT else x_in[(i - NCT) * 128:(i - NCT + 1) * 128, :]
                        else:
                            src = xres[i * 128:(i + 1) * 128, :]
                        xb_, xn_, st_ = xt[i % 2], xn[i % 2], stat[i % 2]
                        C.dma('sp' if i % 2 == 0 else 'pool', xb_[:], src, rk=(xres, i) if l > 0 else None)
                        C.act(junk[:], xb_[:], AF.Square, accum_out=st_[:, 0:1])
                        rs = rstd_from_ss(st_, 1, D)
                        C.ts('dve', xn_[:], xb_[:], rs, None, ALU.mult)
                        for half in range(2):
                            pp = pA if half == 0 else pB
                            for kk in range(4):
                                k = half * 4 + kk
                                C.tr(pp[:, kk * 128:(kk + 1) * 128], xn_[:, k * 128:(k + 1) * 128], ident_f[:])
                            for kk in range(4):
                                k = half * 4 + kk
                                if half == 0:
                                    C.act(hT[:, k, i * 128:(i + 1) * 128], pp[:, kk * 128:(kk + 1) * 128], AF.Identity,
                                          bias=AB[:, 1, k, s:s + 1], scale=AB[:, 0, k, s:s + 1], wk=[(hT, i)])
                                else:
                                    C.ts('dve', hT[:, k, i * 128:(i + 1) * 128], pp[:, kk * 128:(kk + 1) * 128],
                                         AB[:, 0, k, s:s + 1], AB[:, 1, k, s:s + 1], ALU.mult, ALU.add, wk=[(hT, i)])
                if debug == 'hT' and l == 0:
                    d1 = dbg_out("d_modT", [128, 96])
                    C.dma('sp', d1[:, :], modT[:].rearrange("p j s -> p (j s)"))
                    d2 = dbg_out("d_hT", [128, 8 * T], BF16)
                    C.dma('sp', d2[:, :], hT[:].rearrange("p k t -> p (k t)"))
                MIXERS(locals())
            if debug != 'p4' and (debug_mixers is None or 'rwkv' in debug_mixers):
                E2 = dict(locals())
                E2['emit'] = l < DEPTH - 1
                rwkv_main(E2)
            if debug in ('hT', 'mix'):
                break

            ptiles = [i for i in range(NT) if not (last and i < NCT)]
            with C.scope():
                wo = C.sb("wo", [128, 8, D], BF16)
                C.dma('sp', wo[:], wbf[('w_out', l)].rearrange("(k p) n -> p k n", p=128), rk=wbf[('w_out', l)])
                yt = [C.sb(f"yt{i}", [128, D], F32) for i in range(2)]
                ybf = [C.sb(f"ybf{i}", [128, D], BF16) for i in range(2)]
                yT = [C.sb(f"yT{i}", [128, 8, 128], BF16) for i in range(2)]
                xt = [C.sb(f"xt{i}", [128, D], F32) for i in range(2)]
                xm = [C.sb(f"xm{i}", [128, D], F32) for i in range(2)]
                xn = [C.sb(f"xn{i}", [128, D], F32) for i in range(2)]
                h2 = [C.sb(f"h2{i}", [128, 8, 128], BF16) for i in range(2)]
                pT = C.ps("pT", [128, D], BF16)
                pY = [C.ps(f"pY{i}", [128, 512], F32) for i in range(2)]
                pA = C.ps("pA", [128, 512], F32)
                pB = C.ps("pB", [128, 512], F32)
                for n_, i in enumerate(ptiles):
                    s = 1 if i < NCT else 0
                    b2 = n_ % 2
                    st_ = stat[b2]
                    C.dma('sp', yt[b2][:], ycat[i * 128:(i + 1) * 128, :], rk=(ycat, i))
                    if l == 0:
                        src = ctx_in[i * 128:(i + 1) * 128, :] if i < NCT else x_in[(i - NCT) * 128:(i - NCT + 1) * 128, :]
                    else:
                        src = xres[i * 128:(i + 1) * 128, :]
                    C.dma('pool', xt[b2][:], src, rk=(xres, i) if l > 0 else None)
                    C.cp('dve', ybf[b2][:], yt[b2][:])
                    for k in range(8):
                        C.tr(pT[:, k * 128:(k + 1) * 128], ybf[b2][:, k * 128:(k + 1) * 128], ident_b[:])
                    C.cp('act', yT[b2][:].rearrange("p k t -> p (k t)"), pT[:, :])
                    for nh in range(2):
                        for k in range(8):
                            C.mm(pY[nh][:, :], yT[b2][:, k, :], wo[:, k, nh * 512:(nh + 1) * 512],
                                 start=(k == 0), stop=(k == 7))
                    for nh in range(2):
                        C.act(junk[:, nh * 512:(nh + 1) * 512], pY[nh][:, :], AF.Square, accum_out=st_[:, nh:nh + 1])
                    rs = rstd_from_ss(st_, 2, D)
                    for nh in range(2):
                        sl = slice(nh * 512, (nh + 1) * 512)
                        C.stt(xm[b2][:, sl], pY[nh][:, :], rs, Gb[:, 0, s, sl], ALU.mult, ALU.mult)
                    C.tt('dve', xm[b2][:], xm[b2][:], xt[b2][:], ALU.add)
                    C.dma('sp', xmid[i * 128:(i + 1) * 128, :], xm[b2][:], wk=(xmid, i))
                    C.act(junk[:], xm[b2][:], AF.Square, accum_out=st_[:, 5:6])
                    C.ts('pool', st_[:, 6:7], st_[:, 5:6], 1.0 / D, EPS, ALU.mult, ALU.add)
                    C.rpow(st_[:, 7:8], st_[:, 6:7], -0.5)
                    C.ts('dve', xn[b2][:], xm[b2][:], st_[:, 7:8], None, ALU.mult)
                    for half in range(2):
                        pp = pA if half == 0 else pB
                        for kk in range(4):
                            k = half * 4 + kk
                            C.tr(pp[:, kk * 128:(kk + 1) * 128], xn[b2][:, k * 128:(k + 1) * 128], ident_f[:])
                        for kk in range(4):
                            k = half * 4 + kk
                            if half == 0:
                                C.act(h2[b2][:, k, :], pp[:, kk * 128:(kk + 1) * 128], AF.Identity,
                                      bias=AB[:, 3, k, s:s + 1], scale=AB[:, 2, k, s:s + 1])
                            else:
                                C.ts('dve', h2[b2][:, k, :], pp[:, kk * 128:(kk + 1) * 128],
                                     AB[:, 2, k, s:s + 1], AB[:, 3, k, s:s + 1], ALU.mult, ALU.add)
                    C.dma('pool', h2T_d[:, :, i * 128:(i + 1) * 128], h2[b2][:], wk=(h2T_d, i))

            with C.scope():
                w1s = C.sb("w1s", [128, 8, DFF], BF16)
                w2s = C.sb("w2s", [128, 32, D], BF16)
                for k in range(8):
                    C.dma('sp' if k % 2 == 0 else 'pool', w1s[:, k, :], wbf[('w1', l)][k * 128:(k + 1) * 128, :],
                          rk=wbf[('w1', l)], wk=(w1s, k))
                for j4 in range(8):
                    C.dma('sp' if j4 % 2 == 0 else 'pool', w2s[:, j4 * 4:(j4 + 1) * 4, :],
                          wbf[('w2', l)][j4 * 512:(j4 + 1) * 512, :].rearrange("(j p) n -> p j n", p=128),
                          rk=wbf[('w2', l)], wk=(w2s, j4))
                h2g = [C.sb(f"h2g{i}", [128, 8, 256], BF16) for i in range(2)]
                rl = [C.sb(f"rl{i}", [128, 256], BF16) for i in range(2)]
                uT = [C.sb(f"uT{i}", [128, 256], BF16) for i in range(3)]
                xm = [C.sb(f"xm{i}", [128, D], F32) for i in range(2)]
                xo = [C.sb(f"xo{i}", [128, D], F32) for i in range(2)]
                pH = [C.ps(f"pH{i}", [128, 512], F32) for i in range(2)]
                pF = [[C.ps(f"pF{t_}{h_}", [128, 512], F32) for h_ in range(2)] for t_ in range(2)]
                groups = [ptiles[a:a + 2] for a in range(0, len(ptiles), 2)]
                nu = 0
                for gi, grp in enumerate(groups):
                    i0 = grp[0]
                    hg = h2g[gi % 2]
                    C.dma('sp', hg[:], h2T_d[:, :, i0 * 128:(i0 + 2) * 128], rk=(h2T_d, None))
                    for j in range(32):
                        ph = pH[j % 2]
                        for k in range(8):
                            C.mm(ph[:, 0:256], w1s[:, k, j * 128:(j + 1) * 128], hg[:, k, :],
                                 start=(k == 0), stop=(k == 7), rk=[(w1s, k), hg])
                        r_ = rl[j % 2]
                        u_ = uT[nu % 3]
                        nu += 1
                        C.act(r_[:], ph[:, 0:256], AF.Relu)
                        C.tt('dve', u_[:], r_[:], r_[:], ALU.mult)
                        for t_ in range(2):
                            for nh in range(2):
                                C.mm(pF[t_][nh][:, :], u_[:, t_ * 128:(t_ + 1) * 128], w2s[:, j, nh * 512:(nh + 1) * 512],
                                     start=(j == 0), stop=(j == 31), rk=[u_, (w2s, j // 4)])
                    for t_, i in enumerate(grp):
                        s = 1 if i < NCT else 0
                        b2 = (gi * 2 + t_) % 2
                        st_ = stat[b2]
                        C.dma('pool', xm[b2][:], xmid[i * 128:(i + 1) * 128, :], rk=(xmid, i))
                        for nh in range(2):
                            C.act(junk[:, nh * 512:(nh + 1) * 512], pF[t_][nh][:, :], AF.Square,
                                  accum_out=st_[:, nh:nh + 1])
                        rs = rstd_from_ss(st_, 2, D)
                        for nh in range(2):
                            sl = slice(nh * 512, (nh + 1) * 512)
                            C.stt(xo[b2][:, sl], pF[t_][nh][:, :], rs, Gb[:, 1, s, sl], ALU.mult, ALU.mult)
                        C.tt('dve', xo[b2][:], xo[b2][:], xm[b2][:], ALU.add)
                        if last:
                            C.dma('sp', out[(i - NCT) * 128:(i - NCT + 1) * 128, :], xo[b2][:])
                        else:
                            C.dma('sp', xres[i * 128:(i + 1) * 128, :], xo[b2][:], wk=(xres, i))
            if debug == 'p4':
                break
        C.finish()
    return nc, list(dbg.keys())


def mixer_ret(E):
    C, l, hT, wbf, W, K, ycat = E['C'], E['l'], E['hT'], E['wbf'], E['W'], E['K'], E['ycat']
    stat, neghalf, emit = E['stat'], E['neghalf'], E['emit']
    with C.scope():
        wB = C.sb("wB", [128, 8, 1024], BF16)
        C.dma('sp', wB[:], wbf[('w_in', l)][:, 512:1536].rearrange("(k p) n -> p k n", p=128), rk=wbf[('w_in', l)])
        tabs = {}
        for nm, shp in (('ret_qft', [128, 256]), ('ret_qbt', [128, 256]), ('ret_gcc', [128, 2]), ('ret_kf', [128, 256]),
                        ('ret_kb', [128, 256]), ('ret_dt', [128, 512])):
            tabs[nm] = C.sb(nm, shp, F32)
            src = K[nm]
            if len(src.shape) == 3:
                src = src.rearrange("p a b -> p (a b)")
            C.dma('pool', tabs[nm][:], src)
        gng = C.sb("ret_gng", [128, 256], F32)
        C.dma('pool', gng[:], W['ret_gn_g'][l].partition_broadcast(128))
        Gs = [C.sb(f"retG{d}", [128, NT, 128], F32) for d in range(2)]
        Ss = [C.sb(f"retS{d}", [128, NT, 128], BF16) for d in range(2)]
        cur = [C.sb(f"retcur{d}", [128, 128], F32) for d in range(2)]
        import os
        if int(os.environ.get('RET_STOP', '99')) <= 0:
            return
        with C.scope():
            pKV = [C.ps(f"pKV{i}", [128, 512], F32) for i in range(2)]
            pV1 = [C.ps(f"pV1{i}", [128, 512], F32) for i in range(2)]
            pGb = [C.ps(f"pG{i}", [128, 512], F32) for i in range(2)]
            pG = [[pGb[i][:, d * 128:(d + 1) * 128] for d in range(2)] for i in range(2)]
            kd = [[C.sb(f"kd{i}{d}", [128, 256], BF16) for d in range(2)] for i in range(2)]
            vv = [C.sb(f"vv{i}", [128, 256], BF16) for i in range(2)]
            for n in range(NT):
                b2 = n % 2
                for k in range(8):
                    C.mm(pKV[b2][:, 0:256], hT[:, k, n * 128:(n + 1) * 128], wB[:, k, 256:512], start=(k == 0), stop=(k == 7),
                         rk=[(hT, n), wB])
                for k in range(8):
                    C.mm(pV1[b2][:, 256:512], hT[:, k, n * 128:(n + 1) * 128], wB[:, k, 512:768], start=(k == 0), stop=(k == 7),
                         rk=[(hT, n), wB])
                p1 = os.environ.get('RET_P1', '')
                if 'nodve' not in p1:
                    C.tt('dve', kd[b2][0][:], pKV[b2][:, 0:256], tabs['ret_kf'][:], ALU.mult)
                    C.tt('dve', kd[b2][1][:], pKV[b2][:, 0:256], tabs['ret_kb'][:], ALU.mult)
                if 'noact' not in p1:
                    C.cp('act', vv[b2][:], pV1[b2][:, 256:512])
                for d in range(2):
                    if 'nog' in os.environ.get('RET_P1', ''):
                        continue
                    for h in range(4):
                        if os.environ.get('RET_P1') == 'h0' and h != 0:
                            continue
                        hh, cc = h % 2, h // 2
                        C.mm(pGb[b2][hh * 64:(hh + 1) * 64, d * 128 + cc * 64:d * 128 + (cc + 1) * 64], kd[b2][d][:, h * 64:(h + 1) * 64],
                             vv[b2][:, h * 64:(h + 1) * 64])
                    C.cp('act', Gs[d][:, n, :], pG[b2][d], wk=[(Gs[d], n)])
        import os
        stop = int(os.environ.get('RET_STOP', '99'))
        if stop <= 1:
            return
        gcc = tabs['ret_gcc']
        for d in range(2):
            order = list(range(NT)) if d == 0 else [1, 0] + list(range(NT - 1, 1, -1))
            C.memset('dve', cur[d][:], 0.0)
            for n in order:
                C.cp('act', Ss[d][:, n, :], cur[d][:], wk=[(Ss[d], n)])
                for cc in range(2):
                    C.stt(cur[d][:, cc * 64:(cc + 1) * 64], cur[d][:, cc * 64:(cc + 1) * 64], gcc[:, cc:cc + 1],
                          Gs[d][:, n, cc * 64:(cc + 1) * 64], ALU.mult, ALU.add, rk=[cur[d], gcc, (Gs[d], n)])
        if stop <= 2:
            return
        with C.scope():
            pQ = C.ps("pQ", [128, 512], F32)
            pK = C.ps("pK", [128, 512], F32)
            pVG = C.ps("pVG", [128, 512], F32)
            pS = [C.ps(f"pS{i}", [128, 512], F32) for i in range(2)]
            pOb = [C.ps(f"pO{i}", [128, 512], F32) for i in range(2)]
            qT = [C.sb(f"rqT{i}", [128, 2, 128], BF16) for i in range(2)]
            qTf = [C.sb(f"rqTf{i}", [128, 2, 128], BF16) for i in range(2)]
            qTb = [C.sb(f"rqTb{i}", [128, 2, 128], BF16) for i in range(2)]
            kT = [C.sb(f"rkT{i}", [128, 2, 128], BF16) for i in range(2)]
            vv = [C.sb(f"rvv{i}", [128, 256], BF16) for i in range(2)]
            sg = [C.sb(f"rsg{i}", [128, 256], F32) for i in range(2)]
            sTm = [C.sb(f"rsTm{i}", [128, 4, 128], BF16) for i in range(2)]
            oc = [C.sb(f"roc{i}", [128, 256], F32) for i in range(2)]
            osq = [C.sb(f"rosq{i}", [128, 256], F32) for i in range(2)]
            yo = [C.sb(f"ryo{i}", [128, 256], F32) for i in range(2)]
            st4 = [C.sb(f"rst{i}", [128, 8, 4], F32) for i in range(2)]
            nh4 = C.sb("nh4", [128, 4], F32)
            C.memset('pool', nh4[:], -0.5)
            tiles = [n for n in range(NT) if emit or n >= NCT]
            for n in tiles:
                b2 = n % 2
                for j in range(4):
                    col = (j % 2) * 128 + (0 if j < 2 else 256)
                    dst = (pQ if j < 2 else pK)[:, (j % 2) * 128:(j % 2 + 1) * 128]
                    for k in range(8):
                        C.mm(dst, wB[:, k, col:col + 128], hT[:, k, n * 128:(n + 1) * 128],
                             start=(k == 0), stop=(k == 7), rk=[(hT, n), wB])
                for k in range(8):
                    C.mm(pVG[:, :], hT[:, k, n * 128:(n + 1) * 128], wB[:, k, 512:1024], start=(k == 0), stop=(k == 7),
                         rk=[(hT, n), wB])
                C.cp('dve', qT[b2][:].rearrange("p c t -> p (c t)"), pQ[:, 0:256])
                C.tt('dve', qTf[b2][:].rearrange("p c t -> p (c t)"), pQ[:, 0:256], tabs['ret_qft'][:], ALU.mult)
                C.tt('dve', qTb[b2][:].rearrange("p c t -> p (c t)"), pQ[:, 0:256], tabs['ret_qbt'][:], ALU.mult)
                C.act(kT[b2][:].rearrange("p c t -> p (c t)"), pK[:, 0:256], AF.Copy, scale=0.125)
                C.cp('act', vv[b2][:], pVG[:, 0:256])
                C.act(sg[b2][:], pVG[:, 256:512], AF.Silu)
                for h in range(4):
                    hh, cc = h % 2, h // 2
                    C.mm(pS[hh][:, cc * 128:(cc + 1) * 128], kT[b2][hh * 64:(hh + 1) * 64, cc, :],
                         qT[b2][hh * 64:(hh + 1) * 64, cc, :])
                dt4 = tabs['ret_dt'][:].rearrange("p (c hh t) -> p c hh t", hh=2, t=128)
                sT4 = sTm[b2][:].rearrange("p (c hh) t -> p c hh t", hh=2)
                for hh in range(2):
                    C.tt('dve', sT4[:, :, hh, :], pS[hh][:, 0:256].rearrange("p (c t) -> p c t", t=128), dt4[:, :, hh, :],
                         ALU.mult)
                for h in range(4):
                    hh, cc = h % 2, h // 2
                    o_ = pOb[hh][:, cc * 64:(cc + 1) * 64]
                    C.mm(o_, sTm[b2][:, h, :], vv[b2][:, h * 64:(h + 1) * 64], start=True, stop=False)
                    C.mm(o_, qTf[b2][hh * 64:(hh + 1) * 64, cc, :], Ss[0][hh * 64:(hh + 1) * 64, n, cc * 64:(cc + 1) * 64],
                         start=False, stop=False, rk=[qTf[b2], (Ss[0], n)])
                    C.mm(o_, qTb[b2][hh * 64:(hh + 1) * 64, cc, :], Ss[1][hh * 64:(hh + 1) * 64, n, cc * 64:(cc + 1) * 64],
                         start=False, stop=True, rk=[qTb[b2], (Ss[1], n)])
                s4 = st4[b2]
                oc4 = oc[b2][:].rearrange("p (c hh e) -> p c hh e", hh=2, e=64)
                for hh in range(2):
                    C.cp('act', oc4[:, :, hh, :], pOb[hh][:, 0:128].rearrange("p (c e) -> p c e", e=64))
                C.act(osq[b2][:], oc[b2][:], AF.Square)
                o3 = oc[b2][:].rearrange("p (h e) -> p h e", e=64)
                C.red('dve', s4[:, 0, :], o3, ALU.add)
                C.red('dve', s4[:, 1, :], osq[b2][:].rearrange("p (h e) -> p h e", e=64), ALU.add)
                C.ts('pool', s4[:, 2, :], s4[:, 0, :], 1.0 / 64, None, ALU.mult)
                C.tt('pool', s4[:, 3, :], s4[:, 2, :], s4[:, 2, :], ALU.mult)
                C.ts('pool', s4[:, 4, :], s4[:, 1, :], 1.0 / 64, 1e-5, ALU.mult, ALU.add)
                C.tt('pool', s4[:, 4, :], s4[:, 4, :], s4[:, 3, :], ALU.subtract)
                C.rpow(s4[:, 5, :], s4[:, 4, :], -0.5)
                y3 = yo[b2][:].rearrange("p (h e) -> p h e", e=64)
                C.tt('dve', y3, o3, s4[:, 2, :].unsqueeze(2).to_broadcast([128, 4, 64]), ALU.subtract)
                C.tt('dve', y3, y3, s4[:, 5, :].unsqueeze(2).to_broadcast([128, 4, 64]), ALU.mult)
                C.tt('pool', yo[b2][:], yo[b2][:], gng[:], ALU.mult)
                C.tt('pool', yo[b2][:], yo[b2][:], sg[b2][:], ALU.mult)
                C.dma('sp', ycat[n * 128:(n + 1) * 128, 256:512], yo[b2][:], wk=(ycat, n))


def mixer_swa(E):
    C, l, hT, wbf, W, K, ycat = E['C'], E['l'], E['hT'], E['wbf'], E['W'], E['K'], E['ycat']
    emit, ident_b = E['emit'], E['ident_b']
    with C.scope():
        wA = C.sb("wA", [128, 8, 512], BF16)
        C.dma('sp', wA[:], wbf[('w_in', l)][:, 0:512].rearrange("(k p) n -> p k n", p=128), rk=wbf[('w_in', l)])
        wK2 = C.sb("wK2", [128, 8, 2, 128], BF16)
        for hk in range(2):
            for dup in range(2):
                C.dma('pool', wK2[:, :, hk, dup * 64:(dup + 1) * 64],
                      wbf[('w_in', l)][:, 256 + hk * 64:256 + (hk + 1) * 64].rearrange("(k p) n -> p k n", p=128),
                      rk=wbf[('w_in', l)])
        pm = C.sb("swa_pm", [128, 128], BF16)
        C.dma('pool', pm[:], K['swa_pm'][:, :])
        mask = C.sb("swa_mask", [128, 384], F32)
        C.dma('pool', mask[:], K['swa_mask'][:, :])
        sinkb = C.sb("sinkb", [128, 4], F32)
        C.dma('pool', sinkb[:], W['swa_sink'][l].partition_broadcast(128))
        qTr = C.sb("qTr", [128, 2, T], BF16)
        kTz = [C.sb(f"kTz{g}", [128, 2, T], BF16) for g in range(2)]
        vtok = C.sb("vtok", [128, NT, 128], BF16)
        C.memset('pool', kTz[0][64:128, :, :], 0.0)
        C.memset('pool', kTz[1][0:64, :, :], 0.0)
        with C.scope():
            pQ = C.ps("pQ", [128, 512], F32)
            pP = C.ps("pP", [128, 512], F32)
            pV = C.ps("pV", [128, 512], F32)
            ctg = [C.sb(f"ctg{i}", [128, 512], F32) for i in range(2)]
            stg = [C.sb(f"stg{i}", [128, 512], F32) for i in range(2)]
            xq = [C.sb(f"xq{i}", [128, 512], BF16) for i in range(2)]
            t1 = [C.sb(f"t1{i}", [128, 512], F32) for i in range(2)]
            t2 = [C.sb(f"t2{i}", [128, 512], F32) for i in range(2)]
            groups = [(0, LC)] + [(LC + 512 * i, 512) for i in range(8)]
            nx = 0
            for gi, (t0, n) in enumerate(groups):
                rope = t0 >= LC
                if rope:
                    C.dma('sp', ctg[gi % 2][:], K['swa_ct'][:, t0 - LC:t0 - LC + n])
                    C.dma('sp', stg[gi % 2][:], K['swa_st'][:, t0 - LC:t0 - LC + n])
                for j in range(4):
                    hk = j % 2
                    for k in range(8):
                        lhsT = wA[:, k, hk * 128:(hk + 1) * 128] if j < 2 else wK2[:, k, hk, :]
                        C.mm(pQ[:, 0:n], lhsT, hT[:, k, t0:t0 + n], start=(k == 0), stop=(k == 7), rk=[hT, wA, wK2])
                    x_ = xq[nx % 2]
                    nx += 1
                    C.cp('act', x_[:, 0:n], pQ[:, 0:n])
                    if rope:
                        C.mm(pP[:, 0:n], pm[:], x_[:, 0:n])
                        a_, b_ = t1[nx % 2], t2[nx % 2]
                        C.tt('pool', a_[:, 0:n], x_[:, 0:n], ctg[gi % 2][:, 0:n], ALU.mult)
                        C.tt('dve', b_[:, 0:n], pP[:, 0:n], stg[gi % 2][:, 0:n], ALU.mult)
                        if j < 2:
                            C.tt('pool', qTr[:, hk, t0:t0 + n], a_[:, 0:n], b_[:, 0:n], ALU.add, wk=[(qTr, gi)])
                        else:
                            for g in range(2):
                                C.tt('pool', kTz[g][g * 64:(g + 1) * 64, hk, t0:t0 + n], a_[g * 64:(g + 1) * 64, 0:n],
                                     b_[g * 64:(g + 1) * 64, 0:n], ALU.add, wk=[(kTz[g], gi)])
                    else:
                        if j < 2:
                            C.cp('pool', qTr[:, hk, t0:t0 + n], x_[:, 0:n], wk=[(qTr, gi)])
                        else:
                            for g in range(2):
                                C.cp('pool', kTz[g][g * 64:(g + 1) * 64, hk, t0:t0 + n], x_[g * 64:(g + 1) * 64, 0:n],
                                     wk=[(kTz[g], gi)])
                for ti in range(t0 // 128, (t0 + n) // 128):
                    for k in range(8):
                        C.mm(pV[:, 0:128], hT[:, k, ti * 128:(ti + 1) * 128], wA[:, k, 384:512], start=(k == 0),
                             stop=(k == 7), rk=[hT, wA])
                    C.cp('act', vtok[:, ti, :], pV[:, 0:128], wk=[(vtok, ti)])
        with C.scope():
            pSa = [C.ps(f"pSa{i}", [128, 512], F32) for i in range(2)]
            pSb = [C.ps(f"pSb{i}", [128, 512], F32) for i in range(2)]
            pT = [C.ps(f"pTs{i}", [128, 1024], BF16) for i in range(2)]
            pO = C.ps("pOs", [128, 512], F32)
            sc = [C.sb(f"sc{i}", [128, 640], F32) for i in range(2)]
            pb = [C.sb(f"pb{i}", [128, 640], BF16) for i in range(2)]
            pTs = [C.sb(f"pTsb{i}", [128, 640], BF16) for i in range(2)]
            sm = [C.sb(f"sm{i}", [128, 8], F32) for i in range(2)]
            ya = [C.sb(f"ya{i}", [128, 256], F32) for i in range(2)]
            rden = [C.sb(f"rden{i}", [128, 4], F32) for i in range(2)]
            qblocks = [n for n in range(NT) if emit or n >= NCT]
            nh = 0
            for bi, n in enumerate(qblocks):
                isx = n >= NCT
                if isx:
                    lo, hi = max(n - 1, NCT), min(n + 1, NT - 1)
                    ktiles = list(range(lo, hi + 1)) + [0, 1]
                    nloc = hi - lo + 1
                    moff = (lo - (n - 1)) * 128
                else:
                    lo, hi = 0, 1
                    ktiles = [0, 1]
                    nloc = 2
                ncol = len(ktiles) * 128
                for h in range(4):
                    hk, g = h // 2, h % 2
                    b2 = nh % 2
                    nh += 1
                    C.mm(pSa[b2][:, 0:nloc * 128], qTr[:, hk, n * 128:(n + 1) * 128], kTz[g][:, hk, lo * 128:(hi + 1) * 128],
                         rk=[qTr, kTz[g]])
                    if isx:
                        C.mm(pSb[b2][:, 0:256], qTr[:, hk, n * 128:(n + 1) * 128], kTz[g][:, hk, 0:256], rk=[qTr, kTz[g]])
                        C.tt('dve', sc[b2][:, 0:nloc * 128], pSa[b2][:, 0:nloc * 128], mask[:, moff:moff + nloc * 128], ALU.add)
                        C.cp('dve', sc[b2][:, nloc * 128:ncol], pSb[b2][:, 0:256])
                    else:
                        C.cp('dve', sc[b2][:, 0:ncol], pSa[b2][:, 0:ncol])
                    s_ = sm[b2]
                    C.red('dve', s_[:, 0:1], sc[b2][:, 0:ncol], ALU.max)
                    C.ts('dve', s_[:, 1:2], s_[:, 0:1], 0.125, sinkb[:, h:h + 1], ALU.mult, ALU.max)
                    C.ts('dve', s_[:, 2:3], s_[:, 1:2], -1.0, None, ALU.mult)
                    C.act(pb[b2][:, 0:ncol], sc[b2][:, 0:ncol], AF.Exp, bias=s_[:, 2:3], scale=0.125, accum_out=s_[:, 3:4])
                    C.act(s_[:, 4:5], sinkb[:, h:h + 1], AF.Exp, bias=s_[:, 2:3], scale=1.0)
                    C.tt('dve', s_[:, 5:6], s_[:, 3:4], s_[:, 4:5], ALU.add)
                    C.op('dve', [s_], [rden[bi % 2]], lambda e_, o_=rden[bi % 2][:, h:h + 1], i_=s_[:, 5:6]: e_.reciprocal(out=o_, in_=i_))
                    for j in range(len(ktiles)):
                        C.tr(pT[b2][:, j * 128:(j + 1) * 128], pb[b2][:, j * 128:(j + 1) * 128], ident_b[:])
                    C.cp('act', pTs[b2][:, 0:ncol], pT[b2][:, 0:ncol])
                    for j, kt in enumerate(ktiles):
                        C.mm(pO[:, h * 64:(h + 1) * 64], pTs[b2][:, j * 128:(j + 1) * 128], vtok[:, kt, hk * 64:(hk + 1) * 64],
                             start=(j == 0), stop=(j == len(ktiles) - 1), rk=[pTs[b2], vtok])
                for h in range(4):
                    C.act(ya[bi % 2][:, h * 64:(h + 1) * 64], pO[:, h * 64:(h + 1) * 64], AF.Copy, scale=rden[bi % 2][:, h:h + 1])
                C.dma('sp', ycat[n * 128:(n + 1) * 128, 0:256], ya[bi % 2][:], wk=(ycat, n))


def mixer_mla(E):
    C, l, hT, wbf, W, K, ycat = E['C'], E['l'], E['hT'], E['wbf'], E['W'], E['K'], E['ycat']
    emit, ident_b, ident_f, neghalf = E['emit'], E['ident_b'], E['ident_f'], E['neghalf']
    SCALE = 96.0 ** -0.5
    with C.scope():
        wD = C.sb("wD", [128, 8, 416], BF16)
        C.dma('sp', wD[:], wbf[('w_in', l)][:, 2560:2976].rearrange("(k p) n -> p k n", p=128), rk=wbf[('w_in', l)])
        wf = C.sb("mla_wf", [128, 1280], F32)
        wuq = C.sb("wuq", [128, 2, 384], BF16)
        wukv = C.sb("wukv", [128, 512], BF16)
        C.dma('sp', wf[:, 0:768].rearrange("p (c n) -> p c n", c=2), W['mla_w_uq'][l].rearrange("(c p) n -> p c n", p=128))
        C.dma('sp', wf[:, 768:1280], W['mla_w_ukv'][l])
        C.cp('dve', wuq[:].rearrange("p c n -> p (c n)"), wf[:, 0:768])
        C.cp('dve', wukv[:], wf[:, 768:1280])
        gq = C.sb("mla_gq", [128, 2], F32)
        gkv = C.sb("mla_gkv", [128, 1], F32)
        load_cols(C, ident_f, [(gq[:], W['mla_q_norm_g'][l]), (gkv[:], W['mla_kv_norm_g'][l])])
        cosT = C.sb("mla_cos", [128, L // 128, 16], F32)
        sinT = C.sb("mla_sin", [128, L // 128, 16], F32)
        C.dma('pool', cosT[:].rearrange("p n f -> p (n f)"), K['mla_cos'][:, :])
        C.dma('pool', sinT[:].rearrange("p n f -> p (n f)"), K['mla_sin'][:, :])
        ones_b = C.sb("mla_ones", [128, 2], BF16)
        C.memset('pool', ones_b[:], 1.0)
        ones_f = C.sb("mla_onesf", [1, 128], F32)
        C.memset('pool', ones_f[:], 1.0)
        half = C.sb("mla_half", [128, 4], F32)
        C.memset('pool', half[:], 0.5)
        KaT = C.sb("KaT", [128, 4, T], BF16)
        Va = C.sb("Va", [128, NT, 4, 65], BF16)
        C.memset('pool', Va[:, :, :, 64:65], 1.0)
        kmx = C.sb("kmx", [128, 4], F32)
        C.memset('dve', kmx[:], 0.0)
        kmaxb = C.sb("kmaxb", [128, 2], F32)
        aug = [C.sb(f"aug{i}", [128, 4, 128], BF16) for i in range(2)]
        for i in range(2):
            C.memset('pool', aug[i][:], 0.0)
        fT = [C.sb(f"mfT{i}", [128, 2, 128], BF16) for i in range(2)]
        sqT = [C.sb(f"msq{i}", [128, 2, 128], BF16) for i in range(2)]
        qf = [C.sb(f"mqf{i}", [128, 512], F32) for i in range(2)]
        tmp = [C.sb(f"mtmp{i}", [128, 384], F32) for i in range(2)]
        rp = [C.sb(f"mrp{i}", [128, 6, 64], F32) for i in range(2)]
        st_ = [C.sb(f"mst{i}", [128, 16], F32) for i in range(2)]

        def rope_tm(out1, out2, x1, x2, c_, s_, r_, shape):
            n = int(np.prod(shape[1:]))
            v = [r_[:, i, 0:n].rearrange("p (a b) -> p a b", b=shape[-1]) if len(shape) == 3 else
                 r_[:, i, 0:n].rearrange("p (h a b) -> p h a b", a=shape[2], b=shape[3]) for i in range(4)]
            C.tt('dve', v[0], x1, c_, ALU.mult)
            C.tt('dve', v[1], x2, s_, ALU.mult)
            C.tt('dve', v[2], x2, c_, ALU.mult)
            C.tt('dve', v[3], x1, s_, ALU.mult)
            C.tt('dve', out1, v[0], v[1], ALU.subtract)
            C.tt('dve', out2, v[2], v[3], ALU.add)

        import os
        mstop = int(os.environ.get('MLA_STOP', '99'))
        if mstop <= 0:
            return
        with C.scope():
            pA = C.ps("mpA", [128, 512], F32)
            pB = C.ps("mpB", [128, 512], F32)
            pB2 = C.ps("mpB2", [128, 512], F32)
            pT = C.ps("mpT", [128, 1024], BF16)
            for n in range(NT):
                b2 = n % 2
                a_, s4 = aug[b2], st_[b2]
                for k in range(8):
                    C.mm(pA[:, 0:128], wD[:, k, 256:384], hT[:, k, n * 128:(n + 1) * 128], start=(k == 0), stop=(k == 7),
                         rk=[(hT, n), wD])
                C.act(fT[b2][:, 0, :], pA[:, 0:128], AF.Copy, scale=gkv[:, 0:1])
                C.act(sqT[b2][:, 0, :], pA[:, 0:128], AF.Square)
                C.mm(pB[:, :], fT[b2][:, 0, :], wukv[:])
                C.mm(pB2[:, 0:2], sqT[b2][:, 0, :], ones_b[:, 0:2])
                for k in range(8):
                    C.mm(pB2[:, 32:64], hT[:, k, n * 128:(n + 1) * 128], wD[:, k, 384:416], start=(k == 0), stop=(k == 7),
                         rk=[(hT, n), wD])
                C.ts('dve', s4[:, 0:1], pB2[:, 0:1], 1.0 / 128, EPS, ALU.mult, ALU.add)
                C.rpow(s4[:, 1:2], s4[:, 0:1], -0.5)
                kv4 = pB[:, :].rearrange("p (h e) -> p h e", e=128)
                C.ts('dve', a_[:, :, 0:64], kv4[:, :, 0:64], s4[:, 1:2], None, ALU.mult)
                C.ts('dve', Va[:, n, :, 0:64], kv4[:, :, 64:128], s4[:, 1:2], None, ALU.mult, wk=[(Va, n)])
                if n >= NCT:
                    x4 = pB2[:, 32:64].rearrange("p (a b f) -> p a b f", a=2, b=2)
                    c_ = cosT[:, n - NCT, :].rearrange("p (a f) -> p a f", a=2)
                    s_ = sinT[:, n - NCT, :].rearrange("p (a f) -> p a f", a=2)
                    o4 = rp[b2][:, 4, 0:32].rearrange("p (a b f) -> p a b f", a=2, b=2)
                    rope_tm(o4[:, :, 0, :], o4[:, :, 1, :], x4[:, :, 0, :], x4[:, :, 1, :], c_, s_, rp[b2], [128, 2, 8])
                else:
                    C.cp('dve', rp[b2][:, 4, 0:32], pB2[:, 32:64])
                C.cp('dve', a_[:, :, 64:96], rp[b2][:, 4, 0:32].unsqueeze(1).to_broadcast([128, 4, 32]))
                C.memset('pool', a_[:, :, 96:97], 1.0)
                t3 = tmp[b2][:].rearrange("p (h e) -> p h e", e=96)
                C.tt('pool', t3, a_[:, :, 0:96], a_[:, :, 0:96], ALU.mult)
                C.red('dve', s4[:, 4:8], t3, ALU.add)
                C.tt('dve', kmx[:], kmx[:], s4[:, 4:8], ALU.max)
                for h in range(4):
                    C.tr(pT[:, h * 128:(h + 1) * 128], a_[:, h, :], ident_b[:])
                C.cp('act', KaT[:, :, n * 128:(n + 1) * 128], pT[:, 0:512].rearrange("p (h t) -> p h t", t=128),
                     wk=[(KaT, n)])
            if mstop <= 1:
                return
            C.red('dve', st_[0][:, 8:9], kmx[:], ALU.max)
            C.tr(pB[:, 0:128].bitcast(F32)[0:1, 0:128], st_[0][:, 8:9], ident_f[:])
            C.red('dve', st_[0][0:1, 9:10], pB[0:1, 0:128], ALU.max)
            C.rpow(st_[0][0:1, 10:11], st_[0][0:1, 9:10], 0.5)
            C.mm(pB2[:, 0:1], ones_f[0:1, :], st_[0][0:1, 10:11])
            C.ts('dve', kmaxb[:, 0:1], pB2[:, 0:1], -1.0, None, ALU.mult)

        if mstop <= 2:
            return
        with C.scope():
            pA = C.ps("mqA", [128, 512], F32)
            pB = C.ps("mqB", [128, 512], F32)
            pT = C.ps("mqT", [128, 1024], BF16)
            pS = [C.ps(f"mpS{i}", [128, 512], F32) for i in range(2)]
            pO = [C.ps(f"mpO{h}", [128, 512], F32) for h in range(3)]
            QaT = [C.sb(f"QaT{i}", [128, 4, 512], BF16) for i in range(2)]
            PT = [C.sb(f"PTs{i}", [128, 512], BF16) for i in range(3)]
            yd = [C.sb(f"yd{i}", [128, 256], F32) for i in range(2)]
            rd_ = [C.sb(f"mrd{i}", [128, 4], F32) for i in range(2)]
            groups = ([(0, 2, [0, 1])] if emit else []) + [(NCT + 4 * i, 4, list(range(NT))) for i in range(8)]
            nps = 0
            ny = 0
            for gi, (n0, nt, ktiles) in enumerate(groups):
                Qg = QaT[gi % 2]
                for j in range(nt):
                    n = n0 + j
                    b2 = n % 2
                    a_, s4 = aug[b2], st_[b2]
                    for c in range(2):
                        for k in range(8):
                            C.mm(pA[:, c * 128:(c + 1) * 128], wD[:, k, c * 128:(c + 1) * 128], hT[:, k, n * 128:(n + 1) * 128],
                                 start=(k == 0), stop=(k == 7), rk=[(hT, n), wD])
                    for c in range(2):
                        C.act(fT[b2][:, c, :], pA[:, c * 128:(c + 1) * 128], AF.Copy, scale=gq[:, c:c + 1])
                        C.act(sqT[b2][:, c, :], pA[:, c * 128:(c + 1) * 128], AF.Square)
                    for c in range(2):
                        C.mm(pB[:, 0:384], fT[b2][:, c, :], wuq[:, c, :], start=(c == 0), stop=(c == 1))
                    for c in range(2):
                        C.mm(pB[:, 384 + 2 * c:386 + 2 * c], sqT[b2][:, c, :], ones_b[:, 0:2])
                    C.cp('dve', s4[:, 12:16], pB[:, 384:388])
                    C.tt('dve', s4[:, 0:1], s4[:, 12:13], s4[:, 14:15], ALU.add)
                    C.ts('dve', s4[:, 0:1], s4[:, 0:1], 1.0 / 256, EPS, ALU.mult, ALU.add)
                    C.rpow(s4[:, 1:2], s4[:, 0:1], -0.5)
                    C.ts('dve', qf[b2][:, 0:384], pB[:, 0:384], s4[:, 1:2], None, ALU.mult)
                    q3 = qf[b2][:, 0:384].rearrange("p (h e) -> p h e", e=96)
                    C.cp('dve', a_[:, :, 0:64], q3[:, :, 0:64])
                    if n >= NCT and os.environ.get('MLA_VAR') != 'norope':
                        x5 = q3[:, :, 64:96].rearrange("p h (a b f) -> p h a b f", a=2, b=2)
                        o5 = a_[:, :, 64:96].rearrange("p h (a b f) -> p h a b f", a=2, b=2)
                        c_ = cosT[:, n - NCT, :].rearrange("p (a f) -> p a f", a=2).unsqueeze(1).to_broadcast([128, 4, 2, 8])
                        s_ = sinT[:, n - NCT, :].rearrange("p (a f) -> p a f", a=2).unsqueeze(1).to_broadcast([128, 4, 2, 8])
                        rope_tm(o5[:, :, :, 0, :], o5[:, :, :, 1, :], x5[:, :, :, 0, :], x5[:, :, :, 1, :], c_, s_, rp[b2],
                                [128, 4, 2, 8])
                    else:
                        C.cp('dve', a_[:, :, 64:96], q3[:, :, 64:96])
                    t3 = tmp[b2][:].rearrange("p (h e) -> p h e", e=96)
                    C.tt('pool', t3, a_[:, :, 0:96], a_[:, :, 0:96], ALU.mult)
                    C.red('dve', s4[:, 4:8], t3, ALU.add)
                    C.rpow(s4[:, 8:12], s4[:, 4:8], 0.5)
                    C.ts('dve', a_[:, :, 96:97], s4[:, 8:12].unsqueeze(2), kmaxb[:, 0:1], None, ALU.mult)
                    for h in range(4):
                        C.tr(pT[:, h * 128:(h + 1) * 128], a_[:, h, :], ident_b[:])
                    C.cp('act', Qg[:, :, j * 128:(j + 1) * 128], pT[:, 0:512].rearrange("p (h t) -> p h t", t=128))
                nq = nt * 128
                if mstop <= 3:
                    continue
                for h in range(4):
                    po = pO[h % 3]
                    for ki, kt in enumerate(ktiles):
                        ps_ = pS[nps % 2]
                        pt_ = PT[nps % 3]
                        nps += 1
                        C.mm(ps_[:, 0:nq], KaT[0:97, h, kt * 128:(kt + 1) * 128], Qg[0:97, h, 0:nq], rk=[(KaT, kt), Qg])
                        C.act(pt_[:, 0:nq], ps_[:, 0:nq], AF.Exp, scale=SCALE)
                        for j in range(nt):
                            C.mm(po[:, j * 65:(j + 1) * 65], pt_[:, j * 128:(j + 1) * 128], Va[:, kt, h, :],
                                 start=(ki == 0 and j == 0), stop=(ki == len(ktiles) - 1), skip_group_check=True,
                                 rk=[pt_, (Va, kt)])
                    r4 = rd_[h % 2]
                    o3 = po[:, 0:nt * 65].rearrange("p (j e) -> p j e", e=65)
                    C.op('dve', [po], [r4], lambda e_, o_=r4[:, 0:nt], i_=o3[:, :, 64]: e_.reciprocal(out=o_, in_=i_))
                    for j in range(nt):
                        C.ts('dve', E['ydg'][j][:, h * 64:(h + 1) * 64], o3[:, j, 0:64], r4[:, j:j + 1], None, ALU.mult)
                for j in range(nt):
                    n = n0 + j
                    C.dma('sp', ycat[n * 128:(n + 1) * 128, 768:1024], E['ydg'][j][:], wk=(ycat, n))


def rwkv_project(E):
    C, l, hT, wbf, zraw = E['C'], E['l'], E['hT'], E['wbf'], E['zraw']
    with C.scope():
        wC = C.sb("wC", [128, 8, 1024], BF16)
        C.dma('sp', wC[:], wbf[('w_in', l)][:, 1536:2560].rearrange("(k p) n -> p k n", p=128), rk=wbf[('w_in', l)])
        pZ = [C.ps(f"pZ{i}", [128, 512], F32) for i in range(2)]
        stg = [C.sb(f"zstg{i}", [128, 512], F32) for i in range(2)]
        groups = [(512 * i, 512) for i in range(8)] + [(4096, 256)]
        i = 0
        for c in range(8):
            for (t0, n) in groups:
                for k in range(8):
                    C.mm(pZ[i % 2][:, 0:n], wC[:, k, c * 128:(c + 1) * 128], hT[:, k, t0:t0 + n], start=(k == 0), stop=(k == 7),
                         rk=[hT, wC])
                C.cp('act', stg[i % 2][:, 0:n], pZ[i % 2][:, 0:n])
                C.dma('sp' if i % 2 == 0 else 'pool', zraw[c, :, t0:t0 + n], stg[i % 2][:, 0:n], wk=(zraw, c))
                i += 1


def rwkv_main(E):
    C, l, W, K, ycat, zraw, emit, ident_b = E['C'], E['l'], E['W'], E['K'], E['ycat'], E['zraw'], E['emit'], E['ident_b']
    LAM = -math.exp(-0.5)
    WP = T + 4
    with C.scope():
        rkT = C.sb("rkT", [128, 4, T], BF16)
        twz = C.sb("twz", [128, T], BF16)
        sgd = C.sb("sgd", [128, T], BF16)
        vtok = C.sb("rw_vtok", [128, 2, NT, 128], BF16)
        muT = C.sb("muT", [128, 8], F32)
        pp = C.sb("rw_pp", [128, 4, 2], F32)
        w0T = C.sb("rw_w0", [128, 2, 2], F32)
        a0T = C.sb("rw_a0", [128, 2, 2], F32)
        items = [(muT[:], W['rwkv_mu'][l]), (pp[:, 0, :], W['rwkv_kk'][l]), (pp[:, 1, :], W['rwkv_ka'][l]),
                 (pp[:, 3, :], W['rwkv_rk'][l].rearrange("h k -> (h k)"))]
        for d in range(2):
            items += [(w0T[:, d, :], W['rwkv_w0'][l][d]), (a0T[:, d, :], W['rwkv_a0'][l][d])]
        load_cols(C, E['ident_f'], items)
        C.ts('dve', pp[:, 2, :], pp[:, 1, :], -1.0, 1.0, ALU.mult, ALU.add)
        lstage = C.sb("rw_lst", [128, 2, 2, 256], F32)
        C.memset('pool', lstage[:], 0.0)
        wupz = C.sb("wupz", [128, 2, 256], BF16)
        aupz = C.sb("aupz", [128, 2, 256], BF16)
        for d in range(2):
            C.dma('sp', lstage[0:64, 0, d, :], W['rwkv_w_up'][l][d])
            C.dma('sp', lstage[64:128, 1, d, :], W['rwkv_a_up'][l][d])
        C.cp('dve', wupz[:], lstage[:, 0, :, :])
        C.cp('dve', aupz[:], lstage[:, 1, :, :])
        gst = C.sb("rw_gst", [128, 256], F32)
        gup = C.sb("rw_gup", [128, 256], BF16)
        C.dma('sp', gst[:], W['rwkv_g_up'][l])
        C.cp('dve', gup[:], gst[:])
        gng = C.sb("rw_gng", [128, 256], F32)
        C.dma('pool', gng[:], W['rwkv_gn_g'][l].partition_broadcast(128))
        resetm = C.sb("rw_resetm", [128, 512], F32)
        C.dma('pool', resetm[:], K['rw_resetm'][:, :])
        blk = C.sb("rw_blk", [128, 128], BF16)
        C.dma('pool', blk[:], K['rw_blk'][:, :])
        blk2 = C.sb("rw_blk2", [128, 2], BF16)
        C.dma('pool', blk2[:], K['rw_blk2'][:, :])
        mask4 = C.sb("rw_mask4", [128, 2, 512], F32)
        masknt = C.sb("rw_masknt", [128, 2, 256], F32)
        for d in range(2):
            C.dma('pool', mask4[:, d, :], K['rw_mask4'][d])
            C.dma('pool', masknt[:, d, :], K['rw_masknt'][d])
        negh = C.sb("rw_negh", [128, 512], F32)
        C.memset('pool', negh[:], -0.5)
        nh2 = C.sb("rw_nh2", [128, 2], F32)
        C.memset('pool', nh2[:], -0.5)

        with C.scope():
            zr = [C.sb(f"zr{i}", [128, WP], F32) for i in range(2)]
            tA = [C.sb(f"tA{i}", [128, WP], F32) for i in range(2)]
            vb = C.sb("rw_vb", [128, T], BF16)
            pT = C.ps("rw_pT", [128, 1024], BF16)
            for i, c in enumerate([6, 7, 4, 5, 0, 1, 2, 3]):
                z_, t_ = zr[i % 2], tA[i % 2]
                C.memset('pool', z_[:, 0:1], 0.0)
                C.memset('pool', z_[:, 257:259], 0.0)
                C.memset('pool', z_[:, WP - 1:WP], 0.0)
                C.dma('sp', z_[:, 1:257], zraw[c, :, 0:LC], rk=(zraw, c))
                C.dma('pool', z_[:, 259:259 + L], zraw[c, :, LC:T], rk=(zraw, c))
                mid = slice(1, WP - 1)
                C.tt('pool', t_[:, mid], z_[:, 0:WP - 2], z_[:, 2:WP], ALU.add)
                C.stt(t_[:, mid], t_[:, mid], 0.5, z_[:, mid], ALU.mult, ALU.subtract)
                C.stt(t_[:, mid], t_[:, mid], muT[:, c:c + 1], z_[:, mid], ALU.mult, ALU.add)
                segs = ((slice(1, 257), slice(0, LC)), (slice(259, 259 + L), slice(LC, T)))
                for (src, dst) in segs:
                    if c == 6:
                        C.act(twz[0:64, dst], t_[0:64, src], AF.Tanh)
                        C.cp('act', twz[64:128, dst], t_[64:128, src])
                    elif c == 7:
                        C.act(sgd[:, dst], t_[:, src], AF.Sigmoid)
                    elif c in (4, 5):
                        C.cp('act', vb[:, dst], t_[:, src])
                    else:
                        C.cp('act', rkT[:, c, dst], t_[:, src])
                if c in (4, 5):
                    for n0 in range(0, NT, 8):
                        nn = min(8, NT - n0)
                        for j in range(nn):
                            C.tr(pT[:, j * 128:(j + 1) * 128], vb[:, (n0 + j) * 128:(n0 + j + 1) * 128], ident_b[:])
                        C.cp('act', vtok[:, c - 4, n0:n0 + nn, :], pT[:, 0:nn * 128].rearrange("p (j f) -> p j f", f=128))

        import os
        rstop = int(os.environ.get('RW_STOP', '99'))
        if rstop <= 0:
            return
        Oacc = [C.sb(f"rw_O{d}", [128, NT, 128], F32) for d in range(2)]
        bsum = [C.sb(f"rw_bs{d}", [128, NT, 2], F32) for d in range(2)]
        ktI = [C.sb(f"ktI{i}", [128, 512], BF16) for i in range(2)]
        bI = [C.sb(f"bI{i}", [128, 512], BF16) for i in range(2)]
        KR = [[C.sb(f"KR{i}{hh}", [128, 4, 2, 128], BF16) for hh in range(2)] for i in range(2)]
        for i in range(2):
            for hh in range(2):
                C.memset('pool', KR[i][hh][:], 0.0)
        KBg = [C.sb(f"KBg{i}", [128, 4, 2, 128], BF16) for i in range(2)]
        gam = [C.sb(f"gam{i}", [128, 4], F32) for i in range(2)]
        Mst = C.sb("rw_M", [128, 64], F32)
        Mb = [C.sb(f"rw_Mb{i}", [128, 64], BF16) for i in range(2)]
        f32t = {nm: C.sb("rwt_" + nm, [128, 512], F32) for nm in
                ('sw', 'a', 'ci', 'ei', 'ee', 'e1', 'e2', 'e3', 'kt', 'b', 'tK', 'tB')}
        f32t.update(kk=f32t['sw'], rs=f32t['ci'], kh=f32t['ei'], tk=f32t['ee'])
        b16t = {nm: C.sb("rwb_" + nm, [128, 512], BF16) for nm in ('ksq', 'kgT', 'bgT', 'prod')}
        Am = [[C.sb(f"Am{i}{hh}", [128, 4, 128], BF16) for hh in range(2)] for i in range(3)]
        XXi = [C.sb(f"XXi{i}", [128, 2, 128], BF16) for i in range(3)]
        XX = [[C.sb(f"XX{i}{j}", [128, 2, 2, 128], BF16) for j in range(2)] for i in range(3)]
        Pb = [[C.sb(f"Pb{i}{j}", [128, 2, 128], BF16) for j in range(2)] for i in range(3)]
        Xs = [C.sb(f"rwXs{i}", [128, 128], BF16) for i in range(2)]
        Un = [C.sb(f"rwUn{i}", [128, 128], BF16) for i in range(2)]
        fin = {nm: C.sb("rwf_" + nm, [128, 128], F32) for nm in ('o', 'osq', 'y', 'yo')}
        fst = C.sb("rwf_st", [128, 8, 2], F32)
        pL = C.ps("rw_pL", [128, 512], F32)
        pA = [C.ps(f"rw_pA{hh}", [128, 512], F32) for hh in range(2)]
        pI1 = C.ps("rw_pI1", [128, 512], F32)
        pI2 = C.ps("rw_pI2", [128, 512], F32)
        pTb = C.ps("rw_pTb", [128, 1024], BF16)
        pR1 = C.ps("rw_pR1", [128, 512], F32)
        pR2 = C.ps("rw_pR2", [128, 512], F32)

        xgroups = [(NCT + 4 * i, 4) for i in range(8)]
        for fc in range(2):
            for d in range(2):
                glist = [(0, NCT)] + (xgroups if d == 0 else xgroups[::-1])
                C.memset('dve', Mst[:], 0.0)
                C.memset('pool', Mb[0][:], 0.0)
                def do_prep(gi, c0, ncn):
                    pb = gi % 2
                    t0, n = c0 * 128, ncn * 128
                    t = f32t
                    rT_ = rkT[:, fc, t0:t0 + n]
                    kT_ = rkT[:, 2 + fc, t0:t0 + n]
                    sl = slice(0, n)
                    C.mm(pL[:, sl], wupz[:, d, fc * 128:(fc + 1) * 128], twz[:, t0:t0 + n])
                    C.act(t['sw'][:, sl], pL[:, sl], AF.Sigmoid, bias=w0T[:, d, fc:fc + 1])
                    C.mm(pL[:, sl], aupz[:, d, fc * 128:(fc + 1) * 128], twz[:, t0:t0 + n])
                    C.act(t['a'][:, sl], pL[:, sl], AF.Sigmoid, bias=a0T[:, d, fc:fc + 1])
                    C.op('dve', [resetm, t['sw']], [t['ci']],
                         lambda e_, o_=t['ci'][:, sl], a_=resetm[:, sl], b_=t['sw'][:, sl]:
                         e_.tensor_tensor_scan(out=o_, data0=a_, data1=b_, initial=0.0, op0=ALU.mult, op1=ALU.add))
                    ci3 = t['ci'][:, sl].rearrange("p (c t) -> p c t", t=128)
                    tot = ci3[:, :, 127:128]
                    if d == 1:
                        ei3 = t['ei'][:, sl].rearrange("p (c t) -> p c t", t=128)
                        C.tt('dve', ei3, tot.to_broadcast([128, ncn, 128]), ci3, ALU.subtract)
                        C.tt('dve', t['ei'][:, sl], t['ei'][:, sl], t['sw'][:, sl], ALU.add)
                        ei = t['ei']
                    else:
                        ei = t['ci']
                    C.tt('pool', t['ee'][:, sl], ei[:, sl], t['sw'][:, sl], ALU.subtract)
                    C.act(t['e1'][:, sl], t['ee'][:, sl], AF.Exp, scale=LAM)
                    C.act(t['e2'][:, sl], ei[:, sl], AF.Exp, scale=-LAM)
                    C.act(t['e3'][:, sl], ei[:, sl], AF.Exp, scale=LAM)
                    C.act(gam[pb][:, 0:ncn], ci3[:, :, 127], AF.Exp, scale=LAM)
                    C.ts('dve', t['kk'][:, sl], kT_, pp[:, 0, fc:fc + 1], None, ALU.mult)
                    C.tt('pool', b16t['ksq'][:, sl], t['kk'][:, sl], t['kk'][:, sl], ALU.mult)
                    C.mm(pL[:, sl], blk[:], b16t['ksq'][:, sl])
                    C.cp('act', t['rs'][:, sl], pL[:, sl])
                    C.ts('dve', t['rs'][:, sl], t['rs'][:, sl], 1e-12, None, ALU.max)
                    C.rpow(t['rs'][:, sl], t['rs'][:, sl], -0.5)
                    C.tt('dve', t['kh'][:, sl], t['kk'][:, sl], t['rs'][:, sl], ALU.mult)
                    C.ts('dve', t['tk'][:, sl], t['a'][:, sl], pp[:, 1, fc:fc + 1], pp[:, 2, fc:fc + 1], ALU.mult, ALU.add)
                    C.tt('pool', t['kt'][:, sl], t['tk'][:, sl], kT_, ALU.mult)
                    C.tt('pool', t['b'][:, sl], t['kh'][:, sl], t['a'][:, sl], ALU.mult)
                    for hh in range(2):
                        hs = slice(hh * 64, (hh + 1) * 64)
                        C.tt('dve', KR[pb][hh][hs, 0:ncn, 0, :], t['kh'][hs, sl].rearrange("p (c t) -> p c t", t=128),
                             t['e1'][hs, sl].rearrange("p (c t) -> p c t", t=128), ALU.mult, wk=[KR[pb][hh]])
                        C.tt('pool', KR[pb][hh][hs, 0:ncn, 1, :], rT_[hs, :].rearrange("p (c t) -> p c t", t=128),
                             t['e3'][hs, sl].rearrange("p (c t) -> p c t", t=128), ALU.mult, wk=[KR[pb][hh]])
                    C.tt('dve', t['tK'][:, sl], t['kt'][:, sl], t['e2'][:, sl], ALU.mult)
                    C.tt('pool', t['tB'][:, sl], t['b'][:, sl], t['e2'][:, sl], ALU.mult)
                    C.cp('act', ktI[pb][:, sl], t['tK'][:, sl])
                    C.cp('act', bI[pb][:, sl], t['tB'][:, sl])
                    gb = gam[pb][:, 0:ncn].unsqueeze(2).to_broadcast([128, ncn, 128])
                    C.tt('dve', b16t['kgT'][:, sl].rearrange("p (c t) -> p c t", t=128),
                         t['tK'][:, sl].rearrange("p (c t) -> p c t", t=128), gb, ALU.mult)
                    C.tt('pool', b16t['bgT'][:, sl].rearrange("p (c t) -> p c t", t=128),
                         t['tB'][:, sl].rearrange("p (c t) -> p c t", t=128), gb, ALU.mult)
                    for j in range(ncn):
                        C.tr(pTb[:, (2 * j) * 128:(2 * j + 1) * 128], b16t['kgT'][:, j * 128:(j + 1) * 128], ident_b[:])
                        C.tr(pTb[:, (2 * j + 1) * 128:(2 * j + 2) * 128], b16t['bgT'][:, j * 128:(j + 1) * 128], ident_b[:])
                    C.cp('act', KBg[pb][:, 0:ncn, :, :].rearrange("p c k f -> p (c k f)"), pTb[:, 0:ncn * 256])
                    C.stt(b16t['prod'][:, sl], rT_, pp[:, 3, fc:fc + 1], t['kt'][:, sl], ALU.mult, ALU.mult)
                    for j in range(ncn):
                        C.mm(pL[:, 2 * j:2 * j + 2], b16t['prod'][:, j * 128:(j + 1) * 128], blk2[:])
                    C.cp('act', bsum[d][:, c0:c0 + ncn, :].rearrange("p c k -> p (c k)"), pL[:, 0:2 * ncn])


                def gen_inv(idx, n_, gi, c0):
                    pb = gi % 2
                    j = n_ - c0
                    q3 = idx % 3
                    cs = slice(j * 128, (j + 1) * 128)
                    if True:
                        for hh in range(2):
                            kr = KR[pb][hh][:, j, :, :].rearrange("p a t -> p (a t)")
                            C.mm(pA[hh][:, 0:256], ktI[pb][:, cs], kr, rk=[ktI[pb], KR[pb][hh]])
                            C.mm(pA[hh][:, 256:512], bI[pb][:, cs], kr, rk=[bI[pb], KR[pb][hh]])
                            C.mm(pI2[:, 256 + hh * 128:256 + (hh + 1) * 128], KR[pb][hh][:, j, 0, :], bI[pb][:, cs], rk=[bI[pb], KR[pb][hh]])
                            C.tt('dve', Am[q3][hh][:].rearrange("p a t -> p (a t)"), pA[hh][:, :], mask4[:, d, :], ALU.mult)
                        C.tt('dve', XXi[q3][:].rearrange("p a t -> p (a t)"), pI2[:, 256:512], masknt[:, d, :], ALU.mult)
                        for hh in range(2):
                            C.tt('pool', Pb[q3][0][:, hh, :], Am[q3][hh][:, 2, :], ident_b[:], ALU.add, wk=[Pb[q3][0]])
                        yield
                        Xc = [Am[q3][0][:, 2, :], Am[q3][1][:, 2, :]]
                        Xk = [Am[q3][0], Am[q3][1]]
                        XTc = [XXi[q3][:, 0, :], XXi[q3][:, 1, :]]
                        XTk = [XXi[q3], XXi[q3]]
                        Pc = Pb[q3][0]
                        for lvl in range(6):
                            lastl = (lvl == 5)
                            nxt = XX[q3][lvl % 2]
                            for hh in range(2):
                                if not lastl:
                                    C.mm(pI1[:, hh * 128:(hh + 1) * 128], XTc[hh], Xc[hh], rk=[XTk[hh], Xk[hh]])
                                C.mm(pI1[:, 256 + hh * 128:256 + (hh + 1) * 128], Xc[hh], XTc[hh], rk=[XTk[hh], Xk[hh]])
                            if lastl:
                                C.cp('act', nxt[:, 1, :, :].rearrange("p h t -> p (h t)"), pI1[:, 256:512])
                            else:
                                C.cp('act', nxt[:].rearrange("p a h t -> p (a h t)"), pI1[:, :])
                            yield
                            for hh in range(2):
                                C.mm(pI2[:, hh * 128:(hh + 1) * 128], nxt[:, 1, hh, :], Pc[:, hh, :], rk=[nxt, Pc])
                            Pn = Pb[q3][(lvl + 1) % 2]
                            C.tt('dve', Pn[:].rearrange("p h t -> p (h t)"), pI2[:, 0:256], Pc[:].rearrange("p h t -> p (h t)"), ALU.add)
                            yield
                            Xc = [nxt[:, 0, 0, :], nxt[:, 0, 1, :]]
                            XTc = [nxt[:, 1, 0, :], nxt[:, 1, 1, :]]
                            Xk = [nxt, nxt]
                            XTk = [nxt, nxt]
                            Pc = Pn
                def gen_rec(idx, n_, gi, c0):
                    pb = gi % 2
                    j = n_ - c0
                    q3 = idx % 3
                    q2 = idx % 2
                    if True:
                        TT = Pb[q3][0]
                        cur = stt_['cur']
                        Mold = Mb[cur]
                        V = vtok[:, fc, n_, :]
                        for hh in range(2):
                            vs = slice(hh * 64, (hh + 1) * 64)
                            C.mm(pR1[:, vs], KR[pb][hh][:, j, 0, :], Mold[:], start=True, stop=False, rk=[KR[pb][hh], Mold])
                            C.mm(pR1[:, vs], Am[q3][hh][:, 0, :], V[:, vs], start=False, stop=True, rk=[Am[q3][hh], vtok])
                        C.cp('act', Xs[q2][:], pR1[:, 0:128])
                        yield
                        for hh in range(2):
                            vs = slice(hh * 64, (hh + 1) * 64)
                            C.mm(pR2[:, vs], TT[:, hh, :], Xs[q2][:, vs], rk=[TT, Xs[q2]])
                        C.ts('dve', Un[q2][:], pR2[:, 0:128], -1.0, None, ALU.mult)
                        yield
                        for hh in range(2):
                            vs = slice(hh * 64, (hh + 1) * 64)
                            C.mm(pR2[vs, 128:192], KBg[pb][:, j, 0, vs], V[:, vs], start=True, stop=False, rk=[KBg[pb], vtok])
                            C.mm(pR2[vs, 128:192], KBg[pb][:, j, 1, vs], Un[q2][:, vs], start=False, stop=True, rk=[KBg[pb], Un[q2]])
                        want_o = emit or n_ >= NCT
                        if want_o:
                            for hh in range(2):
                                vs = slice(128 + hh * 64, 128 + (hh + 1) * 64)
                                v2 = slice(hh * 64, (hh + 1) * 64)
                                C.mm(pR1[:, vs], KR[pb][hh][:, j, 1, :], Mold[:], start=True, stop=False, rk=[KR[pb][hh], Mold])
                                C.mm(pR1[:, vs], Am[q3][hh][:, 1, :], V[:, v2], start=False, stop=False, rk=[Am[q3][hh], vtok])
                                C.mm(pR1[:, vs], Am[q3][hh][:, 3, :], Un[q2][:, v2], start=False, stop=True, rk=[Am[q3][hh], Un[q2]])
                            C.cp('act', Oacc[d][:, n_, :], pR1[:, 128:256], wk=[(Oacc[d], n_)])
                        C.stt(Mst[:], Mst[:], gam[pb][:, j:j + 1], pR2[:, 128:192], ALU.mult, ALU.add)
                        C.cp('act', Mb[1 - cur][:], Mst[:])
                        stt_['cur'] = 1 - cur
                        yield

                seq = []
                for gi, (c0, ncn) in enumerate(glist):
                    chunks = list(range(c0, c0 + ncn)) if d == 0 else list(range(c0 + ncn - 1, c0 - 1, -1))
                    for n_ in chunks:
                        seq.append((n_, gi, c0, ncn))
                stt_ = {'cur': 0}
                prep_done = set()
                inv_started, inv_done, rec_i, rec_gen, active = 0, set(), 0, None, []
                while rec_i < len(seq):
                    while len(active) < 2 and inv_started < len(seq) and inv_started < rec_i + 3:
                        n_, gi, c0, ncn = seq[inv_started]
                        if gi not in prep_done:
                            do_prep(gi, c0, ncn)
                            prep_done.add(gi)
                        active.append((inv_started, gen_inv(inv_started, n_, gi, c0)))
                        inv_started += 1
                    for item in list(active):
                        try:
                            next(item[1])
                        except StopIteration:
                            inv_done.add(item[0])
                            active.remove(item)
                    if rec_gen is None and rec_i in inv_done:
                        n_, gi, c0, ncn = seq[rec_i]
                        rec_gen = gen_rec(rec_i, n_, gi, c0)
                    if rec_gen is not None:
                        try:
                            next(rec_gen)
                        except StopIteration:
                            rec_gen = None
                            rec_i += 1
            for n_ in range(NT):
                if not (emit or n_ >= NCT) or rstop <= 3:
                    continue
                f = fin
                C.tt('pool', f['o'][:], Oacc[0][:, n_, :], Oacc[1][:, n_, :], ALU.add, rk=[(Oacc[0], n_), (Oacc[1], n_)])
                C.tt('pool', f['osq'][:], f['o'][:], f['o'][:], ALU.mult)
                o3 = f['o'][:].rearrange("p (h e) -> p h e", e=64)
                C.red('dve', fst[:, 0, :], o3, ALU.add)
                C.red('dve', fst[:, 1, :], f['osq'][:].rearrange("p (h e) -> p h e", e=64), ALU.add)
                C.ts('pool', fst[:, 2, :], fst[:, 0, :], 1.0 / 64, None, ALU.mult)
                C.tt('pool', fst[:, 3, :], fst[:, 2, :], fst[:, 2, :], ALU.mult)
                C.ts('pool', fst[:, 4, :], fst[:, 1, :], 1.0 / 64, 64e-5, ALU.mult, ALU.add)
                C.tt('pool', fst[:, 4, :], fst[:, 4, :], fst[:, 3, :], ALU.subtract)
                C.rpow(fst[:, 5, :], fst[:, 4, :], -0.5)
                C.tt('pool', fst[:, 6, :], bsum[0][:, n_, :], bsum[1][:, n_, :], ALU.add)
                y3 = f['y'][:].rearrange("p (h e) -> p h e", e=64)
                C.tt('dve', y3, o3, fst[:, 2, :].unsqueeze(2).to_broadcast([128, 2, 64]), ALU.subtract)
                C.tt('dve', y3, y3, fst[:, 5, :].unsqueeze(2).to_broadcast([128, 2, 64]), ALU.mult)
                C.tt('pool', f['y'][:], f['y'][:], gng[:, fc * 128:(fc + 1) * 128], ALU.mult)
                for hh in range(2):
                    vs = slice(hh * 64, (hh + 1) * 64)
                    C.stt(f['y'][:, vs], vtok[:, fc, n_, vs], fst[:, 6, hh:hh + 1], f['y'][:, vs], ALU.mult, ALU.add)
                C.mm(pR2[:, 0:128], sgd[:, n_ * 128:(n_ + 1) * 128], gup[:, fc * 128:(fc + 1) * 128])
                C.tt('dve', f['yo'][:], f['y'][:], pR2[:, 0:128], ALU.mult)
                C.dma('sp', ycat[n_ * 128:(n_ + 1) * 128, 512 + fc * 128:512 + (fc + 1) * 128], f['yo'][:], wk=(ycat, n_))


def MIXERS(env):
    E = dict(env)
    E['emit'] = env['l'] < DEPTH - 1
    which = env['debug_mixers'] if env.get('debug_mixers') else ('swa', 'ret', 'rwkv', 'mla')
    if 'swa' in which:
        mixer_swa(E)
    if 'ret' in which:
        mixer_ret(E)
    if 'rwkv' in which:
        rwkv_project(E)
    if 'mla' in which:
        with E['C'].scope():
            E['ydg'] = [E['C'].sb(f"ydg{j}", [128, 256], F32) for j in range(4)]
            mixer_mla(E)


_PROG = {}


def kernel(**inputs):
    if 'nc' not in _PROG:
        _PROG['nc'] = build_program()[0]
    nc = _PROG['nc']
    consts = make_consts()
    f32 = lambda a: np.ascontiguousarray(np.asarray(a, dtype=np.float32))
    shared = {k: f32(v) for k, v in inputs.items() if k not in ('x', 'c', 'ctx')}
    shared.update({'k_' + k: v for k, v in consts.items()})
    x, c, ctx = f32(inputs['x']), f32(inputs['c']), f32(inputs['ctx'])
    in_maps = []
    for b in range(8):
        m = dict(shared)
        m['x'] = x[b]
        m['c'] = c[b]
        m['ctx'] = ctx[b]
        in_maps.append(m)
    res = run_bass_kernel_spmd(nc, in_maps, core_ids=list(range(8)))
    return np.stack([np.asarray(r['out'], dtype=np.float32) for r in res.results], axis=0)
```
